# Optimizing a Trainium2 kernel written in Bass

```python
import math, functools
import jax, jax.numpy as jnp
from jax import lax
import numpy as np

D_MODEL = 1024
BATCH = 4
SEQ = 4096
DEPTH = 2

N_HEADS_A = 8
HEAD_DIM = 64
N_KV_GROUPS = 2
Q_PER_GROUP = N_HEADS_A // N_KV_GROUPS
D_ATTN = N_HEADS_A * HEAD_DIM
D_KV = N_KV_GROUPS * HEAD_DIM
CMP_BLOCK = 32
CMP_STRIDE = 16
CMP_HIDDEN = 256
SLC_BLOCK = 64
N_SELECT = 16
N_LOCAL = 2
WINDOW = 512
Q_CHUNK = 128
ATTN_SCALE = HEAD_DIM ** -0.5
NEG = -1e30
FORCE_SCORE = 1e4
D_RNN = D_MODEL
N_RNN_BLOCKS = 16
RNN_BLOCK = D_RNN // N_RNN_BLOCKS
CONV_WIDTH = 4
LRU_C = 8.0
D_FF = 4 * D_MODEL
EPS = 1e-6

IN_SIZES = (D_ATTN, D_KV, D_KV, D_KV, D_KV, D_KV, D_KV, N_HEADS_A * 3, D_RNN, D_RNN, D_MODEL, D_MODEL)
IN_SPLITS = tuple(int(v) for v in np.cumsum(IN_SIZES)[:-1])
D_IN = int(sum(IN_SIZES))

kernel_name = "hybrid_nsa_rglru_gated_block"


def rms_norm(x, w):
    xf = x.astype(jnp.float32)
    var = jnp.mean(xf * xf, axis=-1, keepdims=True)
    return (xf * lax.rsqrt(var + EPS) * w.astype(jnp.float32)).astype(x.dtype)


def masked_softmax(s, mask):
    s = jnp.where(mask, s.astype(jnp.float32), NEG)
    m = jnp.max(s, axis=-1, keepdims=True)
    e = jnp.where(mask, jnp.exp(s - m), 0.0)
    return e / jnp.maximum(jnp.sum(e, axis=-1, keepdims=True), 1e-20)


def compress_blocks(k, pos, w1, w2):
    B, S = k.shape[0], k.shape[1]
    kb = k.reshape(B, S // CMP_STRIDE, CMP_STRIDE, N_KV_GROUPS, HEAD_DIM)
    blocks = jnp.concatenate([kb[:, :-1], kb[:, 1:]], axis=2) + pos[:, None, :]
    nc = blocks.shape[1]
    flat = blocks.transpose(0, 1, 3, 2, 4).reshape(B, nc, N_KV_GROUPS, CMP_BLOCK * HEAD_DIM)
    c = jax.nn.gelu(flat @ w1) @ w2
    return c.transpose(0, 2, 1, 3)


def nsa_attention(q, k_cmp, v_cmp, k_slc, v_slc, k_win, v_win, gates, pos_k, pos_v, ck_w1, ck_w2, cv_w1, cv_w2):
    B, S = q.shape[0], q.shape[1]
    G, R, dh = N_KV_GROUPS, Q_PER_GROUP, HEAD_DIM
    nb = S // SLC_BLOCK
    nc = S // CMP_STRIDE - 1
    nqb = S // Q_CHUNK
    k_sel = min(N_SELECT, nb)
    r_s = SLC_BLOCK // CMP_STRIDE
    r_c = CMP_BLOCK // CMP_STRIDE
    off = r_s + r_c - 2

    qh = q.reshape(B, S, G, R, dh).transpose(0, 2, 3, 1, 4)
    gh = gates.reshape(B, S, G, R, 3).transpose(0, 2, 3, 1, 4)
    kc = compress_blocks(k_cmp, pos_k, ck_w1, ck_w2)
    vc = compress_blocks(v_cmp, pos_v, cv_w1, cv_w2)
    ks_blk = k_slc.transpose(0, 2, 1, 3).reshape(B, G, nb, SLC_BLOCK, dh)
    vs_blk = v_slc.transpose(0, 2, 1, 3).reshape(B, G, nb, SLC_BLOCK, dh)
    pad_w = ((0, 0), (0, 0), (WINDOW, 0), (0, 0))
    kw_pad = jnp.pad(k_win.transpose(0, 2, 1, 3), pad_w)
    vw_pad = jnp.pad(v_win.transpose(0, 2, 1, 3), pad_w)
    cmp_end = jnp.arange(nc) * CMP_STRIDE + CMP_BLOCK - 1
    blk = jnp.arange(nb)
    gather = jax.vmap(jax.vmap(lambda blocks, ix: blocks[ix]))

    def chunk(qb):
        t0 = qb * Q_CHUNK
        qc = lax.dynamic_slice_in_dim(qh, t0, Q_CHUNK, axis=3)
        gc = lax.dynamic_slice_in_dim(gh, t0, Q_CHUNK, axis=3)
        tpos = t0 + jnp.arange(Q_CHUNK)
        s = jnp.einsum('bgrcd,bgnd->bgrcn', qc, kc) * ATTN_SCALE
        p_cmp = masked_softmax(s, cmp_end[None, :] <= tpos[:, None])
        o_cmp = jnp.einsum('bgrcn,bgnd->bgrcd', p_cmp, vc.astype(jnp.float32))
        ps = jnp.pad(p_cmp.sum(axis=2), ((0, 0), (0, 0), (0, 0), (off, 0)))
        imp = 0.0
        for m in range(r_s):
            for n in range(r_c):
                st = off - m - n
                imp = imp + ps[..., st:st + r_s * (nb - 1) + 1:r_s]
        cur = tpos // SLC_BLOCK
        valid_b = blk[None, :] <= cur[:, None]
        forced = (blk[None, :] == 0) | (blk[None, :] > cur[:, None] - N_LOCAL)
        score = jnp.where(valid_b, jnp.where(forced, FORCE_SCORE, imp), NEG)
        top_s, idx = lax.top_k(score, k_sel)
        sel_ok = top_s > NEG / 2
        ksel = gather(ks_blk, idx).reshape(B, G, Q_CHUNK, k_sel * SLC_BLOCK, dh)
        vsel = gather(vs_blk, idx).reshape(B, G, Q_CHUNK, k_sel * SLC_BLOCK, dh)
        kpos = idx[..., None] * SLC_BLOCK + jnp.arange(SLC_BLOCK)
        m_slc = (sel_ok[..., None] & (kpos <= tpos[:, None, None])).reshape(B, G, 1, Q_CHUNK, k_sel * SLC_BLOCK)
        s = jnp.einsum('bgrcd,bgcnd->bgrcn', qc, ksel) * ATTN_SCALE
        o_slc = jnp.einsum('bgrcn,bgcnd->bgrcd', masked_softmax(s, m_slc), vsel.astype(jnp.float32))
        kw = lax.dynamic_slice_in_dim(kw_pad, t0, Q_CHUNK + WINDOW, axis=2)
        vw = lax.dynamic_slice_in_dim(vw_pad, t0, Q_CHUNK + WINDOW, axis=2)
        kpos_w = t0 - WINDOW + jnp.arange(Q_CHUNK + WINDOW)
        diff = tpos[:, None] - kpos_w[None, :]
        m_win = (diff >= 0) & (diff < WINDOW) & (kpos_w[None, :] >= 0)
        s = jnp.einsum('bgrcd,bgnd->bgrcn', qc, kw) * ATTN_SCALE
        o_win = jnp.einsum('bgrcn,bgnd->bgrcd', masked_softmax(s, m_win), vw.astype(jnp.float32))
        out = gc[..., 0:1] * o_cmp + gc[..., 1:2] * o_slc + gc[..., 2:3] * o_win
        return out.astype(q.dtype)

    res = lax.map(chunk, jnp.arange(nqb))
    return res.transpose(1, 0, 4, 2, 3, 5).reshape(B, S, D_ATTN)


def rg_lru_branch(xr, gate, conv_w, conv_b, w_a, b_a, w_i, b_i, lam):
    B, S = xr.shape[0], xr.shape[1]
    xp = jnp.pad(xr, ((0, 0), (CONV_WIDTH - 1, 0), (0, 0)))
    xc = conv_b
    for i in range(CONV_WIDTH):
        xc = xc + xp[:, i:i + S] * conv_w[i]
    xb = xc.reshape(B, S, N_RNN_BLOCKS, RNN_BLOCK)
    r = jax.nn.sigmoid(jnp.einsum('bshi,hij->bshj', xb, w_a).reshape(B, S, D_RNN) + b_a)
    ig = jax.nn.sigmoid(jnp.einsum('bshi,hij->bshj', xb, w_i).reshape(B, S, D_RNN) + b_i)
    log_a = -LRU_C * r.astype(jnp.float32) * jax.nn.softplus(-lam.astype(jnp.float32))
    a = jnp.exp(log_a)
    bt = jnp.sqrt(-jnp.expm1(2.0 * log_a)) * (ig * xc).astype(jnp.float32)

    def combine(lhs, rhs):
        a1, b1 = lhs
        a2, b2 = rhs
        return a1 * a2, a2 * b1 + b2

    _, h = lax.associative_scan(combine, (a, bt), axis=1)
    return h.astype(xr.dtype) * jax.nn.gelu(gate)


def setup_inputs(seed: int = 0) -> dict:
    key = jax.random.key(seed)
    ks = jax.random.split(key, 24)
    f32 = jnp.float32

    def nrm(k, shape, fan_in):
        return jax.random.normal(k, shape, f32) * (fan_in ** -0.5)

    a_c = jax.random.uniform(ks[17], (DEPTH, D_RNN), f32, 0.9, 0.999)
    s_l = a_c ** (1.0 / LRU_C)
    lam = jnp.log(s_l) - jnp.log1p(-s_l)
    return {
        "x": jax.random.normal(ks[0], (BATCH, SEQ, D_MODEL), f32),
        "norm1_w": 1.0 + 0.02 * jax.random.normal(ks[1], (DEPTH, D_MODEL), f32),
        "w_in": nrm(ks[2], (DEPTH, D_MODEL, D_IN), D_MODEL),
        "cmp_pos_k": 0.1 * jax.random.normal(ks[3], (DEPTH, CMP_BLOCK, HEAD_DIM), f32),
        "cmp_pos_v": 0.1 * jax.random.normal(ks[4], (DEPTH, CMP_BLOCK, HEAD_DIM), f32),
        "cmp_k_w1": nrm(ks[5], (DEPTH, CMP_BLOCK * HEAD_DIM, CMP_HIDDEN), CMP_BLOCK * HEAD_DIM),
        "cmp_k_w2": nrm(ks[6], (DEPTH, CMP_HIDDEN, HEAD_DIM), CMP_HIDDEN),
        "cmp_v_w1": nrm(ks[7], (DEPTH, CMP_BLOCK * HEAD_DIM, CMP_HIDDEN), CMP_BLOCK * HEAD_DIM),
        "cmp_v_w2": nrm(ks[8], (DEPTH, CMP_HIDDEN, HEAD_DIM), CMP_HIDDEN),
        "conv_w": nrm(ks[9], (DEPTH, CONV_WIDTH, D_RNN), CONV_WIDTH),
        "conv_b": 0.01 * jax.random.normal(ks[10], (DEPTH, D_RNN), f32),
        "lru_w_a": nrm(ks[11], (DEPTH, N_RNN_BLOCKS, RNN_BLOCK, RNN_BLOCK), RNN_BLOCK),
        "lru_b_a": 0.01 * jax.random.normal(ks[12], (DEPTH, D_RNN), f32),
        "lru_w_i": nrm(ks[13], (DEPTH, N_RNN_BLOCKS, RNN_BLOCK, RNN_BLOCK), RNN_BLOCK),
        "lru_b_i": 0.01 * jax.random.normal(ks[14], (DEPTH, D_RNN), f32),
        "lru_lambda": lam,
        "w_up_attn": nrm(ks[15], (DEPTH, D_ATTN, D_MODEL), D_ATTN),
        "w_up_rnn": nrm(ks[16], (DEPTH, D_RNN, D_MODEL), D_RNN),
        "w_out": nrm(ks[18], (DEPTH, D_MODEL, D_MODEL), D_MODEL),
        "norm2_w": 1.0 + 0.02 * jax.random.normal(ks[19], (DEPTH, D_MODEL), f32),
        "mlp_w1": nrm(ks[20], (DEPTH, D_MODEL, D_FF), D_MODEL),
        "mlp_w2": nrm(ks[21], (DEPTH, D_FF, D_MODEL), D_FF),
        "final_norm_w": 1.0 + 0.02 * jax.random.normal(ks[22], (D_MODEL,), f32),
    }


def reference(x, norm1_w, w_in, cmp_pos_k, cmp_pos_v, cmp_k_w1, cmp_k_w2, cmp_v_w1, cmp_v_w2,
              conv_w, conv_b, lru_w_a, lru_b_a, lru_w_i, lru_b_i, lru_lambda,
              w_up_attn, w_up_rnn, w_out, norm2_w, mlp_w1, mlp_w2, final_norm_w):
    B, S = x.shape[0], x.shape[1]
    for l in range(DEPTH):
        xn = rms_norm(x, norm1_w[l])
        z = xn @ w_in[l]
        (q, k_c, v_c, k_s, v_s, k_w, v_w, g_nsa, xr, gr, g_a, g_b) = jnp.split(z, IN_SPLITS, axis=-1)
        kv_shape = (B, S, N_KV_GROUPS, HEAD_DIM)
        attn = nsa_attention(
            q.reshape(B, S, N_HEADS_A, HEAD_DIM),
            k_c.reshape(kv_shape), v_c.reshape(kv_shape),
            k_s.reshape(kv_shape), v_s.reshape(kv_shape),
            k_w.reshape(kv_shape), v_w.reshape(kv_shape),
            jax.nn.sigmoid(g_nsa).reshape(B, S, N_HEADS_A, 3),
            cmp_pos_k[l], cmp_pos_v[l], cmp_k_w1[l], cmp_k_w2[l], cmp_v_w1[l], cmp_v_w2[l])
        rnn = rg_lru_branch(xr, gr, conv_w[l], conv_b[l], lru_w_a[l], lru_b_a[l],
                            lru_w_i[l], lru_b_i[l], lru_lambda[l])
        merged = jax.nn.sigmoid(g_a) * (attn @ w_up_attn[l]) + jax.nn.sigmoid(g_b) * (rnn @ w_up_rnn[l])
        x = x + merged @ w_out[l]
        hn = rms_norm(x, norm2_w[l])
        x = x + jnp.square(jax.nn.relu(hn @ mlp_w1[l])) @ mlp_w2[l]
    return rms_norm(x, final_norm_w)
```

```python
import numpy as np
import ml_dtypes
from contextlib import ExitStack
import concourse.bass as bass
import concourse.mybir as mybir
from concourse.bass_utils import run_bass_kernel_spmd

F32 = mybir.dt.float32
BF16 = mybir.dt.bfloat16
AF = mybir.ActivationFunctionType
ALU = mybir.AluOpType
AX = mybir.AxisListType

S = 4096
D = 1024
DIN = 5400
NT = S // 128
NEGM = -30000.0
EPS = 1e-6
O_Q, O_KC, O_VC, O_KS, O_VS, O_KW, O_VW, O_GN, O_XR, O_GR, O_GA, O_GB = 0, 512, 640, 768, 896, 1024, 1152, 1280, 1304, 2328, 3352, 4376


ATT_DBG = {"level": 6, "nqb": NT}


class Tok:
    __slots__ = ("w", "r", "sem", "name", "x")

    def __init__(self, name="", x=False):
        self.w = {}
        self.r = {}
        self.sem = None
        self.name = name
        self.x = x


class Prog:
    ENG = ("pe", "act", "dve", "pool", "sp")

    def __init__(self, nc, es, n_dma_sems=80):
        self.nc = nc
        self.eng = {"pe": nc.tensor, "act": nc.scalar, "dve": nc.vector, "pool": nc.gpsimd, "sp": nc.sync}
        self.sems = []
        self.esem = {}
        for e in self.ENG:
            self.esem[e] = len(self.sems)
            self.sems.append(es.enter_context(nc.semaphore("es_" + e)))
        self.dma_ids = []
        for i in range(n_dma_sems):
            self.dma_ids.append(len(self.sems))
            self.sems.append(es.enter_context(nc.semaphore("ds_%d" % i)))
        self.free = list(self.dma_ids)
        self.total = [0] * len(self.sems)
        self.known = {e: [0] * len(self.sems) for e in self.ENG}
        self.nwait = 0
        self.nops = 0

    def _wait(self, eng, deps):
        E = self.eng[eng]
        kn = self.known[eng]
        for s, v in deps.items():
            if s >= 5:
                v = self.total[s]
            if kn[s] < v:
                kn[s] = v
                E.wait_ge(self.sems[s], v)
                self.nwait += 1

    @staticmethod
    def _merge(d, src):
        for s, v in src.items():
            if d.get(s, 0) < v:
                d[s] = v

    def op(self, eng, fn, r=(), w=()):
        deps = {}
        rx = [b for b in r if b.x]
        if rx:
            r = [b for b in r if not b.x]
            w = list(w) + rx
        for b in r:
            self._merge(deps, b.w)
        for b in w:
            self._merge(deps, b.w)
            self._merge(deps, b.r)
        s = self.esem[eng]
        if eng == "pe":
            deps.pop(s, None)
        self._wait(eng, deps)
        self.total[s] += 1
        n = self.total[s]
        fn(self.eng[eng]).then_inc(self.sems[s], 1)
        self.nops += 1
        for b in r:
            b.r[s] = n
        for b in w:
            b.w[s] = n

    def dma(self, eng, out, in_, src, dst, owner):
        deps = {}
        self._merge(deps, src.w)
        self._merge(deps, dst.w)
        self._merge(deps, dst.r)
        self._wait(eng, deps)
        if owner.sem is None:
            owner.sem = self.free.pop()
        s = owner.sem
        self.total[s] += 16
        v = self.total[s]
        self.eng[eng].dma_start(out=out, in_=in_).then_inc(self.sems[s], 16)
        self.nops += 1
        src.r[s] = v
        dst.w[s] = v

    def release(self, toks):
        for t in toks:
            if t.sem is not None:
                self.free.append(t.sem)
                t.sem = None

    def barrier(self):
        for e in self.ENG:
            E = self.eng[e]
            kn = self.known[e]
            for s in range(len(self.sems)):
                if s == self.esem[e]:
                    continue
                v = self.total[s]
                if kn[s] < v:
                    kn[s] = v
                    E.wait_ge(self.sems[s], v)
        arr = {}
        for e in self.ENG:
            s = self.esem[e]
            self.total[s] += 1
            arr[e] = self.total[s]
            if e == "pe":
                self.eng[e].nop().then_inc(self.sems[s], 1) if hasattr(self.eng[e], "nop") else None
            else:
                self.eng[e].nop().then_inc(self.sems[s], 1)
        for e in self.ENG:
            for f in self.ENG:
                if f == e:
                    continue
                s = self.esem[f]
                self.known[e][s] = arr[f]
                self.eng[e].wait_ge(self.sems[s], arr[f])


class Scope:
    def __init__(self, P):
        self.P = P
        self.es = ExitStack()
        self.toks = []

    def __enter__(self):
        self.es.__enter__()
        return self

    def __exit__(self, *a):
        self.P.barrier()
        self.P.release(self.toks)
        return self.es.__exit__(*a)

    uid = [0]

    def sb(self, name, shape, dt):
        Scope.uid[0] += 1
        return self.es.enter_context(self.P.nc.sbuf_tensor("%s_%d" % (name, Scope.uid[0]), list(shape), dt))

    def ps(self, name, shape, dt):
        Scope.uid[0] += 1
        return self.es.enter_context(self.P.nc.psum_tensor("%s_%d" % (name, Scope.uid[0]), list(shape), dt))

    def tok(self, name="", x=False):
        t = Tok(name, x)
        self.toks.append(t)
        return t

    def sbt(self, name, shape, dt):
        return self.sb(name, shape, dt), self.tok(name)

    def pst(self, name, shape, dt):
        return self.ps(name, shape, dt), self.tok(name, True)


def build(debug=False):
    nc = bass.Bass("TRN2", target_bir_lowering=False)
    dbg = {}

    def din(name, shape, dt=F32):
        return nc.dram_tensor(name, list(shape), dt, kind="ExternalInput").ap()

    def dscr(name, shape, dt):
        isd = bool(debug) and (debug is True or name in debug)
        kind = "ExternalOutput" if isd else "Internal"
        t = nc.dram_tensor(name, list(shape), dt, kind=kind).ap()
        if isd:
            dbg[name] = t
        return t

    x_in = din("x", [S, D])
    out_d = nc.dram_tensor("out", [S, D], F32, kind="ExternalOutput").ap()
    w_in = din("w_in", [2, D, DIN])
    n1w = din("n1w", [2, 128, 8])
    n2w = din("n2w", [2, 128, 8])
    fnw = din("fnw", [128, D])
    posk = din("posk", [2, 64, 32, 2])
    posv = din("posv", [2, 64, 32, 2])
    ckw1 = din("ckw1", [2, 64, 32, 256])
    cvw1 = din("cvw1", [2, 64, 32, 256])
    ckw2 = din("ckw2", [2, 128, 2, 64])
    cvw2 = din("cvw2", [2, 128, 2, 64])
    convw = din("convw", [2, 128, 8, 4])
    convb = din("convb", [2, 128, 8])
    lba = din("lba", [2, 128, 8])
    lbi = din("lbi", [2, 128, 8])
    llam = din("llam", [2, 128, 8])
    lwa = din("lwa", [2, 8, 128, 128])
    lwi = din("lwi", [2, 8, 128, 128])
    wua = din("wua", [2, 512, D])
    wur = din("wur", [2, D, D])
    wo = din("wo", [2, D, D])
    w1 = din("w1", [2, D, 4096])
    w2 = din("w2", [2, 4096, D])
    c_ident = din("c_ident", [128, 128], BF16)
    c_tric = din("c_tric", [128, 128], BF16)
    c_triw = din("c_triw", [128, 128], BF16)
    c_E = din("c_E", [64, S], BF16)
    c_band = din("c_band", [128, 9], BF16)
    c_A = din("c_A", [128, 128])
    c_B = din("c_B", [128, 128])

    xs = [dscr("xs0", [S, D], F32), dscr("xs1", [S, D], F32)]
    qT = dscr("qT", [8, 64, S], BF16)
    kcT = dscr("kcT", [2, 64, S], BF16)
    vcT = dscr("vcT", [2, 64, S], BF16)
    ksT = dscr("ksT", [2, 64, S], BF16)
    kwT = dscr("kwT", [2, 64, S], BF16)
    vtm = dscr("vtm", [S, 4, 64], BF16)
    gat = dscr("gat", [S, 24], F32)
    zf = dscr("zf", [4, D, S], F32)
    attnT = dscr("attnT", [512, S], BF16)
    rnnT = dscr("rnnT", [D, S], BF16)

    with ExitStack() as es:
        P = Prog(nc, es)
        T = {n: Tok(n) for n in ["x_in", "out", "w", "xs0", "xs1", "qT", "kcT", "vcT", "ksT", "kwT", "vtm", "gat", "zf", "attnT", "rnnT"]}
        for t in T.values():
            t.sem = None

        ident = es.enter_context(nc.sbuf_tensor("ident", [128, 128], BF16))
        t_const = Tok("const")
        P.dma("sp", ident[:], c_ident[:, :], T["w"], t_const, t_const)

        def convert(sc, dst_ap_fn, src_ap_fn, nrow_tiles, ncols, stg, scale_ap_fn=None, chunk=2048, rows=128, toks=None):
            i = 0
            engs = ("act", "dve", "pool")
            for kt in range(nrow_tiles):
                for c0 in range(0, ncols, chunk):
                    c1 = min(ncols, c0 + chunk)
                    st, stt = stg[i % len(stg)]
                    P.dma("sp", st[0:rows, 0:c1 - c0], src_ap_fn(kt, c0, c1), T["w"], stt, stt)
                    e = engs[i % 3]
                    tk = toks[kt] if toks is not None else None
                    dst = dst_ap_fn(kt, c0, c1)
                    src = st[0:rows, 0:c1 - c0]
                    if scale_ap_fn is None:
                        if e == "act":
                            P.op(e, lambda E, dst=dst, src=src: E.copy(out=dst, in_=src), r=[stt], w=[tk])
                        else:
                            P.op(e, lambda E, dst=dst, src=src: E.tensor_copy(out=dst, in_=src), r=[stt], w=[tk])
                    else:
                        sc_ap, sc_tok = scale_ap_fn(kt)
                        if e == "act":
                            P.op(e, lambda E, dst=dst, src=src, sc_ap=sc_ap: E.activation(out=dst, in_=src, func=AF.Copy, scale=sc_ap), r=[stt, sc_tok], w=[tk])
                        else:
                            P.op(e, lambda E, dst=dst, src=src, sc_ap=sc_ap: E.tensor_scalar(out=dst, in0=src, scalar1=sc_ap, scalar2=None, op0=ALU.mult), r=[stt, sc_tok], w=[tk])
                    i += 1

        def rms_rstd(sc, xt, xtok, junk, junktok, ssq, rstd, sstok):
            P.op("act", lambda E: E.activation(out=junk[:], in_=xt[:], func=AF.Square, accum_out=ssq[:, 0:1]), r=[xtok], w=[junktok, sstok])
            P.op("act", lambda E: E.activation(out=ssq[:, 1:2], in_=ssq[:, 0:1], func=AF.Sqrt, scale=1.0 / D, bias=epsb[:, 0:1]), r=[sstok, t_const], w=[sstok])
            P.op("dve", lambda E: E.reciprocal(out=rstd[:, 0:1], in_=ssq[:, 1:2]), r=[sstok], w=[sstok])

        epsb = es.enter_context(nc.sbuf_tensor("epsb", [128, 4], F32))
        P.op("dve", lambda E: E.memset(epsb[:, 0:1], EPS), w=[t_const])
        P.op("dve", lambda E: E.memset(epsb[:, 1:2], 1.0), w=[t_const])
        P.op("dve", lambda E: E.memset(epsb[:, 2:3], 0.0), w=[t_const])

        def phase_inproj(l, xsrc, xtok):
            with Scope(P) as sc:
                wbf = sc.sb("wbf", [128, 8, DIN], BF16)
                wtok = [sc.tok("wbf%d" % k) for k in range(8)]
                n1 = sc.sb("n1", [128, 8], F32)
                n1t = sc.tok()
                P.dma("sp", n1[:], n1w[l], T["w"], n1t, n1t)
                stg = [sc.sbt("stg%d" % i, [128, 1800], F32) for i in range(3)]
                convert(sc, lambda kt, c0, c1: wbf[:, kt, c0:c1], lambda kt, c0, c1: w_in[l, kt * 128:(kt + 1) * 128, c0:c1], 8, DIN, stg,
                        scale_ap_fn=lambda kt: (n1[:, kt:kt + 1], n1t), chunk=1800, toks=wtok)
                xt = [sc.sbt("xt%d" % i, [128, D], F32) for i in range(2)]
                junk, junkt = sc.sbt("junk", [128, D], BF16)
                ssq = [sc.sbt("ssq%d" % i, [128, 4], F32) for i in range(2)]
                xn = [sc.sbt("xn%d" % i, [128, D], BF16) for i in range(2)]
                xnT = [sc.sbt("xnT%d" % i, [128, 8, 512], BF16) for i in range(2)]
                tp = [sc.pst("tp%d" % i, [128, 8, 128], BF16) for i in range(2)]
                pf = [sc.pst("pf%d" % i, [128, 512], F32) for i in range(4)]
                ptm = [sc.pst("ptm", [128, 512], F32)]
                of32 = [sc.sbt("of32_%d" % i, [128, 512], F32) for i in range(4)]
                obf = [sc.sbt("obf_%d" % i, [128, 512], BF16) for i in range(4)]
                vst = [sc.sbt("vst%d" % i, [128, 256], BF16) for i in range(2)]
                gst = [sc.sbt("gst%d" % i, [128, 24], F32) for i in range(2)]
                cnt = {"pf": 0, "f": 0, "b": 0}
                for sg in range(S // 512):
                    xT, xTt = xnT[sg % 2]
                    for j in range(4):
                        tt = sg * 4 + j
                        x_t, x_tt = xt[tt % 2]
                        sq, sqt = ssq[tt % 2]
                        xb, xbt = xn[tt % 2]
                        tpp, tpt = tp[tt % 2]
                        P.dma("sp", x_t[:], xsrc[tt * 128:(tt + 1) * 128, :], xtok, x_tt, x_tt)
                        rms_rstd(sc, x_t, x_tt, junk, junkt, sq, sq[:, 2:3], sqt)
                        P.op("dve", lambda E, xb=xb, x_t=x_t, sq=sq: E.tensor_scalar(out=xb[:], in0=x_t[:], scalar1=sq[:, 2:3], scalar2=None, op0=ALU.mult), r=[x_tt, sqt], w=[xbt])
                        for kt in range(8):
                            P.op("pe", lambda E, tpp=tpp, xb=xb, kt=kt: E.transpose(out=tpp[:, kt, :], in_=xb[:, kt * 128:(kt + 1) * 128], identity=ident[:]), r=[xbt, t_const], w=[tpt])
                        P.op("act", lambda E, xT=xT, tpp=tpp, j=j: E.copy(out=xT[:, :, j * 128:(j + 1) * 128], in_=tpp[:]), r=[tpt], w=[xTt])
                        pt, ptt = ptm[0]
                        for (c0, n, o0) in ((O_VS, 128, 0), (O_VW, 128, 128), (O_GN, 24, 256)):
                            for kt in range(8):
                                P.op("pe", lambda E, pt=pt, xT=xT, kt=kt, c0=c0, n=n, o0=o0, j=j: E.matmul(pt[:, o0:o0 + n], lhsT=xT[:, kt, j * 128:(j + 1) * 128], rhs=wbf[:, kt, c0:c0 + n], start=(kt == 0), stop=(kt == 7)),
                                     r=[xTt, wtok[kt]], w=[ptt])
                        vs_, vst_ = vst[tt % 2]
                        gs_, gst_ = gst[tt % 2]
                        P.op("dve", lambda E, vs_=vs_, pt=pt: E.tensor_copy(out=vs_[:], in_=pt[:, 0:256]), r=[ptt], w=[vst_])
                        P.op("act", lambda E, gs_=gs_, pt=pt: E.activation(out=gs_[:], in_=pt[:, 256:280], func=AF.Sigmoid), r=[ptt], w=[gst_])
                        P.dma("pool", vtm[tt * 128:(tt + 1) * 128].rearrange("p a d -> p (a d)"), vs_[:], vst_, T["vtm"], vst_)
                        P.dma("pool", gat[tt * 128:(tt + 1) * 128, :], gs_[:], gst_, T["gat"], gst_)
                    tsl = slice(sg * 512, (sg + 1) * 512)
                    jobs = []
                    for h in range(8):
                        jobs.append((O_Q + h * 64, 64, "q", qT[h, :, tsl], "qT"))
                    for g in range(2):
                        jobs.append((O_KC + g * 64, 64, "c", kcT[g, :, tsl], "kcT"))
                        jobs.append((O_VC + g * 64, 64, "c", vcT[g, :, tsl], "vcT"))
                        jobs.append((O_KS + g * 64, 64, "c", ksT[g, :, tsl], "ksT"))
                        jobs.append((O_KW + g * 64, 64, "c", kwT[g, :, tsl], "kwT"))
                    for ft in range(8):
                        jobs.append((O_XR + ft * 128, 128, "f", zf[0, ft * 128:(ft + 1) * 128, tsl], "zf"))
                    for ft in range(8):
                        jobs.append((O_GR + ft * 128, 128, "gelu", zf[1, ft * 128:(ft + 1) * 128, tsl], "zf"))
                    for ft in range(8):
                        jobs.append((O_GA + ft * 128, 128, "sig", zf[2, ft * 128:(ft + 1) * 128, tsl], "zf"))
                    for ft in range(8):
                        jobs.append((O_GB + ft * 128, 128, "sig", zf[3, ft * 128:(ft + 1) * 128, tsl], "zf"))
                    for (c0, m, kind, dst, dtk) in jobs:
                        pp, ppt = pf[cnt["pf"] % 4]
                        cnt["pf"] += 1
                        for kt in range(8):
                            P.op("pe", lambda E, pp=pp, kt=kt, c0=c0, m=m, xT=xT: E.matmul(pp[0:m, :], lhsT=wbf[:, kt, c0:c0 + m], rhs=xT[:, kt, :], start=(kt == 0), stop=(kt == 7)),
                                 r=[xTt, wtok[kt]], w=[ppt])
                        if kind in ("q", "c"):
                            ob, obt = obf[cnt["b"] % 4]
                            cnt["b"] += 1
                            scl = 0.125 if kind == "q" else 1.0
                            P.op("dve", lambda E, ob=ob, pp=pp, m=m, scl=scl: E.tensor_scalar(out=ob[0:m, :], in0=pp[0:m, :], scalar1=scl, scalar2=None, op0=ALU.mult), r=[ppt], w=[obt])
                            P.dma("pool", dst, ob[0:m, :], obt, T[dtk], obt)
                        else:
                            ob, obt = of32[cnt["f"] % 4]
                            cnt["f"] += 1
                            if kind == "f":
                                P.op("dve", lambda E, ob=ob, pp=pp: E.tensor_copy(out=ob[:], in_=pp[:]), r=[ppt], w=[obt])
                            else:
                                fn = AF.Gelu_apprx_tanh if kind == "gelu" else AF.Sigmoid
                                P.op("act", lambda E, ob=ob, pp=pp, fn=fn: E.activation(out=ob[:], in_=pp[:], func=fn), r=[ppt], w=[obt])
                            P.dma("pool", dst, ob[:], obt, T[dtk], obt)

        def phase_mlp(l, xsrc, xtok, xdst, xdtok, final):
            with Scope(P) as sc:
                w1b = sc.sb("w1b", [128, 8, 4096], BF16)
                w1t = [sc.tok() for _ in range(8)]
                w2b = sc.sb("w2b", [128, 32, D], BF16)
                w2t = [sc.tok() for _ in range(32)]
                n2 = sc.sb("n2", [128, 8], F32)
                n2t = sc.tok()
                P.dma("sp", n2[:], n2w[l], T["w"], n2t, n2t)
                stg = [sc.sbt("stg%d" % i, [128, 1024], F32) for i in range(3)]
                convert(sc, lambda kt, c0, c1: w1b[:, kt, c0:c1], lambda kt, c0, c1: w1[l, kt * 128:(kt + 1) * 128, c0:c1], 8, 4096, stg,
                        scale_ap_fn=lambda kt: (n2[:, kt:kt + 1], n2t), chunk=1024, toks=w1t)
                convert(sc, lambda kt, c0, c1: w2b[:, kt, c0:c1], lambda kt, c0, c1: w2[l, kt * 128:(kt + 1) * 128, c0:c1], 32, D, stg, chunk=1024, toks=w2t)
                fw = None
                if final:
                    fw, fwt = sc.sbt("fw", [128, D], F32)
                    P.dma("sp", fw[:], fnw[:, :], T["w"], fwt, fwt)
                xt = [sc.sbt("xt%d" % i, [128, D], F32) for i in range(2)]
                junk, junkt = sc.sbt("junk", [128, D], BF16)
                ssq = [sc.sbt("ssq%d" % i, [128, 4], F32) for i in range(2)]
                xn, xnt = sc.sbt("xn", [128, D], BF16)
                xnT = [sc.sbt("xnT%d" % i, [128, 8, 128], BF16) for i in range(2)]
                hT = [sc.sbt("hT%d" % i, [128, 32, 128], BF16) for i in range(2)]
                hr, hrt = sc.sbt("hr", [128, 512], F32)
                tp = [sc.pst("tp%d" % i, [128, 8, 128], BF16) for i in range(1)]
                ph = [sc.pst("ph%d" % i, [128, 4, 128], F32) for i in range(3)]
                po = [sc.pst("po%d" % i, [128, 512], F32) for i in range(2)]
                xo = [sc.sbt("xo%d" % i, [128, D], F32) for i in range(2)]
                for tt in range(NT):
                    x_t, x_tt = xt[tt % 2]
                    sq, sqt = ssq[tt % 2]
                    tpp, tpt = tp[0]
                    xT, xTt = xnT[tt % 2]
                    h_, ht_ = hT[tt % 2]
                    P.dma("sp", x_t[:], xsrc[tt * 128:(tt + 1) * 128, :], xtok, x_tt, x_tt)
                    rms_rstd(sc, x_t, x_tt, junk, junkt, sq, sq[:, 2:3], sqt)
                    P.op("dve", lambda E, x_t=x_t, sq=sq: E.tensor_scalar(out=xn[:], in0=x_t[:], scalar1=sq[:, 2:3], scalar2=None, op0=ALU.mult), r=[x_tt, sqt], w=[xnt])
                    for kt in range(8):
                        P.op("pe", lambda E, tpp=tpp, kt=kt: E.transpose(out=tpp[:, kt, :], in_=xn[:, kt * 128:(kt + 1) * 128], identity=ident[:]), r=[xnt, t_const], w=[tpt])
                    P.op("act", lambda E, xT=xT, tpp=tpp: E.copy(out=xT[:], in_=tpp[:]), r=[tpt], w=[xTt])
                    for f4 in range(8):
                        pp, ppt = ph[f4 % 3]
                        for fi in range(4):
                            ft = f4 * 4 + fi
                            for kt in range(8):
                                P.op("pe", lambda E, pp=pp, fi=fi, ft=ft, kt=kt, xT=xT: E.matmul(pp[:, fi, :], lhsT=w1b[:, kt, ft * 128:(ft + 1) * 128], rhs=xT[:, kt, :], start=(kt == 0), stop=(kt == 7)),
                                     r=[xTt, w1t[kt]], w=[ppt])
                        P.op("act", lambda E, pp=pp: E.activation(out=hr[:], in_=pp[:].rearrange("p a b -> p (a b)"), func=AF.Relu), r=[ppt], w=[hrt])
                        e2 = "pool" if f4 % 2 else "dve"
                        P.op(e2, lambda E, h_=h_, f4=f4: E.tensor_tensor(out=h_[:, f4 * 4:(f4 + 1) * 4, :].rearrange("p a b -> p (a b)"), in0=hr[:], in1=hr[:], op=ALU.mult), r=[hrt], w=[ht_])
                    xo_, xot_ = xo[tt % 2]
                    for nh in range(2):
                        pq, pqt = po[nh]
                        for kt in range(32):
                            P.op("pe", lambda E, pq=pq, kt=kt, nh=nh, h_=h_: E.matmul(pq[:], lhsT=h_[:, kt, :], rhs=w2b[:, kt, nh * 512:(nh + 1) * 512], start=(kt == 0), stop=(kt == 31)),
                                 r=[ht_, w2t[kt]], w=[pqt])
                        P.op("dve", lambda E, xo_=xo_, pq=pq, nh=nh, x_t=x_t: E.tensor_tensor(out=xo_[:, nh * 512:(nh + 1) * 512], in0=pq[:], in1=x_t[:, nh * 512:(nh + 1) * 512], op=ALU.add), r=[pqt, x_tt], w=[xot_])
                    if not final:
                        P.dma("pool", xdst[tt * 128:(tt + 1) * 128, :], xo_[:], xot_, xdtok, xot_)
                    else:
                        sq2, sq2t = ssq[tt % 2]
                        rms_rstd(sc, xo_, xot_, junk, junkt, sq2, sq2[:, 3:4], sq2t)
                        P.op("dve", lambda E, xo_=xo_, sq2=sq2: E.scalar_tensor_tensor(out=xo_[:], in0=xo_[:], scalar=sq2[:, 3:4], in1=fw[:], op0=ALU.mult, op1=ALU.mult), r=[sq2t, fwt], w=[xot_])
                        P.dma("pool", out_d[tt * 128:(tt + 1) * 128, :], xo_[:], xot_, T["out"], xot_)


        def phase_rnn(l):
            with Scope(P) as sc:
                prm, prmt = sc.sbt("prm", [128, 64], F32)
                P.dma("sp", prm[:, 0:32], convw[l].rearrange("p a b -> p (a b)"), T["w"], prmt, prmt)
                P.dma("sp", prm[:, 32:40], convb[l], T["w"], prmt, prmt)
                P.dma("sp", prm[:, 40:48], lba[l], T["w"], prmt, prmt)
                P.dma("sp", prm[:, 48:56], lbi[l], T["w"], prmt, prmt)
                P.dma("sp", prm[:, 56:64], llam[l], T["w"], prmt, prmt)
                P.op("act", lambda E: E.activation(out=prm[:, 56:64], in_=prm[:, 56:64], func=AF.Exp, scale=-1.0), r=[prmt], w=[prmt])
                P.op("act", lambda E: E.activation(out=prm[:, 56:64], in_=prm[:, 56:64], func=AF.Ln, bias=epsb[:, 1:2]), r=[prmt, t_const], w=[prmt])
                P.op("dve", lambda E: E.tensor_scalar(out=prm[:, 56:64], in0=prm[:, 56:64], scalar1=-8.0, scalar2=None, op0=ALU.mult), r=[prmt], w=[prmt])
                wst, wstt = sc.sbt("wst", [128, 128], F32)
                wab, wabt = sc.sbt("wab", [128, 128], BF16)
                wib, wibt = sc.sbt("wib", [128, 128], BF16)
                xrp, xrpt = sc.sbt("xrp", [128, S + 4], F32)
                gg, ggt = sc.sbt("gg", [128, S], F32)
                xc, xct = sc.sbt("xc", [128, S], F32)
                xcb, xcbt = sc.sbt("xcb", [128, S], BF16)
                rr, rrt = sc.sbt("rr", [128, S], F32)
                ig, igt = sc.sbt("ig", [128, S], F32)
                aa, aat = sc.sbt("aa", [128, S], F32)
                ro, rot = sc.sbt("ro", [128, S], BF16)
                pa = [sc.pst("pa%d" % i, [128, 512], F32) for i in range(4)]
                P.op("dve", lambda E: E.memset(xrp[:, 0:4], 0.0), w=[xrpt])
                for ct in range(8):
                    P.dma("sp", wst[:], lwa[l, ct], T["w"], wstt, wstt)
                    P.op("dve", lambda E: E.tensor_copy(out=wab[:], in_=wst[:]), r=[wstt], w=[wabt])
                    P.dma("sp", wst[:], lwi[l, ct], T["w"], wstt, wstt)
                    P.op("dve", lambda E: E.tensor_copy(out=wib[:], in_=wst[:]), r=[wstt], w=[wibt])
                    P.dma("sp", xrp[:, 4:S + 4], zf[0, ct * 128:(ct + 1) * 128, :], T["zf"], xrpt, xrpt)
                    P.dma("sp", gg[:], zf[1, ct * 128:(ct + 1) * 128, :], T["zf"], ggt, ggt)
                    P.op("act", lambda E, ct=ct: E.activation(out=xc[:], in_=xrp[:, 4:S + 4], func=AF.Identity, scale=prm[:, ct * 4 + 3:ct * 4 + 4], bias=prm[:, 32 + ct:33 + ct]), r=[xrpt, prmt], w=[xct])
                    for i in range(3):
                        P.op("dve", lambda E, ct=ct, i=i: E.scalar_tensor_tensor(out=xc[:], in0=xrp[:, 1 + i:1 + i + S], scalar=prm[:, ct * 4 + i:ct * 4 + i + 1], in1=xc[:], op0=ALU.mult, op1=ALU.add), r=[xrpt, prmt], w=[xct])
                    P.op("pool", lambda E: E.tensor_copy(out=xcb[:], in_=xc[:]), r=[xct], w=[xcbt])
                    for tg in range(8):
                        sl = slice(tg * 512, (tg + 1) * 512)
                        p1, p1t = pa[(2 * tg) % 4]
                        p2, p2t = pa[(2 * tg + 1) % 4]
                        P.op("pe", lambda E, p1=p1, sl=sl: E.matmul(p1[:], lhsT=wab[:], rhs=xcb[:, sl], start=True, stop=True), r=[wabt, xcbt], w=[p1t])
                        P.op("pe", lambda E, p2=p2, sl=sl: E.matmul(p2[:], lhsT=wib[:], rhs=xcb[:, sl], start=True, stop=True), r=[wibt, xcbt], w=[p2t])
                        P.op("act", lambda E, p1=p1, sl=sl, ct=ct: E.activation(out=rr[:, sl], in_=p1[:], func=AF.Sigmoid, bias=prm[:, 40 + ct:41 + ct]), r=[p1t, prmt], w=[rrt])
                        P.op("act", lambda E, p2=p2, sl=sl, ct=ct: E.activation(out=ig[:, sl], in_=p2[:], func=AF.Sigmoid, bias=prm[:, 48 + ct:49 + ct]), r=[p2t, prmt], w=[igt])
                    P.op("act", lambda E, ct=ct: E.activation(out=aa[:], in_=rr[:], func=AF.Exp, scale=prm[:, 56 + ct:57 + ct]), r=[rrt, prmt], w=[aat])
                    P.op("pool", lambda E: E.tensor_tensor(out=rr[:], in0=aa[:], in1=aa[:], op=ALU.mult), r=[aat], w=[rrt])
                    P.op("act", lambda E: E.activation(out=rr[:], in_=rr[:], func=AF.Sqrt, scale=-1.0, bias=epsb[:, 1:2]), r=[rrt, t_const], w=[rrt])
                    P.op("pool", lambda E: E.tensor_tensor(out=ig[:], in0=ig[:], in1=xc[:], op=ALU.mult), r=[xct], w=[igt])
                    P.op("dve", lambda E: E.tensor_tensor(out=ig[:], in0=ig[:], in1=rr[:], op=ALU.mult), r=[rrt], w=[igt])
                    P.op("dve", lambda E: E.tensor_tensor_scan(out=xc[:], data0=aa[:], data1=ig[:], initial=0.0, op0=ALU.mult, op1=ALU.add), r=[aat, igt], w=[xct])
                    P.op("pool", lambda E: E.tensor_tensor(out=ro[:], in0=xc[:], in1=gg[:], op=ALU.mult), r=[xct, ggt], w=[rot])
                    P.dma("pool", rnnT[ct * 128:(ct + 1) * 128, :], ro[:], rot, T["rnnT"], rot)

        def phase_merge(l, xsrc, xtok, xdst, xdtok):
            with Scope(P) as sc:
                wa = sc.sb("wa", [128, 4, D], BF16)
                wat = [sc.tok() for _ in range(4)]
                wr = sc.sb("wr", [128, 8, D], BF16)
                wrt = [sc.tok() for _ in range(8)]
                wob = sc.sb("wob", [128, 8, D], BF16)
                wot = [sc.tok() for _ in range(8)]
                stg = [sc.sbt("stg%d" % i, [128, 1024], F32) for i in range(3)]
                convert(sc, lambda kt, c0, c1: wa[:, kt, c0:c1], lambda kt, c0, c1: wua[l, kt * 128:(kt + 1) * 128, c0:c1], 4, D, stg, chunk=1024, toks=wat)
                convert(sc, lambda kt, c0, c1: wr[:, kt, c0:c1], lambda kt, c0, c1: wur[l, kt * 128:(kt + 1) * 128, c0:c1], 8, D, stg, chunk=1024, toks=wrt)
                convert(sc, lambda kt, c0, c1: wob[:, kt, c0:c1], lambda kt, c0, c1: wo[l, kt * 128:(kt + 1) * 128, c0:c1], 8, D, stg, chunk=1024, toks=wot)
                aT = [sc.sbt("aT%d" % i, [128, 4, 512], BF16) for i in range(2)]
                rT = [sc.sbt("rT%d" % i, [128, 8, 512], BF16) for i in range(2)]
                sa = [sc.sbt("sa%d" % i, [128, 512], F32) for i in range(2)]
                sb_ = [sc.sbt("sb%d" % i, [128, 512], F32) for i in range(2)]
                t1 = [sc.sbt("t1%d" % i, [128, 512], F32) for i in range(2)]
                t2 = [sc.sbt("t2%d" % i, [128, 512], F32) for i in range(2)]
                mT = [sc.sbt("mT%d" % i, [128, 8, 512], BF16) for i in range(2)]
                pA = [sc.pst("pA%d" % i, [128, 512], F32) for i in range(2)]
                pB = [sc.pst("pB%d" % i, [128, 512], F32) for i in range(2)]
                pO = [sc.pst("pO%d" % i, [128, 512], F32) for i in range(2)]
                xt = [sc.sbt("xt%d" % i, [128, D], F32) for i in range(2)]
                xo = [sc.sbt("xo%d" % i, [128, D], F32) for i in range(2)]
                k = 0
                for sg in range(8):
                    tsl = slice(sg * 512, (sg + 1) * 512)
                    a_, at_ = aT[sg % 2]
                    r_, rt_ = rT[sg % 2]
                    m_, mt_ = mT[sg % 2]
                    P.dma("sp", a_[:], attnT[:, tsl].rearrange("(a p) t -> p a t", p=128), T["attnT"], at_, at_)
                    P.dma("sp", r_[:], rnnT[:, tsl].rearrange("(a p) t -> p a t", p=128), T["rnnT"], rt_, rt_)
                    for ft in range(8):
                        fs = slice(ft * 128, (ft + 1) * 128)
                        sa_, sat_ = sa[k % 2]
                        sbb, sbt_ = sb_[k % 2]
                        u1, u1t = t1[k % 2]
                        u2, u2t = t2[k % 2]
                        p_a, pat = pA[k % 2]
                        p_b, pbt = pB[k % 2]
                        k += 1
                        P.dma("sp", sa_[:], zf[2, fs, tsl], T["zf"], sat_, sat_)
                        P.dma("sp", sbb[:], zf[3, fs, tsl], T["zf"], sbt_, sbt_)
                        for kt in range(4):
                            P.op("pe", lambda E, p_a=p_a, kt=kt, fs=fs, a_=a_: E.matmul(p_a[:], lhsT=wa[:, kt, fs], rhs=a_[:, kt, :], start=(kt == 0), stop=(kt == 3)), r=[wat[kt], at_], w=[pat])
                        for kt in range(8):
                            P.op("pe", lambda E, p_b=p_b, kt=kt, fs=fs, r_=r_: E.matmul(p_b[:], lhsT=wr[:, kt, fs], rhs=r_[:, kt, :], start=(kt == 0), stop=(kt == 7)), r=[wrt[kt], rt_], w=[pbt])
                        P.op("dve", lambda E, u1=u1, p_a=p_a, sa_=sa_: E.tensor_tensor(out=u1[:], in0=p_a[:], in1=sa_[:], op=ALU.mult), r=[pat, sat_], w=[u1t])
                        P.op("dve", lambda E, u2=u2, p_b=p_b, sbb=sbb: E.tensor_tensor(out=u2[:], in0=p_b[:], in1=sbb[:], op=ALU.mult), r=[pbt, sbt_], w=[u2t])
                        P.op("pool", lambda E, m_=m_, ft=ft, u1=u1, u2=u2: E.tensor_tensor(out=m_[:, ft, :], in0=u1[:], in1=u2[:], op=ALU.add), r=[u1t, u2t], w=[mt_])
                    for j in range(4):
                        tt = sg * 4 + j
                        x_t, x_tt = xt[tt % 2]
                        xo_, xot_ = xo[tt % 2]
                        P.dma("sp", x_t[:], xsrc[tt * 128:(tt + 1) * 128, :], xtok, x_tt, x_tt)
                        for nh in range(2):
                            pq, pqt = pO[nh]
                            for kt in range(8):
                                P.op("pe", lambda E, pq=pq, kt=kt, nh=nh, m_=m_, j=j: E.matmul(pq[:], lhsT=m_[:, kt, j * 128:(j + 1) * 128], rhs=wob[:, kt, nh * 512:(nh + 1) * 512], start=(kt == 0), stop=(kt == 7)), r=[mt_, wot[kt]], w=[pqt])
                            P.op("dve", lambda E, xo_=xo_, pq=pq, nh=nh, x_t=x_t: E.tensor_tensor(out=xo_[:, nh * 512:(nh + 1) * 512], in0=pq[:], in1=x_t[:, nh * 512:(nh + 1) * 512], op=ALU.add), r=[pqt, x_tt], w=[xot_])
                        P.dma("pool", xdst[tt * 128:(tt + 1) * 128, :], xo_[:], xot_, xdtok, xot_)


        def phase_attn(l):
            with Scope(P) as sc:
                cst = sc.tok("cst")
                tric = sc.sb("tric", [128, 128], BF16)
                triw = sc.sb("triw", [128, 128], BF16)
                Ec = sc.sb("Ec", [64, S], BF16)
                band = sc.sb("band", [128, 9], BF16)
                cA = sc.sb("cA", [128, 128], F32)
                cB = sc.sb("cB", [128, 128], F32)
                for dst, src in ((tric, c_tric), (triw, c_triw), (Ec, c_E), (band, c_band), (cA, c_A), (cB, c_B)):
                    P.dma("sp", dst[:], src[:, :], T["w"], cst, cst)
                kcm = [sc.sbt("kcm%d" % g, [64, 256], BF16) for g in range(2)]
                vcm = [sc.sbt("vcm%d" % g, [128, 2, 64], BF16) for g in range(2)]
                with Scope(P) as s2:
                    stg = [s2.sbt("cstg%d" % i, [64, 2048], F32) for i in range(2)]
                    w2s, w2st = s2.sbt("w2s", [128, 128], F32)
                    pss, psst = s2.sbt("pss", [64, 64], F32)
                    w1b = s2.sb("w1b", [64, 32, 256], BF16)
                    w1bt = s2.tok()
                    w2b, w2bt = s2.sbt("w2b", [128, 2, 64], BF16)
                    posb, posbt = s2.sbt("posb", [64, 32, 2], BF16)
                    kg = [s2.sbt("kg%d" % g, [64, S], BF16) for g in range(2)]
                    hid = [[s2.sbt("hid%d%d" % (g, h), [128, 256], BF16) for h in range(2)] for g in range(2)]
                    bia, biat = s2.sbt("bia", [128, 2], F32)
                    psg = [s2.pst("psg%d" % g, [128, 512], F32) for g in range(2)]
                    psb, psbt = s2.pst("psb", [128, 512], F32)
                    pso, psot = s2.pst("pso", [128, 512], F32)
                    for (w1d, w2d, posd, srcT, srct, is_k) in ((ckw1, ckw2, posk, kcT, "kcT", True), (cvw1, cvw2, posv, vcT, "vcT", False)):
                        convert(s2, lambda kt, c0, c1: w1b[:].rearrange("p a b -> p (a b)")[:, c0:c1], lambda kt, c0, c1: w1d[l].rearrange("p a b -> p (a b)")[:, c0:c1], 1, 32 * 256, stg, chunk=2048, rows=64, toks=[w1bt])
                        P.dma("sp", w2s[:], w2d[l].rearrange("p a b -> p (a b)"), T["w"], w2st, w2st)
                        P.op("dve", lambda E: E.tensor_copy(out=w2b[:].rearrange("p a b -> p (a b)"), in_=w2s[:]), r=[w2st], w=[w2bt])
                        P.dma("sp", pss[:], posd[l].rearrange("p a b -> p (a b)"), T["w"], psst, psst)
                        P.op("dve", lambda E: E.tensor_copy(out=posb[:].rearrange("p a b -> p (a b)"), in_=pss[:]), r=[psst], w=[posbt])
                        for g in range(2):
                            P.dma("sp", kg[g][0][:], srcT[g], T[srct], kg[g][1], kg[g][1])
                        for ht in range(2):
                            hs = slice(ht * 128, (ht + 1) * 128)
                            for p in range(32):
                                for g in range(2):
                                    P.op("pe", lambda E, g=g, p=p, hs=hs: E.matmul(psg[g][0][:, 0:255], lhsT=w1b[:, p, hs], rhs=kg[g][0][:, p:p + 16 * 254 + 1:16], start=(p == 0), stop=(p == 31)),
                                         r=[w1bt, kg[g][1]], w=[psg[g][1]])
                                P.op("pe", lambda E, p=p, hs=hs: E.matmul(psb[:, 0:2], lhsT=w1b[:, p, hs], rhs=posb[:, p, :], start=(p == 0), stop=(p == 31)), r=[w1bt, posbt], w=[psbt])
                            P.op("dve", lambda E: E.tensor_copy(out=bia[:], in_=psb[:, 0:2]), r=[psbt], w=[biat])
                            for g in range(2):
                                P.op("act", lambda E, g=g, ht=ht: E.activation(out=hid[g][ht][0][:, 0:255], in_=psg[g][0][:, 0:255], func=AF.Gelu_apprx_tanh, bias=bia[:, 0:1]), r=[psg[g][1], biat], w=[hid[g][ht][1]])
                        for g in range(2):
                            if is_k:
                                for ht in range(2):
                                    P.op("pe", lambda E, g=g, ht=ht: E.matmul(pso[0:64, 0:255], lhsT=w2b[:, ht, :], rhs=hid[g][ht][0][:, 0:255], start=(ht == 0), stop=(ht == 1)), r=[w2bt, hid[g][ht][1]], w=[psot])
                                P.op("dve", lambda E, g=g: E.tensor_copy(out=kcm[g][0][:, 0:255], in_=pso[0:64, 0:255]), r=[psot], w=[kcm[g][1]])
                            else:
                                for ctile in range(2):
                                    n = 128 if ctile == 0 else 127
                                    for ht in range(2):
                                        P.op("pe", lambda E, g=g, ht=ht, ctile=ctile, n=n: E.matmul(pso[0:n, 256:320], lhsT=hid[g][ht][0][:, ctile * 128:ctile * 128 + n], rhs=w2b[:, ht, :], start=(ht == 0), stop=(ht == 1)), r=[w2bt, hid[g][ht][1]], w=[psot])
                                    P.op("dve", lambda E, g=g, ctile=ctile, n=n: E.tensor_copy(out=vcm[g][0][0:n, ctile, :], in_=pso[0:n, 256:320]), r=[psot], w=[vcm[g][1]])
                LV = ATT_DBG["level"]
                ksl = [sc.sbt("ksl%d" % g, [64, S], BF16) for g in range(2)]
                kwn = [sc.sbt("kwn%d" % g, [64, S], BF16) for g in range(2)]
                vsl = [sc.sbt("vsl%d" % g, [128, 32, 65], BF16) for g in range(2)]
                vwn = [sc.sbt("vwn%d" % g, [128, 32, 65], BF16) for g in range(2)]
                for g in range(2):
                    P.dma("sp", ksl[g][0][:], ksT[g], T["ksT"], ksl[g][1], ksl[g][1])
                    P.dma("sp", kwn[g][0][:], kwT[g], T["kwT"], kwn[g][1], kwn[g][1])
                    for (vv, j) in ((vsl[g], g), (vwn[g], 2 + g)):
                        P.op("pool", lambda E, vv=vv: E.memset(vv[0][:, :, 64:65], 1.0), w=[vv[1]])
                        for k8 in range(8):
                            P.dma("sp", vv[0][:, k8 * 4:(k8 + 1) * 4, 0:64], vtm[k8 * 512:(k8 + 1) * 512, j, :].rearrange("(k p) d -> p k d", p=128), T["vtm"], vv[1], vv[1])
                qc = [sc.sbt("qc%d" % i, [64, 8, 128], BF16) for i in range(2)]
                gt = [sc.sbt("gt%d" % i, [128, 24], F32) for i in range(2)]
                bst = [sc.pst("bst%d" % i, [128, 512], F32) for i in range(2)]
                b_os = [sc.pst("b_os%d" % i, [128, 512], F32) for i in range(2)]
                b_ow = [sc.pst("b_ow%d" % i, [128, 512], F32) for i in range(2)]
                b_oc, b_oct = sc.pst("b_oc", [128, 512], F32)
                b_tp = sc.ps("b_tp", [128, 1024], BF16)
                tp_t = sc.tok("tp", True)
                tpc_t = [tp_t, tp_t]
                ptr_t = tp_t
                atr_t = tp_t
                ee = [sc.sbt("ee%d" % i, [128, 256], F32) for i in range(4)]
                pb = [sc.sbt("pb%d" % i, [128, 256], BF16) for i in range(4)]
                pTc = [sc.sbt("pTc%d" % i, [128, 128], BF16) for i in range(2)]
                pT = [sc.sbt("pT%d" % i, [128, 128], BF16) for i in range(4)]
                sm, smt = sc.sbt("sm", [128, 32], F32)
                P4, P4t = sc.sbt("P4", [128, 264], F32)
                imp, impt = sc.sbt("imp", [128, 64], F32)
                scr, scrt = sc.sbt("scr", [128, 64], F32)
                scr2, scr2t = sc.sbt("scr2", [128, 64], F32)
                m8, m8t = sc.sbt("m8", [128, 16], F32)
                penb, penbt = sc.sbt("penb", [128, 128], BF16)
                G3, G3t = sc.sbt("G3", [128, 64, 64], BF16)
                P.op("pool", lambda E: E.memset(penb[:], 0.0), w=[penbt])
                penT = [sc.sbt("penT%d" % g, [128, 128], BF16) for g in range(2)]
                att, attt = sc.sbt("att", [128, 512], F32)
                attb, attbt = sc.sbt("attb", [128, 512], BF16)
                ast = [sc.sbt("ast%d" % i, [128, 4, 128], BF16) for i in range(2)]
                P.op("dve", lambda E: E.memset(P4[:], 0.0), w=[P4t])
                P4v = P4[:, 0:256].rearrange("p (j f) -> p j f", f=4)
                P4w = P4[:, 4:260].rearrange("p (j f) -> p j f", f=4)
                sidx = [0]
                for qb in range(ATT_DBG["nqb"] if LV >= 3 else 0):
                    qs = slice(qb * 128, (qb + 1) * 128)
                    q_, qt_ = qc[qb % 2]
                    g_, gt_ = gt[qb % 2]
                    P.dma("sp", q_[:], qT[:, :, qs].rearrange("h d t -> d h t"), T["qT"], qt_, qt_)
                    P.dma("sp", g_[:], gat[qs, :], T["gat"], gt_, gt_)
                    Nc = min(255, 8 * qb + 7)
                    cb0 = max(0, 8 * qb - 2)
                    cb1 = min(Nc, 8 * qb + 7)
                    for g in range(2):
                        os_, ost_ = b_os[g]
                        ow_, owt_ = b_ow[g]
                        for hh in range(4):
                            h = g * 4 + hh
                            ps_s = bst[sidx[0] % 2][0][:, 0:256]
                            pst_ = bst[sidx[0] % 2][1]
                            sidx[0] += 1
                            P.op("pe", lambda E, ps_s=ps_s, h=h, g=g, q_=q_: E.matmul(ps_s[:, 0:Nc], lhsT=q_[:, h, :], rhs=kcm[g][0][:, 0:Nc], start=True, stop=False), r=[qt_, kcm[g][1]], w=[pst_])
                            P.op("pe", lambda E, ps_s=ps_s: E.matmul(ps_s[:, cb0:cb1], lhsT=ident[:], rhs=band[:, cb0 - (8 * qb - 2):cb1 - (8 * qb - 2)], start=False, stop=True), r=[t_const, cst], w=[pst_])
                            P.op("dve", lambda E, ps_s=ps_s, hh=hh: E.tensor_reduce(out=sm[:, hh:hh + 1], in_=ps_s[:, 0:Nc], axis=AX.X, op=ALU.max), r=[pst_], w=[smt])
                            P.op("dve", lambda E, hh=hh: E.tensor_scalar(out=sm[:, 4 + hh:5 + hh], in0=sm[:, hh:hh + 1], scalar1=-10000.0, scalar2=-1.0, op0=ALU.max, op1=ALU.mult), r=[smt], w=[smt])
                            e_, et_ = ee[hh]
                            P.op("act", lambda E, e_=e_, ps_s=ps_s, hh=hh: E.activation(out=e_[:, 0:Nc], in_=ps_s[:, 0:Nc], func=AF.Exp, bias=sm[:, 4 + hh:5 + hh], accum_out=sm[:, 8 + hh:9 + hh]), r=[pst_, smt], w=[et_, smt])
                        P.op("dve", lambda E: E.tensor_scalar(out=sm[:, 12:16], in0=sm[:, 8:12], scalar1=1e-20, scalar2=None, op0=ALU.max), r=[smt], w=[smt])
                        P.op("dve", lambda E: E.reciprocal(out=sm[:, 12:16], in_=sm[:, 12:16]), r=[smt], w=[smt])
                        for hh in range(4):
                            e_, et_ = ee[hh]
                            p_, pt_ = pb[hh]
                            P.op("pool", lambda E, p_=p_, e_=e_, hh=hh: E.tensor_scalar(out=p_[:, 0:Nc], in0=e_[:, 0:Nc], scalar1=sm[:, 12 + hh:13 + hh], scalar2=None, op0=ALU.mult), r=[et_, smt], w=[pt_])
                            if hh == 0:
                                P.op("dve", lambda E, e_=e_, hh=hh: E.tensor_scalar(out=P4[:, 4:4 + Nc], in0=e_[:, 0:Nc], scalar1=sm[:, 12 + hh:13 + hh], scalar2=None, op0=ALU.mult), r=[et_, smt], w=[P4t])
                            else:
                                P.op("dve", lambda E, e_=e_, hh=hh: E.scalar_tensor_tensor(out=P4[:, 4:4 + Nc], in0=e_[:, 0:Nc], scalar=sm[:, 12 + hh:13 + hh], in1=P4[:, 4:4 + Nc], op0=ALU.mult, op1=ALU.add), r=[et_, smt], w=[P4t])
                        k2 = 0
                        for hh in range(4 if LV >= 3.5 else 0):
                            p_, pt_ = pb[hh]
                            nct = (Nc + 127) // 128
                            for ctile in range(nct):
                                n = min(128, Nc - ctile * 128)
                                tpc = b_tp[:, (k2 % 2) * 128:(k2 % 2) * 128 + 128]
                                tpt = tpc_t[k2 % 2]
                                pc_, pct_ = pTc[k2 % 2]
                                k2 += 1
                                P.op("pe", lambda E, tpc=tpc, p_=p_, ctile=ctile, n=n: E.transpose(out=tpc[0:n, :], in_=p_[:, ctile * 128:ctile * 128 + n], identity=ident[:]), r=[pt_, t_const], w=[tpt])
                                P.op("act", lambda E, pc_=pc_, tpc=tpc, n=n: E.copy(out=pc_[0:n, :], in_=tpc[0:n, :]), r=[tpt], w=[pct_])
                                P.op("pe", lambda E, pc_=pc_, hh=hh, n=n, ctile=ctile, g=g, nct=nct: E.matmul(b_oc[:, hh * 64:(hh + 1) * 64], lhsT=pc_[0:n, :], rhs=vcm[g][0][0:n, ctile, :], start=(ctile == 0), stop=(ctile == nct - 1)), r=[pct_, vcm[g][1]], w=[b_oct])
                        if LV < 4:
                            continue
                        P.op("dve", lambda E: E.tensor_tensor(out=imp[:], in0=P4v[:, :, 1], in1=P4v[:, :, 2], op=ALU.add), r=[P4t], w=[impt])
                        P.op("dve", lambda E: E.tensor_tensor(out=imp[:], in0=imp[:], in1=P4v[:, :, 3], op=ALU.add), r=[P4t], w=[impt])
                        P.op("dve", lambda E: E.scalar_tensor_tensor(out=imp[:], in0=imp[:], scalar=2.0, in1=P4v[:, :, 0], op0=ALU.mult, op1=ALU.add), r=[P4t], w=[impt])
                        P.op("dve", lambda E: E.tensor_tensor(out=imp[:], in0=imp[:], in1=P4w[:, :, 0], op=ALU.add), r=[P4t], w=[impt])
                        P.op("dve", lambda E: E.tensor_tensor(out=scr[:], in0=imp[:], in1=cA[:, 64 - 2 * qb:128 - 2 * qb], op=ALU.mult), r=[impt, cst], w=[scrt])
                        P.op("dve", lambda E: E.tensor_tensor(out=scr[:], in0=scr[:], in1=cB[:, 64 - 2 * qb:128 - 2 * qb], op=ALU.add), r=[cst], w=[scrt])
                        P.op("dve", lambda E: E.memset(scr[:, 0:1], 1e4), w=[scrt])
                        nb = min(64, 2 * qb + 2)
                        P.op("dve", lambda E, nb=nb: E.tensor_tensor(out=G3[:, 0:nb, 0:nb], in0=scr[:, 0:nb].unsqueeze(1).to_broadcast([128, nb, nb]), in1=scr[:, 0:nb].unsqueeze(2).to_broadcast([128, nb, nb]), op=ALU.is_gt), r=[scrt], w=[G3t])
                        P.op("dve", lambda E, nb=nb: E.tensor_reduce(out=scr2[:, 0:nb], in_=G3[:, 0:nb, 0:nb], axis=AX.X, op=ALU.add), r=[G3t], w=[scr2t])
                        P.op("dve", lambda E, nb=nb: E.tensor_scalar(out=scr2[:, 0:nb], in0=scr2[:, 0:nb], scalar1=16.0, scalar2=None, op0=ALU.is_lt), r=[scr2t], w=[scr2t])
                        P.op("dve", lambda E, nb=nb: E.tensor_scalar(out=penb[:, 0:nb], in0=scr2[:, 0:nb], scalar1=-1.0, scalar2=-NEGM, op0=ALU.add, op1=ALU.mult), r=[scr2t], w=[penbt])
                        if LV < 4.5:
                            continue
                        ptr = b_tp[:, 256:384]
                        P.op("pe", lambda E, ptr=ptr: E.transpose(out=ptr, in_=penb[:], identity=ident[:]), r=[penbt, t_const], w=[ptr_t])
                        pn_, pnt_ = penT[g]
                        P.op("act", lambda E, pn_=pn_, ptr=ptr: E.copy(out=pn_[:], in_=ptr), r=[ptr_t], w=[pnt_])
                        if LV < 5:
                            continue
                        steps = []
                        for hh in range(4):
                            for kt in range(0, qb + 1):
                                steps.append(("s", hh, kt, 0))
                        for hh in range(4):
                            for kt in range(max(0, qb - 4), qb + 1):
                                steps.append(("w", hh, kt, max(0, qb - 4)))
                        LA = 2
                        ring = []
                        for i in range(len(steps) + LA):
                            if i < len(steps):
                                br, hh, kt, k0 = steps[i]
                                h = g * 4 + hh
                                si = sidx[0] % 4
                                stp = bst[sidx[0] % 2][0][:, 0:128]
                                stt_ = bst[sidx[0] % 2][1]
                                sidx[0] += 1
                                kT = ksl[g] if br == "s" else kwn[g]
                                ks_ = slice(kt * 128, (kt + 1) * 128)
                                ex = []
                                if br == "s":
                                    ex.append((Ec[:, ks_], pn_[0:64, :], [cst, pnt_]))
                                if kt == qb:
                                    ex.append((ident[:], tric[:], [t_const, cst]))
                                if br == "w" and kt == qb - 4:
                                    ex.append((ident[:], triw[:], [t_const, cst]))
                                P.op("pe", lambda E, stp=stp, kT=kT, ks_=ks_, h=h, q_=q_, ex=ex: E.matmul(stp, lhsT=kT[0][:, ks_], rhs=q_[:, h, :], start=True, stop=(len(ex) == 0)), r=[kT[1], qt_], w=[stt_])
                                for j, (la, ra, rt) in enumerate(ex):
                                    P.op("pe", lambda E, stp=stp, la=la, ra=ra, j=j, ex=ex: E.matmul(stp, lhsT=la, rhs=ra, start=False, stop=(j == len(ex) - 1)), r=rt, w=[stt_])
                                pt_s, pt_st = pT[si]
                                P.op("act", lambda E, pt_s=pt_s, stp=stp: E.activation(out=pt_s[:], in_=stp, func=AF.Exp), r=[stt_], w=[pt_st])
                                ring.append((br, hh, kt, k0, pt_s, pt_st))
                            if i - LA >= 0:
                                br, hh, kt, k0, pt_s, pt_st = ring[i - LA]
                                ob, obt = (os_, ost_) if br == "s" else (ow_, owt_)
                                V = vsl[g] if br == "s" else vwn[g]
                                P.op("pe", lambda E, ob=ob, hh=hh, pt_s=pt_s, V=V, kt=kt, k0=k0: E.matmul(ob[:, hh * 65:hh * 65 + 65], lhsT=pt_s[:], rhs=V[0][:, kt, :], start=(kt == k0), stop=(kt == qb)), r=[pt_st, V[1]], w=[obt])
                        if LV < 6:
                            continue
                        osv = os_[:, 0:260].rearrange("p (h c) -> p h c", c=65)
                        owv = ow_[:, 0:260].rearrange("p (h c) -> p h c", c=65)
                        gv = g_[:, g * 12:(g + 1) * 12].rearrange("p (h c) -> p h c", c=3)
                        P.op("dve", lambda E, osv=osv: E.reciprocal(out=sm[:, 16:20], in_=osv[:, :, 64]), r=[ost_], w=[smt])
                        P.op("dve", lambda E, gv=gv: E.tensor_tensor(out=sm[:, 16:20], in0=sm[:, 16:20], in1=gv[:, :, 1], op=ALU.mult), r=[gt_], w=[smt])
                        P.op("dve", lambda E, owv=owv: E.reciprocal(out=sm[:, 20:24], in_=owv[:, :, 64]), r=[owt_], w=[smt])
                        P.op("dve", lambda E, gv=gv: E.tensor_tensor(out=sm[:, 20:24], in0=sm[:, 20:24], in1=gv[:, :, 2], op=ALU.mult), r=[gt_], w=[smt])
                        for hh in range(4):
                            h = g * 4 + hh
                            cs_ = slice(h * 64, (h + 1) * 64)
                            P.op("dve", lambda E, hh=hh, h=h, cs_=cs_, g_=g_: E.tensor_scalar(out=att[:, cs_], in0=b_oc[:, hh * 64:(hh + 1) * 64], scalar1=g_[:, h * 3:h * 3 + 1], scalar2=None, op0=ALU.mult), r=[b_oct, gt_], w=[attt])
                            P.op("dve", lambda E, hh=hh, cs_=cs_, os_=os_: E.scalar_tensor_tensor(out=att[:, cs_], in0=os_[:, hh * 65:hh * 65 + 64], scalar=sm[:, 16 + hh:17 + hh], in1=att[:, cs_], op0=ALU.mult, op1=ALU.add), r=[ost_, smt], w=[attt])
                            P.op("dve", lambda E, hh=hh, cs_=cs_, ow_=ow_: E.scalar_tensor_tensor(out=attb[:, cs_], in0=ow_[:, hh * 65:hh * 65 + 64], scalar=sm[:, 20 + hh:21 + hh], in1=att[:, cs_], op0=ALU.mult, op1=ALU.add), r=[owt_, smt, attt], w=[attbt])
                    if LV < 6:
                        continue
                    atr = b_tp[:, 384:896].rearrange("p (a b) -> p a b", b=128)
                    for ft in range(4):
                        P.op("pe", lambda E, ft=ft, atr=atr: E.transpose(out=atr[:, ft, :], in_=attb[:, ft * 128:(ft + 1) * 128], identity=ident[:]), r=[attbt, t_const], w=[atr_t])
                    a_, at_ = ast[qb % 2]
                    P.op("act", lambda E, a_=a_, atr=atr: E.copy(out=a_[:], in_=atr), r=[atr_t], w=[at_])
                    P.dma("pool", attnT[:, qs].rearrange("(a p) t -> p a t", p=128), a_[:], at_, T["attnT"], at_)

        PH = {"inproj": phase_inproj, "mlp": phase_mlp, "rnn": phase_rnn, "merge": phase_merge, "attn": phase_attn}
        build.phases = PH
        build.ctx = dict(P=P, T=T, nc=nc, xs=xs, x_in=x_in, out_d=out_d)
        plan = build.plan
        plan(PH, build.ctx, locals())
        P.barrier()
        print("ops", P.nops, "waits", P.nwait)
    return nc, dbg


def default_plan(PH, ctx, L):
    T = ctx["T"]
    xs = ctx["xs"]
    cur, curt = ctx["x_in"], T["x_in"]
    for l in range(2):
        PH["inproj"](l, cur, curt)
        PH["attn"](l)
        PH["rnn"](l)
        PH["merge"](l, cur, curt, xs[0], T["xs0"])
        PH["mlp"](l, xs[0], T["xs0"], xs[1], T["xs1"], l == 1)
        cur, curt = xs[1], T["xs1"]


build.plan = default_plan


def host_inputs(inp, b):
    bf = ml_dtypes.bfloat16
    f = np.float32

    def pk(v):
        return np.ascontiguousarray(v.reshape(2, 8, 128).transpose(0, 2, 1)).astype(f)

    def bd(wm):
        o = np.zeros((2, 8, 128, 128), f)
        for c in range(8):
            o[:, c, 0:64, 0:64] = wm[:, 2 * c]
            o[:, c, 64:128, 64:128] = wm[:, 2 * c + 1]
        return o

    i_ = np.arange(128)
    m = {
        "x": np.ascontiguousarray(inp["x"][b]),
        "w_in": inp["w_in"],
        "n1w": pk(inp["norm1_w"]), "n2w": pk(inp["norm2_w"]),
        "fnw": np.ascontiguousarray(np.broadcast_to(inp["final_norm_w"][None, :], (128, D))).astype(f),
        "posk": np.ascontiguousarray(np.repeat(inp["cmp_pos_k"].transpose(0, 2, 1)[..., None], 2, axis=-1)),
        "posv": np.ascontiguousarray(np.repeat(inp["cmp_pos_v"].transpose(0, 2, 1)[..., None], 2, axis=-1)),
        "ckw1": np.ascontiguousarray(inp["cmp_k_w1"].reshape(2, 32, 64, 256).transpose(0, 2, 1, 3)),
        "cvw1": np.ascontiguousarray(inp["cmp_v_w1"].reshape(2, 32, 64, 256).transpose(0, 2, 1, 3)),
        "ckw2": np.ascontiguousarray(inp["cmp_k_w2"].reshape(2, 2, 128, 64).transpose(0, 2, 1, 3)),
        "cvw2": np.ascontiguousarray(inp["cmp_v_w2"].reshape(2, 2, 128, 64).transpose(0, 2, 1, 3)),
        "convw": np.ascontiguousarray(inp["conv_w"].reshape(2, 4, 8, 128).transpose(0, 3, 2, 1)),
        "convb": pk(inp["conv_b"]), "lba": pk(inp["lru_b_a"]), "lbi": pk(inp["lru_b_i"]), "llam": pk(inp["lru_lambda"]),
        "lwa": bd(inp["lru_w_a"]), "lwi": bd(inp["lru_w_i"]),
        "wua": inp["w_up_attn"], "wur": inp["w_up_rnn"], "wo": inp["w_out"], "w1": inp["mlp_w1"], "w2": inp["mlp_w2"],
        "c_ident": np.eye(128, dtype=f).astype(bf),
        "c_tric": np.where(i_[:, None] <= i_[None, :], 0.0, NEGM).astype(bf),
        "c_triw": np.where(i_[:, None] > i_[None, :], 0.0, NEGM).astype(bf),
        "c_E": (np.arange(S)[None, :] // 64 == np.arange(64)[:, None]).astype(f).astype(bf),
        "c_band": np.where((np.arange(9)[None, :] - 2) <= ((i_[:, None] + 1) // 16 - 2), 0.0, NEGM).astype(bf),
    }
    hi = (i_ >= 64).astype(np.int64)[:, None]
    jp = (np.arange(128) - 64)[None, :]
    valid = jp <= hi
    forced = jp > hi - 2
    A = np.where(valid & ~forced, 1.0, 0.0)
    Bm = np.where(valid, np.where(forced, 1e4, 0.0), -1e30)
    m["c_A"] = A.astype(f)
    m["c_B"] = Bm.astype(f)
    return {k: np.ascontiguousarray(v) for k, v in m.items()}


def kernel(**inputs):
    inp = {k: np.asarray(v) for k, v in inputs.items()}
    nc, _ = build(False)
    in_maps = [host_inputs(inp, c % 4) for c in range(8)]
    res = run_bass_kernel_spmd(nc, in_maps, core_ids=list(range(8)))
    return np.stack([np.asarray(res.results[c]["out"]) for c in range(4)], axis=0).astype(np.float32)
```

```python
import numpy as np
import ml_dtypes
from contextlib import ExitStack
import concourse.bass as bass
import concourse.mybir as mybir
from concourse.bass_utils import run_bass_kernel_spmd

F32 = mybir.dt.float32
BF16 = mybir.dt.bfloat16
AF = mybir.ActivationFunctionType
ALU = mybir.AluOpType
AX = mybir.AxisListType

S = 4096
D = 1024
DIN = 5400
NT = S // 128
NEGM = -30000.0
EPS = 1e-6
O_Q, O_KC, O_VC, O_KS, O_VS, O_KW, O_VW, O_GN, O_XR, O_GR, O_GA, O_GB = 0, 512, 640, 768, 896, 1024, 1152, 1280, 1304, 2328, 3352, 4376


ATT_DBG = {"level": 6, "nqb": NT}


class Tok:
    __slots__ = ("w", "r", "sem", "name", "x")

    def __init__(self, name="", x=False):
        self.w = {}
        self.r = {}
        self.sem = None
        self.name = name
        self.x = x


class Prog:
    ENG = ("pe", "act", "dve", "pool", "sp")

    def __init__(self, nc, es, n_dma_sems=80):
        self.nc = nc
        self.eng = {"pe": nc.tensor, "act": nc.scalar, "dve": nc.vector, "pool": nc.gpsimd, "sp": nc.sync}
        self.sems = []
        self.esem = {}
        for e in self.ENG:
            self.esem[e] = len(self.sems)
            self.sems.append(es.enter_context(nc.semaphore("es_" + e)))
        self.dma_ids = []
        for i in range(n_dma_sems):
            self.dma_ids.append(len(self.sems))
            self.sems.append(es.enter_context(nc.semaphore("ds_%d" % i)))
        self.free = list(self.dma_ids)
        self.total = [0] * len(self.sems)
        self.known = {e: [0] * len(self.sems) for e in self.ENG}
        self.nwait = 0
        self.nops = 0

    def _wait(self, eng, deps):
        E = self.eng[eng]
        kn = self.known[eng]
        for s, v in deps.items():
            if s >= 5:
                v = self.total[s]
            if kn[s] < v:
                kn[s] = v
                E.wait_ge(self.sems[s], v)
                self.nwait += 1

    @staticmethod
    def _merge(d, src):
        for s, v in src.items():
            if d.get(s, 0) < v:
                d[s] = v

    def op(self, eng, fn, r=(), w=()):
        deps = {}
        rx = [b for b in r if b.x]
        if rx:
            r = [b for b in r if not b.x]
            w = list(w) + rx
        for b in r:
            self._merge(deps, b.w)
        for b in w:
            self._merge(deps, b.w)
            self._merge(deps, b.r)
        s = self.esem[eng]
        if eng == "pe":
            deps.pop(s, None)
        self._wait(eng, deps)
        self.total[s] += 1
        n = self.total[s]
        fn(self.eng[eng]).then_inc(self.sems[s], 1)
        self.nops += 1
        for b in r:
            b.r[s] = n
        for b in w:
            b.w[s] = n

    def dma(self, eng, out, in_, src, dst, owner):
        deps = {}
        self._merge(deps, src.w)
        self._merge(deps, dst.w)
        self._merge(deps, dst.r)
        self._wait(eng, deps)
        if owner.sem is None:
            owner.sem = self.free.pop()
        s = owner.sem
        self.total[s] += 16
        v = self.total[s]
        self.eng[eng].dma_start(out=out, in_=in_).then_inc(self.sems[s], 16)
        self.nops += 1
        src.r[s] = v
        dst.w[s] = v

    def release(self, toks):
        for t in toks:
            if t.sem is not None:
                self.free.append(t.sem)
                t.sem = None

    def barrier(self):
        for e in self.ENG:
            E = self.eng[e]
            kn = self.known[e]
            for s in range(len(self.sems)):
                if s == self.esem[e]:
                    continue
                v = self.total[s]
                if kn[s] < v:
                    kn[s] = v
                    E.wait_ge(self.sems[s], v)
        arr = {}
        for e in self.ENG:
            s = self.esem[e]
            self.total[s] += 1
            arr[e] = self.total[s]
            if e == "pe":
                self.eng[e].nop().then_inc(self.sems[s], 1) if hasattr(self.eng[e], "nop") else None
            else:
                self.eng[e].nop().then_inc(self.sems[s], 1)
        for e in self.ENG:
            for f in self.ENG:
                if f == e:
                    continue
                s = self.esem[f]
                self.known[e][s] = arr[f]
                self.eng[e].wait_ge(self.sems[s], arr[f])


class Scope:
    def __init__(self, P):
        self.P = P
        self.es = ExitStack()
        self.toks = []

    def __enter__(self):
        self.es.__enter__()
        return self

    def __exit__(self, *a):
        self.P.barrier()
        self.P.release(self.toks)
        return self.es.__exit__(*a)

    uid = [0]

    def sb(self, name, shape, dt):
        Scope.uid[0] += 1
        return self.es.enter_context(self.P.nc.sbuf_tensor("%s_%d" % (name, Scope.uid[0]), list(shape), dt))

    def ps(self, name, shape, dt):
        Scope.uid[0] += 1
        return self.es.enter_context(self.P.nc.psum_tensor("%s_%d" % (name, Scope.uid[0]), list(shape), dt))

    def tok(self, name="", x=False):
        t = Tok(name, x)
        self.toks.append(t)
        return t

    def sbt(self, name, shape, dt):
        return self.sb(name, shape, dt), self.tok(name)

    def pst(self, name, shape, dt):
        return self.ps(name, shape, dt), self.tok(name, True)


def build(debug=False):
    nc = bass.Bass("TRN2", target_bir_lowering=False)
    dbg = {}

    def din(name, shape, dt=F32):
        return nc.dram_tensor(name, list(shape), dt, kind="ExternalInput").ap()

    def dscr(name, shape, dt):
        isd = bool(debug) and (debug is True or name in debug)
        kind = "ExternalOutput" if isd else "Internal"
        t = nc.dram_tensor(name, list(shape), dt, kind=kind).ap()
        if isd:
            dbg[name] = t
        return t

    x_in = din("x", [S, D])
    out_d = nc.dram_tensor("out", [S, D], F32, kind="ExternalOutput").ap()
    w_in = din("w_in", [2, D, DIN])
    n1w = din("n1w", [2, 128, 8])
    n2w = din("n2w", [2, 128, 8])
    fnw = din("fnw", [128, D])
    posk = din("posk", [2, 64, 32, 2])
    posv = din("posv", [2, 64, 32, 2])
    ckw1 = din("ckw1", [2, 64, 32, 256])
    cvw1 = din("cvw1", [2, 64, 32, 256])
    ckw2 = din("ckw2", [2, 128, 2, 64])
    cvw2 = din("cvw2", [2, 128, 2, 64])
    convw = din("convw", [2, 128, 8, 4])
    convb = din("convb", [2, 128, 8])
    lba = din("lba", [2, 128, 8])
    lbi = din("lbi", [2, 128, 8])
    llam = din("llam", [2, 128, 8])
    lwa = din("lwa", [2, 8, 128, 128])
    lwi = din("lwi", [2, 8, 128, 128])
    wua = din("wua", [2, 512, D])
    wur = din("wur", [2, D, D])
    wo = din("wo", [2, D, D])
    w1 = din("w1", [2, D, 4096])
    w2 = din("w2", [2, 4096, D])
    c_ident = din("c_ident", [128, 128], BF16)
    c_tric = din("c_tric", [128, 128], BF16)
    c_triw = din("c_triw", [128, 128], BF16)
    c_E = din("c_E", [64, S], BF16)
    c_band = din("c_band", [128, 9], BF16)
    c_A = din("c_A", [128, 128])
    c_B = din("c_B", [128, 128])

    xs = [dscr("xs0", [S, D], F32), dscr("xs1", [S, D], F32)]
    qT = dscr("qT", [8, 64, S], BF16)
    kcT = dscr("kcT", [2, 64, S], BF16)
    vcT = dscr("vcT", [2, 64, S], BF16)
    ksT = dscr("ksT", [2, 64, S], BF16)
    kwT = dscr("kwT", [2, 64, S], BF16)
    vtm = dscr("vtm", [S, 4, 64], BF16)
    gat = dscr("gat", [S, 24], F32)
    zf = dscr("zf", [4, D, S], F32)
    attnT = dscr("attnT", [512, S], BF16)
    rnnT = dscr("rnnT", [D, S], BF16)

    with ExitStack() as es:
        P = Prog(nc, es)
        T = {n: Tok(n) for n in ["x_in", "out", "w", "xs0", "xs1", "qT", "kcT", "vcT", "ksT", "kwT", "vtm", "gat", "zf", "attnT", "rnnT"]}
        for t in T.values():
            t.sem = None

        ident = es.enter_context(nc.sbuf_tensor("ident", [128, 128], BF16))
        t_const = Tok("const")
        P.dma("sp", ident[:], c_ident[:, :], T["w"], t_const, t_const)

        def convert(sc, dst_ap_fn, src_ap_fn, nrow_tiles, ncols, stg, scale_ap_fn=None, chunk=2048, rows=128, toks=None):
            i = 0
            engs = ("act", "dve", "pool")
            for kt in range(nrow_tiles):
                for c0 in range(0, ncols, chunk):
                    c1 = min(ncols, c0 + chunk)
                    st, stt = stg[i % len(stg)]
                    P.dma("sp", st[0:rows, 0:c1 - c0], src_ap_fn(kt, c0, c1), T["w"], stt, stt)
                    e = engs[i % 3]
                    tk = toks[kt] if toks is not None else None
                    dst = dst_ap_fn(kt, c0, c1)
                    src = st[0:rows, 0:c1 - c0]
                    if scale_ap_fn is None:
                        if e == "act":
                            P.op(e, lambda E, dst=dst, src=src: E.copy(out=dst, in_=src), r=[stt], w=[tk])
                        else:
                            P.op(e, lambda E, dst=dst, src=src: E.tensor_copy(out=dst, in_=src), r=[stt], w=[tk])
                    else:
                        sc_ap, sc_tok = scale_ap_fn(kt)
                        if e == "act":
                            P.op(e, lambda E, dst=dst, src=src, sc_ap=sc_ap: E.activation(out=dst, in_=src, func=AF.Copy, scale=sc_ap), r=[stt, sc_tok], w=[tk])
                        else:
                            P.op(e, lambda E, dst=dst, src=src, sc_ap=sc_ap: E.tensor_scalar(out=dst, in0=src, scalar1=sc_ap, scalar2=None, op0=ALU.mult), r=[stt, sc_tok], w=[tk])
                    i += 1

        def rms_rstd(sc, xt, xtok, junk, junktok, ssq, rstd, sstok):
            P.op("act", lambda E: E.activation(out=junk[:], in_=xt[:], func=AF.Square, accum_out=ssq[:, 0:1]), r=[xtok], w=[junktok, sstok])
            P.op("act", lambda E: E.activation(out=ssq[:, 1:2], in_=ssq[:, 0:1], func=AF.Sqrt, scale=1.0 / D, bias=epsb[:, 0:1]), r=[sstok, t_const], w=[sstok])
            P.op("dve", lambda E: E.reciprocal(out=rstd[:, 0:1], in_=ssq[:, 1:2]), r=[sstok], w=[sstok])

        epsb = es.enter_context(nc.sbuf_tensor("epsb", [128, 4], F32))
        P.op("dve", lambda E: E.memset(epsb[:, 0:1], EPS), w=[t_const])
        P.op("dve", lambda E: E.memset(epsb[:, 1:2], 1.0), w=[t_const])
        P.op("dve", lambda E: E.memset(epsb[:, 2:3], 0.0), w=[t_const])

        def phase_inproj(l, xsrc, xtok):
            with Scope(P) as sc:
                wbf = sc.sb("wbf", [128, 8, DIN], BF16)
                wtok = [sc.tok("wbf%d" % k) for k in range(8)]
                n1 = sc.sb("n1", [128, 8], F32)
                n1t = sc.tok()
                P.dma("sp", n1[:], n1w[l], T["w"], n1t, n1t)
                stg = [sc.sbt("stg%d" % i, [128, 1800], F32) for i in range(3)]
                convert(sc, lambda kt, c0, c1: wbf[:, kt, c0:c1], lambda kt, c0, c1: w_in[l, kt * 128:(kt + 1) * 128, c0:c1], 8, DIN, stg,
                        scale_ap_fn=lambda kt: (n1[:, kt:kt + 1], n1t), chunk=1800, toks=wtok)
                xt = [sc.sbt("xt%d" % i, [128, D], F32) for i in range(2)]
                junk, junkt = sc.sbt("junk", [128, D], BF16)
                ssq = [sc.sbt("ssq%d" % i, [128, 4], F32) for i in range(2)]
                xn = [sc.sbt("xn%d" % i, [128, D], BF16) for i in range(2)]
                xnT = [sc.sbt("xnT%d" % i, [128, 8, 512], BF16) for i in range(2)]
                tp = [sc.pst("tp%d" % i, [128, 8, 128], BF16) for i in range(2)]
                pf = [sc.pst("pf%d" % i, [128, 512], F32) for i in range(4)]
                ptm = [sc.pst("ptm", [128, 512], F32)]
                of32 = [sc.sbt("of32_%d" % i, [128, 512], F32) for i in range(4)]
                obf = [sc.sbt("obf_%d" % i, [128, 512], BF16) for i in range(4)]
                vst = [sc.sbt("vst%d" % i, [128, 256], BF16) for i in range(2)]
                gst = [sc.sbt("gst%d" % i, [128, 24], F32) for i in range(2)]
                cnt = {"pf": 0, "f": 0, "b": 0}
                for sg in range(S // 512):
                    xT, xTt = xnT[sg % 2]
                    for j in range(4):
                        tt = sg * 4 + j
                        x_t, x_tt = xt[tt % 2]
                        sq, sqt = ssq[tt % 2]
                        xb, xbt = xn[tt % 2]
                        tpp, tpt = tp[tt % 2]
                        P.dma("sp", x_t[:], xsrc[tt * 128:(tt + 1) * 128, :], xtok, x_tt, x_tt)
                        rms_rstd(sc, x_t, x_tt, junk, junkt, sq, sq[:, 2:3], sqt)
                        P.op("dve", lambda E, xb=xb, x_t=x_t, sq=sq: E.tensor_scalar(out=xb[:], in0=x_t[:], scalar1=sq[:, 2:3], scalar2=None, op0=ALU.mult), r=[x_tt, sqt], w=[xbt])
                        for kt in range(8):
                            P.op("pe", lambda E, tpp=tpp, xb=xb, kt=kt: E.transpose(out=tpp[:, kt, :], in_=xb[:, kt * 128:(kt + 1) * 128], identity=ident[:]), r=[xbt, t_const], w=[tpt])
                        P.op("act", lambda E, xT=xT, tpp=tpp, j=j: E.copy(out=xT[:, :, j * 128:(j + 1) * 128], in_=tpp[:]), r=[tpt], w=[xTt])
                        pt, ptt = ptm[0]
                        for (c0, n, o0) in ((O_VS, 128, 0), (O_VW, 128, 128), (O_GN, 24, 256)):
                            for kt in range(8):
                                P.op("pe", lambda E, pt=pt, xT=xT, kt=kt, c0=c0, n=n, o0=o0, j=j: E.matmul(pt[:, o0:o0 + n], lhsT=xT[:, kt, j * 128:(j + 1) * 128], rhs=wbf[:, kt, c0:c0 + n], start=(kt == 0), stop=(kt == 7)),
                                     r=[xTt, wtok[kt]], w=[ptt])
                        vs_, vst_ = vst[tt % 2]
                        gs_, gst_ = gst[tt % 2]
                        P.op("dve", lambda E, vs_=vs_, pt=pt: E.tensor_copy(out=vs_[:], in_=pt[:, 0:256]), r=[ptt], w=[vst_])
                        P.op("act", lambda E, gs_=gs_, pt=pt: E.activation(out=gs_[:], in_=pt[:, 256:280], func=AF.Sigmoid), r=[ptt], w=[gst_])
                        P.dma("pool", vtm[tt * 128:(tt + 1) * 128].rearrange("p a d -> p (a d)"), vs_[:], vst_, T["vtm"], vst_)
                        P.dma("pool", gat[tt * 128:(tt + 1) * 128, :], gs_[:], gst_, T["gat"], gst_)
                    tsl = slice(sg * 512, (sg + 1) * 512)
                    jobs = []
                    for h in range(8):
                        jobs.append((O_Q + h * 64, 64, "q", qT[h, :, tsl], "qT"))
                    for g in range(2):
                        jobs.append((O_KC + g * 64, 64, "c", kcT[g, :, tsl], "kcT"))
                        jobs.append((O_VC + g * 64, 64, "c", vcT[g, :, tsl], "vcT"))
                        jobs.append((O_KS + g * 64, 64, "c", ksT[g, :, tsl], "ksT"))
                        jobs.append((O_KW + g * 64, 64, "c", kwT[g, :, tsl], "kwT"))
                    for ft in range(8):
                        jobs.append((O_XR + ft * 128, 128, "f", zf[0, ft * 128:(ft + 1) * 128, tsl], "zf"))
                    for ft in range(8):
                        jobs.append((O_GR + ft * 128, 128, "gelu", zf[1, ft * 128:(ft + 1) * 128, tsl], "zf"))
                    for ft in range(8):
                        jobs.append((O_GA + ft * 128, 128, "sig", zf[2, ft * 128:(ft + 1) * 128, tsl], "zf"))
                    for ft in range(8):
                        jobs.append((O_GB + ft * 128, 128, "sig", zf[3, ft * 128:(ft + 1) * 128, tsl], "zf"))
                    for (c0, m, kind, dst, dtk) in jobs:
                        pp, ppt = pf[cnt["pf"] % 4]
                        cnt["pf"] += 1
                        for kt in range(8):
                            P.op("pe", lambda E, pp=pp, kt=kt, c0=c0, m=m, xT=xT: E.matmul(pp[0:m, :], lhsT=wbf[:, kt, c0:c0 + m], rhs=xT[:, kt, :], start=(kt == 0), stop=(kt == 7)),
                                 r=[xTt, wtok[kt]], w=[ppt])
                        if kind in ("q", "c"):
                            ob, obt = obf[cnt["b"] % 4]
                            cnt["b"] += 1
                            scl = 0.125 if kind == "q" else 1.0
                            P.op("dve", lambda E, ob=ob, pp=pp, m=m, scl=scl: E.tensor_scalar(out=ob[0:m, :], in0=pp[0:m, :], scalar1=scl, scalar2=None, op0=ALU.mult), r=[ppt], w=[obt])
                            P.dma("pool", dst, ob[0:m, :], obt, T[dtk], obt)
                        else:
                            ob, obt = of32[cnt["f"] % 4]
                            cnt["f"] += 1
                            if kind == "f":
                                P.op("dve", lambda E, ob=ob, pp=pp: E.tensor_copy(out=ob[:], in_=pp[:]), r=[ppt], w=[obt])
                            else:
                                fn = AF.Gelu_apprx_tanh if kind == "gelu" else AF.Sigmoid
                                P.op("act", lambda E, ob=ob, pp=pp, fn=fn: E.activation(out=ob[:], in_=pp[:], func=fn), r=[ppt], w=[obt])
                            P.dma("pool", dst, ob[:], obt, T[dtk], obt)

        def phase_mlp(l, xsrc, xtok, xdst, xdtok, final):
            with Scope(P) as sc:
                w1b = sc.sb("w1b", [128, 8, 4096], BF16)
                w1t = [sc.tok() for _ in range(8)]
                w2b = sc.sb("w2b", [128, 32, D], BF16)
                w2t = [sc.tok() for _ in range(32)]
                n2 = sc.sb("n2", [128, 8], F32)
                n2t = sc.tok()
                P.dma("sp", n2[:], n2w[l], T["w"], n2t, n2t)
                stg = [sc.sbt("stg%d" % i, [128, 1024], F32) for i in range(3)]
                convert(sc, lambda kt, c0, c1: w1b[:, kt, c0:c1], lambda kt, c0, c1: w1[l, kt * 128:(kt + 1) * 128, c0:c1], 8, 4096, stg,
                        scale_ap_fn=lambda kt: (n2[:, kt:kt + 1], n2t), chunk=1024, toks=w1t)
                convert(sc, lambda kt, c0, c1: w2b[:, kt, c0:c1], lambda kt, c0, c1: w2[l, kt * 128:(kt + 1) * 128, c0:c1], 32, D, stg, chunk=1024, toks=w2t)
                fw = None
                if final:
                    fw, fwt = sc.sbt("fw", [128, D], F32)
                    P.dma("sp", fw[:], fnw[:, :], T["w"], fwt, fwt)
                xt = [sc.sbt("xt%d" % i, [128, D], F32) for i in range(2)]
                junk, junkt = sc.sbt("junk", [128, D], BF16)
                ssq = [sc.sbt("ssq%d" % i, [128, 4], F32) for i in range(2)]
                xn, xnt = sc.sbt("xn", [128, D], BF16)
                xnT = [sc.sbt("xnT%d" % i, [128, 8, 128], BF16) for i in range(2)]
                hT = [sc.sbt("hT%d" % i, [128, 32, 128], BF16) for i in range(2)]
                hr, hrt = sc.sbt("hr", [128, 512], F32)
                tp = [sc.pst("tp%d" % i, [128, 8, 128], BF16) for i in range(1)]
                ph = [sc.pst("ph%d" % i, [128, 4, 128], F32) for i in range(3)]
                po = [sc.pst("po%d" % i, [128, 512], F32) for i in range(2)]
                xo = [sc.sbt("xo%d" % i, [128, D], F32) for i in range(2)]
                for tt in range(NT):
                    x_t, x_tt = xt[tt % 2]
                    sq, sqt = ssq[tt % 2]
                    tpp, tpt = tp[0]
                    xT, xTt = xnT[tt % 2]
                    h_, ht_ = hT[tt % 2]
                    P.dma("sp", x_t[:], xsrc[tt * 128:(tt + 1) * 128, :], xtok, x_tt, x_tt)
                    rms_rstd(sc, x_t, x_tt, junk, junkt, sq, sq[:, 2:3], sqt)
                    P.op("dve", lambda E, x_t=x_t, sq=sq: E.tensor_scalar(out=xn[:], in0=x_t[:], scalar1=sq[:, 2:3], scalar2=None, op0=ALU.mult), r=[x_tt, sqt], w=[xnt])
                    for kt in range(8):
                        P.op("pe", lambda E, tpp=tpp, kt=kt: E.transpose(out=tpp[:, kt, :], in_=xn[:, kt * 128:(kt + 1) * 128], identity=ident[:]), r=[xnt, t_const], w=[tpt])
                    P.op("act", lambda E, xT=xT, tpp=tpp: E.copy(out=xT[:], in_=tpp[:]), r=[tpt], w=[xTt])
                    for f4 in range(8):
                        pp, ppt = ph[f4 % 3]
                        for fi in range(4):
                            ft = f4 * 4 + fi
                            for kt in range(8):
                                P.op("pe", lambda E, pp=pp, fi=fi, ft=ft, kt=kt, xT=xT: E.matmul(pp[:, fi, :], lhsT=w1b[:, kt, ft * 128:(ft + 1) * 128], rhs=xT[:, kt, :], start=(kt == 0), stop=(kt == 7)),
                                     r=[xTt, w1t[kt]], w=[ppt])
                        P.op("act", lambda E, pp=pp: E.activation(out=hr[:], in_=pp[:].rearrange("p a b -> p (a b)"), func=AF.Relu), r=[ppt], w=[hrt])
                        e2 = "pool" if f4 % 2 else "dve"
                        P.op(e2, lambda E, h_=h_, f4=f4: E.tensor_tensor(out=h_[:, f4 * 4:(f4 + 1) * 4, :].rearrange("p a b -> p (a b)"), in0=hr[:], in1=hr[:], op=ALU.mult), r=[hrt], w=[ht_])
                    xo_, xot_ = xo[tt % 2]
                    for nh in range(2):
                        pq, pqt = po[nh]
                        for kt in range(32):
                            P.op("pe", lambda E, pq=pq, kt=kt, nh=nh, h_=h_: E.matmul(pq[:], lhsT=h_[:, kt, :], rhs=w2b[:, kt, nh * 512:(nh + 1) * 512], start=(kt == 0), stop=(kt == 31)),
                                 r=[ht_, w2t[kt]], w=[pqt])
                        P.op("dve", lambda E, xo_=xo_, pq=pq, nh=nh, x_t=x_t: E.tensor_tensor(out=xo_[:, nh * 512:(nh + 1) * 512], in0=pq[:], in1=x_t[:, nh * 512:(nh + 1) * 512], op=ALU.add), r=[pqt, x_tt], w=[xot_])
                    if not final:
                        P.dma("pool", xdst[tt * 128:(tt + 1) * 128, :], xo_[:], xot_, xdtok, xot_)
                    else:
                        sq2, sq2t = ssq[tt % 2]
                        rms_rstd(sc, xo_, xot_, junk, junkt, sq2, sq2[:, 3:4], sq2t)
                        P.op("dve", lambda E, xo_=xo_, sq2=sq2: E.scalar_tensor_tensor(out=xo_[:], in0=xo_[:], scalar=sq2[:, 3:4], in1=fw[:], op0=ALU.mult, op1=ALU.mult), r=[sq2t, fwt], w=[xot_])
                        P.dma("pool", out_d[tt * 128:(tt + 1) * 128, :], xo_[:], xot_, T["out"], xot_)


        def phase_rnn(l):
            with Scope(P) as sc:
                prm, prmt = sc.sbt("prm", [128, 64], F32)
                P.dma("sp", prm[:, 0:32], convw[l].rearrange("p a b -> p (a b)"), T["w"], prmt, prmt)
                P.dma("sp", prm[:, 32:40], convb[l], T["w"], prmt, prmt)
                P.dma("sp", prm[:, 40:48], lba[l], T["w"], prmt, prmt)
                P.dma("sp", prm[:, 48:56], lbi[l], T["w"], prmt, prmt)
                P.dma("sp", prm[:, 56:64], llam[l], T["w"], prmt, prmt)
                P.op("act", lambda E: E.activation(out=prm[:, 56:64], in_=prm[:, 56:64], func=AF.Exp, scale=-1.0), r=[prmt], w=[prmt])
                P.op("act", lambda E: E.activation(out=prm[:, 56:64], in_=prm[:, 56:64], func=AF.Ln, bias=epsb[:, 1:2]), r=[prmt, t_const], w=[prmt])
                P.op("dve", lambda E: E.tensor_scalar(out=prm[:, 56:64], in0=prm[:, 56:64], scalar1=-8.0, scalar2=None, op0=ALU.mult), r=[prmt], w=[prmt])
                wst, wstt = sc.sbt("wst", [128, 128], F32)
                wab, wabt = sc.sbt("wab", [128, 128], BF16)
                wib, wibt = sc.sbt("wib", [128, 128], BF16)
                xrp, xrpt = sc.sbt("xrp", [128, S + 4], F32)
                gg, ggt = sc.sbt("gg", [128, S], F32)
                xc, xct = sc.sbt("xc", [128, S], F32)
                xcb, xcbt = sc.sbt("xcb", [128, S], BF16)
                rr, rrt = sc.sbt("rr", [128, S], F32)
                ig, igt = sc.sbt("ig", [128, S], F32)
                aa, aat = sc.sbt("aa", [128, S], F32)
                ro, rot = sc.sbt("ro", [128, S], BF16)
                pa = [sc.pst("pa%d" % i, [128, 512], F32) for i in range(4)]
                P.op("dve", lambda E: E.memset(xrp[:, 0:4], 0.0), w=[xrpt])
                for ct in range(8):
                    P.dma("sp", wst[:], lwa[l, ct], T["w"], wstt, wstt)
                    P.op("dve", lambda E: E.tensor_copy(out=wab[:], in_=wst[:]), r=[wstt], w=[wabt])
                    P.dma("sp", wst[:], lwi[l, ct], T["w"], wstt, wstt)
                    P.op("dve", lambda E: E.tensor_copy(out=wib[:], in_=wst[:]), r=[wstt], w=[wibt])
                    P.dma("sp", xrp[:, 4:S + 4], zf[0, ct * 128:(ct + 1) * 128, :], T["zf"], xrpt, xrpt)
                    P.dma("sp", gg[:], zf[1, ct * 128:(ct + 1) * 128, :], T["zf"], ggt, ggt)
                    P.op("act", lambda E, ct=ct: E.activation(out=xc[:], in_=xrp[:, 4:S + 4], func=AF.Identity, scale=prm[:, ct * 4 + 3:ct * 4 + 4], bias=prm[:, 32 + ct:33 + ct]), r=[xrpt, prmt], w=[xct])
                    for i in range(3):
                        P.op("dve", lambda E, ct=ct, i=i: E.scalar_tensor_tensor(out=xc[:], in0=xrp[:, 1 + i:1 + i + S], scalar=prm[:, ct * 4 + i:ct * 4 + i + 1], in1=xc[:], op0=ALU.mult, op1=ALU.add), r=[xrpt, prmt], w=[xct])
                    P.op("pool", lambda E: E.tensor_copy(out=xcb[:], in_=xc[:]), r=[xct], w=[xcbt])
                    for tg in range(8):
                        sl = slice(tg * 512, (tg + 1) * 512)
                        p1, p1t = pa[(2 * tg) % 4]
                        p2, p2t = pa[(2 * tg + 1) % 4]
                        P.op("pe", lambda E, p1=p1, sl=sl: E.matmul(p1[:], lhsT=wab[:], rhs=xcb[:, sl], start=True, stop=True), r=[wabt, xcbt], w=[p1t])
                        P.op("pe", lambda E, p2=p2, sl=sl: E.matmul(p2[:], lhsT=wib[:], rhs=xcb[:, sl], start=True, stop=True), r=[wibt, xcbt], w=[p2t])
                        P.op("act", lambda E, p1=p1, sl=sl, ct=ct: E.activation(out=rr[:, sl], in_=p1[:], func=AF.Sigmoid, bias=prm[:, 40 + ct:41 + ct]), r=[p1t, prmt], w=[rrt])
                        P.op("act", lambda E, p2=p2, sl=sl, ct=ct: E.activation(out=ig[:, sl], in_=p2[:], func=AF.Sigmoid, bias=prm[:, 48 + ct:49 + ct]), r=[p2t, prmt], w=[igt])
                    P.op("act", lambda E, ct=ct: E.activation(out=aa[:], in_=rr[:], func=AF.Exp, scale=prm[:, 56 + ct:57 + ct]), r=[rrt, prmt], w=[aat])
                    P.op("pool", lambda E: E.tensor_tensor(out=rr[:], in0=aa[:], in1=aa[:], op=ALU.mult), r=[aat], w=[rrt])
                    P.op("act", lambda E: E.activation(out=rr[:], in_=rr[:], func=AF.Sqrt, scale=-1.0, bias=epsb[:, 1:2]), r=[rrt, t_const], w=[rrt])
                    P.op("pool", lambda E: E.tensor_tensor(out=ig[:], in0=ig[:], in1=xc[:], op=ALU.mult), r=[xct], w=[igt])
                    P.op("dve", lambda E: E.tensor_tensor(out=ig[:], in0=ig[:], in1=rr[:], op=ALU.mult), r=[rrt], w=[igt])
                    P.op("dve", lambda E: E.tensor_tensor_scan(out=xc[:], data0=aa[:], data1=ig[:], initial=0.0, op0=ALU.mult, op1=ALU.add), r=[aat, igt], w=[xct])
                    P.op("pool", lambda E: E.tensor_tensor(out=ro[:], in0=xc[:], in1=gg[:], op=ALU.mult), r=[xct, ggt], w=[rot])
                    P.dma("pool", rnnT[ct * 128:(ct + 1) * 128, :], ro[:], rot, T["rnnT"], rot)

        def phase_merge(l, xsrc, xtok, xdst, xdtok):
            with Scope(P) as sc:
                wa = sc.sb("wa", [128, 4, D], BF16)
                wat = [sc.tok() for _ in range(4)]
                wr = sc.sb("wr", [128, 8, D], BF16)
                wrt = [sc.tok() for _ in range(8)]
                wob = sc.sb("wob", [128, 8, D], BF16)
                wot = [sc.tok() for _ in range(8)]
                stg = [sc.sbt("stg%d" % i, [128, 1024], F32) for i in range(3)]
                convert(sc, lambda kt, c0, c1: wa[:, kt, c0:c1], lambda kt, c0, c1: wua[l, kt * 128:(kt + 1) * 128, c0:c1], 4, D, stg, chunk=1024, toks=wat)
                convert(sc, lambda kt, c0, c1: wr[:, kt, c0:c1], lambda kt, c0, c1: wur[l, kt * 128:(kt + 1) * 128, c0:c1], 8, D, stg, chunk=1024, toks=wrt)
                convert(sc, lambda kt, c0, c1: wob[:, kt, c0:c1], lambda kt, c0, c1: wo[l, kt * 128:(kt + 1) * 128, c0:c1], 8, D, stg, chunk=1024, toks=wot)
                aT = [sc.sbt("aT%d" % i, [128, 4, 512], BF16) for i in range(2)]
                rT = [sc.sbt("rT%d" % i, [128, 8, 512], BF16) for i in range(2)]
                sa = [sc.sbt("sa%d" % i, [128, 512], F32) for i in range(2)]
                sb_ = [sc.sbt("sb%d" % i, [128, 512], F32) for i in range(2)]
                t1 = [sc.sbt("t1%d" % i, [128, 512], F32) for i in range(2)]
                t2 = [sc.sbt("t2%d" % i, [128, 512], F32) for i in range(2)]
                mT = [sc.sbt("mT%d" % i, [128, 8, 512], BF16) for i in range(2)]
                pA = [sc.pst("pA%d" % i, [128, 512], F32) for i in range(2)]
                pB = [sc.pst("pB%d" % i, [128, 512], F32) for i in range(2)]
                pO = [sc.pst("pO%d" % i, [128, 512], F32) for i in range(2)]
                xt = [sc.sbt("xt%d" % i, [128, D], F32) for i in range(2)]
                xo = [sc.sbt("xo%d" % i, [128, D], F32) for i in range(2)]
                k = 0
                for sg in range(8):
                    tsl = slice(sg * 512, (sg + 1) * 512)
                    a_, at_ = aT[sg % 2]
                    r_, rt_ = rT[sg % 2]
                    m_, mt_ = mT[sg % 2]
                    P.dma("sp", a_[:], attnT[:, tsl].rearrange("(a p) t -> p a t", p=128), T["attnT"], at_, at_)
                    P.dma("sp", r_[:], rnnT[:, tsl].rearrange("(a p) t -> p a t", p=128), T["rnnT"], rt_, rt_)
                    for ft in range(8):
                        fs = slice(ft * 128, (ft + 1) * 128)
                        sa_, sat_ = sa[k % 2]
                        sbb, sbt_ = sb_[k % 2]
                        u1, u1t = t1[k % 2]
                        u2, u2t = t2[k % 2]
                        p_a, pat = pA[k % 2]
                        p_b, pbt = pB[k % 2]
                        k += 1
                        P.dma("sp", sa_[:], zf[2, fs, tsl], T["zf"], sat_, sat_)
                        P.dma("sp", sbb[:], zf[3, fs, tsl], T["zf"], sbt_, sbt_)
                        for kt in range(4):
                            P.op("pe", lambda E, p_a=p_a, kt=kt, fs=fs, a_=a_: E.matmul(p_a[:], lhsT=wa[:, kt, fs], rhs=a_[:, kt, :], start=(kt == 0), stop=(kt == 3)), r=[wat[kt], at_], w=[pat])
                        for kt in range(8):
                            P.op("pe", lambda E, p_b=p_b, kt=kt, fs=fs, r_=r_: E.matmul(p_b[:], lhsT=wr[:, kt, fs], rhs=r_[:, kt, :], start=(kt == 0), stop=(kt == 7)), r=[wrt[kt], rt_], w=[pbt])
                        P.op("dve", lambda E, u1=u1, p_a=p_a, sa_=sa_: E.tensor_tensor(out=u1[:], in0=p_a[:], in1=sa_[:], op=ALU.mult), r=[pat, sat_], w=[u1t])
                        P.op("dve", lambda E, u2=u2, p_b=p_b, sbb=sbb: E.tensor_tensor(out=u2[:], in0=p_b[:], in1=sbb[:], op=ALU.mult), r=[pbt, sbt_], w=[u2t])
                        P.op("pool", lambda E, m_=m_, ft=ft, u1=u1, u2=u2: E.tensor_tensor(out=m_[:, ft, :], in0=u1[:], in1=u2[:], op=ALU.add), r=[u1t, u2t], w=[mt_])
                    for j in range(4):
                        tt = sg * 4 + j
                        x_t, x_tt = xt[tt % 2]
                        xo_, xot_ = xo[tt % 2]
                        P.dma("sp", x_t[:], xsrc[tt * 128:(tt + 1) * 128, :], xtok, x_tt, x_tt)
                        for nh in range(2):
                            pq, pqt = pO[nh]
                            for kt in range(8):
                                P.op("pe", lambda E, pq=pq, kt=kt, nh=nh, m_=m_, j=j: E.matmul(pq[:], lhsT=m_[:, kt, j * 128:(j + 1) * 128], rhs=wob[:, kt, nh * 512:(nh + 1) * 512], start=(kt == 0), stop=(kt == 7)), r=[mt_, wot[kt]], w=[pqt])
                            P.op("dve", lambda E, xo_=xo_, pq=pq, nh=nh, x_t=x_t: E.tensor_tensor(out=xo_[:, nh * 512:(nh + 1) * 512], in0=pq[:], in1=x_t[:, nh * 512:(nh + 1) * 512], op=ALU.add), r=[pqt, x_tt], w=[xot_])
                        P.dma("pool", xdst[tt * 128:(tt + 1) * 128, :], xo_[:], xot_, xdtok, xot_)


        def phase_attn(l):
            with Scope(P) as sc:
                cst = sc.tok("cst")
                tric = sc.sb("tric", [128, 128], BF16)
                triw = sc.sb("triw", [128, 128], BF16)
                Ec = sc.sb("Ec", [64, S], BF16)
                band = sc.sb("band", [128, 9], BF16)
                cA = sc.sb("cA", [128, 128], F32)
                cB = sc.sb("cB", [128, 128], F32)
                for dst, src in ((tric, c_tric), (triw, c_triw), (Ec, c_E), (band, c_band), (cA, c_A), (cB, c_B)):
                    P.dma("sp", dst[:], src[:, :], T["w"], cst, cst)
                kcm = [sc.sbt("kcm%d" % g, [64, 256], BF16) for g in range(2)]
                vcm = [sc.sbt("vcm%d" % g, [128, 2, 64], BF16) for g in range(2)]
                with Scope(P) as s2:
                    stg = [s2.sbt("cstg%d" % i, [64, 2048], F32) for i in range(2)]
                    w2s, w2st = s2.sbt("w2s", [128, 128], F32)
                    pss, psst = s2.sbt("pss", [64, 64], F32)
                    w1b = s2.sb("w1b", [64, 32, 256], BF16)
                    w1bt = s2.tok()
                    w2b, w2bt = s2.sbt("w2b", [128, 2, 64], BF16)
                    posb, posbt = s2.sbt("posb", [64, 32, 2], BF16)
                    kg = [s2.sbt("kg%d" % g, [64, S], BF16) for g in range(2)]
                    hid = [[s2.sbt("hid%d%d" % (g, h), [128, 256], BF16) for h in range(2)] for g in range(2)]
                    bia, biat = s2.sbt("bia", [128, 2], F32)
                    psg = [s2.pst("psg%d" % g, [128, 512], F32) for g in range(2)]
                    psb, psbt = s2.pst("psb", [128, 512], F32)
                    pso, psot = s2.pst("pso", [128, 512], F32)
                    for (w1d, w2d, posd, srcT, srct, is_k) in ((ckw1, ckw2, posk, kcT, "kcT", True), (cvw1, cvw2, posv, vcT, "vcT", False)):
                        convert(s2, lambda kt, c0, c1: w1b[:].rearrange("p a b -> p (a b)")[:, c0:c1], lambda kt, c0, c1: w1d[l].rearrange("p a b -> p (a b)")[:, c0:c1], 1, 32 * 256, stg, chunk=2048, rows=64, toks=[w1bt])
                        P.dma("sp", w2s[:], w2d[l].rearrange("p a b -> p (a b)"), T["w"], w2st, w2st)
                        P.op("dve", lambda E: E.tensor_copy(out=w2b[:].rearrange("p a b -> p (a b)"), in_=w2s[:]), r=[w2st], w=[w2bt])
                        P.dma("sp", pss[:], posd[l].rearrange("p a b -> p (a b)"), T["w"], psst, psst)
                        P.op("dve", lambda E: E.tensor_copy(out=posb[:].rearrange("p a b -> p (a b)"), in_=pss[:]), r=[psst], w=[posbt])
                        for g in range(2):
                            P.dma("sp", kg[g][0][:], srcT[g], T[srct], kg[g][1], kg[g][1])
                        for ht in range(2):
                            hs = slice(ht * 128, (ht + 1) * 128)
                            for p in range(32):
                                for g in range(2):
                                    P.op("pe", lambda E, g=g, p=p, hs=hs: E.matmul(psg[g][0][:, 0:255], lhsT=w1b[:, p, hs], rhs=kg[g][0][:, p:p + 16 * 254 + 1:16], start=(p == 0), stop=(p == 31)),
                                         r=[w1bt, kg[g][1]], w=[psg[g][1]])
                                P.op("pe", lambda E, p=p, hs=hs: E.matmul(psb[:, 0:2], lhsT=w1b[:, p, hs], rhs=posb[:, p, :], start=(p == 0), stop=(p == 31)), r=[w1bt, posbt], w=[psbt])
                            P.op("dve", lambda E: E.tensor_copy(out=bia[:], in_=psb[:, 0:2]), r=[psbt], w=[biat])
                            for g in range(2):
                                P.op("act", lambda E, g=g, ht=ht: E.activation(out=hid[g][ht][0][:, 0:255], in_=psg[g][0][:, 0:255], func=AF.Gelu_apprx_tanh, bias=bia[:, 0:1]), r=[psg[g][1], biat], w=[hid[g][ht][1]])
                        for g in range(2):
                            if is_k:
                                for ht in range(2):
                                    P.op("pe", lambda E, g=g, ht=ht: E.matmul(pso[0:64, 0:255], lhsT=w2b[:, ht, :], rhs=hid[g][ht][0][:, 0:255], start=(ht == 0), stop=(ht == 1)), r=[w2bt, hid[g][ht][1]], w=[psot])
                                P.op("dve", lambda E, g=g: E.tensor_copy(out=kcm[g][0][:, 0:255], in_=pso[0:64, 0:255]), r=[psot], w=[kcm[g][1]])
                            else:
                                for ctile in range(2):
                                    n = 128 if ctile == 0 else 127
                                    for ht in range(2):
                                        P.op("pe", lambda E, g=g, ht=ht, ctile=ctile, n=n: E.matmul(pso[0:n, 256:320], lhsT=hid[g][ht][0][:, ctile * 128:ctile * 128 + n], rhs=w2b[:, ht, :], start=(ht == 0), stop=(ht == 1)), r=[w2bt, hid[g][ht][1]], w=[psot])
                                    P.op("dve", lambda E, g=g, ctile=ctile, n=n: E.tensor_copy(out=vcm[g][0][0:n, ctile, :], in_=pso[0:n, 256:320]), r=[psot], w=[vcm[g][1]])
                KE = [sc.sbt("KE%d" % g, [128, S], BF16) for g in range(2)]
                kwn = [sc.sbt("kwn%d" % g, [64, S], BF16) for g in range(2)]
                vsl = [sc.sbt("vsl%d" % g, [128, 32, 65], BF16) for g in range(2)]
                vwn = [sc.sbt("vwn%d" % g, [128, 32, 65], BF16) for g in range(2)]
                for g in range(2):
                    P.dma("sp", KE[g][0][0:64, :], ksT[g], T["ksT"], KE[g][1], KE[g][1])
                    P.dma("sp", KE[g][0][64:128, :], c_E[:, :], T["w"], KE[g][1], KE[g][1])
                    P.dma("sp", kwn[g][0][:], kwT[g], T["kwT"], kwn[g][1], kwn[g][1])
                    for (vv, j) in ((vsl[g], g), (vwn[g], 2 + g)):
                        P.op("pool", lambda E, vv=vv: E.memset(vv[0][:, :, 64:65], 1.0), w=[vv[1]])
                        for k8 in range(8):
                            P.dma("sp", vv[0][:, k8 * 4:(k8 + 1) * 4, 0:64], vtm[k8 * 512:(k8 + 1) * 512, j, :].rearrange("(k p) d -> p k d", p=128), T["vtm"], vv[1], vv[1])
                QP = [[sc.sbt("QP%d_%d" % (i, h), [128, 512], BF16) for h in range(8)] for i in range(2)]
                gt = [sc.sbt("gt%d" % i, [128, 4, 24], F32) for i in range(2)]
                bst = [sc.pst("bst%d" % i, [128, 512], F32) for i in range(2)]
                b_os = [sc.pst("b_os%d" % i, [128, 512], F32) for i in range(2)]
                b_ow = [sc.pst("b_ow%d" % i, [128, 512], F32) for i in range(2)]
                b_xs, b_xst = sc.pst("b_xs", [128, 512], F32)
                b_tp = sc.ps("b_tp", [128, 1024], BF16)
                tp_t = sc.tok("tp", True)
                ee = [sc.sbt("ee%d" % i, [128, 256], F32) for i in range(4)]
                pb = [sc.sbt("pb%d" % i, [128, 256], BF16) for i in range(4)]
                pTc = [sc.sbt("pTc%d" % i, [128, 128], BF16) for i in range(2)]
                pT = [sc.sbt("pT%d" % i, [128, 512], BF16) for i in range(4)]
                sm, smt = sc.sbt("sm", [128, 16], F32)
                sy, syt = sc.sbt("sy", [128, 8], F32)
                P4, P4t = sc.sbt("P4", [128, 264], F32)
                imp, impt = sc.sbt("imp", [128, 64], F32)
                scr, scrt = sc.sbt("scr", [128, 64], F32)
                scr2, scr2t = sc.sbt("scr2", [128, 64], F32)
                penb, penbt = sc.sbt("penb", [128, 128], BF16)
                G3, G3t = sc.sbt("G3", [128, 64, 64], BF16)
                P.op("pool", lambda E: E.memset(penb[:], 0.0), w=[penbt])
                att2 = [sc.sbt("att%d" % i, [128, 4, 512], F32) for i in range(2)]
                attb, attbt = sc.sbt("attb", [128, 4, 512], BF16)
                ast = [sc.sbt("ast%d" % i, [128, 4, 512], BF16) for i in range(2)]
                P.op("dve", lambda E: E.memset(P4[:], 0.0), w=[P4t])
                P4v = P4[:, 0:256].rearrange("p (j f) -> p j f", f=4)
                P4w = P4[:, 4:260].rearrange("p (j f) -> p j f", f=4)
                sidx = [0]
                NSG = ATT_DBG["nqb"] // 4

                def gen_X(sg):
                    ss = slice(sg * 512, (sg + 1) * 512)
                    qp = QP[sg % 2]
                    g_, gt_ = gt[sg % 2]
                    att, attt = att2[sg % 2]
                    for h in range(8):
                        P.dma("sp", qp[h][0][0:64, :], qT[h, :, ss], T["qT"], qp[h][1], qp[h][1])
                    P.dma("sp", g_[:], gat[ss, :].rearrange("(j p) c -> p j c", p=128), T["gat"], gt_, gt_)
                    yield
                    for g in range(2):
                        for j in range(4):
                            qb = sg * 4 + j
                            js = slice(j * 128, (j + 1) * 128)
                            Nc = min(255, 8 * qb + 7)
                            cb0 = max(0, 8 * qb - 2)
                            cb1 = min(Nc, 8 * qb + 7)
                            ps_s = b_xs[:, 0:256]
                            for hh in range(4):
                                h = g * 4 + hh
                                P.op("pe", lambda E, h=h, g=g, js=js, Nc=Nc: E.matmul(ps_s[:, 0:Nc], lhsT=qp[h][0][0:64, js], rhs=kcm[g][0][:, 0:Nc], start=True, stop=False), r=[qp[h][1], kcm[g][1]], w=[b_xst])
                                P.op("pe", lambda E, cb0=cb0, cb1=cb1, qb=qb: E.matmul(ps_s[:, cb0:cb1], lhsT=ident[:], rhs=band[:, cb0 - (8 * qb - 2):cb1 - (8 * qb - 2)], start=False, stop=True), r=[t_const, cst], w=[b_xst])
                                P.op("dve", lambda E, hh=hh, Nc=Nc: E.tensor_reduce(out=sm[:, hh:hh + 1], in_=ps_s[:, 0:Nc], axis=AX.X, op=ALU.max), r=[b_xst], w=[smt])
                                P.op("dve", lambda E, hh=hh: E.tensor_scalar(out=sm[:, 4 + hh:5 + hh], in0=sm[:, hh:hh + 1], scalar1=-10000.0, scalar2=-1.0, op0=ALU.max, op1=ALU.mult), r=[smt], w=[smt])
                                e_, et_ = ee[hh]
                                P.op("act", lambda E, e_=e_, hh=hh, Nc=Nc: E.activation(out=e_[:, 0:Nc], in_=ps_s[:, 0:Nc], func=AF.Exp, bias=sm[:, 4 + hh:5 + hh], accum_out=sm[:, 8 + hh:9 + hh]), r=[b_xst, smt], w=[et_, smt])
                                yield
                            P.op("dve", lambda E: E.tensor_scalar(out=sm[:, 12:16], in0=sm[:, 8:12], scalar1=1e-20, scalar2=None, op0=ALU.max), r=[smt], w=[smt])
                            P.op("dve", lambda E: E.reciprocal(out=sm[:, 12:16], in_=sm[:, 12:16]), r=[smt], w=[smt])
                            for hh in range(4):
                                e_, et_ = ee[hh]
                                p_, pt_ = pb[hh]
                                P.op("pool", lambda E, p_=p_, e_=e_, hh=hh, Nc=Nc: E.tensor_scalar(out=p_[:, 0:Nc], in0=e_[:, 0:Nc], scalar1=sm[:, 12 + hh:13 + hh], scalar2=None, op0=ALU.mult), r=[et_, smt], w=[pt_])
                                if hh == 0:
                                    P.op("dve", lambda E, e_=e_, hh=hh, Nc=Nc: E.tensor_scalar(out=P4[:, 4:4 + Nc], in0=e_[:, 0:Nc], scalar1=sm[:, 12 + hh:13 + hh], scalar2=None, op0=ALU.mult), r=[et_, smt], w=[P4t])
                                else:
                                    P.op("dve", lambda E, e_=e_, hh=hh, Nc=Nc: E.scalar_tensor_tensor(out=P4[:, 4:4 + Nc], in0=e_[:, 0:Nc], scalar=sm[:, 12 + hh:13 + hh], in1=P4[:, 4:4 + Nc], op0=ALU.mult, op1=ALU.add), r=[et_, smt], w=[P4t])
                            yield
                            P.op("dve", lambda E: E.tensor_tensor(out=imp[:], in0=P4v[:, :, 1], in1=P4v[:, :, 2], op=ALU.add), r=[P4t], w=[impt])
                            P.op("dve", lambda E: E.tensor_tensor(out=imp[:], in0=imp[:], in1=P4v[:, :, 3], op=ALU.add), r=[P4t], w=[impt])
                            P.op("dve", lambda E: E.scalar_tensor_tensor(out=imp[:], in0=imp[:], scalar=2.0, in1=P4v[:, :, 0], op0=ALU.mult, op1=ALU.add), r=[P4t], w=[impt])
                            P.op("dve", lambda E: E.tensor_tensor(out=imp[:], in0=imp[:], in1=P4w[:, :, 0], op=ALU.add), r=[P4t], w=[impt])
                            P.op("dve", lambda E, qb=qb: E.tensor_tensor(out=scr[:], in0=imp[:], in1=cA[:, 64 - 2 * qb:128 - 2 * qb], op=ALU.mult), r=[impt, cst], w=[scrt])
                            P.op("dve", lambda E, qb=qb: E.tensor_tensor(out=scr[:], in0=scr[:], in1=cB[:, 64 - 2 * qb:128 - 2 * qb], op=ALU.add), r=[cst], w=[scrt])
                            P.op("dve", lambda E: E.memset(scr[:, 0:1], 1e4), w=[scrt])
                            yield
                            nb = min(64, 2 * qb + 2)
                            if nb > 16:
                                P.op("dve", lambda E, nb=nb: E.tensor_tensor(out=G3[:, 0:nb, 0:nb], in0=scr[:, 0:nb].unsqueeze(1).to_broadcast([128, nb, nb]), in1=scr[:, 0:nb].unsqueeze(2).to_broadcast([128, nb, nb]), op=ALU.is_gt), r=[scrt], w=[G3t])
                                P.op("dve", lambda E, nb=nb: E.tensor_reduce(out=scr2[:, 0:nb], in_=G3[:, 0:nb, 0:nb], axis=AX.X, op=ALU.add), r=[G3t], w=[scr2t])
                                P.op("dve", lambda E, nb=nb: E.tensor_scalar(out=scr2[:, 0:nb], in0=scr2[:, 0:nb], scalar1=16.0, scalar2=None, op0=ALU.is_lt), r=[scr2t], w=[scr2t])
                                P.op("dve", lambda E, nb=nb: E.tensor_scalar(out=penb[:, 64:64 + nb], in0=scr2[:, 0:nb], scalar1=-1.0, scalar2=-NEGM, op0=ALU.add, op1=ALU.mult), r=[scr2t], w=[penbt])
                                yield
                            ptr = b_tp[:, 256:384]
                            P.op("pe", lambda E, ptr=ptr: E.transpose(out=ptr, in_=penb[:], identity=ident[:]), r=[penbt, t_const], w=[tp_t])
                            h0 = g * 4
                            P.op("act", lambda E, ptr=ptr, js=js, h0=h0: E.copy(out=qp[h0][0][64:128, js], in_=ptr[64:128, :]), r=[tp_t], w=[qp[h0][1]])
                            for hh in range(1, 4):
                                P.op("pool", lambda E, js=js, h0=h0, hh=hh: E.tensor_copy(out=qp[h0 + hh][0][64:128, js], in_=qp[h0][0][64:128, js]), r=[qp[h0][1]], w=[qp[h0 + hh][1]])
                            yield
                            k2 = 0
                            for hh in range(4):
                                p_, pt_ = pb[hh]
                                nct = (Nc + 127) // 128
                                for ctile in range(nct):
                                    n = min(128, Nc - ctile * 128)
                                    tpc = b_tp[:, (k2 % 2) * 128:(k2 % 2) * 128 + 128]
                                    pc_, pct_ = pTc[k2 % 2]
                                    k2 += 1
                                    P.op("pe", lambda E, tpc=tpc, p_=p_, ctile=ctile, n=n: E.transpose(out=tpc[0:n, :], in_=p_[:, ctile * 128:ctile * 128 + n], identity=ident[:]), r=[pt_, t_const], w=[tp_t])
                                    P.op("act", lambda E, pc_=pc_, tpc=tpc, n=n: E.copy(out=pc_[0:n, :], in_=tpc[0:n, :]), r=[tp_t], w=[pct_])
                                    P.op("pe", lambda E, pc_=pc_, hh=hh, n=n, ctile=ctile, g=g, nct=nct: E.matmul(b_xs[:, 256 + hh * 64:256 + (hh + 1) * 64], lhsT=pc_[0:n, :], rhs=vcm[g][0][0:n, ctile, :], start=(ctile == 0), stop=(ctile == nct - 1), skip_group_check=True), r=[pct_, vcm[g][1]], w=[b_xst])
                                yield
                            gs_ = slice(g * 256, (g + 1) * 256)
                            P.op("dve", lambda E, j=j, gs_=gs_, g=g, g_=g_, att=att: E.tensor_tensor(out=att[:, j, gs_].rearrange("p (h d) -> p h d", d=64), in0=b_xs[:, 256:512].rearrange("p (h d) -> p h d", d=64),
                                                                         in1=g_[:, j, g * 12:(g + 1) * 12].rearrange("p (h c) -> p h c", c=3)[:, :, 0:1].to_broadcast([128, 4, 64]), op=ALU.mult), r=[b_xst, gt_], w=[attt])
                            yield

                def gen_Y(sg):
                    ss = slice(sg * 512, (sg + 1) * 512)
                    qp = QP[sg % 2]
                    g_, gt_ = gt[sg % 2]
                    att, attt = att2[sg % 2]
                    for g in range(2):
                        for hh in range(4):
                            h = g * 4 + hh
                            q_, qt_ = qp[h]
                            os_, ost_ = b_os[hh % 2]
                            ow_, owt_ = b_ow[hh % 2]
                            steps = []
                            for kt in range(0, 4 * sg + 4):
                                steps.append(("s", kt))
                            for kt in range(max(0, 4 * sg - 4), 4 * sg + 4):
                                steps.append(("w", kt))
                            first = {"s": True, "w": True}
                            LA = 1
                            ring = []
                            for i in range(len(steps) + LA):
                                if i < len(steps):
                                    br, kt = steps[i]
                                    r_ = kt - 4 * sg
                                    if br == "s":
                                        jlo, jhi = max(r_, 0), 3
                                    else:
                                        jlo, jhi = max(r_, 0), min(r_ + 4, 3)
                                    c0, c1 = jlo * 128, (jhi + 1) * 128
                                    si = sidx[0] % 4
                                    stp = bst[sidx[0] % 2][0]
                                    stt_ = bst[sidx[0] % 2][1]
                                    sidx[0] += 1
                                    ks_ = slice(kt * 128, (kt + 1) * 128)
                                    ex = []
                                    if r_ >= 0:
                                        ex.append((r_, tric))
                                    if br == "w" and 0 <= r_ + 4 <= 3:
                                        ex.append((r_ + 4, triw))
                                    if br == "s":
                                        P.op("pe", lambda E, stp=stp, ks_=ks_, q_=q_, g=g, c0=c0, c1=c1, ex=ex: E.matmul(stp[:, c0:c1], lhsT=KE[g][0][:, ks_], rhs=q_[:, c0:c1], start=True, stop=(len(ex) == 0)), r=[KE[g][1], qt_], w=[stt_])
                                    else:
                                        P.op("pe", lambda E, stp=stp, ks_=ks_, q_=q_, g=g, c0=c0, c1=c1, ex=ex: E.matmul(stp[:, c0:c1], lhsT=kwn[g][0][:, ks_], rhs=q_[0:64, c0:c1], start=True, stop=(len(ex) == 0)), r=[kwn[g][1], qt_], w=[stt_])
                                    for xi, (jj, tri_) in enumerate(ex):
                                        P.op("pe", lambda E, stp=stp, jj=jj, tri_=tri_, xi=xi, ex=ex: E.matmul(stp[:, jj * 128:(jj + 1) * 128], lhsT=ident[:], rhs=tri_[:], start=False, stop=(xi == len(ex) - 1)), r=[t_const, cst], w=[stt_])
                                    pt_s, pt_st = pT[si]
                                    P.op("act", lambda E, pt_s=pt_s, stp=stp, c0=c0, c1=c1: E.activation(out=pt_s[:, c0:c1], in_=stp[:, c0:c1], func=AF.Exp), r=[stt_], w=[pt_st])
                                    ring.append((br, kt, jlo, jhi, pt_s, pt_st))
                                if i - LA >= 0:
                                    br, kt, jlo, jhi, pt_s, pt_st = ring[i - LA]
                                    ob, obt = (os_, ost_) if br == "s" else (ow_, owt_)
                                    V = vsl[g] if br == "s" else vwn[g]
                                    for jj in range(jlo, jhi + 1):
                                        st_flag = first[br]
                                        first[br] = False
                                        P.op("pe", lambda E, ob=ob, jj=jj, pt_s=pt_s, V=V, kt=kt, st_flag=st_flag: E.matmul(ob[:, jj * 65:jj * 65 + 65], lhsT=pt_s[:, jj * 128:(jj + 1) * 128], rhs=V[0][:, kt, :], start=st_flag, stop=True, skip_group_check=True), r=[pt_st, V[1]], w=[obt])
                                yield
                            osv = os_[:, 0:260].rearrange("p (j c) -> p j c", c=65)
                            owv = ow_[:, 0:260].rearrange("p (j c) -> p j c", c=65)
                            P.op("dve", lambda E, osv=osv: E.reciprocal(out=sy[:, 0:4], in_=osv[:, :, 64]), r=[ost_], w=[syt])
                            P.op("dve", lambda E, g_=g_, h=h: E.tensor_tensor(out=sy[:, 0:4], in0=sy[:, 0:4], in1=g_[:, :, h * 3 + 1], op=ALU.mult), r=[gt_], w=[syt])
                            P.op("dve", lambda E, owv=owv: E.reciprocal(out=sy[:, 4:8], in_=owv[:, :, 64]), r=[owt_], w=[syt])
                            P.op("dve", lambda E, g_=g_, h=h: E.tensor_tensor(out=sy[:, 4:8], in0=sy[:, 4:8], in1=g_[:, :, h * 3 + 2], op=ALU.mult), r=[gt_], w=[syt])
                            cs_ = slice(h * 64, (h + 1) * 64)
                            for jj in range(4):
                                P.op("dve", lambda E, jj=jj, cs_=cs_, os_=os_, att=att: E.scalar_tensor_tensor(out=att[:, jj, cs_], in0=os_[:, jj * 65:jj * 65 + 64], scalar=sy[:, jj:jj + 1], in1=att[:, jj, cs_], op0=ALU.mult, op1=ALU.add), r=[ost_, syt], w=[attt])
                                P.op("dve", lambda E, jj=jj, cs_=cs_, ow_=ow_, att=att: E.scalar_tensor_tensor(out=attb[:, jj, cs_], in0=ow_[:, jj * 65:jj * 65 + 64], scalar=sy[:, 4 + jj:5 + jj], in1=att[:, jj, cs_], op0=ALU.mult, op1=ALU.add), r=[owt_, syt, attt], w=[attbt])
                            yield
                    a_, at_ = ast[sg % 2]
                    atr = b_tp[:, 384:896].rearrange("p (a b) -> p a b", b=128)
                    for jj in range(4):
                        for ft in range(4):
                            P.op("pe", lambda E, ft=ft, jj=jj, atr=atr: E.transpose(out=atr[:, ft, :], in_=attb[:, jj, ft * 128:(ft + 1) * 128], identity=ident[:]), r=[attbt, t_const], w=[tp_t])
                        P.op("act", lambda E, a_=a_, atr=atr, jj=jj: E.copy(out=a_[:, :, jj * 128:(jj + 1) * 128], in_=atr), r=[tp_t], w=[at_])
                        yield
                    P.dma("pool", attnT[:, ss].rearrange("(a p) t -> p a t", p=128), a_[:], at_, T["attnT"], at_)
                    yield

                def drain(gn):
                    for _ in gn:
                        pass

                if NSG > 0:
                    drain(gen_X(0))
                for sg in range(NSG):
                    gy = gen_Y(sg)
                    gx = gen_X(sg + 1) if sg + 1 < NSG else iter(())
                    ratio = max(1, int(round((32 * sg + 110) / 100.0)))
                    x_done = False
                    y_done = False
                    while not y_done:
                        for _ in range(ratio):
                            try:
                                next(gy)
                            except StopIteration:
                                y_done = True
                                break
                        if not x_done:
                            try:
                                next(gx)
                            except StopIteration:
                                x_done = True
                    if not x_done:
                        drain(gx)

        PH = {"inproj": phase_inproj, "mlp": phase_mlp, "rnn": phase_rnn, "merge": phase_merge, "attn": phase_attn}
        build.phases = PH
        build.ctx = dict(P=P, T=T, nc=nc, xs=xs, x_in=x_in, out_d=out_d)
        plan = build.plan
        plan(PH, build.ctx, locals())
        P.barrier()
        print("ops", P.nops, "waits", P.nwait)
    return nc, dbg


def default_plan(PH, ctx, L):
    T = ctx["T"]
    xs = ctx["xs"]
    cur, curt = ctx["x_in"], T["x_in"]
    for l in range(2):
        PH["inproj"](l, cur, curt)
        PH["attn"](l)
        PH["rnn"](l)
        PH["merge"](l, cur, curt, xs[0], T["xs0"])
        PH["mlp"](l, xs[0], T["xs0"], xs[1], T["xs1"], l == 1)
        cur, curt = xs[1], T["xs1"]


build.plan = default_plan


def host_inputs(inp, b):
    bf = ml_dtypes.bfloat16
    f = np.float32

    def pk(v):
        return np.ascontiguousarray(v.reshape(2, 8, 128).transpose(0, 2, 1)).astype(f)

    def bd(wm):
        o = np.zeros((2, 8, 128, 128), f)
        for c in range(8):
            o[:, c, 0:64, 0:64] = wm[:, 2 * c]
            o[:, c, 64:128, 64:128] = wm[:, 2 * c + 1]
        return o

    i_ = np.arange(128)
    m = {
        "x": np.ascontiguousarray(inp["x"][b]),
        "w_in": inp["w_in"],
        "n1w": pk(inp["norm1_w"]), "n2w": pk(inp["norm2_w"]),
        "fnw": np.ascontiguousarray(np.broadcast_to(inp["final_norm_w"][None, :], (128, D))).astype(f),
        "posk": np.ascontiguousarray(np.repeat(inp["cmp_pos_k"].transpose(0, 2, 1)[..., None], 2, axis=-1)),
        "posv": np.ascontiguousarray(np.repeat(inp["cmp_pos_v"].transpose(0, 2, 1)[..., None], 2, axis=-1)),
        "ckw1": np.ascontiguousarray(inp["cmp_k_w1"].reshape(2, 32, 64, 256).transpose(0, 2, 1, 3)),
        "cvw1": np.ascontiguousarray(inp["cmp_v_w1"].reshape(2, 32, 64, 256).transpose(0, 2, 1, 3)),
        "ckw2": np.ascontiguousarray(inp["cmp_k_w2"].reshape(2, 2, 128, 64).transpose(0, 2, 1, 3)),
        "cvw2": np.ascontiguousarray(inp["cmp_v_w2"].reshape(2, 2, 128, 64).transpose(0, 2, 1, 3)),
        "convw": np.ascontiguousarray(inp["conv_w"].reshape(2, 4, 8, 128).transpose(0, 3, 2, 1)),
        "convb": pk(inp["conv_b"]), "lba": pk(inp["lru_b_a"]), "lbi": pk(inp["lru_b_i"]), "llam": pk(inp["lru_lambda"]),
        "lwa": bd(inp["lru_w_a"]), "lwi": bd(inp["lru_w_i"]),
        "wua": inp["w_up_attn"], "wur": inp["w_up_rnn"], "wo": inp["w_out"], "w1": inp["mlp_w1"], "w2": inp["mlp_w2"],
        "c_ident": np.eye(128, dtype=f).astype(bf),
        "c_tric": np.where(i_[:, None] <= i_[None, :], 0.0, NEGM).astype(bf),
        "c_triw": np.where(i_[:, None] > i_[None, :], 0.0, NEGM).astype(bf),
        "c_E": (np.arange(S)[None, :] // 64 == np.arange(64)[:, None]).astype(f).astype(bf),
        "c_band": np.where((np.arange(9)[None, :] - 2) <= ((i_[:, None] + 1) // 16 - 2), 0.0, NEGM).astype(bf),
    }
    hi = (i_ >= 64).astype(np.int64)[:, None]
    jp = (np.arange(128) - 64)[None, :]
    valid = jp <= hi
    forced = jp > hi - 2
    A = np.where(valid & ~forced, 1.0, 0.0)
    Bm = np.where(valid, np.where(forced, 1e4, 0.0), -1e30)
    m["c_A"] = A.astype(f)
    m["c_B"] = Bm.astype(f)
    return {k: np.ascontiguousarray(v) for k, v in m.items()}


def kernel(**inputs):
    inp = {k: np.asarray(v) for k, v in inputs.items()}
    nc, _ = build(False)
    in_maps = [host_inputs(inp, c % 4) for c in range(8)]
    res = run_bass_kernel_spmd(nc, in_maps, core_ids=list(range(8)))
    return np.stack([np.asarray(res.results[c]["out"]) for c in range(4)], axis=0).astype(np.float32)
```

```python
import numpy as np
import ml_dtypes
from contextlib import ExitStack
import concourse.bass as bass
import concourse.mybir as mybir
from concourse.bass_utils import run_bass_kernel_spmd

F32 = mybir.dt.float32
BF16 = mybir.dt.bfloat16
AF = mybir.ActivationFunctionType
ALU = mybir.AluOpType
AX = mybir.AxisListType

S = 4096
D = 1024
DIN = 5400
NT = S // 128
NEGM = -30000.0
EPS = 1e-6
O_Q, O_KC, O_VC, O_KS, O_VS, O_KW, O_VW, O_GN, O_XR, O_GR, O_GA, O_GB = 0, 512, 640, 768, 896, 1024, 1152, 1280, 1304, 2328, 3352, 4376


ATT_DBG = {"level": 6, "nqb": NT}


class Tok:
    __slots__ = ("w", "r", "sem", "name", "x")

    def __init__(self, name="", x=False):
        self.w = {}
        self.r = {}
        self.sem = None
        self.name = name
        self.x = x


class Prog:
    ENG = ("pe", "act", "dve", "pool", "sp")

    def __init__(self, nc, es, n_dma_sems=80):
        self.nc = nc
        self.eng = {"pe": nc.tensor, "act": nc.scalar, "dve": nc.vector, "pool": nc.gpsimd, "sp": nc.sync}
        self.sems = []
        self.esem = {}
        for e in self.ENG:
            self.esem[e] = len(self.sems)
            self.sems.append(es.enter_context(nc.semaphore("es_" + e)))
        self.dma_ids = []
        for i in range(n_dma_sems):
            self.dma_ids.append(len(self.sems))
            self.sems.append(es.enter_context(nc.semaphore("ds_%d" % i)))
        self.free = list(self.dma_ids)
        self.total = [0] * len(self.sems)
        self.known = {e: [0] * len(self.sems) for e in self.ENG}
        self.nwait = 0
        self.nops = 0

    def _wait(self, eng, deps):
        E = self.eng[eng]
        kn = self.known[eng]
        for s, v in deps.items():
            if s >= 5:
                v = self.total[s]
            if kn[s] < v:
                kn[s] = v
                E.wait_ge(self.sems[s], v)
                self.nwait += 1

    @staticmethod
    def _merge(d, src):
        for s, v in src.items():
            if d.get(s, 0) < v:
                d[s] = v

    def op(self, eng, fn, r=(), w=()):
        deps = {}
        rx = [b for b in r if b.x]
        if rx:
            r = [b for b in r if not b.x]
            w = list(w) + rx
        for b in r:
            self._merge(deps, b.w)
        for b in w:
            self._merge(deps, b.w)
            self._merge(deps, b.r)
        s = self.esem[eng]
        if eng == "pe":
            deps.pop(s, None)
        self._wait(eng, deps)
        self.total[s] += 1
        n = self.total[s]
        fn(self.eng[eng]).then_inc(self.sems[s], 1)
        self.nops += 1
        for b in r:
            b.r[s] = n
        for b in w:
            b.w[s] = n

    def dma(self, eng, out, in_, src, dst, owner):
        deps = {}
        self._merge(deps, src.w)
        self._merge(deps, dst.w)
        self._merge(deps, dst.r)
        self._wait(eng, deps)
        if owner.sem is None:
            owner.sem = self.free.pop()
        s = owner.sem
        self.total[s] += 16
        v = self.total[s]
        self.eng[eng].dma_start(out=out, in_=in_).then_inc(self.sems[s], 16)
        self.nops += 1
        src.r[s] = v
        dst.w[s] = v

    def release(self, toks):
        for t in toks:
            if t.sem is not None:
                self.free.append(t.sem)
                t.sem = None

    def barrier(self):
        for e in self.ENG:
            E = self.eng[e]
            kn = self.known[e]
            for s in range(len(self.sems)):
                if s == self.esem[e]:
                    continue
                v = self.total[s]
                if kn[s] < v:
                    kn[s] = v
                    E.wait_ge(self.sems[s], v)
        arr = {}
        for e in self.ENG:
            s = self.esem[e]
            self.total[s] += 1
            arr[e] = self.total[s]
            if e == "pe":
                self.eng[e].nop().then_inc(self.sems[s], 1) if hasattr(self.eng[e], "nop") else None
            else:
                self.eng[e].nop().then_inc(self.sems[s], 1)
        for e in self.ENG:
            for f in self.ENG:
                if f == e:
                    continue
                s = self.esem[f]
                self.known[e][s] = arr[f]
                self.eng[e].wait_ge(self.sems[s], arr[f])


class Scope:
    def __init__(self, P):
        self.P = P
        self.es = ExitStack()
        self.toks = []

    def __enter__(self):
        self.es.__enter__()
        return self

    def __exit__(self, *a):
        self.P.barrier()
        self.P.release(self.toks)
        return self.es.__exit__(*a)

    uid = [0]

    def sb(self, name, shape, dt):
        Scope.uid[0] += 1
        return self.es.enter_context(self.P.nc.sbuf_tensor("%s_%d" % (name, Scope.uid[0]), list(shape), dt))

    def ps(self, name, shape, dt):
        Scope.uid[0] += 1
        return self.es.enter_context(self.P.nc.psum_tensor("%s_%d" % (name, Scope.uid[0]), list(shape), dt))

    def tok(self, name="", x=False):
        t = Tok(name, x)
        self.toks.append(t)
        return t

    def sbt(self, name, shape, dt):
        return self.sb(name, shape, dt), self.tok(name)

    def pst(self, name, shape, dt):
        return self.ps(name, shape, dt), self.tok(name, True)


def build(debug=False):
    nc = bass.Bass("TRN2", target_bir_lowering=False)
    dbg = {}

    def din(name, shape, dt=F32):
        return nc.dram_tensor(name, list(shape), dt, kind="ExternalInput").ap()

    def dscr(name, shape, dt):
        isd = bool(debug) and (debug is True or name in debug)
        kind = "ExternalOutput" if isd else "Internal"
        t = nc.dram_tensor(name, list(shape), dt, kind=kind).ap()
        if isd:
            dbg[name] = t
        return t

    x_in = din("x", [S, D])
    out_d = nc.dram_tensor("out", [S, D], F32, kind="ExternalOutput").ap()
    w_in = din("w_in", [2, D, DIN])
    n1w = din("n1w", [2, 128, 8])
    n2w = din("n2w", [2, 128, 8])
    fnw = din("fnw", [128, D])
    posk = din("posk", [2, 64, 32, 2])
    posv = din("posv", [2, 64, 32, 2])
    ckw1 = din("ckw1", [2, 64, 32, 256])
    cvw1 = din("cvw1", [2, 64, 32, 256])
    ckw2 = din("ckw2", [2, 128, 2, 64])
    cvw2 = din("cvw2", [2, 128, 2, 64])
    convw = din("convw", [2, 128, 8, 4])
    convb = din("convb", [2, 128, 8])
    lba = din("lba", [2, 128, 8])
    lbi = din("lbi", [2, 128, 8])
    llam = din("llam", [2, 128, 8])
    lwa = din("lwa", [2, 8, 128, 128])
    lwi = din("lwi", [2, 8, 128, 128])
    wua = din("wua", [2, 512, D])
    wur = din("wur", [2, D, D])
    wo = din("wo", [2, D, D])
    w1 = din("w1", [2, D, 4096])
    w2 = din("w2", [2, 4096, D])
    c_ident = din("c_ident", [128, 128], BF16)
    c_tric = din("c_tric", [128, 128], BF16)
    c_triw = din("c_triw", [128, 128], BF16)
    c_E = din("c_E", [64, S], BF16)
    c_band = din("c_band", [128, 9], BF16)
    c_A = din("c_A", [128, 128])
    c_B = din("c_B", [128, 128])

    xs = [dscr("xs0", [S, D], F32), dscr("xs1", [S, D], F32)]
    qT = dscr("qT", [8, 64, S], BF16)
    kcT = dscr("kcT", [2, 64, S], BF16)
    vcT = dscr("vcT", [2, 64, S], BF16)
    ksT = dscr("ksT", [2, 64, S], BF16)
    kwT = dscr("kwT", [2, 64, S], BF16)
    vtm = dscr("vtm", [S, 4, 64], BF16)
    gat = dscr("gat", [S, 24], F32)
    zf = dscr("zf", [4, D, S], F32)
    attnT = dscr("attnT", [512, S], BF16)
    rnnT = dscr("rnnT", [D, S], BF16)

    with ExitStack() as es:
        P = Prog(nc, es)
        T = {n: Tok(n) for n in ["x_in", "out", "w", "xs0", "xs1", "qT", "kcT", "vcT", "ksT", "kwT", "vtm", "gat", "zf", "attnT", "rnnT"]}
        for t in T.values():
            t.sem = None

        ident = es.enter_context(nc.sbuf_tensor("ident", [128, 128], BF16))
        t_const = Tok("const")
        P.dma("sp", ident[:], c_ident[:, :], T["w"], t_const, t_const)

        def convert(sc, dst_ap_fn, src_ap_fn, nrow_tiles, ncols, stg, scale_ap_fn=None, chunk=2048, rows=128, toks=None):
            i = 0
            engs = ("act", "dve", "pool")
            for kt in range(nrow_tiles):
                for c0 in range(0, ncols, chunk):
                    c1 = min(ncols, c0 + chunk)
                    st, stt = stg[i % len(stg)]
                    P.dma("sp", st[0:rows, 0:c1 - c0], src_ap_fn(kt, c0, c1), T["w"], stt, stt)
                    e = engs[i % 3]
                    tk = toks[kt] if toks is not None else None
                    dst = dst_ap_fn(kt, c0, c1)
                    src = st[0:rows, 0:c1 - c0]
                    if scale_ap_fn is None:
                        if e == "act":
                            P.op(e, lambda E, dst=dst, src=src: E.copy(out=dst, in_=src), r=[stt], w=[tk])
                        else:
                            P.op(e, lambda E, dst=dst, src=src: E.tensor_copy(out=dst, in_=src), r=[stt], w=[tk])
                    else:
                        sc_ap, sc_tok = scale_ap_fn(kt)
                        if e == "act":
                            P.op(e, lambda E, dst=dst, src=src, sc_ap=sc_ap: E.activation(out=dst, in_=src, func=AF.Copy, scale=sc_ap), r=[stt, sc_tok], w=[tk])
                        else:
                            P.op(e, lambda E, dst=dst, src=src, sc_ap=sc_ap: E.tensor_scalar(out=dst, in0=src, scalar1=sc_ap, scalar2=None, op0=ALU.mult), r=[stt, sc_tok], w=[tk])
                    i += 1

        def rms_rstd(sc, xt, xtok, junk, junktok, ssq, rstd, sstok):
            P.op("act", lambda E: E.activation(out=junk[:], in_=xt[:], func=AF.Square, accum_out=ssq[:, 0:1]), r=[xtok], w=[junktok, sstok])
            P.op("act", lambda E: E.activation(out=ssq[:, 1:2], in_=ssq[:, 0:1], func=AF.Sqrt, scale=1.0 / D, bias=epsb[:, 0:1]), r=[sstok, t_const], w=[sstok])
            P.op("dve", lambda E: E.reciprocal(out=rstd[:, 0:1], in_=ssq[:, 1:2]), r=[sstok], w=[sstok])

        epsb = es.enter_context(nc.sbuf_tensor("epsb", [128, 4], F32))
        P.op("dve", lambda E: E.memset(epsb[:, 0:1], EPS), w=[t_const])
        P.op("dve", lambda E: E.memset(epsb[:, 1:2], 1.0), w=[t_const])
        P.op("dve", lambda E: E.memset(epsb[:, 2:3], 0.0), w=[t_const])

        def phase_inproj(l, xsrc, xtok):
            with Scope(P) as sc:
                wbf = sc.sb("wbf", [128, 8, DIN], BF16)
                wtok = [sc.tok("wbf%d" % k) for k in range(8)]
                n1 = sc.sb("n1", [128, 8], F32)
                n1t = sc.tok()
                P.dma("sp", n1[:], n1w[l], T["w"], n1t, n1t)
                stg = [sc.sbt("stg%d" % i, [128, 1800], F32) for i in range(3)]
                convert(sc, lambda kt, c0, c1: wbf[:, kt, c0:c1], lambda kt, c0, c1: w_in[l, kt * 128:(kt + 1) * 128, c0:c1], 8, DIN, stg,
                        scale_ap_fn=lambda kt: (n1[:, kt:kt + 1], n1t), chunk=1800, toks=wtok)
                xt = [sc.sbt("xt%d" % i, [128, D], F32) for i in range(2)]
                junk, junkt = sc.sbt("junk", [128, D], BF16)
                ssq = [sc.sbt("ssq%d" % i, [128, 4], F32) for i in range(2)]
                xn = [sc.sbt("xn%d" % i, [128, D], BF16) for i in range(2)]
                xnT = [sc.sbt("xnT%d" % i, [128, 8, 512], BF16) for i in range(2)]
                tp = [sc.pst("tp%d" % i, [128, 8, 128], BF16) for i in range(2)]
                pf = [sc.pst("pf%d" % i, [128, 512], F32) for i in range(4)]
                ptm = [sc.pst("ptm", [128, 512], F32)]
                of32 = [sc.sbt("of32_%d" % i, [128, 512], F32) for i in range(4)]
                obf = [sc.sbt("obf_%d" % i, [128, 512], BF16) for i in range(4)]
                vst = [sc.sbt("vst%d" % i, [128, 256], BF16) for i in range(2)]
                gst = [sc.sbt("gst%d" % i, [128, 24], F32) for i in range(2)]
                cnt = {"pf": 0, "f": 0, "b": 0}
                for sg in range(S // 512):
                    xT, xTt = xnT[sg % 2]
                    for j in range(4):
                        tt = sg * 4 + j
                        x_t, x_tt = xt[tt % 2]
                        sq, sqt = ssq[tt % 2]
                        xb, xbt = xn[tt % 2]
                        tpp, tpt = tp[tt % 2]
                        P.dma("sp", x_t[:], xsrc[tt * 128:(tt + 1) * 128, :], xtok, x_tt, x_tt)
                        rms_rstd(sc, x_t, x_tt, junk, junkt, sq, sq[:, 2:3], sqt)
                        P.op("dve", lambda E, xb=xb, x_t=x_t, sq=sq: E.tensor_scalar(out=xb[:], in0=x_t[:], scalar1=sq[:, 2:3], scalar2=None, op0=ALU.mult), r=[x_tt, sqt], w=[xbt])
                        for kt in range(8):
                            P.op("pe", lambda E, tpp=tpp, xb=xb, kt=kt: E.transpose(out=tpp[:, kt, :], in_=xb[:, kt * 128:(kt + 1) * 128], identity=ident[:]), r=[xbt, t_const], w=[tpt])
                        P.op("act", lambda E, xT=xT, tpp=tpp, j=j: E.copy(out=xT[:, :, j * 128:(j + 1) * 128], in_=tpp[:]), r=[tpt], w=[xTt])
                        pt, ptt = ptm[0]
                        for (c0, n, o0) in ((O_VS, 128, 0), (O_VW, 128, 128), (O_GN, 24, 256)):
                            for kt in range(8):
                                P.op("pe", lambda E, pt=pt, xT=xT, kt=kt, c0=c0, n=n, o0=o0, j=j: E.matmul(pt[:, o0:o0 + n], lhsT=xT[:, kt, j * 128:(j + 1) * 128], rhs=wbf[:, kt, c0:c0 + n], start=(kt == 0), stop=(kt == 7)),
                                     r=[xTt, wtok[kt]], w=[ptt])
                        vs_, vst_ = vst[tt % 2]
                        gs_, gst_ = gst[tt % 2]
                        P.op("dve", lambda E, vs_=vs_, pt=pt: E.tensor_copy(out=vs_[:], in_=pt[:, 0:256]), r=[ptt], w=[vst_])
                        P.op("act", lambda E, gs_=gs_, pt=pt: E.activation(out=gs_[:], in_=pt[:, 256:280], func=AF.Sigmoid), r=[ptt], w=[gst_])
                        P.dma("pool", vtm[tt * 128:(tt + 1) * 128].rearrange("p a d -> p (a d)"), vs_[:], vst_, T["vtm"], vst_)
                        P.dma("pool", gat[tt * 128:(tt + 1) * 128, :], gs_[:], gst_, T["gat"], gst_)
                    tsl = slice(sg * 512, (sg + 1) * 512)
                    jobs = []
                    qTf = qT.rearrange("h d t -> (h d) t")
                    for h2 in range(4):
                        jobs.append((O_Q + h2 * 128, 128, "q", qTf[h2 * 128:(h2 + 1) * 128, tsl], "qT"))
                    jobs.append((O_KC, 128, "c", kcT.rearrange("g d t -> (g d) t")[:, tsl], "kcT"))
                    jobs.append((O_VC, 128, "c", vcT.rearrange("g d t -> (g d) t")[:, tsl], "vcT"))
                    jobs.append((O_KS, 128, "c", ksT.rearrange("g d t -> (g d) t")[:, tsl], "ksT"))
                    jobs.append((O_KW, 128, "c", kwT.rearrange("g d t -> (g d) t")[:, tsl], "kwT"))
                    for ft in range(8):
                        jobs.append((O_XR + ft * 128, 128, "f", zf[0, ft * 128:(ft + 1) * 128, tsl], "zf"))
                    for ft in range(8):
                        jobs.append((O_GR + ft * 128, 128, "gelu", zf[1, ft * 128:(ft + 1) * 128, tsl], "zf"))
                    for ft in range(8):
                        jobs.append((O_GA + ft * 128, 128, "sig", zf[2, ft * 128:(ft + 1) * 128, tsl], "zf"))
                    for ft in range(8):
                        jobs.append((O_GB + ft * 128, 128, "sig", zf[3, ft * 128:(ft + 1) * 128, tsl], "zf"))
                    for (c0, m, kind, dst, dtk) in jobs:
                        pp, ppt = pf[cnt["pf"] % 4]
                        cnt["pf"] += 1
                        for kt in range(8):
                            P.op("pe", lambda E, pp=pp, kt=kt, c0=c0, m=m, xT=xT: E.matmul(pp[0:m, :], lhsT=wbf[:, kt, c0:c0 + m], rhs=xT[:, kt, :], start=(kt == 0), stop=(kt == 7)),
                                 r=[xTt, wtok[kt]], w=[ppt])
                        if kind in ("q", "c"):
                            ob, obt = obf[cnt["b"] % 4]
                            cnt["b"] += 1
                            scl = 0.125 if kind == "q" else 1.0
                            P.op("dve", lambda E, ob=ob, pp=pp, m=m, scl=scl: E.tensor_scalar(out=ob[0:m, :], in0=pp[0:m, :], scalar1=scl, scalar2=None, op0=ALU.mult), r=[ppt], w=[obt])
                            P.dma("pool", dst, ob[0:m, :], obt, T[dtk], obt)
                        else:
                            ob, obt = of32[cnt["f"] % 4]
                            cnt["f"] += 1
                            if kind == "f":
                                P.op("dve", lambda E, ob=ob, pp=pp: E.tensor_copy(out=ob[:], in_=pp[:]), r=[ppt], w=[obt])
                            else:
                                fn = AF.Gelu_apprx_tanh if kind == "gelu" else AF.Sigmoid
                                P.op("act", lambda E, ob=ob, pp=pp, fn=fn: E.activation(out=ob[:], in_=pp[:], func=fn), r=[ppt], w=[obt])
                            P.dma("pool", dst, ob[:], obt, T[dtk], obt)

        def phase_mlp(l, xsrc, xtok, xdst, xdtok, final):
            with Scope(P) as sc:
                w1b = sc.sb("w1b", [128, 8, 4096], BF16)
                w1t = [sc.tok() for _ in range(8)]
                w2b = sc.sb("w2b", [128, 32, D], BF16)
                w2t = [sc.tok() for _ in range(32)]
                n2 = sc.sb("n2", [128, 8], F32)
                n2t = sc.tok()
                P.dma("sp", n2[:], n2w[l], T["w"], n2t, n2t)
                stg = [sc.sbt("stg%d" % i, [128, 1024], F32) for i in range(3)]
                convert(sc, lambda kt, c0, c1: w1b[:, kt, c0:c1], lambda kt, c0, c1: w1[l, kt * 128:(kt + 1) * 128, c0:c1], 8, 4096, stg,
                        scale_ap_fn=lambda kt: (n2[:, kt:kt + 1], n2t), chunk=1024, toks=w1t)
                convert(sc, lambda kt, c0, c1: w2b[:, kt, c0:c1], lambda kt, c0, c1: w2[l, kt * 128:(kt + 1) * 128, c0:c1], 32, D, stg, chunk=1024, toks=w2t)
                fw = None
                if final:
                    fw, fwt = sc.sbt("fw", [128, D], F32)
                    P.dma("sp", fw[:], fnw[:, :], T["w"], fwt, fwt)
                xt = [sc.sbt("xt%d" % i, [128, D], F32) for i in range(2)]
                junk, junkt = sc.sbt("junk", [128, D], BF16)
                ssq = [sc.sbt("ssq%d" % i, [128, 4], F32) for i in range(2)]
                xn, xnt = sc.sbt("xn", [128, D], BF16)
                xnT = [sc.sbt("xnT%d" % i, [128, 8, 128], BF16) for i in range(2)]
                hT = [sc.sbt("hT%d" % i, [128, 32, 128], BF16) for i in range(2)]
                hr, hrt = sc.sbt("hr", [128, 512], F32)
                tp = [sc.pst("tp%d" % i, [128, 8, 128], BF16) for i in range(1)]
                ph = [sc.pst("ph%d" % i, [128, 4, 128], F32) for i in range(3)]
                po = [sc.pst("po%d" % i, [128, 512], F32) for i in range(2)]
                xo = [sc.sbt("xo%d" % i, [128, D], F32) for i in range(2)]
                for tt in range(NT):
                    x_t, x_tt = xt[tt % 2]
                    sq, sqt = ssq[tt % 2]
                    tpp, tpt = tp[0]
                    xT, xTt = xnT[tt % 2]
                    h_, ht_ = hT[tt % 2]
                    P.dma("sp", x_t[:], xsrc[tt * 128:(tt + 1) * 128, :], xtok, x_tt, x_tt)
                    rms_rstd(sc, x_t, x_tt, junk, junkt, sq, sq[:, 2:3], sqt)
                    P.op("dve", lambda E, x_t=x_t, sq=sq: E.tensor_scalar(out=xn[:], in0=x_t[:], scalar1=sq[:, 2:3], scalar2=None, op0=ALU.mult), r=[x_tt, sqt], w=[xnt])
                    for kt in range(8):
                        P.op("pe", lambda E, tpp=tpp, kt=kt: E.transpose(out=tpp[:, kt, :], in_=xn[:, kt * 128:(kt + 1) * 128], identity=ident[:]), r=[xnt, t_const], w=[tpt])
                    P.op("act", lambda E, xT=xT, tpp=tpp: E.copy(out=xT[:], in_=tpp[:]), r=[tpt], w=[xTt])
                    for f4 in range(8):
                        pp, ppt = ph[f4 % 3]
                        for fi in range(4):
                            ft = f4 * 4 + fi
                            for kt in range(8):
                                P.op("pe", lambda E, pp=pp, fi=fi, ft=ft, kt=kt, xT=xT: E.matmul(pp[:, fi, :], lhsT=w1b[:, kt, ft * 128:(ft + 1) * 128], rhs=xT[:, kt, :], start=(kt == 0), stop=(kt == 7)),
                                     r=[xTt, w1t[kt]], w=[ppt])
                        P.op("act", lambda E, pp=pp: E.activation(out=hr[:], in_=pp[:].rearrange("p a b -> p (a b)"), func=AF.Relu), r=[ppt], w=[hrt])
                        e2 = "pool" if f4 % 2 else "dve"
                        P.op(e2, lambda E, h_=h_, f4=f4: E.tensor_tensor(out=h_[:, f4 * 4:(f4 + 1) * 4, :].rearrange("p a b -> p (a b)"), in0=hr[:], in1=hr[:], op=ALU.mult), r=[hrt], w=[ht_])
                    xo_, xot_ = xo[tt % 2]
                    for nh in range(2):
                        pq, pqt = po[nh]
                        for kt in range(32):
                            P.op("pe", lambda E, pq=pq, kt=kt, nh=nh, h_=h_: E.matmul(pq[:], lhsT=h_[:, kt, :], rhs=w2b[:, kt, nh * 512:(nh + 1) * 512], start=(kt == 0), stop=(kt == 31)),
                                 r=[ht_, w2t[kt]], w=[pqt])
                        P.op("dve", lambda E, xo_=xo_, pq=pq, nh=nh, x_t=x_t: E.tensor_tensor(out=xo_[:, nh * 512:(nh + 1) * 512], in0=pq[:], in1=x_t[:, nh * 512:(nh + 1) * 512], op=ALU.add), r=[pqt, x_tt], w=[xot_])
                    if not final:
                        P.dma("pool", xdst[tt * 128:(tt + 1) * 128, :], xo_[:], xot_, xdtok, xot_)
                    else:
                        sq2, sq2t = ssq[tt % 2]
                        rms_rstd(sc, xo_, xot_, junk, junkt, sq2, sq2[:, 3:4], sq2t)
                        P.op("dve", lambda E, xo_=xo_, sq2=sq2: E.scalar_tensor_tensor(out=xo_[:], in0=xo_[:], scalar=sq2[:, 3:4], in1=fw[:], op0=ALU.mult, op1=ALU.mult), r=[sq2t, fwt], w=[xot_])
                        P.dma("pool", out_d[tt * 128:(tt + 1) * 128, :], xo_[:], xot_, T["out"], xot_)


        def phase_rnn(l):
            with Scope(P) as sc:
                prm, prmt = sc.sbt("prm", [128, 64], F32)
                P.dma("sp", prm[:, 0:32], convw[l].rearrange("p a b -> p (a b)"), T["w"], prmt, prmt)
                P.dma("sp", prm[:, 32:40], convb[l], T["w"], prmt, prmt)
                P.dma("sp", prm[:, 40:48], lba[l], T["w"], prmt, prmt)
                P.dma("sp", prm[:, 48:56], lbi[l], T["w"], prmt, prmt)
                P.dma("sp", prm[:, 56:64], llam[l], T["w"], prmt, prmt)
                P.op("act", lambda E: E.activation(out=prm[:, 56:64], in_=prm[:, 56:64], func=AF.Exp, scale=-1.0), r=[prmt], w=[prmt])
                P.op("act", lambda E: E.activation(out=prm[:, 56:64], in_=prm[:, 56:64], func=AF.Ln, bias=epsb[:, 1:2]), r=[prmt, t_const], w=[prmt])
                P.op("dve", lambda E: E.tensor_scalar(out=prm[:, 56:64], in0=prm[:, 56:64], scalar1=-8.0, scalar2=None, op0=ALU.mult), r=[prmt], w=[prmt])
                wst, wstt = sc.sbt("wst", [128, 128], F32)
                wab, wabt = sc.sbt("wab", [128, 128], BF16)
                wib, wibt = sc.sbt("wib", [128, 128], BF16)
                xrp, xrpt = sc.sbt("xrp", [128, S + 4], F32)
                gg, ggt = sc.sbt("gg", [128, S], F32)
                xc, xct = sc.sbt("xc", [128, S], F32)
                xcb, xcbt = sc.sbt("xcb", [128, S], BF16)
                rr, rrt = sc.sbt("rr", [128, S], F32)
                ig, igt = sc.sbt("ig", [128, S], F32)
                aa, aat = sc.sbt("aa", [128, S], F32)
                ro, rot = sc.sbt("ro", [128, S], BF16)
                pa = [sc.pst("pa%d" % i, [128, 512], F32) for i in range(4)]
                P.op("dve", lambda E: E.memset(xrp[:, 0:4], 0.0), w=[xrpt])
                for ct in range(8):
                    P.dma("sp", wst[:], lwa[l, ct], T["w"], wstt, wstt)
                    P.op("dve", lambda E: E.tensor_copy(out=wab[:], in_=wst[:]), r=[wstt], w=[wabt])
                    P.dma("sp", wst[:], lwi[l, ct], T["w"], wstt, wstt)
                    P.op("dve", lambda E: E.tensor_copy(out=wib[:], in_=wst[:]), r=[wstt], w=[wibt])
                    P.dma("sp", xrp[:, 4:S + 4], zf[0, ct * 128:(ct + 1) * 128, :], T["zf"], xrpt, xrpt)
                    P.dma("sp", gg[:], zf[1, ct * 128:(ct + 1) * 128, :], T["zf"], ggt, ggt)
                    P.op("act", lambda E, ct=ct: E.activation(out=xc[:], in_=xrp[:, 4:S + 4], func=AF.Identity, scale=prm[:, ct * 4 + 3:ct * 4 + 4], bias=prm[:, 32 + ct:33 + ct]), r=[xrpt, prmt], w=[xct])
                    for i in range(3):
                        P.op("dve", lambda E, ct=ct, i=i: E.scalar_tensor_tensor(out=xc[:], in0=xrp[:, 1 + i:1 + i + S], scalar=prm[:, ct * 4 + i:ct * 4 + i + 1], in1=xc[:], op0=ALU.mult, op1=ALU.add), r=[xrpt, prmt], w=[xct])
                    P.op("pool", lambda E: E.tensor_copy(out=xcb[:], in_=xc[:]), r=[xct], w=[xcbt])
                    for tg in range(8):
                        sl = slice(tg * 512, (tg + 1) * 512)
                        p1, p1t = pa[(2 * tg) % 4]
                        p2, p2t = pa[(2 * tg + 1) % 4]
                        P.op("pe", lambda E, p1=p1, sl=sl: E.matmul(p1[:], lhsT=wab[:], rhs=xcb[:, sl], start=True, stop=True), r=[wabt, xcbt], w=[p1t])
                        P.op("pe", lambda E, p2=p2, sl=sl: E.matmul(p2[:], lhsT=wib[:], rhs=xcb[:, sl], start=True, stop=True), r=[wibt, xcbt], w=[p2t])
                        P.op("act", lambda E, p1=p1, sl=sl, ct=ct: E.activation(out=rr[:, sl], in_=p1[:], func=AF.Sigmoid, bias=prm[:, 40 + ct:41 + ct]), r=[p1t, prmt], w=[rrt])
                        P.op("act", lambda E, p2=p2, sl=sl, ct=ct: E.activation(out=ig[:, sl], in_=p2[:], func=AF.Sigmoid, bias=prm[:, 48 + ct:49 + ct]), r=[p2t, prmt], w=[igt])
                    P.op("act", lambda E, ct=ct: E.activation(out=aa[:], in_=rr[:], func=AF.Exp, scale=prm[:, 56 + ct:57 + ct]), r=[rrt, prmt], w=[aat])
                    P.op("pool", lambda E: E.tensor_tensor(out=rr[:], in0=aa[:], in1=aa[:], op=ALU.mult), r=[aat], w=[rrt])
                    P.op("act", lambda E: E.activation(out=rr[:], in_=rr[:], func=AF.Sqrt, scale=-1.0, bias=epsb[:, 1:2]), r=[rrt, t_const], w=[rrt])
                    P.op("pool", lambda E: E.tensor_tensor(out=ig[:], in0=ig[:], in1=xc[:], op=ALU.mult), r=[xct], w=[igt])
                    P.op("dve", lambda E: E.tensor_tensor(out=ig[:], in0=ig[:], in1=rr[:], op=ALU.mult), r=[rrt], w=[igt])
                    P.op("dve", lambda E: E.tensor_tensor_scan(out=xc[:], data0=aa[:], data1=ig[:], initial=0.0, op0=ALU.mult, op1=ALU.add), r=[aat, igt], w=[xct])
                    P.op("pool", lambda E: E.tensor_tensor(out=ro[:], in0=xc[:], in1=gg[:], op=ALU.mult), r=[xct, ggt], w=[rot])
                    P.dma("pool", rnnT[ct * 128:(ct + 1) * 128, :], ro[:], rot, T["rnnT"], rot)

        def phase_merge(l, xsrc, xtok, xdst, xdtok):
            with Scope(P) as sc:
                wa = sc.sb("wa", [128, 4, D], BF16)
                wat = [sc.tok() for _ in range(4)]
                wr = sc.sb("wr", [128, 8, D], BF16)
                wrt = [sc.tok() for _ in range(8)]
                wob = sc.sb("wob", [128, 8, D], BF16)
                wot = [sc.tok() for _ in range(8)]
                stg = [sc.sbt("stg%d" % i, [128, 1024], F32) for i in range(3)]
                convert(sc, lambda kt, c0, c1: wa[:, kt, c0:c1], lambda kt, c0, c1: wua[l, kt * 128:(kt + 1) * 128, c0:c1], 4, D, stg, chunk=1024, toks=wat)
                convert(sc, lambda kt, c0, c1: wr[:, kt, c0:c1], lambda kt, c0, c1: wur[l, kt * 128:(kt + 1) * 128, c0:c1], 8, D, stg, chunk=1024, toks=wrt)
                convert(sc, lambda kt, c0, c1: wob[:, kt, c0:c1], lambda kt, c0, c1: wo[l, kt * 128:(kt + 1) * 128, c0:c1], 8, D, stg, chunk=1024, toks=wot)
                aT = [sc.sbt("aT%d" % i, [128, 4, 512], BF16) for i in range(2)]
                rT = [sc.sbt("rT%d" % i, [128, 8, 512], BF16) for i in range(2)]
                sa = [sc.sbt("sa%d" % i, [128, 512], F32) for i in range(2)]
                sb_ = [sc.sbt("sb%d" % i, [128, 512], F32) for i in range(2)]
                t1 = [sc.sbt("t1%d" % i, [128, 512], F32) for i in range(2)]
                t2 = [sc.sbt("t2%d" % i, [128, 512], F32) for i in range(2)]
                mT = [sc.sbt("mT%d" % i, [128, 8, 512], BF16) for i in range(2)]
                pA = [sc.pst("pA%d" % i, [128, 512], F32) for i in range(2)]
                pB = [sc.pst("pB%d" % i, [128, 512], F32) for i in range(2)]
                pO = [sc.pst("pO%d" % i, [128, 512], F32) for i in range(2)]
                xt = [sc.sbt("xt%d" % i, [128, D], F32) for i in range(2)]
                xo = [sc.sbt("xo%d" % i, [128, D], F32) for i in range(2)]
                k = 0
                for sg in range(8):
                    tsl = slice(sg * 512, (sg + 1) * 512)
                    a_, at_ = aT[sg % 2]
                    r_, rt_ = rT[sg % 2]
                    m_, mt_ = mT[sg % 2]
                    P.dma("sp", a_[:], attnT[:, tsl].rearrange("(a p) t -> p a t", p=128), T["attnT"], at_, at_)
                    P.dma("sp", r_[:], rnnT[:, tsl].rearrange("(a p) t -> p a t", p=128), T["rnnT"], rt_, rt_)
                    for ft in range(8):
                        fs = slice(ft * 128, (ft + 1) * 128)
                        sa_, sat_ = sa[k % 2]
                        sbb, sbt_ = sb_[k % 2]
                        u1, u1t = t1[k % 2]
                        u2, u2t = t2[k % 2]
                        p_a, pat = pA[k % 2]
                        p_b, pbt = pB[k % 2]
                        k += 1
                        P.dma("sp", sa_[:], zf[2, fs, tsl], T["zf"], sat_, sat_)
                        P.dma("sp", sbb[:], zf[3, fs, tsl], T["zf"], sbt_, sbt_)
                        for kt in range(4):
                            P.op("pe", lambda E, p_a=p_a, kt=kt, fs=fs, a_=a_: E.matmul(p_a[:], lhsT=wa[:, kt, fs], rhs=a_[:, kt, :], start=(kt == 0), stop=(kt == 3)), r=[wat[kt], at_], w=[pat])
                        for kt in range(8):
                            P.op("pe", lambda E, p_b=p_b, kt=kt, fs=fs, r_=r_: E.matmul(p_b[:], lhsT=wr[:, kt, fs], rhs=r_[:, kt, :], start=(kt == 0), stop=(kt == 7)), r=[wrt[kt], rt_], w=[pbt])
                        P.op("dve", lambda E, u1=u1, p_a=p_a, sa_=sa_: E.tensor_tensor(out=u1[:], in0=p_a[:], in1=sa_[:], op=ALU.mult), r=[pat, sat_], w=[u1t])
                        P.op("dve", lambda E, u2=u2, p_b=p_b, sbb=sbb: E.tensor_tensor(out=u2[:], in0=p_b[:], in1=sbb[:], op=ALU.mult), r=[pbt, sbt_], w=[u2t])
                        P.op("pool", lambda E, m_=m_, ft=ft, u1=u1, u2=u2: E.tensor_tensor(out=m_[:, ft, :], in0=u1[:], in1=u2[:], op=ALU.add), r=[u1t, u2t], w=[mt_])
                    for j in range(4):
                        tt = sg * 4 + j
                        x_t, x_tt = xt[tt % 2]
                        xo_, xot_ = xo[tt % 2]
                        P.dma("sp", x_t[:], xsrc[tt * 128:(tt + 1) * 128, :], xtok, x_tt, x_tt)
                        for nh in range(2):
                            pq, pqt = pO[nh]
                            for kt in range(8):
                                P.op("pe", lambda E, pq=pq, kt=kt, nh=nh, m_=m_, j=j: E.matmul(pq[:], lhsT=m_[:, kt, j * 128:(j + 1) * 128], rhs=wob[:, kt, nh * 512:(nh + 1) * 512], start=(kt == 0), stop=(kt == 7)), r=[mt_, wot[kt]], w=[pqt])
                            P.op("dve", lambda E, xo_=xo_, pq=pq, nh=nh, x_t=x_t: E.tensor_tensor(out=xo_[:, nh * 512:(nh + 1) * 512], in0=pq[:], in1=x_t[:, nh * 512:(nh + 1) * 512], op=ALU.add), r=[pqt, x_tt], w=[xot_])
                        P.dma("pool", xdst[tt * 128:(tt + 1) * 128, :], xo_[:], xot_, xdtok, xot_)


        def phase_attn(l):
            with Scope(P) as sc:
                cst = sc.tok("cst")
                tric = sc.sb("tric", [128, 128], BF16)
                triw = sc.sb("triw", [128, 128], BF16)
                Ec = sc.sb("Ec", [64, S], BF16)
                band = sc.sb("band", [128, 9], BF16)
                cA = sc.sb("cA", [128, 128], F32)
                cB = sc.sb("cB", [128, 128], F32)
                for dst, src in ((tric, c_tric), (triw, c_triw), (Ec, c_E), (band, c_band), (cA, c_A), (cB, c_B)):
                    P.dma("sp", dst[:], src[:, :], T["w"], cst, cst)
                kcm = [sc.sbt("kcm%d" % g, [64, 256], BF16) for g in range(2)]
                vcm = [sc.sbt("vcm%d" % g, [128, 2, 64], BF16) for g in range(2)]
                with Scope(P) as s2:
                    stg = [s2.sbt("cstg%d" % i, [64, 2048], F32) for i in range(2)]
                    w2s, w2st = s2.sbt("w2s", [128, 128], F32)
                    pss, psst = s2.sbt("pss", [64, 64], F32)
                    w1b = s2.sb("w1b", [64, 32, 256], BF16)
                    w1bt = s2.tok()
                    w2b, w2bt = s2.sbt("w2b", [128, 2, 64], BF16)
                    posb, posbt = s2.sbt("posb", [64, 32, 2], BF16)
                    kg = [s2.sbt("kg%d" % g, [64, S], BF16) for g in range(2)]
                    hid = [[s2.sbt("hid%d%d" % (g, h), [128, 256], BF16) for h in range(2)] for g in range(2)]
                    bia, biat = s2.sbt("bia", [128, 2], F32)
                    psg = [s2.pst("psg%d" % g, [128, 512], F32) for g in range(2)]
                    psb, psbt = s2.pst("psb", [128, 512], F32)
                    pso, psot = s2.pst("pso", [128, 512], F32)
                    for (w1d, w2d, posd, srcT, srct, is_k) in ((ckw1, ckw2, posk, kcT, "kcT", True), (cvw1, cvw2, posv, vcT, "vcT", False)):
                        convert(s2, lambda kt, c0, c1: w1b[:].rearrange("p a b -> p (a b)")[:, c0:c1], lambda kt, c0, c1: w1d[l].rearrange("p a b -> p (a b)")[:, c0:c1], 1, 32 * 256, stg, chunk=2048, rows=64, toks=[w1bt])
                        P.dma("sp", w2s[:], w2d[l].rearrange("p a b -> p (a b)"), T["w"], w2st, w2st)
                        P.op("dve", lambda E: E.tensor_copy(out=w2b[:].rearrange("p a b -> p (a b)"), in_=w2s[:]), r=[w2st], w=[w2bt])
                        P.dma("sp", pss[:], posd[l].rearrange("p a b -> p (a b)"), T["w"], psst, psst)
                        P.op("dve", lambda E: E.tensor_copy(out=posb[:].rearrange("p a b -> p (a b)"), in_=pss[:]), r=[psst], w=[posbt])
                        for g in range(2):
                            P.dma("sp", kg[g][0][:], srcT[g], T[srct], kg[g][1], kg[g][1])
                        for ht in range(2):
                            hs = slice(ht * 128, (ht + 1) * 128)
                            for p in range(32):
                                for g in range(2):
                                    P.op("pe", lambda E, g=g, p=p, hs=hs: E.matmul(psg[g][0][:, 0:255], lhsT=w1b[:, p, hs], rhs=kg[g][0][:, p:p + 16 * 254 + 1:16], start=(p == 0), stop=(p == 31)),
                                         r=[w1bt, kg[g][1]], w=[psg[g][1]])
                                P.op("pe", lambda E, p=p, hs=hs: E.matmul(psb[:, 0:2], lhsT=w1b[:, p, hs], rhs=posb[:, p, :], start=(p == 0), stop=(p == 31)), r=[w1bt, posbt], w=[psbt])
                            P.op("dve", lambda E: E.tensor_copy(out=bia[:], in_=psb[:, 0:2]), r=[psbt], w=[biat])
                            for g in range(2):
                                P.op("act", lambda E, g=g, ht=ht: E.activation(out=hid[g][ht][0][:, 0:255], in_=psg[g][0][:, 0:255], func=AF.Gelu_apprx_tanh, bias=bia[:, 0:1]), r=[psg[g][1], biat], w=[hid[g][ht][1]])
                        for g in range(2):
                            if is_k:
                                for ht in range(2):
                                    P.op("pe", lambda E, g=g, ht=ht: E.matmul(pso[0:64, 0:255], lhsT=w2b[:, ht, :], rhs=hid[g][ht][0][:, 0:255], start=(ht == 0), stop=(ht == 1)), r=[w2bt, hid[g][ht][1]], w=[psot])
                                P.op("dve", lambda E, g=g: E.tensor_copy(out=kcm[g][0][:, 0:255], in_=pso[0:64, 0:255]), r=[psot], w=[kcm[g][1]])
                            else:
                                for ctile in range(2):
                                    n = 128 if ctile == 0 else 127
                                    for ht in range(2):
                                        P.op("pe", lambda E, g=g, ht=ht, ctile=ctile, n=n: E.matmul(pso[0:n, 256:320], lhsT=hid[g][ht][0][:, ctile * 128:ctile * 128 + n], rhs=w2b[:, ht, :], start=(ht == 0), stop=(ht == 1)), r=[w2bt, hid[g][ht][1]], w=[psot])
                                    P.op("dve", lambda E, g=g, ctile=ctile, n=n: E.tensor_copy(out=vcm[g][0][0:n, ctile, :], in_=pso[0:n, 256:320]), r=[psot], w=[vcm[g][1]])
                KE = [sc.sbt("KE%d" % g, [128, S], BF16) for g in range(2)]
                kwn = [sc.sbt("kwn%d" % g, [64, S], BF16) for g in range(2)]
                vsl = [sc.sbt("vsl%d" % g, [128, 32, 65], BF16) for g in range(2)]
                vwn = [sc.sbt("vwn%d" % g, [128, 32, 65], BF16) for g in range(2)]
                for g in range(2):
                    P.dma("sp", KE[g][0][0:64, :], ksT[g], T["ksT"], KE[g][1], KE[g][1])
                    P.dma("sp", KE[g][0][64:128, :], c_E[:, :], T["w"], KE[g][1], KE[g][1])
                    P.dma("sp", kwn[g][0][:], kwT[g], T["kwT"], kwn[g][1], kwn[g][1])
                    for (vv, j) in ((vsl[g], g), (vwn[g], 2 + g)):
                        P.op("pool", lambda E, vv=vv: E.memset(vv[0][:, :, 64:65], 1.0), w=[vv[1]])
                        for k8 in range(8):
                            P.dma("sp", vv[0][:, k8 * 4:(k8 + 1) * 4, 0:64], vtm[k8 * 512:(k8 + 1) * 512, j, :].rearrange("(k p) d -> p k d", p=128), T["vtm"], vv[1], vv[1])
                QP = [[sc.sbt("QP%d_%d" % (i, h), [128, 512], BF16) for h in range(8)] for i in range(2)]
                gt = [sc.sbt("gt%d" % i, [128, 4, 24], F32) for i in range(2)]
                bst = [sc.pst("bst%d" % i, [128, 512], F32) for i in range(2)]
                b_os = [sc.pst("b_os%d" % i, [128, 512], F32) for i in range(2)]
                b_ow = [sc.pst("b_ow%d" % i, [128, 512], F32) for i in range(2)]
                b_xs, b_xst = sc.pst("b_xs", [128, 512], F32)
                b_tp = sc.ps("b_tp", [128, 1024], BF16)
                tp_t = sc.tok("tp", True)
                ee = [sc.sbt("ee%d" % i, [128, 256], F32) for i in range(4)]
                pb = [sc.sbt("pb%d" % i, [128, 256], BF16) for i in range(4)]
                pTc = [sc.sbt("pTc%d" % i, [128, 128], BF16) for i in range(2)]
                pT = [sc.sbt("pT%d" % i, [128, 512], BF16) for i in range(4)]
                sm, smt = sc.sbt("sm", [128, 16], F32)
                sy, syt = sc.sbt("sy", [128, 8], F32)
                P4, P4t = sc.sbt("P4", [128, 264], F32)
                imp, impt = sc.sbt("imp", [128, 64], F32)
                scr, scrt = sc.sbt("scr", [128, 64], F32)
                scr2, scr2t = sc.sbt("scr2", [128, 64], F32)
                penb, penbt = sc.sbt("penb", [128, 128], BF16)
                G3, G3t = sc.sbt("G3", [128, 64, 64], BF16)
                P.op("pool", lambda E: E.memset(penb[:], 0.0), w=[penbt])
                att2 = [sc.sbt("att%d" % i, [128, 4, 512], F32) for i in range(2)]
                attb, attbt = sc.sbt("attb", [128, 4, 512], BF16)
                ast = [sc.sbt("ast%d" % i, [128, 4, 512], BF16) for i in range(2)]
                P.op("dve", lambda E: E.memset(P4[:], 0.0), w=[P4t])
                P4v = P4[:, 0:256].rearrange("p (j f) -> p j f", f=4)
                P4w = P4[:, 4:260].rearrange("p (j f) -> p j f", f=4)
                sidx = [0]
                NSG = ATT_DBG["nqb"] // 4

                def gen_X(sg):
                    ss = slice(sg * 512, (sg + 1) * 512)
                    qp = QP[sg % 2]
                    g_, gt_ = gt[sg % 2]
                    att, attt = att2[sg % 2]
                    for h in range(8):
                        P.dma("sp", qp[h][0][0:64, :], qT[h, :, ss], T["qT"], qp[h][1], qp[h][1])
                    P.dma("sp", g_[:], gat[ss, :].rearrange("(j p) c -> p j c", p=128), T["gat"], gt_, gt_)
                    yield
                    for g in range(2):
                        for j in range(4):
                            qb = sg * 4 + j
                            js = slice(j * 128, (j + 1) * 128)
                            Nc = min(255, 8 * qb + 7)
                            cb0 = max(0, 8 * qb - 2)
                            cb1 = min(Nc, 8 * qb + 7)
                            ps_s = b_xs[:, 0:256]
                            for hh in range(4):
                                h = g * 4 + hh
                                P.op("pe", lambda E, h=h, g=g, js=js, Nc=Nc: E.matmul(ps_s[:, 0:Nc], lhsT=qp[h][0][0:64, js], rhs=kcm[g][0][:, 0:Nc], start=True, stop=False), r=[qp[h][1], kcm[g][1]], w=[b_xst])
                                P.op("pe", lambda E, cb0=cb0, cb1=cb1, qb=qb: E.matmul(ps_s[:, cb0:cb1], lhsT=ident[:], rhs=band[:, cb0 - (8 * qb - 2):cb1 - (8 * qb - 2)], start=False, stop=True), r=[t_const, cst], w=[b_xst])
                                P.op("dve", lambda E, hh=hh, Nc=Nc: E.tensor_reduce(out=sm[:, hh:hh + 1], in_=ps_s[:, 0:Nc], axis=AX.X, op=ALU.max), r=[b_xst], w=[smt])
                                P.op("dve", lambda E, hh=hh: E.tensor_scalar(out=sm[:, 4 + hh:5 + hh], in0=sm[:, hh:hh + 1], scalar1=-10000.0, scalar2=-1.0, op0=ALU.max, op1=ALU.mult), r=[smt], w=[smt])
                                e_, et_ = ee[hh]
                                P.op("act", lambda E, e_=e_, hh=hh, Nc=Nc: E.activation(out=e_[:, 0:Nc], in_=ps_s[:, 0:Nc], func=AF.Exp, bias=sm[:, 4 + hh:5 + hh], accum_out=sm[:, 8 + hh:9 + hh]), r=[b_xst, smt], w=[et_, smt])
                                yield
                            P.op("dve", lambda E: E.tensor_scalar(out=sm[:, 12:16], in0=sm[:, 8:12], scalar1=1e-20, scalar2=None, op0=ALU.max), r=[smt], w=[smt])
                            P.op("dve", lambda E: E.reciprocal(out=sm[:, 12:16], in_=sm[:, 12:16]), r=[smt], w=[smt])
                            for hh in range(4):
                                e_, et_ = ee[hh]
                                p_, pt_ = pb[hh]
                                P.op("pool", lambda E, p_=p_, e_=e_, hh=hh, Nc=Nc: E.tensor_scalar(out=p_[:, 0:Nc], in0=e_[:, 0:Nc], scalar1=sm[:, 12 + hh:13 + hh], scalar2=None, op0=ALU.mult), r=[et_, smt], w=[pt_])
                                if hh == 0:
                                    P.op("dve", lambda E, e_=e_, hh=hh, Nc=Nc: E.tensor_scalar(out=P4[:, 4:4 + Nc], in0=e_[:, 0:Nc], scalar1=sm[:, 12 + hh:13 + hh], scalar2=None, op0=ALU.mult), r=[et_, smt], w=[P4t])
                                else:
                                    P.op("dve", lambda E, e_=e_, hh=hh, Nc=Nc: E.scalar_tensor_tensor(out=P4[:, 4:4 + Nc], in0=e_[:, 0:Nc], scalar=sm[:, 12 + hh:13 + hh], in1=P4[:, 4:4 + Nc], op0=ALU.mult, op1=ALU.add), r=[et_, smt], w=[P4t])
                            yield
                            P.op("dve", lambda E: E.tensor_tensor(out=imp[:], in0=P4v[:, :, 1], in1=P4v[:, :, 2], op=ALU.add), r=[P4t], w=[impt])
                            P.op("dve", lambda E: E.tensor_tensor(out=imp[:], in0=imp[:], in1=P4v[:, :, 3], op=ALU.add), r=[P4t], w=[impt])
                            P.op("dve", lambda E: E.scalar_tensor_tensor(out=imp[:], in0=imp[:], scalar=2.0, in1=P4v[:, :, 0], op0=ALU.mult, op1=ALU.add), r=[P4t], w=[impt])
                            P.op("dve", lambda E: E.tensor_tensor(out=imp[:], in0=imp[:], in1=P4w[:, :, 0], op=ALU.add), r=[P4t], w=[impt])
                            P.op("dve", lambda E, qb=qb: E.tensor_tensor(out=scr[:], in0=imp[:], in1=cA[:, 64 - 2 * qb:128 - 2 * qb], op=ALU.mult), r=[impt, cst], w=[scrt])
                            P.op("dve", lambda E, qb=qb: E.tensor_tensor(out=scr[:], in0=scr[:], in1=cB[:, 64 - 2 * qb:128 - 2 * qb], op=ALU.add), r=[cst], w=[scrt])
                            P.op("dve", lambda E: E.memset(scr[:, 0:1], 1e4), w=[scrt])
                            yield
                            nb = min(64, 2 * qb + 2)
                            if nb > 16:
                                P.op("dve", lambda E, nb=nb: E.tensor_tensor(out=G3[:, 0:nb, 0:nb], in0=scr[:, 0:nb].unsqueeze(1).to_broadcast([128, nb, nb]), in1=scr[:, 0:nb].unsqueeze(2).to_broadcast([128, nb, nb]), op=ALU.is_gt), r=[scrt], w=[G3t])
                                P.op("dve", lambda E, nb=nb: E.tensor_reduce(out=scr2[:, 0:nb], in_=G3[:, 0:nb, 0:nb], axis=AX.X, op=ALU.add), r=[G3t], w=[scr2t])
                                P.op("dve", lambda E, nb=nb: E.tensor_scalar(out=scr2[:, 0:nb], in0=scr2[:, 0:nb], scalar1=16.0, scalar2=None, op0=ALU.is_lt), r=[scr2t], w=[scr2t])
                                P.op("dve", lambda E, nb=nb: E.tensor_scalar(out=penb[:, 64:64 + nb], in0=scr2[:, 0:nb], scalar1=-1.0, scalar2=-NEGM, op0=ALU.add, op1=ALU.mult), r=[scr2t], w=[penbt])
                                yield
                            ptr = b_tp[:, 256:384]
                            P.op("pe", lambda E, ptr=ptr: E.transpose(out=ptr, in_=penb[:], identity=ident[:]), r=[penbt, t_const], w=[tp_t])
                            h0 = g * 4
                            P.op("act", lambda E, ptr=ptr, js=js, h0=h0: E.copy(out=qp[h0][0][64:128, js], in_=ptr[64:128, :]), r=[tp_t], w=[qp[h0][1]])
                            for hh in range(1, 4):
                                P.op("pool", lambda E, js=js, h0=h0, hh=hh: E.tensor_copy(out=qp[h0 + hh][0][64:128, js], in_=qp[h0][0][64:128, js]), r=[qp[h0][1]], w=[qp[h0 + hh][1]])
                            yield
                            k2 = 0
                            for hh in range(4):
                                p_, pt_ = pb[hh]
                                nct = (Nc + 127) // 128
                                for ctile in range(nct):
                                    n = min(128, Nc - ctile * 128)
                                    tpc = b_tp[:, (k2 % 2) * 128:(k2 % 2) * 128 + 128]
                                    pc_, pct_ = pTc[k2 % 2]
                                    k2 += 1
                                    P.op("pe", lambda E, tpc=tpc, p_=p_, ctile=ctile, n=n: E.transpose(out=tpc[0:n, :], in_=p_[:, ctile * 128:ctile * 128 + n], identity=ident[:]), r=[pt_, t_const], w=[tp_t])
                                    P.op("act", lambda E, pc_=pc_, tpc=tpc, n=n: E.copy(out=pc_[0:n, :], in_=tpc[0:n, :]), r=[tp_t], w=[pct_])
                                    P.op("pe", lambda E, pc_=pc_, hh=hh, n=n, ctile=ctile, g=g, nct=nct: E.matmul(b_xs[:, 256 + hh * 64:256 + (hh + 1) * 64], lhsT=pc_[0:n, :], rhs=vcm[g][0][0:n, ctile, :], start=(ctile == 0), stop=(ctile == nct - 1), skip_group_check=True), r=[pct_, vcm[g][1]], w=[b_xst])
                                yield
                            gs_ = slice(g * 256, (g + 1) * 256)
                            P.op("dve", lambda E, j=j, gs_=gs_, g=g, g_=g_, att=att: E.tensor_tensor(out=att[:, j, gs_].rearrange("p (h d) -> p h d", d=64), in0=b_xs[:, 256:512].rearrange("p (h d) -> p h d", d=64),
                                                                         in1=g_[:, j, g * 12:(g + 1) * 12].rearrange("p (h c) -> p h c", c=3)[:, :, 0:1].to_broadcast([128, 4, 64]), op=ALU.mult), r=[b_xst, gt_], w=[attt])
                            yield

                def gen_Y(sg):
                    ss = slice(sg * 512, (sg + 1) * 512)
                    qp = QP[sg % 2]
                    g_, gt_ = gt[sg % 2]
                    att, attt = att2[sg % 2]
                    for g in range(2):
                        for hh in range(4):
                            h = g * 4 + hh
                            q_, qt_ = qp[h]
                            os_, ost_ = b_os[hh % 2]
                            ow_, owt_ = b_ow[hh % 2]
                            steps = []
                            for kt in range(0, 4 * sg + 4):
                                steps.append(("s", kt))
                            for kt in range(max(0, 4 * sg - 4), 4 * sg + 4):
                                steps.append(("w", kt))
                            first = {"s": True, "w": True}
                            LA = 1
                            ring = []
                            for i in range(len(steps) + LA):
                                if i < len(steps):
                                    br, kt = steps[i]
                                    r_ = kt - 4 * sg
                                    if br == "s":
                                        jlo, jhi = max(r_, 0), 3
                                    else:
                                        jlo, jhi = max(r_, 0), min(r_ + 4, 3)
                                    c0, c1 = jlo * 128, (jhi + 1) * 128
                                    si = sidx[0] % 4
                                    stp = bst[sidx[0] % 2][0]
                                    stt_ = bst[sidx[0] % 2][1]
                                    sidx[0] += 1
                                    ks_ = slice(kt * 128, (kt + 1) * 128)
                                    ex = []
                                    if r_ >= 0:
                                        ex.append((r_, tric))
                                    if br == "w" and 0 <= r_ + 4 <= 3:
                                        ex.append((r_ + 4, triw))
                                    if br == "s":
                                        P.op("pe", lambda E, stp=stp, ks_=ks_, q_=q_, g=g, c0=c0, c1=c1, ex=ex: E.matmul(stp[:, c0:c1], lhsT=KE[g][0][:, ks_], rhs=q_[:, c0:c1], start=True, stop=(len(ex) == 0)), r=[KE[g][1], qt_], w=[stt_])
                                    else:
                                        P.op("pe", lambda E, stp=stp, ks_=ks_, q_=q_, g=g, c0=c0, c1=c1, ex=ex: E.matmul(stp[:, c0:c1], lhsT=kwn[g][0][:, ks_], rhs=q_[0:64, c0:c1], start=True, stop=(len(ex) == 0)), r=[kwn[g][1], qt_], w=[stt_])
                                    for xi, (jj, tri_) in enumerate(ex):
                                        P.op("pe", lambda E, stp=stp, jj=jj, tri_=tri_, xi=xi, ex=ex: E.matmul(stp[:, jj * 128:(jj + 1) * 128], lhsT=ident[:], rhs=tri_[:], start=False, stop=(xi == len(ex) - 1)), r=[t_const, cst], w=[stt_])
                                    pt_s, pt_st = pT[si]
                                    P.op("act", lambda E, pt_s=pt_s, stp=stp, c0=c0, c1=c1: E.activation(out=pt_s[:, c0:c1], in_=stp[:, c0:c1], func=AF.Exp), r=[stt_], w=[pt_st])
                                    ring.append((br, kt, jlo, jhi, pt_s, pt_st))
                                if i - LA >= 0:
                                    br, kt, jlo, jhi, pt_s, pt_st = ring[i - LA]
                                    ob, obt = (os_, ost_) if br == "s" else (ow_, owt_)
                                    V = vsl[g] if br == "s" else vwn[g]
                                    for jj in range(jlo, jhi + 1):
                                        st_flag = first[br]
                                        first[br] = False
                                        P.op("pe", lambda E, ob=ob, jj=jj, pt_s=pt_s, V=V, kt=kt, st_flag=st_flag: E.matmul(ob[:, jj * 65:jj * 65 + 65], lhsT=pt_s[:, jj * 128:(jj + 1) * 128], rhs=V[0][:, kt, :], start=st_flag, stop=True, skip_group_check=True), r=[pt_st, V[1]], w=[obt])
                                yield
                            osv = os_[:, 0:260].rearrange("p (j c) -> p j c", c=65)
                            owv = ow_[:, 0:260].rearrange("p (j c) -> p j c", c=65)
                            P.op("dve", lambda E, osv=osv: E.reciprocal(out=sy[:, 0:4], in_=osv[:, :, 64]), r=[ost_], w=[syt])
                            P.op("dve", lambda E, g_=g_, h=h: E.tensor_tensor(out=sy[:, 0:4], in0=sy[:, 0:4], in1=g_[:, :, h * 3 + 1], op=ALU.mult), r=[gt_], w=[syt])
                            P.op("dve", lambda E, owv=owv: E.reciprocal(out=sy[:, 4:8], in_=owv[:, :, 64]), r=[owt_], w=[syt])
                            P.op("dve", lambda E, g_=g_, h=h: E.tensor_tensor(out=sy[:, 4:8], in0=sy[:, 4:8], in1=g_[:, :, h * 3 + 2], op=ALU.mult), r=[gt_], w=[syt])
                            cs_ = slice(h * 64, (h + 1) * 64)
                            for jj in range(4):
                                P.op("dve", lambda E, jj=jj, cs_=cs_, os_=os_, att=att: E.scalar_tensor_tensor(out=att[:, jj, cs_], in0=os_[:, jj * 65:jj * 65 + 64], scalar=sy[:, jj:jj + 1], in1=att[:, jj, cs_], op0=ALU.mult, op1=ALU.add), r=[ost_, syt], w=[attt])
                                P.op("dve", lambda E, jj=jj, cs_=cs_, ow_=ow_, att=att: E.scalar_tensor_tensor(out=attb[:, jj, cs_], in0=ow_[:, jj * 65:jj * 65 + 64], scalar=sy[:, 4 + jj:5 + jj], in1=att[:, jj, cs_], op0=ALU.mult, op1=ALU.add), r=[owt_, syt, attt], w=[attbt])
                            yield
                    a_, at_ = ast[sg % 2]
                    atr = b_tp[:, 384:896].rearrange("p (a b) -> p a b", b=128)
                    for jj in range(4):
                        for ft in range(4):
                            P.op("pe", lambda E, ft=ft, jj=jj, atr=atr: E.transpose(out=atr[:, ft, :], in_=attb[:, jj, ft * 128:(ft + 1) * 128], identity=ident[:]), r=[attbt, t_const], w=[tp_t])
                        P.op("act", lambda E, a_=a_, atr=atr, jj=jj: E.copy(out=a_[:, :, jj * 128:(jj + 1) * 128], in_=atr), r=[tp_t], w=[at_])
                        yield
                    P.dma("pool", attnT[:, ss].rearrange("(a p) t -> p a t", p=128), a_[:], at_, T["attnT"], at_)
                    yield

                def drain(gn):
                    for _ in gn:
                        pass

                if NSG > 0:
                    drain(gen_X(0))
                for sg in range(NSG):
                    gy = gen_Y(sg)
                    gx = gen_X(sg + 1) if sg + 1 < NSG else iter(())
                    ratio = max(1, int(round((32 * sg + 110) / 100.0)))
                    x_done = False
                    y_done = False
                    while not y_done:
                        for _ in range(ratio):
                            try:
                                next(gy)
                            except StopIteration:
                                y_done = True
                                break
                        if not x_done:
                            try:
                                next(gx)
                            except StopIteration:
                                x_done = True
                    if not x_done:
                        drain(gx)

        PH = {"inproj": phase_inproj, "mlp": phase_mlp, "rnn": phase_rnn, "merge": phase_merge, "attn": phase_attn}
        build.phases = PH
        build.ctx = dict(P=P, T=T, nc=nc, xs=xs, x_in=x_in, out_d=out_d)
        plan = build.plan
        plan(PH, build.ctx, locals())
        P.barrier()
        print("ops", P.nops, "waits", P.nwait)
    return nc, dbg


def default_plan(PH, ctx, L):
    T = ctx["T"]
    xs = ctx["xs"]
    cur, curt = ctx["x_in"], T["x_in"]
    for l in range(2):
        PH["inproj"](l, cur, curt)
        PH["attn"](l)
        PH["rnn"](l)
        PH["merge"](l, cur, curt, xs[0], T["xs0"])
        PH["mlp"](l, xs[0], T["xs0"], xs[1], T["xs1"], l == 1)
        cur, curt = xs[1], T["xs1"]


build.plan = default_plan


def host_inputs(inp, b):
    bf = ml_dtypes.bfloat16
    f = np.float32

    def pk(v):
        return np.ascontiguousarray(v.reshape(2, 8, 128).transpose(0, 2, 1)).astype(f)

    def bd(wm):
        o = np.zeros((2, 8, 128, 128), f)
        for c in range(8):
            o[:, c, 0:64, 0:64] = wm[:, 2 * c]
            o[:, c, 64:128, 64:128] = wm[:, 2 * c + 1]
        return o

    i_ = np.arange(128)
    m = {
        "x": np.ascontiguousarray(inp["x"][b]),
        "w_in": inp["w_in"],
        "n1w": pk(inp["norm1_w"]), "n2w": pk(inp["norm2_w"]),
        "fnw": np.ascontiguousarray(np.broadcast_to(inp["final_norm_w"][None, :], (128, D))).astype(f),
        "posk": np.ascontiguousarray(np.repeat(inp["cmp_pos_k"].transpose(0, 2, 1)[..., None], 2, axis=-1)),
        "posv": np.ascontiguousarray(np.repeat(inp["cmp_pos_v"].transpose(0, 2, 1)[..., None], 2, axis=-1)),
        "ckw1": np.ascontiguousarray(inp["cmp_k_w1"].reshape(2, 32, 64, 256).transpose(0, 2, 1, 3)),
        "cvw1": np.ascontiguousarray(inp["cmp_v_w1"].reshape(2, 32, 64, 256).transpose(0, 2, 1, 3)),
        "ckw2": np.ascontiguousarray(inp["cmp_k_w2"].reshape(2, 2, 128, 64).transpose(0, 2, 1, 3)),
        "cvw2": np.ascontiguousarray(inp["cmp_v_w2"].reshape(2, 2, 128, 64).transpose(0, 2, 1, 3)),
        "convw": np.ascontiguousarray(inp["conv_w"].reshape(2, 4, 8, 128).transpose(0, 3, 2, 1)),
        "convb": pk(inp["conv_b"]), "lba": pk(inp["lru_b_a"]), "lbi": pk(inp["lru_b_i"]), "llam": pk(inp["lru_lambda"]),
        "lwa": bd(inp["lru_w_a"]), "lwi": bd(inp["lru_w_i"]),
        "wua": inp["w_up_attn"], "wur": inp["w_up_rnn"], "wo": inp["w_out"], "w1": inp["mlp_w1"], "w2": inp["mlp_w2"],
        "c_ident": np.eye(128, dtype=f).astype(bf),
        "c_tric": np.where(i_[:, None] <= i_[None, :], 0.0, NEGM).astype(bf),
        "c_triw": np.where(i_[:, None] > i_[None, :], 0.0, NEGM).astype(bf),
        "c_E": (np.arange(S)[None, :] // 64 == np.arange(64)[:, None]).astype(f).astype(bf),
        "c_band": np.where((np.arange(9)[None, :] - 2) <= ((i_[:, None] + 1) // 16 - 2), 0.0, NEGM).astype(bf),
    }
    hi = (i_ >= 64).astype(np.int64)[:, None]
    jp = (np.arange(128) - 64)[None, :]
    valid = jp <= hi
    forced = jp > hi - 2
    A = np.where(valid & ~forced, 1.0, 0.0)
    Bm = np.where(valid, np.where(forced, 1e4, 0.0), -1e30)
    m["c_A"] = A.astype(f)
    m["c_B"] = Bm.astype(f)
    return {k: np.ascontiguousarray(v) for k, v in m.items()}


def kernel(**inputs):
    inp = {k: np.asarray(v) for k, v in inputs.items()}
    nc, _ = build(False)
    in_maps = [host_inputs(inp, c % 4) for c in range(8)]
    res = run_bass_kernel_spmd(nc, in_maps, core_ids=list(range(8)))
    return np.stack([np.asarray(res.results[c]["out"]) for c in range(4)], axis=0).astype(np.float32)
```

```python
import numpy as np
import ml_dtypes
from contextlib import ExitStack
import concourse.bass as bass
import concourse.mybir as mybir
from concourse.bass_utils import run_bass_kernel_spmd

F32 = mybir.dt.float32
BF16 = mybir.dt.bfloat16
AF = mybir.ActivationFunctionType
ALU = mybir.AluOpType
AX = mybir.AxisListType

S = 4096
D = 1024
DIN = 5400
NT = S // 128
NEGM = -30000.0
EPS = 1e-6
O_Q, O_KC, O_VC, O_KS, O_VS, O_KW, O_VW, O_GN, O_XR, O_GR, O_GA, O_GB = 0, 512, 640, 768, 896, 1024, 1152, 1280, 1304, 2328, 3352, 4376


ATT_DBG = {"level": 6, "nqb": NT}


class Tok:
    __slots__ = ("w", "r", "sem", "name", "x")

    def __init__(self, name="", x=False):
        self.w = {}
        self.r = {}
        self.sem = None
        self.name = name
        self.x = x


class Prog:
    ENG = ("pe", "act", "dve", "pool", "sp")

    def __init__(self, nc, es, n_dma_sems=80):
        self.nc = nc
        self.eng = {"pe": nc.tensor, "act": nc.scalar, "dve": nc.vector, "pool": nc.gpsimd, "sp": nc.sync}
        self.sems = []
        self.esem = {}
        for e in self.ENG:
            self.esem[e] = len(self.sems)
            self.sems.append(es.enter_context(nc.semaphore("es_" + e)))
        self.dma_ids = []
        for i in range(n_dma_sems):
            self.dma_ids.append(len(self.sems))
            self.sems.append(es.enter_context(nc.semaphore("ds_%d" % i)))
        self.free = list(self.dma_ids)
        self.total = [0] * len(self.sems)
        self.known = {e: [0] * len(self.sems) for e in self.ENG}
        self.nwait = 0
        self.nops = 0

    def _wait(self, eng, deps):
        E = self.eng[eng]
        kn = self.known[eng]
        for s, v in deps.items():
            if s >= 5:
                v = self.total[s]
            if kn[s] < v:
                kn[s] = v
                E.wait_ge(self.sems[s], v)
                self.nwait += 1

    @staticmethod
    def _merge(d, src):
        for s, v in src.items():
            if d.get(s, 0) < v:
                d[s] = v

    def op(self, eng, fn, r=(), w=()):
        deps = {}
        rx = [b for b in r if b.x]
        if rx:
            r = [b for b in r if not b.x]
            w = list(w) + rx
        for b in r:
            self._merge(deps, b.w)
        for b in w:
            self._merge(deps, b.w)
            self._merge(deps, b.r)
        s = self.esem[eng]
        if eng == "pe":
            deps.pop(s, None)
        self._wait(eng, deps)
        self.total[s] += 1
        n = self.total[s]
        fn(self.eng[eng]).then_inc(self.sems[s], 1)
        self.nops += 1
        for b in r:
            b.r[s] = n
        for b in w:
            b.w[s] = n

    def dma(self, eng, out, in_, src, dst, owner):
        deps = {}
        self._merge(deps, src.w)
        self._merge(deps, dst.w)
        self._merge(deps, dst.r)
        self._wait(eng, deps)
        if owner.sem is None:
            owner.sem = self.free.pop()
        s = owner.sem
        self.total[s] += 16
        v = self.total[s]
        self.eng[eng].dma_start(out=out, in_=in_).then_inc(self.sems[s], 16)
        self.nops += 1
        src.r[s] = v
        dst.w[s] = v

    def release(self, toks):
        for t in toks:
            if t.sem is not None:
                self.free.append(t.sem)
                t.sem = None

    def barrier(self):
        for e in self.ENG:
            E = self.eng[e]
            kn = self.known[e]
            for s in range(len(self.sems)):
                if s == self.esem[e]:
                    continue
                v = self.total[s]
                if kn[s] < v:
                    kn[s] = v
                    E.wait_ge(self.sems[s], v)
        arr = {}
        for e in self.ENG:
            s = self.esem[e]
            self.total[s] += 1
            arr[e] = self.total[s]
            if e == "pe":
                self.eng[e].nop().then_inc(self.sems[s], 1) if hasattr(self.eng[e], "nop") else None
            else:
                self.eng[e].nop().then_inc(self.sems[s], 1)
        for e in self.ENG:
            for f in self.ENG:
                if f == e:
                    continue
                s = self.esem[f]
                self.known[e][s] = arr[f]
                self.eng[e].wait_ge(self.sems[s], arr[f])


class Scope:
    def __init__(self, P):
        self.P = P
        self.es = ExitStack()
        self.toks = []

    def __enter__(self):
        self.es.__enter__()
        return self

    def __exit__(self, *a):
        self.P.barrier()
        self.P.release(self.toks)
        return self.es.__exit__(*a)

    uid = [0]

    def sb(self, name, shape, dt):
        Scope.uid[0] += 1
        return self.es.enter_context(self.P.nc.sbuf_tensor("%s_%d" % (name, Scope.uid[0]), list(shape), dt))

    def ps(self, name, shape, dt):
        Scope.uid[0] += 1
        return self.es.enter_context(self.P.nc.psum_tensor("%s_%d" % (name, Scope.uid[0]), list(shape), dt))

    def tok(self, name="", x=False):
        t = Tok(name, x)
        self.toks.append(t)
        return t

    def sbt(self, name, shape, dt):
        return self.sb(name, shape, dt), self.tok(name)

    def pst(self, name, shape, dt):
        return self.ps(name, shape, dt), self.tok(name, True)


def build(debug=False):
    nc = bass.Bass("TRN2", target_bir_lowering=False)
    dbg = {}

    def din(name, shape, dt=F32):
        return nc.dram_tensor(name, list(shape), dt, kind="ExternalInput").ap()

    def dscr(name, shape, dt):
        isd = bool(debug) and (debug is True or name in debug)
        kind = "ExternalOutput" if isd else "Internal"
        t = nc.dram_tensor(name, list(shape), dt, kind=kind).ap()
        if isd:
            dbg[name] = t
        return t

    x_in = din("x", [S, D])
    out_d = nc.dram_tensor("out", [S, D], F32, kind="ExternalOutput").ap()
    w_in = din("w_in", [2, D, DIN])
    n1w = din("n1w", [2, 128, 8])
    n2w = din("n2w", [2, 128, 8])
    fnw = din("fnw", [128, D])
    posk = din("posk", [2, 64, 32, 2])
    posv = din("posv", [2, 64, 32, 2])
    ckw1 = din("ckw1", [2, 64, 32, 256])
    cvw1 = din("cvw1", [2, 64, 32, 256])
    ckw2 = din("ckw2", [2, 128, 2, 64])
    cvw2 = din("cvw2", [2, 128, 2, 64])
    convw = din("convw", [2, 128, 8, 4])
    convb = din("convb", [2, 128, 8])
    lba = din("lba", [2, 128, 8])
    lbi = din("lbi", [2, 128, 8])
    llam = din("llam", [2, 128, 8])
    lwa = din("lwa", [2, 8, 128, 128])
    lwi = din("lwi", [2, 8, 128, 128])
    wua = din("wua", [2, 512, D])
    wur = din("wur", [2, D, D])
    wo = din("wo", [2, D, D])
    w1 = din("w1", [2, D, 4096])
    w2 = din("w2", [2, 4096, D])
    c_ident = din("c_ident", [128, 128], BF16)
    c_tric = din("c_tric", [128, 128], BF16)
    c_triw = din("c_triw", [128, 128], BF16)
    c_E = din("c_E", [64, S], BF16)
    c_band = din("c_band", [128, 9], BF16)
    c_A = din("c_A", [128, 128])
    c_B = din("c_B", [128, 128])

    xs = [dscr("xs0", [S, D], F32), dscr("xs1", [S, D], F32)]
    qT = dscr("qT", [8, 64, S], BF16)
    kcT = dscr("kcT", [2, 64, S], BF16)
    vcT = dscr("vcT", [2, 64, S], BF16)
    ksT = dscr("ksT", [2, 64, S], BF16)
    kwT = dscr("kwT", [2, 64, S], BF16)
    vtm = dscr("vtm", [S, 4, 64], BF16)
    gat = dscr("gat", [S, 24], F32)
    zf = dscr("zf", [4, D, S], F32)
    attnT = dscr("attnT", [512, S], BF16)
    rnnT = dscr("rnnT", [D, S], BF16)

    with ExitStack() as es:
        P = Prog(nc, es)
        T = {n: Tok(n) for n in ["x_in", "out", "w", "xs0", "xs1", "qT", "kcT", "vcT", "ksT", "kwT", "vtm", "gat", "zf", "attnT", "rnnT"]}
        for t in T.values():
            t.sem = None

        ident = es.enter_context(nc.sbuf_tensor("ident", [128, 128], BF16))
        identf = es.enter_context(nc.sbuf_tensor("identf", [128, 128], F32))
        t_const = Tok("const")
        P.dma("sp", ident[:], c_ident[:, :], T["w"], t_const, t_const)
        P.op("dve", lambda E: E.tensor_copy(out=identf[:], in_=ident[:]), r=[t_const], w=[t_const])

        def convert(sc, dst_ap_fn, src_ap_fn, nrow_tiles, ncols, stg, scale_ap_fn=None, chunk=2048, rows=128, toks=None):
            i = 0
            engs = ("act", "dve", "pool")
            for kt in range(nrow_tiles):
                for c0 in range(0, ncols, chunk):
                    c1 = min(ncols, c0 + chunk)
                    st, stt = stg[i % len(stg)]
                    P.dma("sp", st[0:rows, 0:c1 - c0], src_ap_fn(kt, c0, c1), T["w"], stt, stt)
                    e = engs[i % 3]
                    tk = toks[kt] if toks is not None else None
                    dst = dst_ap_fn(kt, c0, c1)
                    src = st[0:rows, 0:c1 - c0]
                    if scale_ap_fn is None:
                        if e == "act":
                            P.op(e, lambda E, dst=dst, src=src: E.copy(out=dst, in_=src), r=[stt], w=[tk])
                        else:
                            P.op(e, lambda E, dst=dst, src=src: E.tensor_copy(out=dst, in_=src), r=[stt], w=[tk])
                    else:
                        sc_ap, sc_tok = scale_ap_fn(kt)
                        if e == "act":
                            P.op(e, lambda E, dst=dst, src=src, sc_ap=sc_ap: E.activation(out=dst, in_=src, func=AF.Copy, scale=sc_ap), r=[stt, sc_tok], w=[tk])
                        else:
                            P.op(e, lambda E, dst=dst, src=src, sc_ap=sc_ap: E.tensor_scalar(out=dst, in0=src, scalar1=sc_ap, scalar2=None, op0=ALU.mult), r=[stt, sc_tok], w=[tk])
                    i += 1

        def rms_rstd(sc, xt, xtok, junk, junktok, ssq, rstd, sstok):
            P.op("act", lambda E: E.activation(out=junk[:], in_=xt[:], func=AF.Square, accum_out=ssq[:, 0:1]), r=[xtok], w=[junktok, sstok])
            P.op("act", lambda E: E.activation(out=ssq[:, 1:2], in_=ssq[:, 0:1], func=AF.Sqrt, scale=1.0 / D, bias=epsb[:, 0:1]), r=[sstok, t_const], w=[sstok])
            P.op("dve", lambda E: E.reciprocal(out=rstd[:, 0:1], in_=ssq[:, 1:2]), r=[sstok], w=[sstok])

        epsb = es.enter_context(nc.sbuf_tensor("epsb", [128, 4], F32))
        P.op("dve", lambda E: E.memset(epsb[:, 0:1], EPS), w=[t_const])
        P.op("dve", lambda E: E.memset(epsb[:, 1:2], 1.0), w=[t_const])
        P.op("dve", lambda E: E.memset(epsb[:, 2:3], 0.0), w=[t_const])

        def phase_inproj(l, xsrc, xtok):
            with Scope(P) as sc:
                wbf = sc.sb("wbf", [128, 8, DIN], BF16)
                wtok = [sc.tok("wbf%d" % k) for k in range(8)]
                n1 = sc.sb("n1", [128, 8], F32)
                n1t = sc.tok()
                P.dma("sp", n1[:], n1w[l], T["w"], n1t, n1t)
                stg = [sc.sbt("stg%d" % i, [128, 1800], F32) for i in range(3)]
                convert(sc, lambda kt, c0, c1: wbf[:, kt, c0:c1], lambda kt, c0, c1: w_in[l, kt * 128:(kt + 1) * 128, c0:c1], 8, DIN, stg,
                        scale_ap_fn=lambda kt: (n1[:, kt:kt + 1], n1t), chunk=1800, toks=wtok)
                xt = [sc.sbt("xt%d" % i, [128, D], F32) for i in range(2)]
                junk, junkt = sc.sbt("junk", [128, D], BF16)
                ssq = [sc.sbt("ssq%d" % i, [128, 4], F32) for i in range(2)]
                xn = [sc.sbt("xn%d" % i, [128, D], BF16) for i in range(2)]
                xnT = [sc.sbt("xnT%d" % i, [128, 8, 512], BF16) for i in range(2)]
                tp = [sc.pst("tp%d" % i, [128, 8, 128], BF16) for i in range(2)]
                pf = [sc.pst("pf%d" % i, [128, 512], F32) for i in range(4)]
                ptm = [sc.pst("ptm", [128, 512], F32)]
                of32 = [sc.sbt("of32_%d" % i, [128, 512], F32) for i in range(4)]
                obf = [sc.sbt("obf_%d" % i, [128, 512], BF16) for i in range(4)]
                vst = [sc.sbt("vst%d" % i, [128, 256], BF16) for i in range(2)]
                gst = [sc.sbt("gst%d" % i, [128, 24], F32) for i in range(2)]
                cnt = {"pf": 0, "f": 0, "b": 0}
                for sg in range(S // 512):
                    xT, xTt = xnT[sg % 2]
                    for j in range(4):
                        tt = sg * 4 + j
                        x_t, x_tt = xt[tt % 2]
                        sq, sqt = ssq[tt % 2]
                        xb, xbt = xn[tt % 2]
                        tpp, tpt = tp[tt % 2]
                        P.dma("sp", x_t[:], xsrc[tt * 128:(tt + 1) * 128, :], xtok, x_tt, x_tt)
                        rms_rstd(sc, x_t, x_tt, junk, junkt, sq, sq[:, 2:3], sqt)
                        P.op("dve", lambda E, xb=xb, x_t=x_t, sq=sq: E.tensor_scalar(out=xb[:], in0=x_t[:], scalar1=sq[:, 2:3], scalar2=None, op0=ALU.mult), r=[x_tt, sqt], w=[xbt])
                        for kt in range(8):
                            P.op("pe", lambda E, tpp=tpp, xb=xb, kt=kt: E.transpose(out=tpp[:, kt, :], in_=xb[:, kt * 128:(kt + 1) * 128], identity=ident[:]), r=[xbt, t_const], w=[tpt])
                        P.op("act", lambda E, xT=xT, tpp=tpp, j=j: E.copy(out=xT[:, :, j * 128:(j + 1) * 128], in_=tpp[:]), r=[tpt], w=[xTt])
                        pt, ptt = ptm[0]
                        for (c0, n, o0) in ((O_VS, 128, 0), (O_VW, 128, 128), (O_GN, 24, 256)):
                            for kt in range(8):
                                P.op("pe", lambda E, pt=pt, xT=xT, kt=kt, c0=c0, n=n, o0=o0, j=j: E.matmul(pt[:, o0:o0 + n], lhsT=xT[:, kt, j * 128:(j + 1) * 128], rhs=wbf[:, kt, c0:c0 + n], start=(kt == 0), stop=(kt == 7)),
                                     r=[xTt, wtok[kt]], w=[ptt])
                        vs_, vst_ = vst[tt % 2]
                        gs_, gst_ = gst[tt % 2]
                        P.op("dve", lambda E, vs_=vs_, pt=pt: E.tensor_copy(out=vs_[:], in_=pt[:, 0:256]), r=[ptt], w=[vst_])
                        P.op("act", lambda E, gs_=gs_, pt=pt: E.activation(out=gs_[:], in_=pt[:, 256:280], func=AF.Sigmoid), r=[ptt], w=[gst_])
                        P.dma("pool", vtm[tt * 128:(tt + 1) * 128].rearrange("p a d -> p (a d)"), vs_[:], vst_, T["vtm"], vst_)
                        P.dma("pool", gat[tt * 128:(tt + 1) * 128, :], gs_[:], gst_, T["gat"], gst_)
                    tsl = slice(sg * 512, (sg + 1) * 512)
                    jobs = []
                    qTf = qT.rearrange("h d t -> (h d) t")
                    for h2 in range(4):
                        jobs.append((O_Q + h2 * 128, 128, "q", qTf[h2 * 128:(h2 + 1) * 128, tsl], "qT"))
                    jobs.append((O_KC, 128, "c", kcT.rearrange("g d t -> (g d) t")[:, tsl], "kcT"))
                    jobs.append((O_VC, 128, "c", vcT.rearrange("g d t -> (g d) t")[:, tsl], "vcT"))
                    jobs.append((O_KS, 128, "c", ksT.rearrange("g d t -> (g d) t")[:, tsl], "ksT"))
                    jobs.append((O_KW, 128, "c", kwT.rearrange("g d t -> (g d) t")[:, tsl], "kwT"))
                    for ft in range(8):
                        jobs.append((O_XR + ft * 128, 128, "f", zf[0, ft * 128:(ft + 1) * 128, tsl], "zf"))
                    for ft in range(8):
                        jobs.append((O_GR + ft * 128, 128, "gelu", zf[1, ft * 128:(ft + 1) * 128, tsl], "zf"))
                    for ft in range(8):
                        jobs.append((O_GA + ft * 128, 128, "sig", zf[2, ft * 128:(ft + 1) * 128, tsl], "zf"))
                    for ft in range(8):
                        jobs.append((O_GB + ft * 128, 128, "sig", zf[3, ft * 128:(ft + 1) * 128, tsl], "zf"))
                    for (c0, m, kind, dst, dtk) in jobs:
                        pp, ppt = pf[cnt["pf"] % 4]
                        cnt["pf"] += 1
                        for kt in range(8):
                            P.op("pe", lambda E, pp=pp, kt=kt, c0=c0, m=m, xT=xT: E.matmul(pp[0:m, :], lhsT=wbf[:, kt, c0:c0 + m], rhs=xT[:, kt, :], start=(kt == 0), stop=(kt == 7)),
                                 r=[xTt, wtok[kt]], w=[ppt])
                        if kind in ("q", "c"):
                            ob, obt = obf[cnt["b"] % 4]
                            cnt["b"] += 1
                            scl = 0.125 if kind == "q" else 1.0
                            P.op("dve", lambda E, ob=ob, pp=pp, m=m, scl=scl: E.tensor_scalar(out=ob[0:m, :], in0=pp[0:m, :], scalar1=scl, scalar2=None, op0=ALU.mult), r=[ppt], w=[obt])
                            P.dma("pool", dst, ob[0:m, :], obt, T[dtk], obt)
                        else:
                            ob, obt = of32[cnt["f"] % 4]
                            cnt["f"] += 1
                            if kind == "f":
                                P.op("dve", lambda E, ob=ob, pp=pp: E.tensor_copy(out=ob[:], in_=pp[:]), r=[ppt], w=[obt])
                            else:
                                fn = AF.Gelu_apprx_tanh if kind == "gelu" else AF.Sigmoid
                                P.op("act", lambda E, ob=ob, pp=pp, fn=fn: E.activation(out=ob[:], in_=pp[:], func=fn), r=[ppt], w=[obt])
                            P.dma("pool", dst, ob[:], obt, T[dtk], obt)

        def phase_mlp(l, xsrc, xtok, xdst, xdtok, final):
            with Scope(P) as sc:
                w1b = sc.sb("w1b", [128, 8, 4096], BF16)
                w1t = [sc.tok() for _ in range(8)]
                w2b = sc.sb("w2b", [128, 32, D], BF16)
                w2t = [sc.tok() for _ in range(32)]
                n2 = sc.sb("n2", [128, 8], F32)
                n2t = sc.tok()
                P.dma("sp", n2[:], n2w[l], T["w"], n2t, n2t)
                stg = [sc.sbt("stg%d" % i, [128, 1024], F32) for i in range(3)]
                convert(sc, lambda kt, c0, c1: w1b[:, kt, c0:c1], lambda kt, c0, c1: w1[l, kt * 128:(kt + 1) * 128, c0:c1], 8, 4096, stg,
                        scale_ap_fn=lambda kt: (n2[:, kt:kt + 1], n2t), chunk=1024, toks=w1t)
                convert(sc, lambda kt, c0, c1: w2b[:, kt, c0:c1], lambda kt, c0, c1: w2[l, kt * 128:(kt + 1) * 128, c0:c1], 32, D, stg, chunk=1024, toks=w2t)
                fw = None
                if final:
                    fw, fwt = sc.sbt("fw", [128, D], F32)
                    P.dma("sp", fw[:], fnw[:, :], T["w"], fwt, fwt)
                xt = [sc.sbt("xt%d" % i, [128, D], F32) for i in range(2)]
                junk, junkt = sc.sbt("junk", [128, D], BF16)
                ssq = [sc.sbt("ssq%d" % i, [128, 4], F32) for i in range(2)]
                xn, xnt = sc.sbt("xn", [128, D], BF16)
                xnT = [sc.sbt("xnT%d" % i, [128, 8, 128], BF16) for i in range(2)]
                hT = [sc.sbt("hT%d" % i, [128, 32, 128], BF16) for i in range(2)]
                hr, hrt = sc.sbt("hr", [128, 512], F32)
                tp = [sc.pst("tp%d" % i, [128, 8, 128], BF16) for i in range(1)]
                ph = [sc.pst("ph%d" % i, [128, 4, 128], F32) for i in range(3)]
                po = [sc.pst("po%d" % i, [128, 512], F32) for i in range(2)]
                xo = [sc.sbt("xo%d" % i, [128, D], F32) for i in range(2)]
                for tt in range(NT):
                    x_t, x_tt = xt[tt % 2]
                    sq, sqt = ssq[tt % 2]
                    tpp, tpt = tp[0]
                    xT, xTt = xnT[tt % 2]
                    h_, ht_ = hT[tt % 2]
                    P.dma("sp", x_t[:], xsrc[tt * 128:(tt + 1) * 128, :], xtok, x_tt, x_tt)
                    rms_rstd(sc, x_t, x_tt, junk, junkt, sq, sq[:, 2:3], sqt)
                    P.op("dve", lambda E, x_t=x_t, sq=sq: E.tensor_scalar(out=xn[:], in0=x_t[:], scalar1=sq[:, 2:3], scalar2=None, op0=ALU.mult), r=[x_tt, sqt], w=[xnt])
                    for kt in range(8):
                        P.op("pe", lambda E, tpp=tpp, kt=kt: E.transpose(out=tpp[:, kt, :], in_=xn[:, kt * 128:(kt + 1) * 128], identity=ident[:]), r=[xnt, t_const], w=[tpt])
                    P.op("act", lambda E, xT=xT, tpp=tpp: E.copy(out=xT[:], in_=tpp[:]), r=[tpt], w=[xTt])
                    for f4 in range(8):
                        pp, ppt = ph[f4 % 3]
                        for fi in range(4):
                            ft = f4 * 4 + fi
                            for kt in range(8):
                                P.op("pe", lambda E, pp=pp, fi=fi, ft=ft, kt=kt, xT=xT: E.matmul(pp[:, fi, :], lhsT=w1b[:, kt, ft * 128:(ft + 1) * 128], rhs=xT[:, kt, :], start=(kt == 0), stop=(kt == 7)),
                                     r=[xTt, w1t[kt]], w=[ppt])
                        P.op("act", lambda E, pp=pp: E.activation(out=hr[:], in_=pp[:].rearrange("p a b -> p (a b)"), func=AF.Relu), r=[ppt], w=[hrt])
                        e2 = "pool" if f4 % 2 else "dve"
                        P.op(e2, lambda E, h_=h_, f4=f4: E.tensor_tensor(out=h_[:, f4 * 4:(f4 + 1) * 4, :].rearrange("p a b -> p (a b)"), in0=hr[:], in1=hr[:], op=ALU.mult), r=[hrt], w=[ht_])
                    xo_, xot_ = xo[tt % 2]
                    for nh in range(2):
                        pq, pqt = po[nh]
                        for kt in range(32):
                            P.op("pe", lambda E, pq=pq, kt=kt, nh=nh, h_=h_: E.matmul(pq[:], lhsT=h_[:, kt, :], rhs=w2b[:, kt, nh * 512:(nh + 1) * 512], start=(kt == 0), stop=(kt == 31)),
                                 r=[ht_, w2t[kt]], w=[pqt])
                        P.op("dve", lambda E, xo_=xo_, pq=pq, nh=nh, x_t=x_t: E.tensor_tensor(out=xo_[:, nh * 512:(nh + 1) * 512], in0=pq[:], in1=x_t[:, nh * 512:(nh + 1) * 512], op=ALU.add), r=[pqt, x_tt], w=[xot_])
                    if not final:
                        P.dma("pool", xdst[tt * 128:(tt + 1) * 128, :], xo_[:], xot_, xdtok, xot_)
                    else:
                        sq2, sq2t = ssq[tt % 2]
                        rms_rstd(sc, xo_, xot_, junk, junkt, sq2, sq2[:, 3:4], sq2t)
                        P.op("dve", lambda E, xo_=xo_, sq2=sq2: E.scalar_tensor_tensor(out=xo_[:], in0=xo_[:], scalar=sq2[:, 3:4], in1=fw[:], op0=ALU.mult, op1=ALU.mult), r=[sq2t, fwt], w=[xot_])
                        P.dma("pool", out_d[tt * 128:(tt + 1) * 128, :], xo_[:], xot_, T["out"], xot_)


        def phase_rnn(l):
            with Scope(P) as sc:
                prm, prmt = sc.sbt("prm", [128, 64], F32)
                P.dma("sp", prm[:, 0:32], convw[l].rearrange("p a b -> p (a b)"), T["w"], prmt, prmt)
                P.dma("sp", prm[:, 32:40], convb[l], T["w"], prmt, prmt)
                P.dma("sp", prm[:, 40:48], lba[l], T["w"], prmt, prmt)
                P.dma("sp", prm[:, 48:56], lbi[l], T["w"], prmt, prmt)
                P.dma("sp", prm[:, 56:64], llam[l], T["w"], prmt, prmt)
                P.op("act", lambda E: E.activation(out=prm[:, 56:64], in_=prm[:, 56:64], func=AF.Exp, scale=-1.0), r=[prmt], w=[prmt])
                P.op("act", lambda E: E.activation(out=prm[:, 56:64], in_=prm[:, 56:64], func=AF.Ln, bias=epsb[:, 1:2]), r=[prmt, t_const], w=[prmt])
                P.op("dve", lambda E: E.tensor_scalar(out=prm[:, 56:64], in0=prm[:, 56:64], scalar1=-8.0, scalar2=None, op0=ALU.mult), r=[prmt], w=[prmt])
                H = S // 2
                wst = [sc.sbt("wst%d" % i, [128, 128], F32) for i in range(2)]
                wab = [sc.sbt("wab%d" % i, [128, 128], BF16) for i in range(2)]
                wib = [sc.sbt("wib%d" % i, [128, 128], BF16) for i in range(2)]
                sets = []
                for i in range(2):
                    d = {}
                    d["xrp"] = sc.sbt("xrp%d" % i, [128, H + 4], F32)
                    d["gg"] = sc.sbt("gg%d" % i, [128, H], F32)
                    d["xc"] = sc.sbt("xc%d" % i, [128, H], F32)
                    d["xcb"] = sc.sbt("xcb%d" % i, [128, H], BF16)
                    d["rr"] = sc.sbt("rr%d" % i, [128, H], F32)
                    d["ig"] = sc.sbt("ig%d" % i, [128, H], F32)
                    d["aa"] = sc.sbt("aa%d" % i, [128, H], F32)
                    d["ro"] = sc.sbt("ro%d" % i, [128, H], BF16)
                    sets.append(d)
                pa = [sc.pst("pa%d" % i, [128, 512], F32) for i in range(6)]
                pkc = [0]
                wts = {}

                def gen_w(ct):
                    wa_, wat_ = wab[ct % 2]
                    wi_, wit_ = wib[ct % 2]
                    ws_, wst_ = wst[0]
                    ws2, wst2 = wst[1]
                    P.dma("sp", ws_[:], lwa[l, ct], T["w"], wst_, wst_)
                    P.op("dve", lambda E, wa_=wa_, ws_=ws_: E.tensor_copy(out=wa_[:], in_=ws_[:]), r=[wst_], w=[wat_])
                    P.dma("sp", ws2[:], lwi[l, ct], T["w"], wst2, wst2)
                    P.op("dve", lambda E, wi_=wi_, ws2=ws2: E.tensor_copy(out=wi_[:], in_=ws2[:]), r=[wst2], w=[wit_])

                def gen_it(it):
                    ct, hf = it // 2, it % 2
                    if hf == 0:
                        gen_w(ct)
                    wa_, wat_ = wab[ct % 2]
                    wi_, wit_ = wib[ct % 2]
                    d = sets[it % 2]
                    dprev = sets[(it + 1) % 2]
                    xrp, xrpt = d["xrp"]
                    gg, ggt = d["gg"]
                    xc, xct = d["xc"]
                    xcb, xcbt = d["xcb"]
                    rr, rrt = d["rr"]
                    ig, igt = d["ig"]
                    aa, aat = d["aa"]
                    ro, rot = d["ro"]
                    t0 = hf * H
                    if hf == 0:
                        P.op("pool", lambda E, xrp=xrp: E.memset(xrp[:, 0:4], 0.0), w=[xrpt])
                        P.dma("sp", xrp[:, 4:H + 4], zf[0, ct * 128:(ct + 1) * 128, 0:H], T["zf"], xrpt, xrpt)
                    else:
                        P.dma("sp", xrp[:, 0:H + 4], zf[0, ct * 128:(ct + 1) * 128, H - 4:S], T["zf"], xrpt, xrpt)
                    P.dma("sp", gg[:], zf[1, ct * 128:(ct + 1) * 128, t0:t0 + H], T["zf"], ggt, ggt)
                    yield
                    P.op("act", lambda E: E.activation(out=xc[:], in_=xrp[:, 4:H + 4], func=AF.Identity, scale=prm[:, ct * 4 + 3:ct * 4 + 4], bias=prm[:, 32 + ct:33 + ct]), r=[xrpt, prmt], w=[xct])
                    yield
                    for i in range(3):
                        P.op("dve", lambda E, i=i: E.scalar_tensor_tensor(out=xc[:], in0=xrp[:, 1 + i:1 + i + H], scalar=prm[:, ct * 4 + i:ct * 4 + i + 1], in1=xc[:], op0=ALU.mult, op1=ALU.add), r=[xrpt, prmt], w=[xct])
                    yield
                    P.op("pool", lambda E: E.tensor_copy(out=xcb[:], in_=xc[:]), r=[xct], w=[xcbt])
                    yield
                    for tg in range(H // 512):
                        sl = slice(tg * 512, (tg + 1) * 512)
                        p1, p1t = pa[pkc[0] % 6]
                        p2, p2t = pa[(pkc[0] + 1) % 6]
                        pkc[0] += 2
                        P.op("pe", lambda E, p1=p1, sl=sl: E.matmul(p1[:], lhsT=wa_[:], rhs=xcb[:, sl], start=True, stop=True), r=[wat_, xcbt], w=[p1t])
                        P.op("pe", lambda E, p2=p2, sl=sl: E.matmul(p2[:], lhsT=wi_[:], rhs=xcb[:, sl], start=True, stop=True), r=[wit_, xcbt], w=[p2t])
                        P.op("act", lambda E, p1=p1, sl=sl: E.activation(out=rr[:, sl], in_=p1[:], func=AF.Sigmoid, bias=prm[:, 40 + ct:41 + ct]), r=[p1t, prmt], w=[rrt])
                        P.op("act", lambda E, p2=p2, sl=sl: E.activation(out=ig[:, sl], in_=p2[:], func=AF.Sigmoid, bias=prm[:, 48 + ct:49 + ct]), r=[p2t, prmt], w=[igt])
                    yield
                    P.op("act", lambda E: E.activation(out=aa[:], in_=rr[:], func=AF.Exp, scale=prm[:, 56 + ct:57 + ct]), r=[rrt, prmt], w=[aat])
                    P.op("pool", lambda E: E.tensor_tensor(out=ig[:], in0=ig[:], in1=xc[:], op=ALU.mult), r=[xct], w=[igt])
                    yield
                    P.op("pool", lambda E: E.tensor_tensor(out=rr[:], in0=aa[:], in1=aa[:], op=ALU.mult), r=[aat], w=[rrt])
                    yield
                    P.op("act", lambda E: E.activation(out=rr[:], in_=rr[:], func=AF.Sqrt, scale=-1.0, bias=epsb[:, 1:2]), r=[rrt, t_const], w=[rrt])
                    yield
                    P.op("dve", lambda E: E.tensor_tensor(out=ig[:], in0=ig[:], in1=rr[:], op=ALU.mult), r=[rrt], w=[igt])
                    if hf == 0:
                        P.op("dve", lambda E: E.tensor_tensor_scan(out=xc[:], data0=aa[:], data1=ig[:], initial=0.0, op0=ALU.mult, op1=ALU.add), r=[aat, igt], w=[xct])
                    else:
                        xcp, xcpt = dprev["xc"]
                        P.op("dve", lambda E: E.tensor_tensor_scan(out=xc[:], data0=aa[:], data1=ig[:], initial=xcp[:, H - 1:H], op0=ALU.mult, op1=ALU.add), r=[aat, igt, xcpt], w=[xct])
                    yield
                    P.op("pool", lambda E: E.tensor_tensor(out=ro[:], in0=xc[:], in1=gg[:], op=ALU.mult), r=[xct, ggt], w=[rot])
                    P.dma("pool", rnnT[ct * 128:(ct + 1) * 128, t0:t0 + H], ro[:], rot, T["rnnT"], rot)
                    yield

                active = []
                nxt = 0
                while nxt < 16 or active:
                    while len(active) < 2 and nxt < 16:
                        active.append(gen_it(nxt))
                        nxt += 1
                    for gnr in list(active):
                        try:
                            next(gnr)
                        except StopIteration:
                            active.remove(gnr)

        def phase_merge(l, xsrc, xtok, xdst, xdtok):
            with Scope(P) as sc:
                wa = sc.sb("wa", [128, 4, D], BF16)
                wat = [sc.tok() for _ in range(4)]
                wr = sc.sb("wr", [128, 8, D], BF16)
                wrt = [sc.tok() for _ in range(8)]
                wob = sc.sb("wob", [128, 8, D], BF16)
                wot = [sc.tok() for _ in range(8)]
                stg = [sc.sbt("stg%d" % i, [128, 1024], F32) for i in range(3)]
                convert(sc, lambda kt, c0, c1: wa[:, kt, c0:c1], lambda kt, c0, c1: wua[l, kt * 128:(kt + 1) * 128, c0:c1], 4, D, stg, chunk=1024, toks=wat)
                convert(sc, lambda kt, c0, c1: wr[:, kt, c0:c1], lambda kt, c0, c1: wur[l, kt * 128:(kt + 1) * 128, c0:c1], 8, D, stg, chunk=1024, toks=wrt)
                convert(sc, lambda kt, c0, c1: wob[:, kt, c0:c1], lambda kt, c0, c1: wo[l, kt * 128:(kt + 1) * 128, c0:c1], 8, D, stg, chunk=1024, toks=wot)
                aT = [sc.sbt("aT%d" % i, [128, 4, 512], BF16) for i in range(2)]
                rT = [sc.sbt("rT%d" % i, [128, 8, 512], BF16) for i in range(2)]
                sa = [sc.sbt("sa%d" % i, [128, 512], F32) for i in range(2)]
                sb_ = [sc.sbt("sb%d" % i, [128, 512], F32) for i in range(2)]
                t1 = [sc.sbt("t1%d" % i, [128, 512], F32) for i in range(2)]
                t2 = [sc.sbt("t2%d" % i, [128, 512], F32) for i in range(2)]
                mT = [sc.sbt("mT%d" % i, [128, 8, 512], BF16) for i in range(2)]
                pA = [sc.pst("pA%d" % i, [128, 512], F32) for i in range(2)]
                pB = [sc.pst("pB%d" % i, [128, 512], F32) for i in range(2)]
                pO = [sc.pst("pO%d" % i, [128, 512], F32) for i in range(2)]
                xt = [sc.sbt("xt%d" % i, [128, D], F32) for i in range(2)]
                xo = [sc.sbt("xo%d" % i, [128, D], F32) for i in range(2)]
                k = 0
                for sg in range(8):
                    tsl = slice(sg * 512, (sg + 1) * 512)
                    a_, at_ = aT[sg % 2]
                    r_, rt_ = rT[sg % 2]
                    m_, mt_ = mT[sg % 2]
                    P.dma("sp", a_[:], attnT[:, tsl].rearrange("(a p) t -> p a t", p=128), T["attnT"], at_, at_)
                    P.dma("sp", r_[:], rnnT[:, tsl].rearrange("(a p) t -> p a t", p=128), T["rnnT"], rt_, rt_)
                    for ft in range(8):
                        fs = slice(ft * 128, (ft + 1) * 128)
                        sa_, sat_ = sa[k % 2]
                        sbb, sbt_ = sb_[k % 2]
                        u1, u1t = t1[k % 2]
                        u2, u2t = t2[k % 2]
                        p_a, pat = pA[k % 2]
                        p_b, pbt = pB[k % 2]
                        k += 1
                        P.dma("sp", sa_[:], zf[2, fs, tsl], T["zf"], sat_, sat_)
                        P.dma("sp", sbb[:], zf[3, fs, tsl], T["zf"], sbt_, sbt_)
                        for kt in range(4):
                            P.op("pe", lambda E, p_a=p_a, kt=kt, fs=fs, a_=a_: E.matmul(p_a[:], lhsT=wa[:, kt, fs], rhs=a_[:, kt, :], start=(kt == 0), stop=(kt == 3)), r=[wat[kt], at_], w=[pat])
                        for kt in range(8):
                            P.op("pe", lambda E, p_b=p_b, kt=kt, fs=fs, r_=r_: E.matmul(p_b[:], lhsT=wr[:, kt, fs], rhs=r_[:, kt, :], start=(kt == 0), stop=(kt == 7)), r=[wrt[kt], rt_], w=[pbt])
                        P.op("dve", lambda E, u1=u1, p_a=p_a, sa_=sa_: E.tensor_tensor(out=u1[:], in0=p_a[:], in1=sa_[:], op=ALU.mult), r=[pat, sat_], w=[u1t])
                        P.op("dve", lambda E, u2=u2, p_b=p_b, sbb=sbb: E.tensor_tensor(out=u2[:], in0=p_b[:], in1=sbb[:], op=ALU.mult), r=[pbt, sbt_], w=[u2t])
                        P.op("pool", lambda E, m_=m_, ft=ft, u1=u1, u2=u2: E.tensor_tensor(out=m_[:, ft, :], in0=u1[:], in1=u2[:], op=ALU.add), r=[u1t, u2t], w=[mt_])
                    for j in range(4):
                        tt = sg * 4 + j
                        x_t, x_tt = xt[tt % 2]
                        xo_, xot_ = xo[tt % 2]
                        P.dma("sp", x_t[:], xsrc[tt * 128:(tt + 1) * 128, :], xtok, x_tt, x_tt)
                        for nh in range(2):
                            pq, pqt = pO[nh]
                            for kt in range(8):
                                P.op("pe", lambda E, pq=pq, kt=kt, nh=nh, m_=m_, j=j: E.matmul(pq[:], lhsT=m_[:, kt, j * 128:(j + 1) * 128], rhs=wob[:, kt, nh * 512:(nh + 1) * 512], start=(kt == 0), stop=(kt == 7)), r=[mt_, wot[kt]], w=[pqt])
                            P.op("dve", lambda E, xo_=xo_, pq=pq, nh=nh, x_t=x_t: E.tensor_tensor(out=xo_[:, nh * 512:(nh + 1) * 512], in0=pq[:], in1=x_t[:, nh * 512:(nh + 1) * 512], op=ALU.add), r=[pqt, x_tt], w=[xot_])
                        P.dma("pool", xdst[tt * 128:(tt + 1) * 128, :], xo_[:], xot_, xdtok, xot_)


        def phase_attn(l):
            with Scope(P) as sc:
                cst = sc.tok("cst")
                tric = sc.sb("tric", [128, 128], BF16)
                triw = sc.sb("triw", [128, 128], BF16)
                Ec = sc.sb("Ec", [64, S], BF16)
                band = sc.sb("band", [128, 9], BF16)
                cA = sc.sb("cA", [128, 128], F32)
                cB = sc.sb("cB", [128, 128], F32)
                for dst, src in ((tric, c_tric), (triw, c_triw), (Ec, c_E), (band, c_band), (cA, c_A), (cB, c_B)):
                    P.dma("sp", dst[:], src[:, :], T["w"], cst, cst)
                kcm = [sc.sbt("kcm%d" % g, [64, 256], BF16) for g in range(2)]
                vcm = [sc.sbt("vcm%d" % g, [128, 2, 64], BF16) for g in range(2)]
                with Scope(P) as s2:
                    stg = [s2.sbt("cstg%d" % i, [64, 2048], F32) for i in range(2)]
                    w2s, w2st = s2.sbt("w2s", [128, 128], F32)
                    pss, psst = s2.sbt("pss", [64, 64], F32)
                    w1b = s2.sb("w1b", [64, 32, 256], BF16)
                    w1bt = s2.tok()
                    w2b, w2bt = s2.sbt("w2b", [128, 2, 64], BF16)
                    posb, posbt = s2.sbt("posb", [64, 32, 2], BF16)
                    kg = [s2.sbt("kg%d" % g, [64, S], BF16) for g in range(2)]
                    hid = [[s2.sbt("hid%d%d" % (g, h), [128, 256], BF16) for h in range(2)] for g in range(2)]
                    bia, biat = s2.sbt("bia", [128, 2], F32)
                    psg = [s2.pst("psg%d" % g, [128, 512], F32) for g in range(2)]
                    psb, psbt = s2.pst("psb", [128, 512], F32)
                    pso, psot = s2.pst("pso", [128, 512], F32)
                    for (w1d, w2d, posd, srcT, srct, is_k) in ((ckw1, ckw2, posk, kcT, "kcT", True), (cvw1, cvw2, posv, vcT, "vcT", False)):
                        convert(s2, lambda kt, c0, c1: w1b[:].rearrange("p a b -> p (a b)")[:, c0:c1], lambda kt, c0, c1: w1d[l].rearrange("p a b -> p (a b)")[:, c0:c1], 1, 32 * 256, stg, chunk=2048, rows=64, toks=[w1bt])
                        P.dma("sp", w2s[:], w2d[l].rearrange("p a b -> p (a b)"), T["w"], w2st, w2st)
                        P.op("dve", lambda E: E.tensor_copy(out=w2b[:].rearrange("p a b -> p (a b)"), in_=w2s[:]), r=[w2st], w=[w2bt])
                        P.dma("sp", pss[:], posd[l].rearrange("p a b -> p (a b)"), T["w"], psst, psst)
                        P.op("dve", lambda E: E.tensor_copy(out=posb[:].rearrange("p a b -> p (a b)"), in_=pss[:]), r=[psst], w=[posbt])
                        for g in range(2):
                            P.dma("sp", kg[g][0][:], srcT[g], T[srct], kg[g][1], kg[g][1])
                        for ht in range(2):
                            hs = slice(ht * 128, (ht + 1) * 128)
                            for p in range(32):
                                for g in range(2):
                                    P.op("pe", lambda E, g=g, p=p, hs=hs: E.matmul(psg[g][0][:, 0:255], lhsT=w1b[:, p, hs], rhs=kg[g][0][:, p:p + 16 * 254 + 1:16], start=(p == 0), stop=(p == 31)),
                                         r=[w1bt, kg[g][1]], w=[psg[g][1]])
                                P.op("pe", lambda E, p=p, hs=hs: E.matmul(psb[:, 0:2], lhsT=w1b[:, p, hs], rhs=posb[:, p, :], start=(p == 0), stop=(p == 31)), r=[w1bt, posbt], w=[psbt])
                            P.op("dve", lambda E: E.tensor_copy(out=bia[:], in_=psb[:, 0:2]), r=[psbt], w=[biat])
                            for g in range(2):
                                P.op("act", lambda E, g=g, ht=ht: E.activation(out=hid[g][ht][0][:, 0:255], in_=psg[g][0][:, 0:255], func=AF.Gelu_apprx_tanh, bias=bia[:, 0:1]), r=[psg[g][1], biat], w=[hid[g][ht][1]])
                        for g in range(2):
                            if is_k:
                                for ht in range(2):
                                    P.op("pe", lambda E, g=g, ht=ht: E.matmul(pso[0:64, 0:255], lhsT=w2b[:, ht, :], rhs=hid[g][ht][0][:, 0:255], start=(ht == 0), stop=(ht == 1)), r=[w2bt, hid[g][ht][1]], w=[psot])
                                P.op("dve", lambda E, g=g: E.tensor_copy(out=kcm[g][0][:, 0:255], in_=pso[0:64, 0:255]), r=[psot], w=[kcm[g][1]])
                            else:
                                for ctile in range(2):
                                    n = 128 if ctile == 0 else 127
                                    for ht in range(2):
                                        P.op("pe", lambda E, g=g, ht=ht, ctile=ctile, n=n: E.matmul(pso[0:n, 256:320], lhsT=hid[g][ht][0][:, ctile * 128:ctile * 128 + n], rhs=w2b[:, ht, :], start=(ht == 0), stop=(ht == 1)), r=[w2bt, hid[g][ht][1]], w=[psot])
                                    P.op("dve", lambda E, g=g, ctile=ctile, n=n: E.tensor_copy(out=vcm[g][0][0:n, ctile, :], in_=pso[0:n, 256:320]), r=[psot], w=[vcm[g][1]])
                KE = [sc.sbt("KE%d" % g, [128, S], BF16) for g in range(2)]
                kwn = [sc.sbt("kwn%d" % g, [64, S], BF16) for g in range(2)]
                vsl = [sc.sbt("vsl%d" % g, [128, 32, 65], BF16) for g in range(2)]
                vwn = [sc.sbt("vwn%d" % g, [128, 32, 65], BF16) for g in range(2)]
                for g in range(2):
                    P.dma("sp", KE[g][0][0:64, :], ksT[g], T["ksT"], KE[g][1], KE[g][1])
                    P.dma("sp", KE[g][0][64:128, :], c_E[:, :], T["w"], KE[g][1], KE[g][1])
                    P.dma("sp", kwn[g][0][:], kwT[g], T["kwT"], kwn[g][1], kwn[g][1])
                    for (vv, j) in ((vsl[g], g), (vwn[g], 2 + g)):
                        P.op("pool", lambda E, vv=vv: E.memset(vv[0][:, :, 64:65], 1.0), w=[vv[1]])
                        for k8 in range(8):
                            P.dma("sp", vv[0][:, k8 * 4:(k8 + 1) * 4, 0:64], vtm[k8 * 512:(k8 + 1) * 512, j, :].rearrange("(k p) d -> p k d", p=128), T["vtm"], vv[1], vv[1])
                QP = [[sc.sbt("QP%d_%d" % (i, h), [128, 512], BF16) for h in range(8)] for i in range(2)]
                gt = [sc.sbt("gt%d" % i, [128, 4, 24], F32) for i in range(2)]
                bst = [sc.pst("bst%d" % i, [128, 512], F32) for i in range(2)]
                b_os = [sc.pst("b_os%d" % i, [128, 512], F32) for i in range(2)]
                b_ow = [sc.pst("b_ow%d" % i, [128, 512], F32) for i in range(2)]
                b_xs, b_xst = sc.pst("b_xs", [128, 512], F32)
                b_tp = sc.ps("b_tp", [128, 1024], BF16)
                tp_t = sc.tok("tp", True)
                NCH = 4
                CH = []
                for c in range(NCH):
                    d = {}
                    d["ee"] = [sc.sbt("ee%d_%d" % (c, i), [128, 256], F32) for i in range(4)]
                    d["pb"] = [sc.sbt("pb%d_%d" % (c, i), [128, 256], BF16) for i in range(4)]
                    d["pTc"] = [sc.sbt("pTc%d_%d" % (c, i), [128, 128], BF16) for i in range(2)]
                    d["sm"] = sc.sbt("sm%d" % c, [128, 16], F32)
                    d["P4"] = sc.sbt("P4_%d" % c, [128, 264], F32)
                    d["imp"] = sc.sbt("imp%d" % c, [128, 64], F32)
                    d["scr"] = sc.sbt("scr%d" % c, [128, 64], F32)
                    d["scr2"] = sc.sbt("scr2_%d" % c, [128, 64], F32)
                    d["m8"] = sc.sbt("m8_%d" % c, [128, 16], F32)
                    d["penb"] = sc.sbt("penb%d" % c, [128, 128], BF16)
                    P.op("pool", lambda E, d=d: E.memset(d["penb"][0][:], 0.0), w=[d["penb"][1]])
                    P.op("dve", lambda E, d=d: E.memset(d["P4"][0][:], 0.0), w=[d["P4"][1]])
                    CH.append(d)
                pT = [sc.sbt("pT%d" % i, [128, 512], BF16) for i in range(4)]
                sy, syt = sc.sbt("sy", [128, 8], F32)
                att2 = [sc.sbt("att%d" % i, [128, 4, 512], F32) for i in range(2)]
                attb, attbt = sc.sbt("attb", [128, 4, 512], BF16)
                ast = [sc.sbt("ast%d" % i, [128, 4, 512], BF16) for i in range(2)]
                sidx = [0]
                NSG = ATT_DBG["nqb"] // 4

                def gen_X(sg):
                    ss = slice(sg * 512, (sg + 1) * 512)
                    qp = QP[sg % 2]
                    g_, gt_ = gt[sg % 2]
                    for h in range(8):
                        P.dma("sp", qp[h][0][0:64, :], qT[h, :, ss], T["qT"], qp[h][1], qp[h][1])
                    P.dma("sp", g_[:], gat[ss, :].rearrange("(j p) c -> p j c", p=128), T["gat"], gt_, gt_)
                    yield
                    for g in range(2):
                        act = [gen_Xchain(sg, g, j) for j in range(4)]
                        while act:
                            for gn in list(act):
                                try:
                                    next(gn)
                                    yield
                                except StopIteration:
                                    act.remove(gn)

                def gen_Xchain(sg, g, j):
                    qp = QP[sg % 2]
                    g_, gt_ = gt[sg % 2]
                    att, attt = att2[sg % 2]
                    d = CH[j % NCH]
                    ee, pb, pTc = d["ee"], d["pb"], d["pTc"]
                    sm, smt = d["sm"]
                    P4, P4t = d["P4"]
                    imp, impt = d["imp"]
                    scr, scrt = d["scr"]
                    scr2, scr2t = d["scr2"]
                    m8, m8t = d["m8"]
                    penb, penbt = d["penb"]
                    P4v = P4[:, 0:256].rearrange("p (j f) -> p j f", f=4)
                    P4w = P4[:, 4:260].rearrange("p (j f) -> p j f", f=4)
                    if True:
                        if True:
                            qb = sg * 4 + j
                            js = slice(j * 128, (j + 1) * 128)
                            Nc = min(255, 8 * qb + 7)
                            cb0 = max(0, 8 * qb - 2)
                            cb1 = min(Nc, 8 * qb + 7)
                            ps_s = b_xs[:, 0:256]
                            for hh in range(4):
                                h = g * 4 + hh
                                P.op("pe", lambda E, h=h, g=g, js=js, Nc=Nc: E.matmul(ps_s[:, 0:Nc], lhsT=qp[h][0][0:64, js], rhs=kcm[g][0][:, 0:Nc], start=True, stop=False), r=[qp[h][1], kcm[g][1]], w=[b_xst])
                                P.op("pe", lambda E, cb0=cb0, cb1=cb1, qb=qb: E.matmul(ps_s[:, cb0:cb1], lhsT=ident[:], rhs=band[:, cb0 - (8 * qb - 2):cb1 - (8 * qb - 2)], start=False, stop=True), r=[t_const, cst], w=[b_xst])
                                e_, et_ = ee[hh]
                                P.op("act", lambda E, e_=e_, hh=hh, Nc=Nc: E.activation(out=e_[:, 0:Nc], in_=ps_s[:, 0:Nc], func=AF.Exp, accum_out=sm[:, 8 + hh:9 + hh]), r=[b_xst], w=[et_, smt])
                                yield
                            P.op("dve", lambda E: E.tensor_scalar(out=sm[:, 12:16], in0=sm[:, 8:12], scalar1=1e-20, scalar2=None, op0=ALU.max), r=[smt], w=[smt])
                            P.op("dve", lambda E: E.reciprocal(out=sm[:, 12:16], in_=sm[:, 12:16]), r=[smt], w=[smt])
                            P.op("dve", lambda E: E.tensor_tensor(out=sm[:, 0:4], in0=sm[:, 12:16], in1=g_[:, j, g * 12:(g + 1) * 12].rearrange("p (h c) -> p h c", c=3)[:, :, 0], op=ALU.mult), r=[smt, gt_], w=[smt])
                            for hh in range(4):
                                e_, et_ = ee[hh]
                                if hh == 0:
                                    P.op("dve", lambda E, e_=e_, hh=hh, Nc=Nc: E.tensor_scalar(out=P4[:, 4:4 + Nc], in0=e_[:, 0:Nc], scalar1=sm[:, 12 + hh:13 + hh], scalar2=None, op0=ALU.mult), r=[et_, smt], w=[P4t])
                                else:
                                    P.op("dve", lambda E, e_=e_, hh=hh, Nc=Nc: E.scalar_tensor_tensor(out=P4[:, 4:4 + Nc], in0=e_[:, 0:Nc], scalar=sm[:, 12 + hh:13 + hh], in1=P4[:, 4:4 + Nc], op0=ALU.mult, op1=ALU.add), r=[et_, smt], w=[P4t])
                            yield
                            P.op("dve", lambda E: E.tensor_tensor(out=imp[:], in0=P4v[:, :, 1], in1=P4v[:, :, 2], op=ALU.add), r=[P4t], w=[impt])
                            P.op("dve", lambda E: E.tensor_tensor(out=imp[:], in0=imp[:], in1=P4v[:, :, 3], op=ALU.add), r=[P4t], w=[impt])
                            P.op("dve", lambda E: E.scalar_tensor_tensor(out=imp[:], in0=imp[:], scalar=2.0, in1=P4v[:, :, 0], op0=ALU.mult, op1=ALU.add), r=[P4t], w=[impt])
                            P.op("dve", lambda E: E.tensor_tensor(out=imp[:], in0=imp[:], in1=P4w[:, :, 0], op=ALU.add), r=[P4t], w=[impt])
                            P.op("dve", lambda E, qb=qb: E.tensor_tensor(out=scr[:], in0=imp[:], in1=cA[:, 64 - 2 * qb:128 - 2 * qb], op=ALU.mult), r=[impt, cst], w=[scrt])
                            P.op("dve", lambda E, qb=qb: E.tensor_tensor(out=scr[:], in0=scr[:], in1=cB[:, 64 - 2 * qb:128 - 2 * qb], op=ALU.add), r=[cst], w=[scrt])
                            P.op("dve", lambda E: E.memset(scr[:, 0:1], 1e4), w=[scrt])
                            yield
                            nb = min(64, 2 * qb + 2)
                            if nb > 16:
                                P.op("dve", lambda E: E.max(out=m8[:, 0:8], in_=scr[:]), r=[scrt], w=[m8t])
                                P.op("dve", lambda E: E.match_replace(out=scr2[:], in_to_replace=m8[:, 0:8], in_values=scr[:], imm_value=-3e38), r=[scrt, m8t], w=[scr2t])
                                P.op("dve", lambda E: E.max(out=m8[:, 8:16], in_=scr2[:]), r=[scr2t], w=[m8t])
                                P.op("dve", lambda E, nb=nb: E.tensor_scalar(out=scr2[:, 0:nb], in0=scr[:, 0:nb], scalar1=m8[:, 15:16], scalar2=None, op0=ALU.is_ge), r=[scrt, m8t], w=[scr2t])
                                P.op("dve", lambda E, nb=nb: E.tensor_scalar(out=penb[:, 64:64 + nb], in0=scr2[:, 0:nb], scalar1=-1.0, scalar2=-NEGM, op0=ALU.add, op1=ALU.mult), r=[scr2t], w=[penbt])
                                yield
                            ptr = b_tp[:, 256:384]
                            P.op("pe", lambda E, ptr=ptr: E.transpose(out=ptr, in_=penb[:], identity=ident[:]), r=[penbt, t_const], w=[tp_t])
                            h0 = g * 4
                            P.op("act", lambda E, ptr=ptr, js=js, h0=h0: E.copy(out=qp[h0][0][64:128, js], in_=ptr[64:128, :]), r=[tp_t], w=[qp[h0][1]])
                            for hh in range(1, 4):
                                P.op("pool", lambda E, js=js, h0=h0, hh=hh: E.tensor_copy(out=qp[h0 + hh][0][64:128, js], in_=qp[h0][0][64:128, js]), r=[qp[h0][1]], w=[qp[h0 + hh][1]])
                            yield
                            k2 = 0
                            tpf = b_tp[:].bitcast(F32)
                            for hh in range(4):
                                h = g * 4 + hh
                                e_, et_ = ee[hh]
                                nct = (Nc + 127) // 128
                                for ctile in range(nct):
                                    n = min(128, Nc - ctile * 128)
                                    tpc = tpf[:, (k2 % 2) * 128:(k2 % 2) * 128 + 128]
                                    pc_, pct_ = pTc[k2 % 2]
                                    k2 += 1
                                    P.op("pe", lambda E, tpc=tpc, e_=e_, ctile=ctile, n=n: E.transpose(out=tpc[0:n, :], in_=e_[:, ctile * 128:ctile * 128 + n], identity=identf[:]), r=[et_, t_const], w=[tp_t])
                                    P.op("act", lambda E, pc_=pc_, tpc=tpc, n=n: E.copy(out=pc_[0:n, :], in_=tpc[0:n, :]), r=[tp_t], w=[pct_])
                                    P.op("pe", lambda E, pc_=pc_, n=n, ctile=ctile, nct=nct: E.matmul(b_xs[:, 256:320], lhsT=pc_[0:n, :], rhs=vcm[g][0][0:n, ctile, :], start=(ctile == 0), stop=(ctile == nct - 1), skip_group_check=True), r=[pct_, vcm[g][1]], w=[b_xst])
                                P.op("dve", lambda E, h=h, hh=hh: E.tensor_scalar(out=att[:, j, h * 64:(h + 1) * 64], in0=b_xs[:, 256:320], scalar1=sm[:, hh:hh + 1], scalar2=None, op0=ALU.mult), r=[b_xst, smt], w=[attt])
                                yield

                def gen_Y(sg):
                    ss = slice(sg * 512, (sg + 1) * 512)
                    qp = QP[sg % 2]
                    g_, gt_ = gt[sg % 2]
                    att, attt = att2[sg % 2]
                    for g in range(2):
                        for hh in range(4):
                            h = g * 4 + hh
                            q_, qt_ = qp[h]
                            os_, ost_ = b_os[hh % 2]
                            ow_, owt_ = b_ow[hh % 2]
                            steps = []
                            for kt in range(0, 4 * sg + 4):
                                steps.append(("s", kt))
                            for kt in range(max(0, 4 * sg - 4), 4 * sg + 4):
                                steps.append(("w", kt))
                            first = {"s": True, "w": True}
                            LA = 1
                            ring = []
                            for i in range(len(steps) + LA):
                                if i < len(steps):
                                    br, kt = steps[i]
                                    r_ = kt - 4 * sg
                                    if br == "s":
                                        jlo, jhi = max(r_, 0), 3
                                    else:
                                        jlo, jhi = max(r_, 0), min(r_ + 4, 3)
                                    c0, c1 = jlo * 128, (jhi + 1) * 128
                                    si = sidx[0] % 4
                                    stp = bst[sidx[0] % 2][0]
                                    stt_ = bst[sidx[0] % 2][1]
                                    sidx[0] += 1
                                    ks_ = slice(kt * 128, (kt + 1) * 128)
                                    ex = []
                                    if r_ >= 0:
                                        ex.append((r_, tric))
                                    if br == "w" and 0 <= r_ + 4 <= 3:
                                        ex.append((r_ + 4, triw))
                                    if br == "s":
                                        P.op("pe", lambda E, stp=stp, ks_=ks_, q_=q_, g=g, c0=c0, c1=c1, ex=ex: E.matmul(stp[:, c0:c1], lhsT=KE[g][0][:, ks_], rhs=q_[:, c0:c1], start=True, stop=(len(ex) == 0)), r=[KE[g][1], qt_], w=[stt_])
                                    else:
                                        P.op("pe", lambda E, stp=stp, ks_=ks_, q_=q_, g=g, c0=c0, c1=c1, ex=ex: E.matmul(stp[:, c0:c1], lhsT=kwn[g][0][:, ks_], rhs=q_[0:64, c0:c1], start=True, stop=(len(ex) == 0)), r=[kwn[g][1], qt_], w=[stt_])
                                    for xi, (jj, tri_) in enumerate(ex):
                                        P.op("pe", lambda E, stp=stp, jj=jj, tri_=tri_, xi=xi, ex=ex: E.matmul(stp[:, jj * 128:(jj + 1) * 128], lhsT=ident[:], rhs=tri_[:], start=False, stop=(xi == len(ex) - 1)), r=[t_const, cst], w=[stt_])
                                    pt_s, pt_st = pT[si]
                                    P.op("act", lambda E, pt_s=pt_s, stp=stp, c0=c0, c1=c1: E.activation(out=pt_s[:, c0:c1], in_=stp[:, c0:c1], func=AF.Exp), r=[stt_], w=[pt_st])
                                    ring.append((br, kt, jlo, jhi, pt_s, pt_st))
                                if i - LA >= 0:
                                    br, kt, jlo, jhi, pt_s, pt_st = ring[i - LA]
                                    ob, obt = (os_, ost_) if br == "s" else (ow_, owt_)
                                    V = vsl[g] if br == "s" else vwn[g]
                                    for jj in range(jlo, jhi + 1):
                                        st_flag = first[br]
                                        first[br] = False
                                        P.op("pe", lambda E, ob=ob, jj=jj, pt_s=pt_s, V=V, kt=kt, st_flag=st_flag: E.matmul(ob[:, jj * 65:jj * 65 + 65], lhsT=pt_s[:, jj * 128:(jj + 1) * 128], rhs=V[0][:, kt, :], start=st_flag, stop=True, skip_group_check=True), r=[pt_st, V[1]], w=[obt])
                                yield
                            osv = os_[:, 0:260].rearrange("p (j c) -> p j c", c=65)
                            owv = ow_[:, 0:260].rearrange("p (j c) -> p j c", c=65)
                            P.op("dve", lambda E, osv=osv: E.reciprocal(out=sy[:, 0:4], in_=osv[:, :, 64]), r=[ost_], w=[syt])
                            P.op("dve", lambda E, g_=g_, h=h: E.tensor_tensor(out=sy[:, 0:4], in0=sy[:, 0:4], in1=g_[:, :, h * 3 + 1], op=ALU.mult), r=[gt_], w=[syt])
                            P.op("dve", lambda E, owv=owv: E.reciprocal(out=sy[:, 4:8], in_=owv[:, :, 64]), r=[owt_], w=[syt])
                            P.op("dve", lambda E, g_=g_, h=h: E.tensor_tensor(out=sy[:, 4:8], in0=sy[:, 4:8], in1=g_[:, :, h * 3 + 2], op=ALU.mult), r=[gt_], w=[syt])
                            cs_ = slice(h * 64, (h + 1) * 64)
                            for jj in range(4):
                                P.op("dve", lambda E, jj=jj, cs_=cs_, os_=os_, att=att: E.scalar_tensor_tensor(out=att[:, jj, cs_], in0=os_[:, jj * 65:jj * 65 + 64], scalar=sy[:, jj:jj + 1], in1=att[:, jj, cs_], op0=ALU.mult, op1=ALU.add), r=[ost_, syt], w=[attt])
                                P.op("dve", lambda E, jj=jj, cs_=cs_, ow_=ow_, att=att: E.scalar_tensor_tensor(out=attb[:, jj, cs_], in0=ow_[:, jj * 65:jj * 65 + 64], scalar=sy[:, 4 + jj:5 + jj], in1=att[:, jj, cs_], op0=ALU.mult, op1=ALU.add), r=[owt_, syt, attt], w=[attbt])
                            yield
                    a_, at_ = ast[sg % 2]
                    atr = b_tp[:, 384:896].rearrange("p (a b) -> p a b", b=128)
                    for jj in range(4):
                        for ft in range(4):
                            P.op("pe", lambda E, ft=ft, jj=jj, atr=atr: E.transpose(out=atr[:, ft, :], in_=attb[:, jj, ft * 128:(ft + 1) * 128], identity=ident[:]), r=[attbt, t_const], w=[tp_t])
                        P.op("act", lambda E, a_=a_, atr=atr, jj=jj: E.copy(out=a_[:, :, jj * 128:(jj + 1) * 128], in_=atr), r=[tp_t], w=[at_])
                        yield
                    P.dma("pool", attnT[:, ss].rearrange("(a p) t -> p a t", p=128), a_[:], at_, T["attnT"], at_)
                    yield

                def drain(gn):
                    for _ in gn:
                        pass

                if NSG > 0:
                    drain(gen_X(0))
                for sg in range(NSG):
                    gy = gen_Y(sg) if not ATT_DBG.get("skip_sw") else iter(())
                    gx = gen_X(sg + 1) if sg + 1 < NSG else iter(())
                    ratio = max(1, int(round((32 * sg + 110) / 100.0)))
                    x_done = False
                    y_done = False
                    while not y_done:
                        for _ in range(ratio):
                            try:
                                next(gy)
                            except StopIteration:
                                y_done = True
                                break
                        if not x_done:
                            try:
                                next(gx)
                            except StopIteration:
                                x_done = True
                    if not x_done:
                        drain(gx)

        PH = {"inproj": phase_inproj, "mlp": phase_mlp, "rnn": phase_rnn, "merge": phase_merge, "attn": phase_attn}
        build.phases = PH
        build.ctx = dict(P=P, T=T, nc=nc, xs=xs, x_in=x_in, out_d=out_d)
        plan = build.plan
        plan(PH, build.ctx, locals())
        P.barrier()
        print("ops", P.nops, "waits", P.nwait)
    return nc, dbg


def default_plan(PH, ctx, L):
    T = ctx["T"]
    xs = ctx["xs"]
    cur, curt = ctx["x_in"], T["x_in"]
    for l in range(2):
        PH["inproj"](l, cur, curt)
        PH["attn"](l)
        PH["rnn"](l)
        PH["merge"](l, cur, curt, xs[0], T["xs0"])
        PH["mlp"](l, xs[0], T["xs0"], xs[1], T["xs1"], l == 1)
        cur, curt = xs[1], T["xs1"]


build.plan = default_plan


def host_inputs(inp, b):
    bf = ml_dtypes.bfloat16
    f = np.float32

    def pk(v):
        return np.ascontiguousarray(v.reshape(2, 8, 128).transpose(0, 2, 1)).astype(f)

    def bd(wm):
        o = np.zeros((2, 8, 128, 128), f)
        for c in range(8):
            o[:, c, 0:64, 0:64] = wm[:, 2 * c]
            o[:, c, 64:128, 64:128] = wm[:, 2 * c + 1]
        return o

    i_ = np.arange(128)
    m = {
        "x": np.ascontiguousarray(inp["x"][b]),
        "w_in": inp["w_in"],
        "n1w": pk(inp["norm1_w"]), "n2w": pk(inp["norm2_w"]),
        "fnw": np.ascontiguousarray(np.broadcast_to(inp["final_norm_w"][None, :], (128, D))).astype(f),
        "posk": np.ascontiguousarray(np.repeat(inp["cmp_pos_k"].transpose(0, 2, 1)[..., None], 2, axis=-1)),
        "posv": np.ascontiguousarray(np.repeat(inp["cmp_pos_v"].transpose(0, 2, 1)[..., None], 2, axis=-1)),
        "ckw1": np.ascontiguousarray(inp["cmp_k_w1"].reshape(2, 32, 64, 256).transpose(0, 2, 1, 3)),
        "cvw1": np.ascontiguousarray(inp["cmp_v_w1"].reshape(2, 32, 64, 256).transpose(0, 2, 1, 3)),
        "ckw2": np.ascontiguousarray(inp["cmp_k_w2"].reshape(2, 2, 128, 64).transpose(0, 2, 1, 3)),
        "cvw2": np.ascontiguousarray(inp["cmp_v_w2"].reshape(2, 2, 128, 64).transpose(0, 2, 1, 3)),
        "convw": np.ascontiguousarray(inp["conv_w"].reshape(2, 4, 8, 128).transpose(0, 3, 2, 1)),
        "convb": pk(inp["conv_b"]), "lba": pk(inp["lru_b_a"]), "lbi": pk(inp["lru_b_i"]), "llam": pk(inp["lru_lambda"]),
        "lwa": bd(inp["lru_w_a"]), "lwi": bd(inp["lru_w_i"]),
        "wua": inp["w_up_attn"], "wur": inp["w_up_rnn"], "wo": inp["w_out"], "w1": inp["mlp_w1"], "w2": inp["mlp_w2"],
        "c_ident": np.eye(128, dtype=f).astype(bf),
        "c_tric": np.where(i_[:, None] <= i_[None, :], 0.0, NEGM).astype(bf),
        "c_triw": np.where(i_[:, None] > i_[None, :], 0.0, NEGM).astype(bf),
        "c_E": (np.arange(S)[None, :] // 64 == np.arange(64)[:, None]).astype(f).astype(bf),
        "c_band": np.where((np.arange(9)[None, :] - 2) <= ((i_[:, None] + 1) // 16 - 2), 0.0, NEGM).astype(bf),
    }
    hi = (i_ >= 64).astype(np.int64)[:, None]
    jp = (np.arange(128) - 64)[None, :]
    valid = jp <= hi
    forced = jp > hi - 2
    A = np.where(valid & ~forced, 1.0, 0.0)
    Bm = np.where(valid, np.where(forced, 1e4, 0.0), -1e30)
    m["c_A"] = A.astype(f)
    m["c_B"] = Bm.astype(f)
    return {k: np.ascontiguousarray(v) for k, v in m.items()}


def kernel(**inputs):
    inp = {k: np.asarray(v) for k, v in inputs.items()}
    nc, _ = build(False)
    in_maps = [host_inputs(inp, c % 4) for c in range(8)]
    res = run_bass_kernel_spmd(nc, in_maps, core_ids=list(range(8)))
    return np.stack([np.asarray(res.results[c]["out"]) for c in range(4)], axis=0).astype(np.float32)
```

```python
import numpy as np
import ml_dtypes
from contextlib import ExitStack
import concourse.bass as bass
import concourse.mybir as mybir
from concourse.bass_utils import run_bass_kernel_spmd

F32 = mybir.dt.float32
BF16 = mybir.dt.bfloat16
AF = mybir.ActivationFunctionType
ALU = mybir.AluOpType
AX = mybir.AxisListType

S = 4096
D = 1024
DIN = 5400
NT = S // 128
NEGM = -30000.0
EPS = 1e-6
O_Q, O_KC, O_VC, O_KS, O_VS, O_KW, O_VW, O_GN, O_XR, O_GR, O_GA, O_GB = 0, 512, 640, 768, 896, 1024, 1152, 1280, 1304, 2328, 3352, 4376


ATT_DBG = {"level": 6, "nqb": NT}


class Tok:
    __slots__ = ("w", "r", "sem", "name", "x")

    def __init__(self, name="", x=False):
        self.w = {}
        self.r = {}
        self.sem = None
        self.name = name
        self.x = x


class Prog:
    ENG = ("pe", "act", "dve", "pool", "sp")

    def __init__(self, nc, es, n_dma_sems=80):
        self.nc = nc
        self.eng = {"pe": nc.tensor, "act": nc.scalar, "dve": nc.vector, "pool": nc.gpsimd, "sp": nc.sync}
        self.sems = []
        self.esem = {}
        for e in self.ENG:
            self.esem[e] = len(self.sems)
            self.sems.append(es.enter_context(nc.semaphore("es_" + e)))
        self.dma_ids = []
        for i in range(n_dma_sems):
            self.dma_ids.append(len(self.sems))
            self.sems.append(es.enter_context(nc.semaphore("ds_%d" % i)))
        self.free = list(self.dma_ids)
        self.total = [0] * len(self.sems)
        self.known = {e: [0] * len(self.sems) for e in self.ENG}
        self.nwait = 0
        self.nops = 0

    def _wait(self, eng, deps):
        E = self.eng[eng]
        kn = self.known[eng]
        for s, v in deps.items():
            if s >= 5:
                v = self.total[s]
            if kn[s] < v:
                kn[s] = v
                E.wait_ge(self.sems[s], v)
                self.nwait += 1

    @staticmethod
    def _merge(d, src):
        for s, v in src.items():
            if d.get(s, 0) < v:
                d[s] = v

    def op(self, eng, fn, r=(), w=()):
        deps = {}
        rx = [b for b in r if b.x]
        if rx:
            r = [b for b in r if not b.x]
            w = list(w) + rx
        for b in r:
            self._merge(deps, b.w)
        for b in w:
            self._merge(deps, b.w)
            self._merge(deps, b.r)
        s = self.esem[eng]
        if eng == "pe":
            deps.pop(s, None)
        self._wait(eng, deps)
        self.total[s] += 1
        n = self.total[s]
        fn(self.eng[eng]).then_inc(self.sems[s], 1)
        self.nops += 1
        for b in r:
            b.r[s] = n
        for b in w:
            b.w[s] = n

    def dma(self, eng, out, in_, src, dst, owner):
        deps = {}
        self._merge(deps, src.w)
        self._merge(deps, dst.w)
        self._merge(deps, dst.r)
        self._wait(eng, deps)
        if owner.sem is None:
            owner.sem = self.free.pop()
        s = owner.sem
        self.total[s] += 16
        v = self.total[s]
        self.eng[eng].dma_start(out=out, in_=in_).then_inc(self.sems[s], 16)
        self.nops += 1
        src.r[s] = v
        dst.w[s] = v

    def release(self, toks):
        for t in toks:
            if t.sem is not None:
                self.free.append(t.sem)
                t.sem = None

    def barrier(self):
        for e in self.ENG:
            E = self.eng[e]
            kn = self.known[e]
            for s in range(len(self.sems)):
                if s == self.esem[e]:
                    continue
                v = self.total[s]
                if kn[s] < v:
                    kn[s] = v
                    E.wait_ge(self.sems[s], v)
        arr = {}
        for e in self.ENG:
            s = self.esem[e]
            self.total[s] += 1
            arr[e] = self.total[s]
            if e == "pe":
                self.eng[e].nop().then_inc(self.sems[s], 1) if hasattr(self.eng[e], "nop") else None
            else:
                self.eng[e].nop().then_inc(self.sems[s], 1)
        for e in self.ENG:
            for f in self.ENG:
                if f == e:
                    continue
                s = self.esem[f]
                self.known[e][s] = arr[f]
                self.eng[e].wait_ge(self.sems[s], arr[f])


class Scope:
    def __init__(self, P):
        self.P = P
        self.es = ExitStack()
        self.toks = []

    def __enter__(self):
        self.es.__enter__()
        return self

    def __exit__(self, *a):
        self.P.barrier()
        self.P.release(self.toks)
        return self.es.__exit__(*a)

    uid = [0]

    def sb(self, name, shape, dt):
        Scope.uid[0] += 1
        return self.es.enter_context(self.P.nc.sbuf_tensor("%s_%d" % (name, Scope.uid[0]), list(shape), dt))

    def ps(self, name, shape, dt):
        Scope.uid[0] += 1
        return self.es.enter_context(self.P.nc.psum_tensor("%s_%d" % (name, Scope.uid[0]), list(shape), dt))

    def tok(self, name="", x=False):
        t = Tok(name, x)
        self.toks.append(t)
        return t

    def sbt(self, name, shape, dt):
        return self.sb(name, shape, dt), self.tok(name)

    def pst(self, name, shape, dt):
        return self.ps(name, shape, dt), self.tok(name, True)


def build(debug=False):
    nc = bass.Bass("TRN2", target_bir_lowering=False)
    dbg = {}

    def din(name, shape, dt=F32):
        return nc.dram_tensor(name, list(shape), dt, kind="ExternalInput").ap()

    def dscr(name, shape, dt):
        isd = bool(debug) and (debug is True or name in debug)
        kind = "ExternalOutput" if isd else "Internal"
        t = nc.dram_tensor(name, list(shape), dt, kind=kind).ap()
        if isd:
            dbg[name] = t
        return t

    x_in = din("x", [S, D])
    out_d = nc.dram_tensor("out", [S, D], F32, kind="ExternalOutput").ap()
    w_in = din("w_in", [2, D, DIN])
    n1w = din("n1w", [2, 128, 8])
    n2w = din("n2w", [2, 128, 8])
    fnw = din("fnw", [128, D])
    posk = din("posk", [2, 64, 32, 2])
    posv = din("posv", [2, 64, 32, 2])
    ckw1 = din("ckw1", [2, 64, 32, 256])
    cvw1 = din("cvw1", [2, 64, 32, 256])
    ckw2 = din("ckw2", [2, 128, 2, 64])
    cvw2 = din("cvw2", [2, 128, 2, 64])
    convw = din("convw", [2, 128, 8, 4])
    convb = din("convb", [2, 128, 8])
    lba = din("lba", [2, 128, 8])
    lbi = din("lbi", [2, 128, 8])
    llam = din("llam", [2, 128, 8])
    lwa = din("lwa", [2, 8, 128, 128])
    lwi = din("lwi", [2, 8, 128, 128])
    wua = din("wua", [2, 512, D])
    wur = din("wur", [2, D, D])
    wo = din("wo", [2, D, D])
    w1 = din("w1", [2, D, 4096])
    w2 = din("w2", [2, 4096, D])
    c_ident = din("c_ident", [128, 128], BF16)
    c_tric = din("c_tric", [128, 128], BF16)
    c_triw = din("c_triw", [128, 128], BF16)
    c_E = din("c_E", [64, S], BF16)
    c_band = din("c_band", [128, 9], BF16)
    c_A = din("c_A", [128, 128])
    c_B = din("c_B", [128, 128])

    xs = [dscr("xs0", [S, D], F32), dscr("xs1", [S, D], F32)]
    qT = dscr("qT", [8, 64, S], BF16)
    kcT = dscr("kcT", [2, 64, S], BF16)
    vcT = dscr("vcT", [2, 64, S], BF16)
    ksT = dscr("ksT", [2, 64, S], BF16)
    kwT = dscr("kwT", [2, 64, S], BF16)
    vtm = dscr("vtm", [S, 4, 64], BF16)
    gat = dscr("gat", [S, 24], F32)
    zf = dscr("zf", [4, D, S], F32)
    attnT = dscr("attnT", [512, S], BF16)
    rnnT = dscr("rnnT", [D, S], BF16)

    with ExitStack() as es:
        P = Prog(nc, es)
        T = {n: Tok(n) for n in ["x_in", "out", "w", "xs0", "xs1", "qT", "kcT", "vcT", "ksT", "kwT", "vtm", "gat", "zf", "attnT", "rnnT"]}
        for t in T.values():
            t.sem = None

        ident = es.enter_context(nc.sbuf_tensor("ident", [128, 128], BF16))
        identf = es.enter_context(nc.sbuf_tensor("identf", [128, 128], F32))
        t_const = Tok("const")
        P.dma("sp", ident[:], c_ident[:, :], T["w"], t_const, t_const)
        P.op("dve", lambda E: E.tensor_copy(out=identf[:], in_=ident[:]), r=[t_const], w=[t_const])

        def convert(sc, dst_ap_fn, src_ap_fn, nrow_tiles, ncols, stg, scale_ap_fn=None, chunk=2048, rows=128, toks=None):
            i = 0
            engs = ("act", "dve", "pool")
            for kt in range(nrow_tiles):
                for c0 in range(0, ncols, chunk):
                    c1 = min(ncols, c0 + chunk)
                    st, stt = stg[i % len(stg)]
                    P.dma("sp", st[0:rows, 0:c1 - c0], src_ap_fn(kt, c0, c1), T["w"], stt, stt)
                    e = engs[i % 3]
                    tk = toks[kt] if toks is not None else None
                    dst = dst_ap_fn(kt, c0, c1)
                    src = st[0:rows, 0:c1 - c0]
                    if scale_ap_fn is None:
                        if e == "act":
                            P.op(e, lambda E, dst=dst, src=src: E.copy(out=dst, in_=src), r=[stt], w=[tk])
                        else:
                            P.op(e, lambda E, dst=dst, src=src: E.tensor_copy(out=dst, in_=src), r=[stt], w=[tk])
                    else:
                        sc_ap, sc_tok = scale_ap_fn(kt)
                        if e == "act":
                            P.op(e, lambda E, dst=dst, src=src, sc_ap=sc_ap: E.activation(out=dst, in_=src, func=AF.Copy, scale=sc_ap), r=[stt, sc_tok], w=[tk])
                        else:
                            P.op(e, lambda E, dst=dst, src=src, sc_ap=sc_ap: E.tensor_scalar(out=dst, in0=src, scalar1=sc_ap, scalar2=None, op0=ALU.mult), r=[stt, sc_tok], w=[tk])
                    i += 1

        def rms_rstd(sc, xt, xtok, junk, junktok, ssq, rstd, sstok):
            P.op("act", lambda E: E.activation(out=junk[:], in_=xt[:], func=AF.Square, accum_out=ssq[:, 0:1]), r=[xtok], w=[junktok, sstok])
            P.op("act", lambda E: E.activation(out=ssq[:, 1:2], in_=ssq[:, 0:1], func=AF.Sqrt, scale=1.0 / D, bias=epsb[:, 0:1]), r=[sstok, t_const], w=[sstok])
            P.op("dve", lambda E: E.reciprocal(out=rstd[:, 0:1], in_=ssq[:, 1:2]), r=[sstok], w=[sstok])

        epsb = es.enter_context(nc.sbuf_tensor("epsb", [128, 4], F32))
        P.op("dve", lambda E: E.memset(epsb[:, 0:1], EPS), w=[t_const])
        P.op("dve", lambda E: E.memset(epsb[:, 1:2], 1.0), w=[t_const])
        P.op("dve", lambda E: E.memset(epsb[:, 2:3], 0.0), w=[t_const])

        def phase_inproj(l, xsrc, xtok):
            with Scope(P) as sc:
                wbf = sc.sb("wbf", [128, 8, DIN], BF16)
                wtok = [sc.tok("wbf%d" % k) for k in range(8)]
                n1 = sc.sb("n1", [128, 8], F32)
                n1t = sc.tok()
                P.dma("sp", n1[:], n1w[l], T["w"], n1t, n1t)
                stg = [sc.sbt("stg%d" % i, [128, 1800], F32) for i in range(3)]
                convert(sc, lambda kt, c0, c1: wbf[:, kt, c0:c1], lambda kt, c0, c1: w_in[l, kt * 128:(kt + 1) * 128, c0:c1], 8, DIN, stg,
                        scale_ap_fn=lambda kt: (n1[:, kt:kt + 1], n1t), chunk=1800, toks=wtok)
                xt = [sc.sbt("xt%d" % i, [128, D], F32) for i in range(2)]
                junk, junkt = sc.sbt("junk", [128, D], BF16)
                ssq = [sc.sbt("ssq%d" % i, [128, 4], F32) for i in range(2)]
                xn = [sc.sbt("xn%d" % i, [128, D], BF16) for i in range(2)]
                xnT = [sc.sbt("xnT%d" % i, [128, 8, 512], BF16) for i in range(2)]
                tp = [sc.pst("tp%d" % i, [128, 8, 128], BF16) for i in range(2)]
                pf = [sc.pst("pf%d" % i, [128, 512], F32) for i in range(4)]
                ptm = [sc.pst("ptm", [128, 512], F32)]
                of32 = [sc.sbt("of32_%d" % i, [128, 512], F32) for i in range(4)]
                obf = [sc.sbt("obf_%d" % i, [128, 512], BF16) for i in range(4)]
                vst = [sc.sbt("vst%d" % i, [128, 256], BF16) for i in range(2)]
                gst = [sc.sbt("gst%d" % i, [128, 24], F32) for i in range(2)]
                cnt = {"pf": 0, "f": 0, "b": 0}
                def gen_prep(sg):
                    xT, xTt = xnT[sg % 2]
                    for j in range(4):
                        tt = sg * 4 + j
                        x_t, x_tt = xt[tt % 2]
                        sq, sqt = ssq[tt % 2]
                        xb, xbt = xn[tt % 2]
                        tpp, tpt = tp[tt % 2]
                        P.dma("sp", x_t[:], xsrc[tt * 128:(tt + 1) * 128, :], xtok, x_tt, x_tt)
                        rms_rstd(sc, x_t, x_tt, junk, junkt, sq, sq[:, 2:3], sqt)
                        yield
                        P.op("dve", lambda E, xb=xb, x_t=x_t, sq=sq: E.tensor_scalar(out=xb[:], in0=x_t[:], scalar1=sq[:, 2:3], scalar2=None, op0=ALU.mult), r=[x_tt, sqt], w=[xbt])
                        for kt in range(8):
                            P.op("pe", lambda E, tpp=tpp, xb=xb, kt=kt: E.transpose(out=tpp[:, kt, :], in_=xb[:, kt * 128:(kt + 1) * 128], identity=ident[:]), r=[xbt, t_const], w=[tpt])
                        P.op("act", lambda E, xT=xT, tpp=tpp, j=j: E.copy(out=xT[:, :, j * 128:(j + 1) * 128], in_=tpp[:]), r=[tpt], w=[xTt])
                        yield
                        pt, ptt = ptm[0]
                        for (c0, n, o0) in ((O_VS, 128, 0), (O_VW, 128, 128), (O_GN, 24, 256)):
                            for kt in range(8):
                                P.op("pe", lambda E, pt=pt, xT=xT, kt=kt, c0=c0, n=n, o0=o0, j=j: E.matmul(pt[:, o0:o0 + n], lhsT=xT[:, kt, j * 128:(j + 1) * 128], rhs=wbf[:, kt, c0:c0 + n], start=(kt == 0), stop=(kt == 7)),
                                     r=[xTt, wtok[kt]], w=[ptt])
                        vs_, vst_ = vst[tt % 2]
                        gs_, gst_ = gst[tt % 2]
                        P.op("dve", lambda E, vs_=vs_, pt=pt: E.tensor_copy(out=vs_[:], in_=pt[:, 0:256]), r=[ptt], w=[vst_])
                        P.op("act", lambda E, gs_=gs_, pt=pt: E.activation(out=gs_[:], in_=pt[:, 256:280], func=AF.Sigmoid), r=[ptt], w=[gst_])
                        P.dma("pool", vtm[tt * 128:(tt + 1) * 128].rearrange("p a d -> p (a d)"), vs_[:], vst_, T["vtm"], vst_)
                        P.dma("pool", gat[tt * 128:(tt + 1) * 128, :], gs_[:], gst_, T["gat"], gst_)
                        yield

                def gen_main(sg):
                    xT, xTt = xnT[sg % 2]
                    tsl = slice(sg * 512, (sg + 1) * 512)
                    jobs = []
                    qTf = qT.rearrange("h d t -> (h d) t")
                    for h2 in range(4):
                        jobs.append((O_Q + h2 * 128, 128, "q", qTf[h2 * 128:(h2 + 1) * 128, tsl], "qT"))
                    jobs.append((O_KC, 128, "c", kcT.rearrange("g d t -> (g d) t")[:, tsl], "kcT"))
                    jobs.append((O_VC, 128, "c", vcT.rearrange("g d t -> (g d) t")[:, tsl], "vcT"))
                    jobs.append((O_KS, 128, "c", ksT.rearrange("g d t -> (g d) t")[:, tsl], "ksT"))
                    jobs.append((O_KW, 128, "c", kwT.rearrange("g d t -> (g d) t")[:, tsl], "kwT"))
                    for ft in range(8):
                        jobs.append((O_XR + ft * 128, 128, "f", zf[0, ft * 128:(ft + 1) * 128, tsl], "zf"))
                    for ft in range(8):
                        jobs.append((O_GR + ft * 128, 128, "gelu", zf[1, ft * 128:(ft + 1) * 128, tsl], "zf"))
                    for ft in range(8):
                        jobs.append((O_GA + ft * 128, 128, "sig", zf[2, ft * 128:(ft + 1) * 128, tsl], "zf"))
                    for ft in range(8):
                        jobs.append((O_GB + ft * 128, 128, "sig", zf[3, ft * 128:(ft + 1) * 128, tsl], "zf"))
                    for (c0, m, kind, dst, dtk) in jobs:
                        pp, ppt = pf[cnt["pf"] % 4]
                        cnt["pf"] += 1
                        for kt in range(8):
                            P.op("pe", lambda E, pp=pp, kt=kt, c0=c0, m=m, xT=xT: E.matmul(pp[0:m, :], lhsT=wbf[:, kt, c0:c0 + m], rhs=xT[:, kt, :], start=(kt == 0), stop=(kt == 7)),
                                 r=[xTt, wtok[kt]], w=[ppt])
                        if kind in ("q", "c"):
                            ob, obt = obf[cnt["b"] % 4]
                            cnt["b"] += 1
                            scl = 0.125 if kind == "q" else 1.0
                            P.op("dve", lambda E, ob=ob, pp=pp, m=m, scl=scl: E.tensor_scalar(out=ob[0:m, :], in0=pp[0:m, :], scalar1=scl, scalar2=None, op0=ALU.mult), r=[ppt], w=[obt])
                            P.dma("pool", dst, ob[0:m, :], obt, T[dtk], obt)
                            yield
                        else:
                            ob, obt = of32[cnt["f"] % 4]
                            cnt["f"] += 1
                            if kind == "f":
                                P.op("dve", lambda E, ob=ob, pp=pp: E.tensor_copy(out=ob[:], in_=pp[:]), r=[ppt], w=[obt])
                            else:
                                fn = AF.Gelu_apprx_tanh if kind == "gelu" else AF.Sigmoid
                                P.op("act", lambda E, ob=ob, pp=pp, fn=fn: E.activation(out=ob[:], in_=pp[:], func=fn), r=[ppt], w=[obt])
                            P.dma("pool", dst, ob[:], obt, T[dtk], obt)
                            yield


                NG = S // 512
                for _ in gen_prep(0):
                    pass
                for sg in range(NG):
                    gm = gen_main(sg)
                    gp = gen_prep(sg + 1) if sg + 1 < NG else iter(())
                    m_done = False
                    p_done = False
                    k = 0
                    while not m_done:
                        try:
                            next(gm)
                        except StopIteration:
                            m_done = True
                        k += 1
                        if not p_done and k % 3 == 0:
                            try:
                                next(gp)
                            except StopIteration:
                                p_done = True
                    if not p_done:
                        for _ in gp:
                            pass

        def phase_mlp(l, xsrc, xtok, xdst, xdtok, final):
            with Scope(P) as sc:
                w1b = sc.sb("w1b", [128, 8, 4096], BF16)
                w1t = [sc.tok() for _ in range(8)]
                w2b = sc.sb("w2b", [128, 32, D], BF16)
                w2t = [sc.tok() for _ in range(32)]
                n2 = sc.sb("n2", [128, 8], F32)
                n2t = sc.tok()
                P.dma("sp", n2[:], n2w[l], T["w"], n2t, n2t)
                stg = [sc.sbt("stg%d" % i, [128, 1024], F32) for i in range(3)]
                convert(sc, lambda kt, c0, c1: w1b[:, kt, c0:c1], lambda kt, c0, c1: w1[l, kt * 128:(kt + 1) * 128, c0:c1], 8, 4096, stg,
                        scale_ap_fn=lambda kt: (n2[:, kt:kt + 1], n2t), chunk=1024, toks=w1t)
                convert(sc, lambda kt, c0, c1: w2b[:, kt, c0:c1], lambda kt, c0, c1: w2[l, kt * 128:(kt + 1) * 128, c0:c1], 32, D, stg, chunk=1024, toks=w2t)
                fw = None
                if final:
                    fw, fwt = sc.sbt("fw", [128, D], F32)
                    P.dma("sp", fw[:], fnw[:, :], T["w"], fwt, fwt)
                xt = [sc.sbt("xt%d" % i, [128, D], F32) for i in range(2)]
                junk, junkt = sc.sbt("junk", [128, D], BF16)
                ssq = [sc.sbt("ssq%d" % i, [128, 4], F32) for i in range(2)]
                xn, xnt = sc.sbt("xn", [128, D], BF16)
                xnT = [sc.sbt("xnT%d" % i, [128, 8, 128], BF16) for i in range(2)]
                hT = [sc.sbt("hT%d" % i, [128, 32, 128], BF16) for i in range(2)]
                hr, hrt = sc.sbt("hr", [128, 512], F32)
                tp = [sc.pst("tp%d" % i, [128, 8, 128], BF16) for i in range(1)]
                ph = [sc.pst("ph%d" % i, [128, 4, 128], F32) for i in range(3)]
                po = [sc.pst("po%d" % i, [128, 512], F32) for i in range(2)]
                xo = [sc.sbt("xo%d" % i, [128, D], F32) for i in range(2)]
                for tt in range(NT):
                    x_t, x_tt = xt[tt % 2]
                    sq, sqt = ssq[tt % 2]
                    tpp, tpt = tp[0]
                    xT, xTt = xnT[tt % 2]
                    h_, ht_ = hT[tt % 2]
                    P.dma("sp", x_t[:], xsrc[tt * 128:(tt + 1) * 128, :], xtok, x_tt, x_tt)
                    rms_rstd(sc, x_t, x_tt, junk, junkt, sq, sq[:, 2:3], sqt)
                    P.op("dve", lambda E, x_t=x_t, sq=sq: E.tensor_scalar(out=xn[:], in0=x_t[:], scalar1=sq[:, 2:3], scalar2=None, op0=ALU.mult), r=[x_tt, sqt], w=[xnt])
                    for kt in range(8):
                        P.op("pe", lambda E, tpp=tpp, kt=kt: E.transpose(out=tpp[:, kt, :], in_=xn[:, kt * 128:(kt + 1) * 128], identity=ident[:]), r=[xnt, t_const], w=[tpt])
                    P.op("act", lambda E, xT=xT, tpp=tpp: E.copy(out=xT[:], in_=tpp[:]), r=[tpt], w=[xTt])
                    for f4 in range(8):
                        pp, ppt = ph[f4 % 3]
                        for fi in range(4):
                            ft = f4 * 4 + fi
                            for kt in range(8):
                                P.op("pe", lambda E, pp=pp, fi=fi, ft=ft, kt=kt, xT=xT: E.matmul(pp[:, fi, :], lhsT=w1b[:, kt, ft * 128:(ft + 1) * 128], rhs=xT[:, kt, :], start=(kt == 0), stop=(kt == 7)),
                                     r=[xTt, w1t[kt]], w=[ppt])
                        P.op("act", lambda E, pp=pp: E.activation(out=hr[:], in_=pp[:].rearrange("p a b -> p (a b)"), func=AF.Relu), r=[ppt], w=[hrt])
                        e2 = "pool" if f4 % 2 else "dve"
                        P.op(e2, lambda E, h_=h_, f4=f4: E.tensor_tensor(out=h_[:, f4 * 4:(f4 + 1) * 4, :].rearrange("p a b -> p (a b)"), in0=hr[:], in1=hr[:], op=ALU.mult), r=[hrt], w=[ht_])
                    xo_, xot_ = xo[tt % 2]
                    for nh in range(2):
                        pq, pqt = po[nh]
                        for kt in range(32):
                            P.op("pe", lambda E, pq=pq, kt=kt, nh=nh, h_=h_: E.matmul(pq[:], lhsT=h_[:, kt, :], rhs=w2b[:, kt, nh * 512:(nh + 1) * 512], start=(kt == 0), stop=(kt == 31)),
                                 r=[ht_, w2t[kt]], w=[pqt])
                        P.op("dve", lambda E, xo_=xo_, pq=pq, nh=nh, x_t=x_t: E.tensor_tensor(out=xo_[:, nh * 512:(nh + 1) * 512], in0=pq[:], in1=x_t[:, nh * 512:(nh + 1) * 512], op=ALU.add), r=[pqt, x_tt], w=[xot_])
                    if not final:
                        P.dma("pool", xdst[tt * 128:(tt + 1) * 128, :], xo_[:], xot_, xdtok, xot_)
                    else:
                        sq2, sq2t = ssq[tt % 2]
                        rms_rstd(sc, xo_, xot_, junk, junkt, sq2, sq2[:, 3:4], sq2t)
                        P.op("dve", lambda E, xo_=xo_, sq2=sq2: E.scalar_tensor_tensor(out=xo_[:], in0=xo_[:], scalar=sq2[:, 3:4], in1=fw[:], op0=ALU.mult, op1=ALU.mult), r=[sq2t, fwt], w=[xot_])
                        P.dma("pool", out_d[tt * 128:(tt + 1) * 128, :], xo_[:], xot_, T["out"], xot_)


        def phase_rnn(l):
            with Scope(P) as sc:
                prm, prmt = sc.sbt("prm", [128, 64], F32)
                P.dma("sp", prm[:, 0:32], convw[l].rearrange("p a b -> p (a b)"), T["w"], prmt, prmt)
                P.dma("sp", prm[:, 32:40], convb[l], T["w"], prmt, prmt)
                P.dma("sp", prm[:, 40:48], lba[l], T["w"], prmt, prmt)
                P.dma("sp", prm[:, 48:56], lbi[l], T["w"], prmt, prmt)
                P.dma("sp", prm[:, 56:64], llam[l], T["w"], prmt, prmt)
                P.op("act", lambda E: E.activation(out=prm[:, 56:64], in_=prm[:, 56:64], func=AF.Exp, scale=-1.0), r=[prmt], w=[prmt])
                P.op("act", lambda E: E.activation(out=prm[:, 56:64], in_=prm[:, 56:64], func=AF.Ln, bias=epsb[:, 1:2]), r=[prmt, t_const], w=[prmt])
                P.op("dve", lambda E: E.tensor_scalar(out=prm[:, 56:64], in0=prm[:, 56:64], scalar1=-8.0, scalar2=None, op0=ALU.mult), r=[prmt], w=[prmt])
                H = S // 2
                wst = [sc.sbt("wst%d" % i, [128, 128], F32) for i in range(2)]
                wab = [sc.sbt("wab%d" % i, [128, 128], BF16) for i in range(2)]
                wib = [sc.sbt("wib%d" % i, [128, 128], BF16) for i in range(2)]
                sets = []
                for i in range(2):
                    d = {}
                    d["xrp"] = sc.sbt("xrp%d" % i, [128, H + 4], F32)
                    d["gg"] = sc.sbt("gg%d" % i, [128, H], F32)
                    d["xc"] = sc.sbt("xc%d" % i, [128, H], F32)
                    d["xcb"] = sc.sbt("xcb%d" % i, [128, H], BF16)
                    d["rr"] = sc.sbt("rr%d" % i, [128, H], F32)
                    d["ig"] = sc.sbt("ig%d" % i, [128, H], F32)
                    d["aa"] = sc.sbt("aa%d" % i, [128, H], F32)
                    d["ro"] = sc.sbt("ro%d" % i, [128, H], BF16)
                    sets.append(d)
                pa = [sc.pst("pa%d" % i, [128, 512], F32) for i in range(6)]
                pkc = [0]
                wts = {}

                def gen_w(ct):
                    wa_, wat_ = wab[ct % 2]
                    wi_, wit_ = wib[ct % 2]
                    ws_, wst_ = wst[0]
                    ws2, wst2 = wst[1]
                    P.dma("sp", ws_[:], lwa[l, ct], T["w"], wst_, wst_)
                    P.op("dve", lambda E, wa_=wa_, ws_=ws_: E.tensor_copy(out=wa_[:], in_=ws_[:]), r=[wst_], w=[wat_])
                    P.dma("sp", ws2[:], lwi[l, ct], T["w"], wst2, wst2)
                    P.op("dve", lambda E, wi_=wi_, ws2=ws2: E.tensor_copy(out=wi_[:], in_=ws2[:]), r=[wst2], w=[wit_])

                def gen_it(it):
                    ct, hf = it // 2, it % 2
                    if hf == 0:
                        gen_w(ct)
                    wa_, wat_ = wab[ct % 2]
                    wi_, wit_ = wib[ct % 2]
                    d = sets[it % 2]
                    dprev = sets[(it + 1) % 2]
                    xrp, xrpt = d["xrp"]
                    gg, ggt = d["gg"]
                    xc, xct = d["xc"]
                    xcb, xcbt = d["xcb"]
                    rr, rrt = d["rr"]
                    ig, igt = d["ig"]
                    aa, aat = d["aa"]
                    ro, rot = d["ro"]
                    t0 = hf * H
                    if hf == 0:
                        P.op("pool", lambda E, xrp=xrp: E.memset(xrp[:, 0:4], 0.0), w=[xrpt])
                        P.dma("sp", xrp[:, 4:H + 4], zf[0, ct * 128:(ct + 1) * 128, 0:H], T["zf"], xrpt, xrpt)
                    else:
                        P.dma("sp", xrp[:, 0:H + 4], zf[0, ct * 128:(ct + 1) * 128, H - 4:S], T["zf"], xrpt, xrpt)
                    P.dma("sp", gg[:], zf[1, ct * 128:(ct + 1) * 128, t0:t0 + H], T["zf"], ggt, ggt)
                    yield
                    P.op("act", lambda E: E.activation(out=xc[:], in_=xrp[:, 4:H + 4], func=AF.Identity, scale=prm[:, ct * 4 + 3:ct * 4 + 4], bias=prm[:, 32 + ct:33 + ct]), r=[xrpt, prmt], w=[xct])
                    yield
                    for i in range(3):
                        P.op("dve", lambda E, i=i: E.scalar_tensor_tensor(out=xc[:], in0=xrp[:, 1 + i:1 + i + H], scalar=prm[:, ct * 4 + i:ct * 4 + i + 1], in1=xc[:], op0=ALU.mult, op1=ALU.add), r=[xrpt, prmt], w=[xct])
                    yield
                    P.op("pool", lambda E: E.tensor_copy(out=xcb[:], in_=xc[:]), r=[xct], w=[xcbt])
                    yield
                    for tg in range(H // 512):
                        sl = slice(tg * 512, (tg + 1) * 512)
                        p1, p1t = pa[pkc[0] % 6]
                        p2, p2t = pa[(pkc[0] + 1) % 6]
                        pkc[0] += 2
                        P.op("pe", lambda E, p1=p1, sl=sl: E.matmul(p1[:], lhsT=wa_[:], rhs=xcb[:, sl], start=True, stop=True), r=[wat_, xcbt], w=[p1t])
                        P.op("pe", lambda E, p2=p2, sl=sl: E.matmul(p2[:], lhsT=wi_[:], rhs=xcb[:, sl], start=True, stop=True), r=[wit_, xcbt], w=[p2t])
                        P.op("act", lambda E, p1=p1, sl=sl: E.activation(out=rr[:, sl], in_=p1[:], func=AF.Sigmoid, bias=prm[:, 40 + ct:41 + ct]), r=[p1t, prmt], w=[rrt])
                        P.op("act", lambda E, p2=p2, sl=sl: E.activation(out=ig[:, sl], in_=p2[:], func=AF.Sigmoid, bias=prm[:, 48 + ct:49 + ct]), r=[p2t, prmt], w=[igt])
                    yield
                    P.op("act", lambda E: E.activation(out=aa[:], in_=rr[:], func=AF.Exp, scale=prm[:, 56 + ct:57 + ct]), r=[rrt, prmt], w=[aat])
                    P.op("pool", lambda E: E.tensor_tensor(out=ig[:], in0=ig[:], in1=xc[:], op=ALU.mult), r=[xct], w=[igt])
                    yield
                    P.op("pool", lambda E: E.tensor_tensor(out=rr[:], in0=aa[:], in1=aa[:], op=ALU.mult), r=[aat], w=[rrt])
                    yield
                    P.op("act", lambda E: E.activation(out=rr[:], in_=rr[:], func=AF.Sqrt, scale=-1.0, bias=epsb[:, 1:2]), r=[rrt, t_const], w=[rrt])
                    yield
                    P.op("dve", lambda E: E.tensor_tensor(out=ig[:], in0=ig[:], in1=rr[:], op=ALU.mult), r=[rrt], w=[igt])
                    if hf == 0:
                        P.op("dve", lambda E: E.tensor_tensor_scan(out=xc[:], data0=aa[:], data1=ig[:], initial=0.0, op0=ALU.mult, op1=ALU.add), r=[aat, igt], w=[xct])
                    else:
                        xcp, xcpt = dprev["xc"]
                        P.op("dve", lambda E: E.tensor_tensor_scan(out=xc[:], data0=aa[:], data1=ig[:], initial=xcp[:, H - 1:H], op0=ALU.mult, op1=ALU.add), r=[aat, igt, xcpt], w=[xct])
                    yield
                    P.op("pool", lambda E: E.tensor_tensor(out=ro[:], in0=xc[:], in1=gg[:], op=ALU.mult), r=[xct, ggt], w=[rot])
                    P.dma("pool", rnnT[ct * 128:(ct + 1) * 128, t0:t0 + H], ro[:], rot, T["rnnT"], rot)
                    yield

                active = []
                nxt = 0
                while nxt < 16 or active:
                    while len(active) < 2 and nxt < 16:
                        active.append(gen_it(nxt))
                        nxt += 1
                    for gnr in list(active):
                        try:
                            next(gnr)
                        except StopIteration:
                            active.remove(gnr)

        def phase_merge(l, xsrc, xtok, xdst, xdtok):
            with Scope(P) as sc:
                wa = sc.sb("wa", [128, 4, D], BF16)
                wat = [sc.tok() for _ in range(4)]
                wr = sc.sb("wr", [128, 8, D], BF16)
                wrt = [sc.tok() for _ in range(8)]
                wob = sc.sb("wob", [128, 8, D], BF16)
                wot = [sc.tok() for _ in range(8)]
                stg = [sc.sbt("stg%d" % i, [128, 1024], F32) for i in range(3)]
                convert(sc, lambda kt, c0, c1: wa[:, kt, c0:c1], lambda kt, c0, c1: wua[l, kt * 128:(kt + 1) * 128, c0:c1], 4, D, stg, chunk=1024, toks=wat)
                convert(sc, lambda kt, c0, c1: wr[:, kt, c0:c1], lambda kt, c0, c1: wur[l, kt * 128:(kt + 1) * 128, c0:c1], 8, D, stg, chunk=1024, toks=wrt)
                convert(sc, lambda kt, c0, c1: wob[:, kt, c0:c1], lambda kt, c0, c1: wo[l, kt * 128:(kt + 1) * 128, c0:c1], 8, D, stg, chunk=1024, toks=wot)
                aT = [sc.sbt("aT%d" % i, [128, 4, 512], BF16) for i in range(2)]
                rT = [sc.sbt("rT%d" % i, [128, 8, 512], BF16) for i in range(2)]
                sa = [sc.sbt("sa%d" % i, [128, 512], F32) for i in range(2)]
                sb_ = [sc.sbt("sb%d" % i, [128, 512], F32) for i in range(2)]
                t1 = [sc.sbt("t1%d" % i, [128, 512], F32) for i in range(2)]
                t2 = [sc.sbt("t2%d" % i, [128, 512], F32) for i in range(2)]
                mT = [sc.sbt("mT%d" % i, [128, 8, 512], BF16) for i in range(2)]
                pA = [sc.pst("pA%d" % i, [128, 512], F32) for i in range(2)]
                pB = [sc.pst("pB%d" % i, [128, 512], F32) for i in range(2)]
                pO = [sc.pst("pO%d" % i, [128, 512], F32) for i in range(2)]
                xt = [sc.sbt("xt%d" % i, [128, D], F32) for i in range(2)]
                xo = [sc.sbt("xo%d" % i, [128, D], F32) for i in range(2)]
                k = 0
                for sg in range(8):
                    tsl = slice(sg * 512, (sg + 1) * 512)
                    a_, at_ = aT[sg % 2]
                    r_, rt_ = rT[sg % 2]
                    m_, mt_ = mT[sg % 2]
                    P.dma("sp", a_[:], attnT[:, tsl].rearrange("(a p) t -> p a t", p=128), T["attnT"], at_, at_)
                    P.dma("sp", r_[:], rnnT[:, tsl].rearrange("(a p) t -> p a t", p=128), T["rnnT"], rt_, rt_)
                    for ft in range(8):
                        fs = slice(ft * 128, (ft + 1) * 128)
                        sa_, sat_ = sa[k % 2]
                        sbb, sbt_ = sb_[k % 2]
                        u1, u1t = t1[k % 2]
                        u2, u2t = t2[k % 2]
                        p_a, pat = pA[k % 2]
                        p_b, pbt = pB[k % 2]
                        k += 1
                        P.dma("sp", sa_[:], zf[2, fs, tsl], T["zf"], sat_, sat_)
                        P.dma("sp", sbb[:], zf[3, fs, tsl], T["zf"], sbt_, sbt_)
                        for kt in range(4):
                            P.op("pe", lambda E, p_a=p_a, kt=kt, fs=fs, a_=a_: E.matmul(p_a[:], lhsT=wa[:, kt, fs], rhs=a_[:, kt, :], start=(kt == 0), stop=(kt == 3)), r=[wat[kt], at_], w=[pat])
                        for kt in range(8):
                            P.op("pe", lambda E, p_b=p_b, kt=kt, fs=fs, r_=r_: E.matmul(p_b[:], lhsT=wr[:, kt, fs], rhs=r_[:, kt, :], start=(kt == 0), stop=(kt == 7)), r=[wrt[kt], rt_], w=[pbt])
                        P.op("dve", lambda E, u1=u1, p_a=p_a, sa_=sa_: E.tensor_tensor(out=u1[:], in0=p_a[:], in1=sa_[:], op=ALU.mult), r=[pat, sat_], w=[u1t])
                        P.op("dve", lambda E, u2=u2, p_b=p_b, sbb=sbb: E.tensor_tensor(out=u2[:], in0=p_b[:], in1=sbb[:], op=ALU.mult), r=[pbt, sbt_], w=[u2t])
                        P.op("pool", lambda E, m_=m_, ft=ft, u1=u1, u2=u2: E.tensor_tensor(out=m_[:, ft, :], in0=u1[:], in1=u2[:], op=ALU.add), r=[u1t, u2t], w=[mt_])
                    for j in range(4):
                        tt = sg * 4 + j
                        x_t, x_tt = xt[tt % 2]
                        xo_, xot_ = xo[tt % 2]
                        P.dma("sp", x_t[:], xsrc[tt * 128:(tt + 1) * 128, :], xtok, x_tt, x_tt)
                        for nh in range(2):
                            pq, pqt = pO[nh]
                            for kt in range(8):
                                P.op("pe", lambda E, pq=pq, kt=kt, nh=nh, m_=m_, j=j: E.matmul(pq[:], lhsT=m_[:, kt, j * 128:(j + 1) * 128], rhs=wob[:, kt, nh * 512:(nh + 1) * 512], start=(kt == 0), stop=(kt == 7)), r=[mt_, wot[kt]], w=[pqt])
                            P.op("dve", lambda E, xo_=xo_, pq=pq, nh=nh, x_t=x_t: E.tensor_tensor(out=xo_[:, nh * 512:(nh + 1) * 512], in0=pq[:], in1=x_t[:, nh * 512:(nh + 1) * 512], op=ALU.add), r=[pqt, x_tt], w=[xot_])
                        P.dma("pool", xdst[tt * 128:(tt + 1) * 128, :], xo_[:], xot_, xdtok, xot_)


        def phase_attn(l):
            with Scope(P) as sc:
                cst = sc.tok("cst")
                tric = sc.sb("tric", [128, 128], BF16)
                triw = sc.sb("triw", [128, 128], BF16)
                Ec = sc.sb("Ec", [64, S], BF16)
                band = sc.sb("band", [128, 9], BF16)
                cA = sc.sb("cA", [128, 128], F32)
                cB = sc.sb("cB", [128, 128], F32)
                for dst, src in ((tric, c_tric), (triw, c_triw), (Ec, c_E), (band, c_band), (cA, c_A), (cB, c_B)):
                    P.dma("sp", dst[:], src[:, :], T["w"], cst, cst)
                kcm = [sc.sbt("kcm%d" % g, [64, 256], BF16) for g in range(2)]
                vcm = [sc.sbt("vcm%d" % g, [128, 2, 64], BF16) for g in range(2)]
                with Scope(P) as s2:
                    stg = [s2.sbt("cstg%d" % i, [64, 2048], F32) for i in range(2)]
                    w2s, w2st = s2.sbt("w2s", [128, 128], F32)
                    pss, psst = s2.sbt("pss", [64, 64], F32)
                    w1b = s2.sb("w1b", [64, 32, 256], BF16)
                    w1bt = s2.tok()
                    w2b, w2bt = s2.sbt("w2b", [128, 2, 64], BF16)
                    posb, posbt = s2.sbt("posb", [64, 32, 2], BF16)
                    kg = [s2.sbt("kg%d" % g, [64, S], BF16) for g in range(2)]
                    hid = [[s2.sbt("hid%d%d" % (g, h), [128, 256], BF16) for h in range(2)] for g in range(2)]
                    bia, biat = s2.sbt("bia", [128, 2], F32)
                    psg = [s2.pst("psg%d" % g, [128, 512], F32) for g in range(2)]
                    psb, psbt = s2.pst("psb", [128, 512], F32)
                    pso, psot = s2.pst("pso", [128, 512], F32)
                    for (w1d, w2d, posd, srcT, srct, is_k) in ((ckw1, ckw2, posk, kcT, "kcT", True), (cvw1, cvw2, posv, vcT, "vcT", False)):
                        convert(s2, lambda kt, c0, c1: w1b[:].rearrange("p a b -> p (a b)")[:, c0:c1], lambda kt, c0, c1: w1d[l].rearrange("p a b -> p (a b)")[:, c0:c1], 1, 32 * 256, stg, chunk=2048, rows=64, toks=[w1bt])
                        P.dma("sp", w2s[:], w2d[l].rearrange("p a b -> p (a b)"), T["w"], w2st, w2st)
                        P.op("dve", lambda E: E.tensor_copy(out=w2b[:].rearrange("p a b -> p (a b)"), in_=w2s[:]), r=[w2st], w=[w2bt])
                        P.dma("sp", pss[:], posd[l].rearrange("p a b -> p (a b)"), T["w"], psst, psst)
                        P.op("dve", lambda E: E.tensor_copy(out=posb[:].rearrange("p a b -> p (a b)"), in_=pss[:]), r=[psst], w=[posbt])
                        for g in range(2):
                            P.dma("sp", kg[g][0][:], srcT[g], T[srct], kg[g][1], kg[g][1])
                        for ht in range(2):
                            hs = slice(ht * 128, (ht + 1) * 128)
                            for p in range(32):
                                for g in range(2):
                                    P.op("pe", lambda E, g=g, p=p, hs=hs: E.matmul(psg[g][0][:, 0:255], lhsT=w1b[:, p, hs], rhs=kg[g][0][:, p:p + 16 * 254 + 1:16], start=(p == 0), stop=(p == 31)),
                                         r=[w1bt, kg[g][1]], w=[psg[g][1]])
                                P.op("pe", lambda E, p=p, hs=hs: E.matmul(psb[:, 0:2], lhsT=w1b[:, p, hs], rhs=posb[:, p, :], start=(p == 0), stop=(p == 31)), r=[w1bt, posbt], w=[psbt])
                            P.op("dve", lambda E: E.tensor_copy(out=bia[:], in_=psb[:, 0:2]), r=[psbt], w=[biat])
                            for g in range(2):
                                P.op("act", lambda E, g=g, ht=ht: E.activation(out=hid[g][ht][0][:, 0:255], in_=psg[g][0][:, 0:255], func=AF.Gelu_apprx_tanh, bias=bia[:, 0:1]), r=[psg[g][1], biat], w=[hid[g][ht][1]])
                        for g in range(2):
                            if is_k:
                                for ht in range(2):
                                    P.op("pe", lambda E, g=g, ht=ht: E.matmul(pso[0:64, 0:255], lhsT=w2b[:, ht, :], rhs=hid[g][ht][0][:, 0:255], start=(ht == 0), stop=(ht == 1)), r=[w2bt, hid[g][ht][1]], w=[psot])
                                P.op("dve", lambda E, g=g: E.tensor_copy(out=kcm[g][0][:, 0:255], in_=pso[0:64, 0:255]), r=[psot], w=[kcm[g][1]])
                            else:
                                for ctile in range(2):
                                    n = 128 if ctile == 0 else 127
                                    for ht in range(2):
                                        P.op("pe", lambda E, g=g, ht=ht, ctile=ctile, n=n: E.matmul(pso[0:n, 256:320], lhsT=hid[g][ht][0][:, ctile * 128:ctile * 128 + n], rhs=w2b[:, ht, :], start=(ht == 0), stop=(ht == 1)), r=[w2bt, hid[g][ht][1]], w=[psot])
                                    P.op("dve", lambda E, g=g, ctile=ctile, n=n: E.tensor_copy(out=vcm[g][0][0:n, ctile, :], in_=pso[0:n, 256:320]), r=[psot], w=[vcm[g][1]])
                KE = [sc.sbt("KE%d" % g, [128, S], BF16) for g in range(2)]
                kwn = [sc.sbt("kwn%d" % g, [64, S], BF16) for g in range(2)]
                vsl = [sc.sbt("vsl%d" % g, [128, 32, 65], BF16) for g in range(2)]
                vwn = [sc.sbt("vwn%d" % g, [128, 32, 65], BF16) for g in range(2)]
                for g in range(2):
                    P.dma("sp", KE[g][0][0:64, :], ksT[g], T["ksT"], KE[g][1], KE[g][1])
                    P.dma("sp", KE[g][0][64:128, :], c_E[:, :], T["w"], KE[g][1], KE[g][1])
                    P.dma("sp", kwn[g][0][:], kwT[g], T["kwT"], kwn[g][1], kwn[g][1])
                    for (vv, j) in ((vsl[g], g), (vwn[g], 2 + g)):
                        P.op("pool", lambda E, vv=vv: E.memset(vv[0][:, :, 64:65], 1.0), w=[vv[1]])
                        for k8 in range(8):
                            P.dma("sp", vv[0][:, k8 * 4:(k8 + 1) * 4, 0:64], vtm[k8 * 512:(k8 + 1) * 512, j, :].rearrange("(k p) d -> p k d", p=128), T["vtm"], vv[1], vv[1])
                QP = [[sc.sbt("QP%d_%d" % (i, h), [128, 512], BF16) for h in range(8)] for i in range(2)]
                gt = [sc.sbt("gt%d" % i, [128, 4, 24], F32) for i in range(2)]
                bst = [sc.pst("bst%d" % i, [128, 512], F32) for i in range(2)]
                b_os = [sc.pst("b_os%d" % i, [128, 512], F32) for i in range(2)]
                b_ow = [sc.pst("b_ow%d" % i, [128, 512], F32) for i in range(2)]
                b_xs, b_xst = sc.pst("b_xs", [128, 512], F32)
                b_tp = sc.ps("b_tp", [128, 1024], BF16)
                tp_t = sc.tok("tp", True)
                NCH = 4
                CH = []
                for c in range(NCH):
                    d = {}
                    d["ee"] = [sc.sbt("ee%d_%d" % (c, i), [128, 256], F32) for i in range(4)]
                    d["pb"] = [sc.sbt("pb%d_%d" % (c, i), [128, 256], BF16) for i in range(4)]
                    d["pTc"] = [sc.sbt("pTc%d_%d" % (c, i), [128, 128], BF16) for i in range(2)]
                    d["sm"] = sc.sbt("sm%d" % c, [128, 16], F32)
                    d["P4"] = sc.sbt("P4_%d" % c, [128, 264], F32)
                    d["imp"] = sc.sbt("imp%d" % c, [128, 64], F32)
                    d["scr"] = sc.sbt("scr%d" % c, [128, 64], F32)
                    d["scr2"] = sc.sbt("scr2_%d" % c, [128, 64], F32)
                    d["m8"] = sc.sbt("m8_%d" % c, [128, 16], F32)
                    d["penb"] = sc.sbt("penb%d" % c, [128, 128], BF16)
                    P.op("pool", lambda E, d=d: E.memset(d["penb"][0][:], 0.0), w=[d["penb"][1]])
                    P.op("dve", lambda E, d=d: E.memset(d["P4"][0][:], 0.0), w=[d["P4"][1]])
                    CH.append(d)
                pT = [sc.sbt("pT%d" % i, [128, 512], BF16) for i in range(4)]
                sy, syt = sc.sbt("sy", [128, 8], F32)
                att2 = [sc.sbt("att%d" % i, [128, 4, 512], F32) for i in range(2)]
                attb, attbt = sc.sbt("attb", [128, 4, 512], BF16)
                ast = [sc.sbt("ast%d" % i, [128, 4, 512], BF16) for i in range(2)]
                sidx = [0]
                NSG = ATT_DBG["nqb"] // 4

                def gen_X(sg):
                    ss = slice(sg * 512, (sg + 1) * 512)
                    qp = QP[sg % 2]
                    g_, gt_ = gt[sg % 2]
                    for h in range(8):
                        P.dma("sp", qp[h][0][0:64, :], qT[h, :, ss], T["qT"], qp[h][1], qp[h][1])
                    P.dma("sp", g_[:], gat[ss, :].rearrange("(j p) c -> p j c", p=128), T["gat"], gt_, gt_)
                    yield
                    for g in range(2):
                        act = [gen_Xchain(sg, g, j) for j in range(4)]
                        while act:
                            for gn in list(act):
                                try:
                                    next(gn)
                                    yield
                                except StopIteration:
                                    act.remove(gn)

                def gen_Xchain(sg, g, j):
                    qp = QP[sg % 2]
                    g_, gt_ = gt[sg % 2]
                    att, attt = att2[sg % 2]
                    d = CH[j % NCH]
                    ee, pb, pTc = d["ee"], d["pb"], d["pTc"]
                    sm, smt = d["sm"]
                    P4, P4t = d["P4"]
                    imp, impt = d["imp"]
                    scr, scrt = d["scr"]
                    scr2, scr2t = d["scr2"]
                    m8, m8t = d["m8"]
                    penb, penbt = d["penb"]
                    P4v = P4[:, 0:256].rearrange("p (j f) -> p j f", f=4)
                    P4w = P4[:, 4:260].rearrange("p (j f) -> p j f", f=4)
                    if True:
                        if True:
                            qb = sg * 4 + j
                            js = slice(j * 128, (j + 1) * 128)
                            Nc = min(255, 8 * qb + 7)
                            cb0 = max(0, 8 * qb - 2)
                            cb1 = min(Nc, 8 * qb + 7)
                            ps_s = b_xs[:, 0:256]
                            for hh in range(4):
                                h = g * 4 + hh
                                P.op("pe", lambda E, h=h, g=g, js=js, Nc=Nc: E.matmul(ps_s[:, 0:Nc], lhsT=qp[h][0][0:64, js], rhs=kcm[g][0][:, 0:Nc], start=True, stop=False), r=[qp[h][1], kcm[g][1]], w=[b_xst])
                                P.op("pe", lambda E, cb0=cb0, cb1=cb1, qb=qb: E.matmul(ps_s[:, cb0:cb1], lhsT=ident[:], rhs=band[:, cb0 - (8 * qb - 2):cb1 - (8 * qb - 2)], start=False, stop=True), r=[t_const, cst], w=[b_xst])
                                e_, et_ = ee[hh]
                                P.op("act", lambda E, e_=e_, hh=hh, Nc=Nc: E.activation(out=e_[:, 0:Nc], in_=ps_s[:, 0:Nc], func=AF.Exp, accum_out=sm[:, 8 + hh:9 + hh]), r=[b_xst], w=[et_, smt])
                                yield
                            P.op("dve", lambda E: E.tensor_scalar(out=sm[:, 12:16], in0=sm[:, 8:12], scalar1=1e-20, scalar2=None, op0=ALU.max), r=[smt], w=[smt])
                            P.op("dve", lambda E: E.reciprocal(out=sm[:, 12:16], in_=sm[:, 12:16]), r=[smt], w=[smt])
                            P.op("dve", lambda E: E.tensor_tensor(out=sm[:, 0:4], in0=sm[:, 12:16], in1=g_[:, j, g * 12:(g + 1) * 12].rearrange("p (h c) -> p h c", c=3)[:, :, 0], op=ALU.mult), r=[smt, gt_], w=[smt])
                            for hh in range(4):
                                e_, et_ = ee[hh]
                                if hh == 0:
                                    P.op("dve", lambda E, e_=e_, hh=hh, Nc=Nc: E.tensor_scalar(out=P4[:, 4:4 + Nc], in0=e_[:, 0:Nc], scalar1=sm[:, 12 + hh:13 + hh], scalar2=None, op0=ALU.mult), r=[et_, smt], w=[P4t])
                                else:
                                    P.op("dve", lambda E, e_=e_, hh=hh, Nc=Nc: E.scalar_tensor_tensor(out=P4[:, 4:4 + Nc], in0=e_[:, 0:Nc], scalar=sm[:, 12 + hh:13 + hh], in1=P4[:, 4:4 + Nc], op0=ALU.mult, op1=ALU.add), r=[et_, smt], w=[P4t])
                            yield
                            P.op("dve", lambda E: E.tensor_tensor(out=imp[:], in0=P4v[:, :, 1], in1=P4v[:, :, 2], op=ALU.add), r=[P4t], w=[impt])
                            P.op("dve", lambda E: E.tensor_tensor(out=imp[:], in0=imp[:], in1=P4v[:, :, 3], op=ALU.add), r=[P4t], w=[impt])
                            P.op("dve", lambda E: E.scalar_tensor_tensor(out=imp[:], in0=imp[:], scalar=2.0, in1=P4v[:, :, 0], op0=ALU.mult, op1=ALU.add), r=[P4t], w=[impt])
                            P.op("dve", lambda E: E.tensor_tensor(out=imp[:], in0=imp[:], in1=P4w[:, :, 0], op=ALU.add), r=[P4t], w=[impt])
                            P.op("dve", lambda E, qb=qb: E.tensor_tensor(out=scr[:], in0=imp[:], in1=cA[:, 64 - 2 * qb:128 - 2 * qb], op=ALU.mult), r=[impt, cst], w=[scrt])
                            P.op("dve", lambda E, qb=qb: E.tensor_tensor(out=scr[:], in0=scr[:], in1=cB[:, 64 - 2 * qb:128 - 2 * qb], op=ALU.add), r=[cst], w=[scrt])
                            P.op("dve", lambda E: E.memset(scr[:, 0:1], 1e4), w=[scrt])
                            yield
                            nb = min(64, 2 * qb + 2)
                            if nb > 16:
                                P.op("dve", lambda E: E.max(out=m8[:, 0:8], in_=scr[:]), r=[scrt], w=[m8t])
                                P.op("dve", lambda E: E.match_replace(out=scr2[:], in_to_replace=m8[:, 0:8], in_values=scr[:], imm_value=-3e38), r=[scrt, m8t], w=[scr2t])
                                P.op("dve", lambda E: E.max(out=m8[:, 8:16], in_=scr2[:]), r=[scr2t], w=[m8t])
                                P.op("dve", lambda E, nb=nb: E.tensor_scalar(out=scr2[:, 0:nb], in0=scr[:, 0:nb], scalar1=m8[:, 15:16], scalar2=None, op0=ALU.is_ge), r=[scrt, m8t], w=[scr2t])
                                P.op("dve", lambda E, nb=nb: E.tensor_scalar(out=penb[:, 64:64 + nb], in0=scr2[:, 0:nb], scalar1=-1.0, scalar2=-NEGM, op0=ALU.add, op1=ALU.mult), r=[scr2t], w=[penbt])
                                yield
                            ptr = b_tp[:, 256:384]
                            P.op("pe", lambda E, ptr=ptr: E.transpose(out=ptr, in_=penb[:], identity=ident[:]), r=[penbt, t_const], w=[tp_t])
                            h0 = g * 4
                            P.op("act", lambda E, ptr=ptr, js=js, h0=h0: E.copy(out=qp[h0][0][64:128, js], in_=ptr[64:128, :]), r=[tp_t], w=[qp[h0][1]])
                            for hh in range(1, 4):
                                P.op("pool", lambda E, js=js, h0=h0, hh=hh: E.tensor_copy(out=qp[h0 + hh][0][64:128, js], in_=qp[h0][0][64:128, js]), r=[qp[h0][1]], w=[qp[h0 + hh][1]])
                            yield
                            k2 = 0
                            tpf = b_tp[:].bitcast(F32)
                            for hh in range(4):
                                h = g * 4 + hh
                                e_, et_ = ee[hh]
                                nct = (Nc + 127) // 128
                                for ctile in range(nct):
                                    n = min(128, Nc - ctile * 128)
                                    tpc = tpf[:, (k2 % 2) * 128:(k2 % 2) * 128 + 128]
                                    pc_, pct_ = pTc[k2 % 2]
                                    k2 += 1
                                    P.op("pe", lambda E, tpc=tpc, e_=e_, ctile=ctile, n=n: E.transpose(out=tpc[0:n, :], in_=e_[:, ctile * 128:ctile * 128 + n], identity=identf[:]), r=[et_, t_const], w=[tp_t])
                                    P.op("act", lambda E, pc_=pc_, tpc=tpc, n=n: E.copy(out=pc_[0:n, :], in_=tpc[0:n, :]), r=[tp_t], w=[pct_])
                                    P.op("pe", lambda E, pc_=pc_, n=n, ctile=ctile, nct=nct: E.matmul(b_xs[:, 256:320], lhsT=pc_[0:n, :], rhs=vcm[g][0][0:n, ctile, :], start=(ctile == 0), stop=(ctile == nct - 1), skip_group_check=True), r=[pct_, vcm[g][1]], w=[b_xst])
                                P.op("dve", lambda E, h=h, hh=hh: E.tensor_scalar(out=att[:, j, h * 64:(h + 1) * 64], in0=b_xs[:, 256:320], scalar1=sm[:, hh:hh + 1], scalar2=None, op0=ALU.mult), r=[b_xst, smt], w=[attt])
                                yield

                def gen_Y(sg):
                    ss = slice(sg * 512, (sg + 1) * 512)
                    qp = QP[sg % 2]
                    g_, gt_ = gt[sg % 2]
                    att, attt = att2[sg % 2]
                    pending = []
                    for g in range(2):
                        for hh in range(4):
                            h = g * 4 + hh
                            q_, qt_ = qp[h]
                            os_, ost_ = b_os[hh % 2]
                            ow_, owt_ = b_ow[hh % 2]
                            steps = []
                            for kt in range(0, 4 * sg + 4):
                                steps.append(("s", kt))
                            for kt in range(max(0, 4 * sg - 4), 4 * sg + 4):
                                steps.append(("w", kt))
                            first = {"s": True, "w": True}
                            LA = 1
                            ring = []
                            for i in range(len(steps) + LA):
                                if i == min(3, len(steps) - 1) and pending:
                                    pending.pop(0)()
                                if i < len(steps):
                                    br, kt = steps[i]
                                    r_ = kt - 4 * sg
                                    if br == "s":
                                        jlo, jhi = max(r_, 0), 3
                                    else:
                                        jlo, jhi = max(r_, 0), min(r_ + 4, 3)
                                    c0, c1 = jlo * 128, (jhi + 1) * 128
                                    si = sidx[0] % 4
                                    stp = bst[sidx[0] % 2][0]
                                    stt_ = bst[sidx[0] % 2][1]
                                    sidx[0] += 1
                                    ks_ = slice(kt * 128, (kt + 1) * 128)
                                    ex = []
                                    if r_ >= 0:
                                        ex.append((r_, tric))
                                    if br == "w" and 0 <= r_ + 4 <= 3:
                                        ex.append((r_ + 4, triw))
                                    if br == "s":
                                        P.op("pe", lambda E, stp=stp, ks_=ks_, q_=q_, g=g, c0=c0, c1=c1, ex=ex: E.matmul(stp[:, c0:c1], lhsT=KE[g][0][:, ks_], rhs=q_[:, c0:c1], start=True, stop=(len(ex) == 0)), r=[KE[g][1], qt_], w=[stt_])
                                    else:
                                        P.op("pe", lambda E, stp=stp, ks_=ks_, q_=q_, g=g, c0=c0, c1=c1, ex=ex: E.matmul(stp[:, c0:c1], lhsT=kwn[g][0][:, ks_], rhs=q_[0:64, c0:c1], start=True, stop=(len(ex) == 0)), r=[kwn[g][1], qt_], w=[stt_])
                                    for xi, (jj, tri_) in enumerate(ex):
                                        P.op("pe", lambda E, stp=stp, jj=jj, tri_=tri_, xi=xi, ex=ex: E.matmul(stp[:, jj * 128:(jj + 1) * 128], lhsT=ident[:], rhs=tri_[:], start=False, stop=(xi == len(ex) - 1)), r=[t_const, cst], w=[stt_])
                                    pt_s, pt_st = pT[si]
                                    P.op("act", lambda E, pt_s=pt_s, stp=stp, c0=c0, c1=c1: E.activation(out=pt_s[:, c0:c1], in_=stp[:, c0:c1], func=AF.Exp), r=[stt_], w=[pt_st])
                                    ring.append((br, kt, jlo, jhi, pt_s, pt_st))
                                if i - LA >= 0:
                                    br, kt, jlo, jhi, pt_s, pt_st = ring[i - LA]
                                    ob, obt = (os_, ost_) if br == "s" else (ow_, owt_)
                                    V = vsl[g] if br == "s" else vwn[g]
                                    for jj in range(jlo, jhi + 1):
                                        st_flag = first[br]
                                        first[br] = False
                                        P.op("pe", lambda E, ob=ob, jj=jj, pt_s=pt_s, V=V, kt=kt, st_flag=st_flag: E.matmul(ob[:, jj * 65:jj * 65 + 65], lhsT=pt_s[:, jj * 128:(jj + 1) * 128], rhs=V[0][:, kt, :], start=st_flag, stop=True, skip_group_check=True), r=[pt_st, V[1]], w=[obt])
                                yield
                            def combine(h=h, os_=os_, ost_=ost_, ow_=ow_, owt_=owt_):
                                osv = os_[:, 0:260].rearrange("p (j c) -> p j c", c=65)
                                owv = ow_[:, 0:260].rearrange("p (j c) -> p j c", c=65)
                                P.op("dve", lambda E: E.reciprocal(out=sy[:, 0:4], in_=osv[:, :, 64]), r=[ost_], w=[syt])
                                P.op("dve", lambda E: E.tensor_tensor(out=sy[:, 0:4], in0=sy[:, 0:4], in1=g_[:, :, h * 3 + 1], op=ALU.mult), r=[gt_], w=[syt])
                                P.op("dve", lambda E: E.reciprocal(out=sy[:, 4:8], in_=owv[:, :, 64]), r=[owt_], w=[syt])
                                P.op("dve", lambda E: E.tensor_tensor(out=sy[:, 4:8], in0=sy[:, 4:8], in1=g_[:, :, h * 3 + 2], op=ALU.mult), r=[gt_], w=[syt])
                                cs_ = slice(h * 64, (h + 1) * 64)
                                for jj in range(4):
                                    P.op("dve", lambda E, jj=jj: E.scalar_tensor_tensor(out=att[:, jj, cs_], in0=os_[:, jj * 65:jj * 65 + 64], scalar=sy[:, jj:jj + 1], in1=att[:, jj, cs_], op0=ALU.mult, op1=ALU.add), r=[ost_, syt], w=[attt])
                                    P.op("dve", lambda E, jj=jj: E.scalar_tensor_tensor(out=attb[:, jj, cs_], in0=ow_[:, jj * 65:jj * 65 + 64], scalar=sy[:, 4 + jj:5 + jj], in1=att[:, jj, cs_], op0=ALU.mult, op1=ALU.add), r=[owt_, syt, attt], w=[attbt])
                            pending.append(combine)
                            yield
                    while pending:
                        pending.pop(0)()
                    yield
                    a_, at_ = ast[sg % 2]
                    atr = b_tp[:, 384:896].rearrange("p (a b) -> p a b", b=128)
                    for jj in range(4):
                        for ft in range(4):
                            P.op("pe", lambda E, ft=ft, jj=jj, atr=atr: E.transpose(out=atr[:, ft, :], in_=attb[:, jj, ft * 128:(ft + 1) * 128], identity=ident[:]), r=[attbt, t_const], w=[tp_t])
                        P.op("act", lambda E, a_=a_, atr=atr, jj=jj: E.copy(out=a_[:, :, jj * 128:(jj + 1) * 128], in_=atr), r=[tp_t], w=[at_])
                        yield
                    P.dma("pool", attnT[:, ss].rearrange("(a p) t -> p a t", p=128), a_[:], at_, T["attnT"], at_)
                    yield

                def drain(gn):
                    for _ in gn:
                        pass

                if NSG > 0:
                    drain(gen_X(0))
                for sg in range(NSG):
                    gy = gen_Y(sg) if not ATT_DBG.get("skip_sw") else iter(())
                    gx = gen_X(sg + 1) if sg + 1 < NSG else iter(())
                    ratio = max(1, int(round((32 * sg + 110) / 100.0)))
                    x_done = False
                    y_done = False
                    while not y_done:
                        for _ in range(ratio):
                            try:
                                next(gy)
                            except StopIteration:
                                y_done = True
                                break
                        if not x_done:
                            try:
                                next(gx)
                            except StopIteration:
                                x_done = True
                    if not x_done:
                        drain(gx)

        PH = {"inproj": phase_inproj, "mlp": phase_mlp, "rnn": phase_rnn, "merge": phase_merge, "attn": phase_attn}
        build.phases = PH
        build.ctx = dict(P=P, T=T, nc=nc, xs=xs, x_in=x_in, out_d=out_d)
        plan = build.plan
        plan(PH, build.ctx, locals())
        P.barrier()
        print("ops", P.nops, "waits", P.nwait)
    return nc, dbg


def default_plan(PH, ctx, L):
    T = ctx["T"]
    xs = ctx["xs"]
    cur, curt = ctx["x_in"], T["x_in"]
    for l in range(2):
        PH["inproj"](l, cur, curt)
        PH["attn"](l)
        PH["rnn"](l)
        PH["merge"](l, cur, curt, xs[0], T["xs0"])
        PH["mlp"](l, xs[0], T["xs0"], xs[1], T["xs1"], l == 1)
        cur, curt = xs[1], T["xs1"]


build.plan = default_plan


def host_inputs(inp, b):
    bf = ml_dtypes.bfloat16
    f = np.float32

    def pk(v):
        return np.ascontiguousarray(v.reshape(2, 8, 128).transpose(0, 2, 1)).astype(f)

    def bd(wm):
        o = np.zeros((2, 8, 128, 128), f)
        for c in range(8):
            o[:, c, 0:64, 0:64] = wm[:, 2 * c]
            o[:, c, 64:128, 64:128] = wm[:, 2 * c + 1]
        return o

    i_ = np.arange(128)
    m = {
        "x": np.ascontiguousarray(inp["x"][b]),
        "w_in": inp["w_in"],
        "n1w": pk(inp["norm1_w"]), "n2w": pk(inp["norm2_w"]),
        "fnw": np.ascontiguousarray(np.broadcast_to(inp["final_norm_w"][None, :], (128, D))).astype(f),
        "posk": np.ascontiguousarray(np.repeat(inp["cmp_pos_k"].transpose(0, 2, 1)[..., None], 2, axis=-1)),
        "posv": np.ascontiguousarray(np.repeat(inp["cmp_pos_v"].transpose(0, 2, 1)[..., None], 2, axis=-1)),
        "ckw1": np.ascontiguousarray(inp["cmp_k_w1"].reshape(2, 32, 64, 256).transpose(0, 2, 1, 3)),
        "cvw1": np.ascontiguousarray(inp["cmp_v_w1"].reshape(2, 32, 64, 256).transpose(0, 2, 1, 3)),
        "ckw2": np.ascontiguousarray(inp["cmp_k_w2"].reshape(2, 2, 128, 64).transpose(0, 2, 1, 3)),
        "cvw2": np.ascontiguousarray(inp["cmp_v_w2"].reshape(2, 2, 128, 64).transpose(0, 2, 1, 3)),
        "convw": np.ascontiguousarray(inp["conv_w"].reshape(2, 4, 8, 128).transpose(0, 3, 2, 1)),
        "convb": pk(inp["conv_b"]), "lba": pk(inp["lru_b_a"]), "lbi": pk(inp["lru_b_i"]), "llam": pk(inp["lru_lambda"]),
        "lwa": bd(inp["lru_w_a"]), "lwi": bd(inp["lru_w_i"]),
        "wua": inp["w_up_attn"], "wur": inp["w_up_rnn"], "wo": inp["w_out"], "w1": inp["mlp_w1"], "w2": inp["mlp_w2"],
        "c_ident": np.eye(128, dtype=f).astype(bf),
        "c_tric": np.where(i_[:, None] <= i_[None, :], 0.0, NEGM).astype(bf),
        "c_triw": np.where(i_[:, None] > i_[None, :], 0.0, NEGM).astype(bf),
        "c_E": (np.arange(S)[None, :] // 64 == np.arange(64)[:, None]).astype(f).astype(bf),
        "c_band": np.where((np.arange(9)[None, :] - 2) <= ((i_[:, None] + 1) // 16 - 2), 0.0, NEGM).astype(bf),
    }
    hi = (i_ >= 64).astype(np.int64)[:, None]
    jp = (np.arange(128) - 64)[None, :]
    valid = jp <= hi
    forced = jp > hi - 2
    A = np.where(valid & ~forced, 1.0, 0.0)
    Bm = np.where(valid, np.where(forced, 1e4, 0.0), -1e30)
    m["c_A"] = A.astype(f)
    m["c_B"] = Bm.astype(f)
    return {k: np.ascontiguousarray(v) for k, v in m.items()}


def kernel(**inputs):
    inp = {k: np.asarray(v) for k, v in inputs.items()}
    nc, _ = build(False)
    in_maps = [host_inputs(inp, c % 4) for c in range(8)]
    res = run_bass_kernel_spmd(nc, in_maps, core_ids=list(range(8)))
    return np.stack([np.asarray(res.results[c]["out"]) for c in range(4)], axis=0).astype(np.float32)
```

```python
import numpy as np
import ml_dtypes
from contextlib import ExitStack
import concourse.bass as bass
import concourse.mybir as mybir
from concourse.bass_utils import run_bass_kernel_spmd

F32 = mybir.dt.float32
BF16 = mybir.dt.bfloat16
AF = mybir.ActivationFunctionType
ALU = mybir.AluOpType
AX = mybir.AxisListType

S = 4096
D = 1024
DIN = 5400
NT = S // 128
NEGM = -30000.0
EPS = 1e-6
O_Q, O_KC, O_VC, O_KS, O_VS, O_KW, O_VW, O_GN, O_XR, O_GR, O_GA, O_GB = 0, 512, 640, 768, 896, 1024, 1152, 1280, 1304, 2328, 3352, 4376


ATT_DBG = {"level": 6, "nqb": NT}


class Tok:
    __slots__ = ("w", "r", "sem", "name", "x")

    def __init__(self, name="", x=False):
        self.w = {}
        self.r = {}
        self.sem = None
        self.name = name
        self.x = x


class Prog:
    ENG = ("pe", "act", "dve", "pool", "sp")

    def __init__(self, nc, es, n_dma_sems=80):
        self.nc = nc
        self.eng = {"pe": nc.tensor, "act": nc.scalar, "dve": nc.vector, "pool": nc.gpsimd, "sp": nc.sync}
        self.sems = []
        self.esem = {}
        for e in self.ENG:
            self.esem[e] = len(self.sems)
            self.sems.append(es.enter_context(nc.semaphore("es_" + e)))
        self.dma_ids = []
        for i in range(n_dma_sems):
            self.dma_ids.append(len(self.sems))
            self.sems.append(es.enter_context(nc.semaphore("ds_%d" % i)))
        self.free = list(self.dma_ids)
        self.total = [0] * len(self.sems)
        self.known = {e: [0] * len(self.sems) for e in self.ENG}
        self.nwait = 0
        self.nops = 0

    def _wait(self, eng, deps):
        E = self.eng[eng]
        kn = self.known[eng]
        for s, v in deps.items():
            if s >= 5:
                v = self.total[s]
            if kn[s] < v:
                kn[s] = v
                E.wait_ge(self.sems[s], v)
                self.nwait += 1

    @staticmethod
    def _merge(d, src):
        for s, v in src.items():
            if d.get(s, 0) < v:
                d[s] = v

    def op(self, eng, fn, r=(), w=()):
        deps = {}
        rx = [b for b in r if b.x]
        if rx:
            r = [b for b in r if not b.x]
            w = list(w) + rx
        for b in r:
            self._merge(deps, b.w)
        for b in w:
            self._merge(deps, b.w)
            self._merge(deps, b.r)
        s = self.esem[eng]
        if eng == "pe":
            deps.pop(s, None)
        self._wait(eng, deps)
        self.total[s] += 1
        n = self.total[s]
        fn(self.eng[eng]).then_inc(self.sems[s], 1)
        self.nops += 1
        for b in r:
            b.r[s] = n
        for b in w:
            b.w[s] = n

    def dma(self, eng, out, in_, src, dst, owner):
        deps = {}
        self._merge(deps, src.w)
        self._merge(deps, dst.w)
        self._merge(deps, dst.r)
        self._wait(eng, deps)
        if owner.sem is None:
            owner.sem = self.free.pop()
        s = owner.sem
        self.total[s] += 16
        v = self.total[s]
        self.eng[eng].dma_start(out=out, in_=in_).then_inc(self.sems[s], 16)
        self.nops += 1
        src.r[s] = v
        dst.w[s] = v

    def release(self, toks):
        for t in toks:
            if t.sem is not None:
                self.free.append(t.sem)
                t.sem = None

    def barrier(self):
        for e in self.ENG:
            E = self.eng[e]
            kn = self.known[e]
            for s in range(len(self.sems)):
                if s == self.esem[e]:
                    continue
                v = self.total[s]
                if kn[s] < v:
                    kn[s] = v
                    E.wait_ge(self.sems[s], v)
        arr = {}
        for e in self.ENG:
            s = self.esem[e]
            self.total[s] += 1
            arr[e] = self.total[s]
            if e == "pe":
                self.eng[e].nop().then_inc(self.sems[s], 1) if hasattr(self.eng[e], "nop") else None
            else:
                self.eng[e].nop().then_inc(self.sems[s], 1)
        for e in self.ENG:
            for f in self.ENG:
                if f == e:
                    continue
                s = self.esem[f]
                self.known[e][s] = arr[f]
                self.eng[e].wait_ge(self.sems[s], arr[f])


class Scope:
    def __init__(self, P):
        self.P = P
        self.es = ExitStack()
        self.toks = []

    def __enter__(self):
        self.es.__enter__()
        return self

    def __exit__(self, *a):
        self.P.barrier()
        self.P.release(self.toks)
        return self.es.__exit__(*a)

    uid = [0]

    def sb(self, name, shape, dt):
        Scope.uid[0] += 1
        return self.es.enter_context(self.P.nc.sbuf_tensor("%s_%d" % (name, Scope.uid[0]), list(shape), dt))

    def ps(self, name, shape, dt):
        Scope.uid[0] += 1
        return self.es.enter_context(self.P.nc.psum_tensor("%s_%d" % (name, Scope.uid[0]), list(shape), dt))

    def tok(self, name="", x=False):
        t = Tok(name, x)
        self.toks.append(t)
        return t

    def sbt(self, name, shape, dt):
        return self.sb(name, shape, dt), self.tok(name)

    def pst(self, name, shape, dt):
        return self.ps(name, shape, dt), self.tok(name, True)


def build(debug=False):
    nc = bass.Bass("TRN2", target_bir_lowering=False)
    dbg = {}

    def din(name, shape, dt=F32):
        return nc.dram_tensor(name, list(shape), dt, kind="ExternalInput").ap()

    def dscr(name, shape, dt):
        isd = bool(debug) and (debug is True or name in debug)
        kind = "ExternalOutput" if isd else "Internal"
        t = nc.dram_tensor(name, list(shape), dt, kind=kind).ap()
        if isd:
            dbg[name] = t
        return t

    x_in = din("x", [S, D])
    out_d = nc.dram_tensor("out", [S, D], F32, kind="ExternalOutput").ap()
    w_in = din("w_in", [2, D, DIN])
    n1w = din("n1w", [2, 128, 8])
    n2w = din("n2w", [2, 128, 8])
    fnw = din("fnw", [128, D])
    posk = din("posk", [2, 64, 32, 2])
    posv = din("posv", [2, 64, 32, 2])
    ckw1 = din("ckw1", [2, 64, 32, 256])
    cvw1 = din("cvw1", [2, 64, 32, 256])
    ckw2 = din("ckw2", [2, 128, 2, 64])
    cvw2 = din("cvw2", [2, 128, 2, 64])
    convw = din("convw", [2, 128, 8, 4])
    convb = din("convb", [2, 128, 8])
    lba = din("lba", [2, 128, 8])
    lbi = din("lbi", [2, 128, 8])
    llam = din("llam", [2, 128, 8])
    lwa = din("lwa", [2, 8, 128, 128])
    lwi = din("lwi", [2, 8, 128, 128])
    wua = din("wua", [2, 512, D])
    wur = din("wur", [2, D, D])
    wo = din("wo", [2, D, D])
    w1 = din("w1", [2, D, 4096])
    w2 = din("w2", [2, 4096, D])
    c_ident = din("c_ident", [128, 128], BF16)
    c_tric = din("c_tric", [128, 128], BF16)
    c_triw = din("c_triw", [128, 128], BF16)
    c_E = din("c_E", [64, S], BF16)
    c_band = din("c_band", [128, 9], BF16)
    c_A = din("c_A", [128, 128])
    c_B = din("c_B", [128, 128])

    xs = [dscr("xs0", [S, D], F32), dscr("xs1", [S, D], F32)]
    qT = dscr("qT", [8, 64, S], BF16)
    kcT = dscr("kcT", [2, 64, S], BF16)
    vcT = dscr("vcT", [2, 64, S], BF16)
    ksT = dscr("ksT", [2, 64, S], BF16)
    kwT = dscr("kwT", [2, 64, S], BF16)
    vtm = dscr("vtm", [S, 4, 64], BF16)
    gat = dscr("gat", [S, 24], F32)
    zf = dscr("zf", [4, D, S], F32)
    attnT = dscr("attnT", [512, S], BF16)
    rnnT = dscr("rnnT", [D, S], BF16)

    with ExitStack() as es:
        P = Prog(nc, es)
        T = {n: Tok(n) for n in ["x_in", "out", "w", "xs0", "xs1", "qT", "kcT", "vcT", "ksT", "kwT", "vtm", "gat", "zf", "attnT", "rnnT"]}
        for t in T.values():
            t.sem = None

        ident = es.enter_context(nc.sbuf_tensor("ident", [128, 128], BF16))
        identf = es.enter_context(nc.sbuf_tensor("identf", [128, 128], F32))
        t_const = Tok("const")
        P.dma("sp", ident[:], c_ident[:, :], T["w"], t_const, t_const)
        P.op("dve", lambda E: E.tensor_copy(out=identf[:], in_=ident[:]), r=[t_const], w=[t_const])

        def convert(sc, dst_ap_fn, src_ap_fn, nrow_tiles, ncols, stg, scale_ap_fn=None, chunk=2048, rows=128, toks=None):
            i = 0
            engs = ("act", "dve", "pool")
            for kt in range(nrow_tiles):
                for c0 in range(0, ncols, chunk):
                    c1 = min(ncols, c0 + chunk)
                    st, stt = stg[i % len(stg)]
                    P.dma("sp", st[0:rows, 0:c1 - c0], src_ap_fn(kt, c0, c1), T["w"], stt, stt)
                    e = engs[i % 3]
                    tk = toks[kt] if toks is not None else None
                    dst = dst_ap_fn(kt, c0, c1)
                    src = st[0:rows, 0:c1 - c0]
                    if scale_ap_fn is None:
                        if e == "act":
                            P.op(e, lambda E, dst=dst, src=src: E.copy(out=dst, in_=src), r=[stt], w=[tk])
                        else:
                            P.op(e, lambda E, dst=dst, src=src: E.tensor_copy(out=dst, in_=src), r=[stt], w=[tk])
                    else:
                        sc_ap, sc_tok = scale_ap_fn(kt)
                        if e == "act":
                            P.op(e, lambda E, dst=dst, src=src, sc_ap=sc_ap: E.activation(out=dst, in_=src, func=AF.Copy, scale=sc_ap), r=[stt, sc_tok], w=[tk])
                        else:
                            P.op(e, lambda E, dst=dst, src=src, sc_ap=sc_ap: E.tensor_scalar(out=dst, in0=src, scalar1=sc_ap, scalar2=None, op0=ALU.mult), r=[stt, sc_tok], w=[tk])
                    i += 1

        def rms_rstd(sc, xt, xtok, junk, junktok, ssq, rstd, sstok):
            P.op("act", lambda E: E.activation(out=junk[:], in_=xt[:], func=AF.Square, accum_out=ssq[:, 0:1]), r=[xtok], w=[junktok, sstok])
            P.op("act", lambda E: E.activation(out=ssq[:, 1:2], in_=ssq[:, 0:1], func=AF.Sqrt, scale=1.0 / D, bias=epsb[:, 0:1]), r=[sstok, t_const], w=[sstok])
            P.op("dve", lambda E: E.reciprocal(out=rstd[:, 0:1], in_=ssq[:, 1:2]), r=[sstok], w=[sstok])

        epsb = es.enter_context(nc.sbuf_tensor("epsb", [128, 4], F32))
        P.op("dve", lambda E: E.memset(epsb[:, 0:1], EPS), w=[t_const])
        P.op("dve", lambda E: E.memset(epsb[:, 1:2], 1.0), w=[t_const])
        P.op("dve", lambda E: E.memset(epsb[:, 2:3], 0.0), w=[t_const])

        def phase_inproj(l, xsrc, xtok):
            with Scope(P) as sc:
                wbf = sc.sb("wbf", [128, 8, DIN], BF16)
                wtok = [sc.tok("wbf%d" % k) for k in range(8)]
                n1 = sc.sb("n1", [128, 8], F32)
                n1t = sc.tok()
                P.dma("sp", n1[:], n1w[l], T["w"], n1t, n1t)
                stg = [sc.sbt("stg%d" % i, [128, 1800], F32) for i in range(3)]
                convert(sc, lambda kt, c0, c1: wbf[:, kt, c0:c1], lambda kt, c0, c1: w_in[l, kt * 128:(kt + 1) * 128, c0:c1], 8, DIN, stg,
                        scale_ap_fn=lambda kt: (n1[:, kt:kt + 1], n1t), chunk=1800, toks=wtok)
                xt = [sc.sbt("xt%d" % i, [128, D], F32) for i in range(2)]
                junk, junkt = sc.sbt("junk", [128, D], BF16)
                ssq = [sc.sbt("ssq%d" % i, [128, 4], F32) for i in range(2)]
                xn = [sc.sbt("xn%d" % i, [128, D], BF16) for i in range(2)]
                xnT = [sc.sbt("xnT%d" % i, [128, 8, 512], BF16) for i in range(2)]
                tp = [sc.pst("tp%d" % i, [128, 8, 128], BF16) for i in range(2)]
                pf = [sc.pst("pf%d" % i, [128, 512], F32) for i in range(4)]
                ptm = [sc.pst("ptm", [128, 512], F32)]
                of32 = [sc.sbt("of32_%d" % i, [128, 512], F32) for i in range(4)]
                obf = [sc.sbt("obf_%d" % i, [128, 512], BF16) for i in range(4)]
                vst = [sc.sbt("vst%d" % i, [128, 256], BF16) for i in range(2)]
                gst = [sc.sbt("gst%d" % i, [128, 24], F32) for i in range(2)]
                cnt = {"pf": 0, "f": 0, "b": 0}
                def gen_prep(sg):
                    xT, xTt = xnT[sg % 2]
                    for j in range(4):
                        tt = sg * 4 + j
                        x_t, x_tt = xt[tt % 2]
                        sq, sqt = ssq[tt % 2]
                        xb, xbt = xn[tt % 2]
                        tpp, tpt = tp[tt % 2]
                        P.dma("sp", x_t[:], xsrc[tt * 128:(tt + 1) * 128, :], xtok, x_tt, x_tt)
                        rms_rstd(sc, x_t, x_tt, junk, junkt, sq, sq[:, 2:3], sqt)
                        yield
                        P.op("dve", lambda E, xb=xb, x_t=x_t, sq=sq: E.tensor_scalar(out=xb[:], in0=x_t[:], scalar1=sq[:, 2:3], scalar2=None, op0=ALU.mult), r=[x_tt, sqt], w=[xbt])
                        for kt in range(8):
                            P.op("pe", lambda E, tpp=tpp, xb=xb, kt=kt: E.transpose(out=tpp[:, kt, :], in_=xb[:, kt * 128:(kt + 1) * 128], identity=ident[:]), r=[xbt, t_const], w=[tpt])
                        P.op("act", lambda E, xT=xT, tpp=tpp, j=j: E.copy(out=xT[:, :, j * 128:(j + 1) * 128], in_=tpp[:]), r=[tpt], w=[xTt])
                        yield
                        pt, ptt = ptm[0]
                        for (c0, n, o0) in ((O_VS, 128, 0), (O_VW, 128, 128), (O_GN, 24, 256)):
                            for kt in range(8):
                                P.op("pe", lambda E, pt=pt, xT=xT, kt=kt, c0=c0, n=n, o0=o0, j=j: E.matmul(pt[:, o0:o0 + n], lhsT=xT[:, kt, j * 128:(j + 1) * 128], rhs=wbf[:, kt, c0:c0 + n], start=(kt == 0), stop=(kt == 7)),
                                     r=[xTt, wtok[kt]], w=[ptt])
                        vs_, vst_ = vst[tt % 2]
                        gs_, gst_ = gst[tt % 2]
                        P.op("dve", lambda E, vs_=vs_, pt=pt: E.tensor_copy(out=vs_[:], in_=pt[:, 0:256]), r=[ptt], w=[vst_])
                        P.op("act", lambda E, gs_=gs_, pt=pt: E.activation(out=gs_[:], in_=pt[:, 256:280], func=AF.Sigmoid), r=[ptt], w=[gst_])
                        P.dma("pool", vtm[tt * 128:(tt + 1) * 128].rearrange("p a d -> p (a d)"), vs_[:], vst_, T["vtm"], vst_)
                        P.dma("pool", gat[tt * 128:(tt + 1) * 128, :], gs_[:], gst_, T["gat"], gst_)
                        yield

                def gen_main(sg):
                    xT, xTt = xnT[sg % 2]
                    tsl = slice(sg * 512, (sg + 1) * 512)
                    jobs = []
                    qTf = qT.rearrange("h d t -> (h d) t")
                    for h2 in range(4):
                        jobs.append((O_Q + h2 * 128, 128, "q", qTf[h2 * 128:(h2 + 1) * 128, tsl], "qT"))
                    jobs.append((O_KC, 128, "c", kcT.rearrange("g d t -> (g d) t")[:, tsl], "kcT"))
                    jobs.append((O_VC, 128, "c", vcT.rearrange("g d t -> (g d) t")[:, tsl], "vcT"))
                    jobs.append((O_KS, 128, "c", ksT.rearrange("g d t -> (g d) t")[:, tsl], "ksT"))
                    jobs.append((O_KW, 128, "c", kwT.rearrange("g d t -> (g d) t")[:, tsl], "kwT"))
                    for ft in range(8):
                        jobs.append((O_XR + ft * 128, 128, "f", zf[0, ft * 128:(ft + 1) * 128, tsl], "zf"))
                    for ft in range(8):
                        jobs.append((O_GR + ft * 128, 128, "gelu", zf[1, ft * 128:(ft + 1) * 128, tsl], "zf"))
                    for ft in range(8):
                        jobs.append((O_GA + ft * 128, 128, "sig", zf[2, ft * 128:(ft + 1) * 128, tsl], "zf"))
                    for ft in range(8):
                        jobs.append((O_GB + ft * 128, 128, "sig", zf[3, ft * 128:(ft + 1) * 128, tsl], "zf"))
                    for (c0, m, kind, dst, dtk) in jobs:
                        pp, ppt = pf[cnt["pf"] % 4]
                        cnt["pf"] += 1
                        for kt in range(8):
                            P.op("pe", lambda E, pp=pp, kt=kt, c0=c0, m=m, xT=xT: E.matmul(pp[0:m, :], lhsT=wbf[:, kt, c0:c0 + m], rhs=xT[:, kt, :], start=(kt == 0), stop=(kt == 7)),
                                 r=[xTt, wtok[kt]], w=[ppt])
                        if kind in ("q", "c"):
                            ob, obt = obf[cnt["b"] % 4]
                            cnt["b"] += 1
                            scl = 0.125 if kind == "q" else 1.0
                            P.op("dve", lambda E, ob=ob, pp=pp, m=m, scl=scl: E.tensor_scalar(out=ob[0:m, :], in0=pp[0:m, :], scalar1=scl, scalar2=None, op0=ALU.mult), r=[ppt], w=[obt])
                            P.dma("pool", dst, ob[0:m, :], obt, T[dtk], obt)
                            yield
                        else:
                            ob, obt = of32[cnt["f"] % 4]
                            cnt["f"] += 1
                            if kind == "f":
                                P.op("dve", lambda E, ob=ob, pp=pp: E.tensor_copy(out=ob[:], in_=pp[:]), r=[ppt], w=[obt])
                            else:
                                fn = AF.Gelu_apprx_tanh if kind == "gelu" else AF.Sigmoid
                                P.op("act", lambda E, ob=ob, pp=pp, fn=fn: E.activation(out=ob[:], in_=pp[:], func=fn), r=[ppt], w=[obt])
                            P.dma("pool", dst, ob[:], obt, T[dtk], obt)
                            yield


                NG = S // 512
                for _ in gen_prep(0):
                    pass
                for sg in range(NG):
                    gm = gen_main(sg)
                    gp = gen_prep(sg + 1) if sg + 1 < NG else iter(())
                    m_done = False
                    p_done = False
                    k = 0
                    while not m_done:
                        try:
                            next(gm)
                        except StopIteration:
                            m_done = True
                        k += 1
                        if not p_done and k % 3 == 0:
                            try:
                                next(gp)
                            except StopIteration:
                                p_done = True
                    if not p_done:
                        for _ in gp:
                            pass

        def phase_mlp(l, xsrc, xtok, xdst, xdtok, final):
            with Scope(P) as sc:
                w1b = sc.sb("w1b", [128, 8, 4096], BF16)
                w1t = [sc.tok() for _ in range(8)]
                w2b = sc.sb("w2b", [128, 32, D], BF16)
                w2t = [sc.tok() for _ in range(32)]
                n2 = sc.sb("n2", [128, 8], F32)
                n2t = sc.tok()
                P.dma("sp", n2[:], n2w[l], T["w"], n2t, n2t)
                stg = [sc.sbt("stg%d" % i, [128, 512], F32) for i in range(2)]
                convert(sc, lambda kt, c0, c1: w1b[:, kt, c0:c1], lambda kt, c0, c1: w1[l, kt * 128:(kt + 1) * 128, c0:c1], 8, 4096, stg,
                        scale_ap_fn=lambda kt: (n2[:, kt:kt + 1], n2t), chunk=512, toks=w1t)
                convert(sc, lambda kt, c0, c1: w2b[:, kt, c0:c1], lambda kt, c0, c1: w2[l, kt * 128:(kt + 1) * 128, c0:c1], 32, D, stg, chunk=512, toks=w2t)
                fw = None
                if final:
                    fw, fwt = sc.sbt("fw", [128, D], F32)
                    P.dma("sp", fw[:], fnw[:, :], T["w"], fwt, fwt)
                GT = 2
                GW = GT * 128
                xt = [sc.sbt("xt%d" % i, [128, D], F32) for i in range(4)]
                ssq = [sc.sbt("ssq%d" % i, [128, 4], F32) for i in range(4)]
                xn, xnt = sc.sbt("xn", [128, D], BF16)
                xnT = [sc.sbt("xnT%d" % i, [128, 8, GW], BF16) for i in range(2)]
                hT, hTt = sc.sbt("hT", [128, 32, GW], BF16)
                hr, hrt = sc.sbt("hr", [128, 512], F32)
                tp = [sc.pst("tp%d" % i, [128, 8, 128], BF16) for i in range(1)]
                ph = [sc.pst("ph%d" % i, [128, 2, GW], F32) for i in range(3)]
                po = [sc.pst("po%d" % i, [128, 512], F32) for i in range(2)]
                xo = [sc.sbt("xo%d" % i, [128, D], F32) for i in range(2)]
                NGR = NT // GT
                cph = [0]

                def gen_prep(gi):
                    xT, xTt = xnT[gi % 2]
                    for j in range(GT):
                        tt = gi * GT + j
                        x_t, x_tt = xt[tt % 4]
                        sq, sqt = ssq[tt % 4]
                        tpp, tpt = tp[0]
                        P.dma("sp", x_t[:], xsrc[tt * 128:(tt + 1) * 128, :], xtok, x_tt, x_tt)
                        rms_rstd(sc, x_t, x_tt, xn, xnt, sq, sq[:, 2:3], sqt)
                        yield
                        P.op("dve", lambda E, x_t=x_t, sq=sq: E.tensor_scalar(out=xn[:], in0=x_t[:], scalar1=sq[:, 2:3], scalar2=None, op0=ALU.mult), r=[x_tt, sqt], w=[xnt])
                        for kt in range(8):
                            P.op("pe", lambda E, tpp=tpp, kt=kt: E.transpose(out=tpp[:, kt, :], in_=xn[:, kt * 128:(kt + 1) * 128], identity=ident[:]), r=[xnt, t_const], w=[tpt])
                        P.op("act", lambda E, xT=xT, tpp=tpp, j=j: E.copy(out=xT[:, :, j * 128:(j + 1) * 128], in_=tpp[:]), r=[tpt], w=[xTt])
                        yield

                def gen_main(gi):
                    xT, xTt = xnT[gi % 2]
                    for f2 in range(16):
                        pp, ppt = ph[cph[0] % 3]
                        cph[0] += 1
                        for fi in range(2):
                            ft = f2 * 2 + fi
                            for kt in range(8):
                                P.op("pe", lambda E, pp=pp, fi=fi, ft=ft, kt=kt: E.matmul(pp[:, fi, :], lhsT=w1b[:, kt, ft * 128:(ft + 1) * 128], rhs=xT[:, kt, :], start=(kt == 0), stop=(kt == 7)),
                                     r=[xTt, w1t[kt]], w=[ppt])
                        P.op("act", lambda E, pp=pp: E.activation(out=hr[:], in_=pp[:].rearrange("p a b -> p (a b)"), func=AF.Relu), r=[ppt], w=[hrt])
                        e2 = "pool" if f2 % 2 else "dve"
                        P.op(e2, lambda E, f2=f2: E.tensor_tensor(out=hT[:, f2 * 2:(f2 + 1) * 2, :].rearrange("p a b -> p (a b)"), in0=hr[:], in1=hr[:], op=ALU.mult), r=[hrt], w=[hTt])
                        yield
                    for j in range(GT):
                        tt = gi * GT + j
                        x_t, x_tt = xt[tt % 4]
                        xo_, xot_ = xo[tt % 2]
                        for nh in range(2):
                            pq, pqt = po[nh]
                            for kt in range(32):
                                P.op("pe", lambda E, pq=pq, kt=kt, nh=nh, j=j: E.matmul(pq[:], lhsT=hT[:, kt, j * 128:(j + 1) * 128], rhs=w2b[:, kt, nh * 512:(nh + 1) * 512], start=(kt == 0), stop=(kt == 31)),
                                     r=[hTt, w2t[kt]], w=[pqt])
                            P.op("dve", lambda E, xo_=xo_, pq=pq, nh=nh, x_t=x_t: E.tensor_tensor(out=xo_[:, nh * 512:(nh + 1) * 512], in0=pq[:], in1=x_t[:, nh * 512:(nh + 1) * 512], op=ALU.add), r=[pqt, x_tt], w=[xot_])
                            yield
                        if not final:
                            P.dma("pool", xdst[tt * 128:(tt + 1) * 128, :], xo_[:], xot_, xdtok, xot_)
                        else:
                            sq2, sq2t = ssq[tt % 4]
                            rms_rstd(sc, xo_, xot_, xn, xnt, sq2, sq2[:, 3:4], sq2t)
                            P.op("dve", lambda E, xo_=xo_, sq2=sq2: E.scalar_tensor_tensor(out=xo_[:], in0=xo_[:], scalar=sq2[:, 3:4], in1=fw[:], op0=ALU.mult, op1=ALU.mult), r=[sq2t, fwt], w=[xot_])
                            P.dma("pool", out_d[tt * 128:(tt + 1) * 128, :], xo_[:], xot_, T["out"], xot_)
                        yield

                for _ in gen_prep(0):
                    pass
                for gi in range(NGR):
                    gm = gen_main(gi)
                    gp = gen_prep(gi + 1) if gi + 1 < NGR else iter(())
                    m_done = False
                    p_done = False
                    k = 0
                    while not m_done:
                        try:
                            next(gm)
                        except StopIteration:
                            m_done = True
                        k += 1
                        if not p_done and k % 4 == 0:
                            try:
                                next(gp)
                            except StopIteration:
                                p_done = True
                    if not p_done:
                        for _ in gp:
                            pass

        def phase_rnn(l):
            with Scope(P) as sc:
                prm, prmt = sc.sbt("prm", [128, 64], F32)
                P.dma("sp", prm[:, 0:32], convw[l].rearrange("p a b -> p (a b)"), T["w"], prmt, prmt)
                P.dma("sp", prm[:, 32:40], convb[l], T["w"], prmt, prmt)
                P.dma("sp", prm[:, 40:48], lba[l], T["w"], prmt, prmt)
                P.dma("sp", prm[:, 48:56], lbi[l], T["w"], prmt, prmt)
                P.dma("sp", prm[:, 56:64], llam[l], T["w"], prmt, prmt)
                P.op("act", lambda E: E.activation(out=prm[:, 56:64], in_=prm[:, 56:64], func=AF.Exp, scale=-1.0), r=[prmt], w=[prmt])
                P.op("act", lambda E: E.activation(out=prm[:, 56:64], in_=prm[:, 56:64], func=AF.Ln, bias=epsb[:, 1:2]), r=[prmt, t_const], w=[prmt])
                P.op("dve", lambda E: E.tensor_scalar(out=prm[:, 56:64], in0=prm[:, 56:64], scalar1=-8.0, scalar2=None, op0=ALU.mult), r=[prmt], w=[prmt])
                H = S // 2
                wst = [sc.sbt("wst%d" % i, [128, 128], F32) for i in range(2)]
                wab = [sc.sbt("wab%d" % i, [128, 128], BF16) for i in range(2)]
                wib = [sc.sbt("wib%d" % i, [128, 128], BF16) for i in range(2)]
                sets = []
                for i in range(2):
                    d = {}
                    d["xrp"] = sc.sbt("xrp%d" % i, [128, H + 4], F32)
                    d["gg"] = sc.sbt("gg%d" % i, [128, H], F32)
                    d["xc"] = sc.sbt("xc%d" % i, [128, H], F32)
                    d["xcb"] = sc.sbt("xcb%d" % i, [128, H], BF16)
                    d["rr"] = sc.sbt("rr%d" % i, [128, H], F32)
                    d["ig"] = sc.sbt("ig%d" % i, [128, H], F32)
                    d["aa"] = sc.sbt("aa%d" % i, [128, H], F32)
                    d["ro"] = sc.sbt("ro%d" % i, [128, H], BF16)
                    sets.append(d)
                pa = [sc.pst("pa%d" % i, [128, 512], F32) for i in range(6)]
                pkc = [0]
                wts = {}

                def gen_w(ct):
                    wa_, wat_ = wab[ct % 2]
                    wi_, wit_ = wib[ct % 2]
                    ws_, wst_ = wst[0]
                    ws2, wst2 = wst[1]
                    P.dma("sp", ws_[:], lwa[l, ct], T["w"], wst_, wst_)
                    P.op("dve", lambda E, wa_=wa_, ws_=ws_: E.tensor_copy(out=wa_[:], in_=ws_[:]), r=[wst_], w=[wat_])
                    P.dma("sp", ws2[:], lwi[l, ct], T["w"], wst2, wst2)
                    P.op("dve", lambda E, wi_=wi_, ws2=ws2: E.tensor_copy(out=wi_[:], in_=ws2[:]), r=[wst2], w=[wit_])

                def gen_it(it):
                    ct, hf = it // 2, it % 2
                    if hf == 0:
                        gen_w(ct)
                    wa_, wat_ = wab[ct % 2]
                    wi_, wit_ = wib[ct % 2]
                    d = sets[it % 2]
                    dprev = sets[(it + 1) % 2]
                    xrp, xrpt = d["xrp"]
                    gg, ggt = d["gg"]
                    xc, xct = d["xc"]
                    xcb, xcbt = d["xcb"]
                    rr, rrt = d["rr"]
                    ig, igt = d["ig"]
                    aa, aat = d["aa"]
                    ro, rot = d["ro"]
                    t0 = hf * H
                    if hf == 0:
                        P.op("pool", lambda E, xrp=xrp: E.memset(xrp[:, 0:4], 0.0), w=[xrpt])
                        P.dma("sp", xrp[:, 4:H + 4], zf[0, ct * 128:(ct + 1) * 128, 0:H], T["zf"], xrpt, xrpt)
                    else:
                        P.dma("sp", xrp[:, 0:H + 4], zf[0, ct * 128:(ct + 1) * 128, H - 4:S], T["zf"], xrpt, xrpt)
                    P.dma("sp", gg[:], zf[1, ct * 128:(ct + 1) * 128, t0:t0 + H], T["zf"], ggt, ggt)
                    yield
                    P.op("act", lambda E: E.activation(out=xc[:], in_=xrp[:, 4:H + 4], func=AF.Identity, scale=prm[:, ct * 4 + 3:ct * 4 + 4], bias=prm[:, 32 + ct:33 + ct]), r=[xrpt, prmt], w=[xct])
                    yield
                    for i in range(3):
                        P.op("dve", lambda E, i=i: E.scalar_tensor_tensor(out=xc[:], in0=xrp[:, 1 + i:1 + i + H], scalar=prm[:, ct * 4 + i:ct * 4 + i + 1], in1=xc[:], op0=ALU.mult, op1=ALU.add), r=[xrpt, prmt], w=[xct])
                    yield
                    P.op("pool", lambda E: E.tensor_copy(out=xcb[:], in_=xc[:]), r=[xct], w=[xcbt])
                    yield
                    for tg in range(H // 512):
                        sl = slice(tg * 512, (tg + 1) * 512)
                        p1, p1t = pa[pkc[0] % 6]
                        p2, p2t = pa[(pkc[0] + 1) % 6]
                        pkc[0] += 2
                        P.op("pe", lambda E, p1=p1, sl=sl: E.matmul(p1[:], lhsT=wa_[:], rhs=xcb[:, sl], start=True, stop=True), r=[wat_, xcbt], w=[p1t])
                        P.op("pe", lambda E, p2=p2, sl=sl: E.matmul(p2[:], lhsT=wi_[:], rhs=xcb[:, sl], start=True, stop=True), r=[wit_, xcbt], w=[p2t])
                        P.op("act", lambda E, p1=p1, sl=sl: E.activation(out=rr[:, sl], in_=p1[:], func=AF.Sigmoid, bias=prm[:, 40 + ct:41 + ct]), r=[p1t, prmt], w=[rrt])
                        P.op("act", lambda E, p2=p2, sl=sl: E.activation(out=ig[:, sl], in_=p2[:], func=AF.Sigmoid, bias=prm[:, 48 + ct:49 + ct]), r=[p2t, prmt], w=[igt])
                    yield
                    P.op("act", lambda E: E.activation(out=aa[:], in_=rr[:], func=AF.Exp, scale=prm[:, 56 + ct:57 + ct]), r=[rrt, prmt], w=[aat])
                    P.op("pool", lambda E: E.tensor_tensor(out=ig[:], in0=ig[:], in1=xc[:], op=ALU.mult), r=[xct], w=[igt])
                    yield
                    P.op("pool", lambda E: E.tensor_tensor(out=rr[:], in0=aa[:], in1=aa[:], op=ALU.mult), r=[aat], w=[rrt])
                    yield
                    P.op("act", lambda E: E.activation(out=rr[:], in_=rr[:], func=AF.Sqrt, scale=-1.0, bias=epsb[:, 1:2]), r=[rrt, t_const], w=[rrt])
                    yield
                    P.op("dve", lambda E: E.tensor_tensor(out=ig[:], in0=ig[:], in1=rr[:], op=ALU.mult), r=[rrt], w=[igt])
                    if hf == 0:
                        P.op("dve", lambda E: E.tensor_tensor_scan(out=xc[:], data0=aa[:], data1=ig[:], initial=0.0, op0=ALU.mult, op1=ALU.add), r=[aat, igt], w=[xct])
                    else:
                        xcp, xcpt = dprev["xc"]
                        P.op("dve", lambda E: E.tensor_tensor_scan(out=xc[:], data0=aa[:], data1=ig[:], initial=xcp[:, H - 1:H], op0=ALU.mult, op1=ALU.add), r=[aat, igt, xcpt], w=[xct])
                    yield
                    P.op("pool", lambda E: E.tensor_tensor(out=ro[:], in0=xc[:], in1=gg[:], op=ALU.mult), r=[xct, ggt], w=[rot])
                    P.dma("pool", rnnT[ct * 128:(ct + 1) * 128, t0:t0 + H], ro[:], rot, T["rnnT"], rot)
                    yield

                active = []
                nxt = 0
                while nxt < 16 or active:
                    while len(active) < 2 and nxt < 16:
                        active.append(gen_it(nxt))
                        nxt += 1
                    for gnr in list(active):
                        try:
                            next(gnr)
                        except StopIteration:
                            active.remove(gnr)

        def phase_merge(l, xsrc, xtok, xdst, xdtok):
            with Scope(P) as sc:
                wa = sc.sb("wa", [128, 4, D], BF16)
                wat = [sc.tok() for _ in range(4)]
                wr = sc.sb("wr", [128, 8, D], BF16)
                wrt = [sc.tok() for _ in range(8)]
                wob = sc.sb("wob", [128, 8, D], BF16)
                wot = [sc.tok() for _ in range(8)]
                stg = [sc.sbt("stg%d" % i, [128, 1024], F32) for i in range(3)]
                convert(sc, lambda kt, c0, c1: wa[:, kt, c0:c1], lambda kt, c0, c1: wua[l, kt * 128:(kt + 1) * 128, c0:c1], 4, D, stg, chunk=1024, toks=wat)
                convert(sc, lambda kt, c0, c1: wr[:, kt, c0:c1], lambda kt, c0, c1: wur[l, kt * 128:(kt + 1) * 128, c0:c1], 8, D, stg, chunk=1024, toks=wrt)
                convert(sc, lambda kt, c0, c1: wob[:, kt, c0:c1], lambda kt, c0, c1: wo[l, kt * 128:(kt + 1) * 128, c0:c1], 8, D, stg, chunk=1024, toks=wot)
                aT = [sc.sbt("aT%d" % i, [128, 4, 512], BF16) for i in range(2)]
                rT = [sc.sbt("rT%d" % i, [128, 8, 512], BF16) for i in range(2)]
                sa = [sc.sbt("sa%d" % i, [128, 512], F32) for i in range(2)]
                sb_ = [sc.sbt("sb%d" % i, [128, 512], F32) for i in range(2)]
                t1 = [sc.sbt("t1%d" % i, [128, 512], F32) for i in range(2)]
                t2 = [sc.sbt("t2%d" % i, [128, 512], F32) for i in range(2)]
                mT = [sc.sbt("mT%d" % i, [128, 8, 512], BF16) for i in range(2)]
                pA = [sc.pst("pA%d" % i, [128, 512], F32) for i in range(2)]
                pB = [sc.pst("pB%d" % i, [128, 512], F32) for i in range(2)]
                pO = [sc.pst("pO%d" % i, [128, 512], F32) for i in range(2)]
                xt = [sc.sbt("xt%d" % i, [128, D], F32) for i in range(2)]
                xo = [sc.sbt("xo%d" % i, [128, D], F32) for i in range(2)]
                k = 0
                for sg in range(8):
                    tsl = slice(sg * 512, (sg + 1) * 512)
                    a_, at_ = aT[sg % 2]
                    r_, rt_ = rT[sg % 2]
                    m_, mt_ = mT[sg % 2]
                    P.dma("sp", a_[:], attnT[:, tsl].rearrange("(a p) t -> p a t", p=128), T["attnT"], at_, at_)
                    P.dma("sp", r_[:], rnnT[:, tsl].rearrange("(a p) t -> p a t", p=128), T["rnnT"], rt_, rt_)
                    for ft in range(8):
                        fs = slice(ft * 128, (ft + 1) * 128)
                        sa_, sat_ = sa[k % 2]
                        sbb, sbt_ = sb_[k % 2]
                        u1, u1t = t1[k % 2]
                        u2, u2t = t2[k % 2]
                        p_a, pat = pA[k % 2]
                        p_b, pbt = pB[k % 2]
                        k += 1
                        P.dma("sp", sa_[:], zf[2, fs, tsl], T["zf"], sat_, sat_)
                        P.dma("sp", sbb[:], zf[3, fs, tsl], T["zf"], sbt_, sbt_)
                        for kt in range(4):
                            P.op("pe", lambda E, p_a=p_a, kt=kt, fs=fs, a_=a_: E.matmul(p_a[:], lhsT=wa[:, kt, fs], rhs=a_[:, kt, :], start=(kt == 0), stop=(kt == 3)), r=[wat[kt], at_], w=[pat])
                        for kt in range(8):
                            P.op("pe", lambda E, p_b=p_b, kt=kt, fs=fs, r_=r_: E.matmul(p_b[:], lhsT=wr[:, kt, fs], rhs=r_[:, kt, :], start=(kt == 0), stop=(kt == 7)), r=[wrt[kt], rt_], w=[pbt])
                        P.op("dve", lambda E, u1=u1, p_a=p_a, sa_=sa_: E.tensor_tensor(out=u1[:], in0=p_a[:], in1=sa_[:], op=ALU.mult), r=[pat, sat_], w=[u1t])
                        P.op("dve", lambda E, u2=u2, p_b=p_b, sbb=sbb: E.tensor_tensor(out=u2[:], in0=p_b[:], in1=sbb[:], op=ALU.mult), r=[pbt, sbt_], w=[u2t])
                        P.op("pool", lambda E, m_=m_, ft=ft, u1=u1, u2=u2: E.tensor_tensor(out=m_[:, ft, :], in0=u1[:], in1=u2[:], op=ALU.add), r=[u1t, u2t], w=[mt_])
                    for j in range(4):
                        tt = sg * 4 + j
                        x_t, x_tt = xt[tt % 2]
                        xo_, xot_ = xo[tt % 2]
                        P.dma("sp", x_t[:], xsrc[tt * 128:(tt + 1) * 128, :], xtok, x_tt, x_tt)
                        for nh in range(2):
                            pq, pqt = pO[nh]
                            for kt in range(8):
                                P.op("pe", lambda E, pq=pq, kt=kt, nh=nh, m_=m_, j=j: E.matmul(pq[:], lhsT=m_[:, kt, j * 128:(j + 1) * 128], rhs=wob[:, kt, nh * 512:(nh + 1) * 512], start=(kt == 0), stop=(kt == 7)), r=[mt_, wot[kt]], w=[pqt])
                            P.op("dve", lambda E, xo_=xo_, pq=pq, nh=nh, x_t=x_t: E.tensor_tensor(out=xo_[:, nh * 512:(nh + 1) * 512], in0=pq[:], in1=x_t[:, nh * 512:(nh + 1) * 512], op=ALU.add), r=[pqt, x_tt], w=[xot_])
                        P.dma("pool", xdst[tt * 128:(tt + 1) * 128, :], xo_[:], xot_, xdtok, xot_)


        def phase_attn(l):
            with Scope(P) as sc:
                cst = sc.tok("cst")
                tric = sc.sb("tric", [128, 128], BF16)
                triw = sc.sb("triw", [128, 128], BF16)
                Ec = sc.sb("Ec", [64, S], BF16)
                band = sc.sb("band", [128, 9], BF16)
                cA = sc.sb("cA", [128, 128], F32)
                cB = sc.sb("cB", [128, 128], F32)
                for dst, src in ((tric, c_tric), (triw, c_triw), (Ec, c_E), (band, c_band), (cA, c_A), (cB, c_B)):
                    P.dma("sp", dst[:], src[:, :], T["w"], cst, cst)
                kcm = [sc.sbt("kcm%d" % g, [64, 256], BF16) for g in range(2)]
                vcm = [sc.sbt("vcm%d" % g, [128, 2, 64], BF16) for g in range(2)]
                with Scope(P) as s2:
                    stg = [s2.sbt("cstg%d" % i, [64, 2048], F32) for i in range(2)]
                    w2s, w2st = s2.sbt("w2s", [128, 128], F32)
                    pss, psst = s2.sbt("pss", [64, 64], F32)
                    w1b = s2.sb("w1b", [64, 32, 256], BF16)
                    w1bt = s2.tok()
                    w2b, w2bt = s2.sbt("w2b", [128, 2, 64], BF16)
                    posb, posbt = s2.sbt("posb", [64, 32, 2], BF16)
                    kg = [s2.sbt("kg%d" % g, [64, S], BF16) for g in range(2)]
                    hid = [[s2.sbt("hid%d%d" % (g, h), [128, 256], BF16) for h in range(2)] for g in range(2)]
                    bia, biat = s2.sbt("bia", [128, 2], F32)
                    psg = [s2.pst("psg%d" % g, [128, 512], F32) for g in range(2)]
                    psb, psbt = s2.pst("psb", [128, 512], F32)
                    pso, psot = s2.pst("pso", [128, 512], F32)
                    for (w1d, w2d, posd, srcT, srct, is_k) in ((ckw1, ckw2, posk, kcT, "kcT", True), (cvw1, cvw2, posv, vcT, "vcT", False)):
                        convert(s2, lambda kt, c0, c1: w1b[:].rearrange("p a b -> p (a b)")[:, c0:c1], lambda kt, c0, c1: w1d[l].rearrange("p a b -> p (a b)")[:, c0:c1], 1, 32 * 256, stg, chunk=2048, rows=64, toks=[w1bt])
                        P.dma("sp", w2s[:], w2d[l].rearrange("p a b -> p (a b)"), T["w"], w2st, w2st)
                        P.op("dve", lambda E: E.tensor_copy(out=w2b[:].rearrange("p a b -> p (a b)"), in_=w2s[:]), r=[w2st], w=[w2bt])
                        P.dma("sp", pss[:], posd[l].rearrange("p a b -> p (a b)"), T["w"], psst, psst)
                        P.op("dve", lambda E: E.tensor_copy(out=posb[:].rearrange("p a b -> p (a b)"), in_=pss[:]), r=[psst], w=[posbt])
                        for g in range(2):
                            P.dma("sp", kg[g][0][:], srcT[g], T[srct], kg[g][1], kg[g][1])
                        for ht in range(2):
                            hs = slice(ht * 128, (ht + 1) * 128)
                            for p in range(32):
                                for g in range(2):
                                    P.op("pe", lambda E, g=g, p=p, hs=hs: E.matmul(psg[g][0][:, 0:255], lhsT=w1b[:, p, hs], rhs=kg[g][0][:, p:p + 16 * 254 + 1:16], start=(p == 0), stop=(p == 31)),
                                         r=[w1bt, kg[g][1]], w=[psg[g][1]])
                                P.op("pe", lambda E, p=p, hs=hs: E.matmul(psb[:, 0:2], lhsT=w1b[:, p, hs], rhs=posb[:, p, :], start=(p == 0), stop=(p == 31)), r=[w1bt, posbt], w=[psbt])
                            P.op("dve", lambda E: E.tensor_copy(out=bia[:], in_=psb[:, 0:2]), r=[psbt], w=[biat])
                            for g in range(2):
                                P.op("act", lambda E, g=g, ht=ht: E.activation(out=hid[g][ht][0][:, 0:255], in_=psg[g][0][:, 0:255], func=AF.Gelu_apprx_tanh, bias=bia[:, 0:1]), r=[psg[g][1], biat], w=[hid[g][ht][1]])
                        for g in range(2):
                            if is_k:
                                for ht in range(2):
                                    P.op("pe", lambda E, g=g, ht=ht: E.matmul(pso[0:64, 0:255], lhsT=w2b[:, ht, :], rhs=hid[g][ht][0][:, 0:255], start=(ht == 0), stop=(ht == 1)), r=[w2bt, hid[g][ht][1]], w=[psot])
                                P.op("dve", lambda E, g=g: E.tensor_copy(out=kcm[g][0][:, 0:255], in_=pso[0:64, 0:255]), r=[psot], w=[kcm[g][1]])
                            else:
                                for ctile in range(2):
                                    n = 128 if ctile == 0 else 127
                                    for ht in range(2):
                                        P.op("pe", lambda E, g=g, ht=ht, ctile=ctile, n=n: E.matmul(pso[0:n, 256:320], lhsT=hid[g][ht][0][:, ctile * 128:ctile * 128 + n], rhs=w2b[:, ht, :], start=(ht == 0), stop=(ht == 1)), r=[w2bt, hid[g][ht][1]], w=[psot])
                                    P.op("dve", lambda E, g=g, ctile=ctile, n=n: E.tensor_copy(out=vcm[g][0][0:n, ctile, :], in_=pso[0:n, 256:320]), r=[psot], w=[vcm[g][1]])
                KE = [sc.sbt("KE%d" % g, [128, S], BF16) for g in range(2)]
                kwn = [sc.sbt("kwn%d" % g, [64, S], BF16) for g in range(2)]
                vsl = [sc.sbt("vsl%d" % g, [128, 32, 65], BF16) for g in range(2)]
                vwn = [sc.sbt("vwn%d" % g, [128, 32, 65], BF16) for g in range(2)]
                for g in range(2):
                    P.dma("sp", KE[g][0][0:64, :], ksT[g], T["ksT"], KE[g][1], KE[g][1])
                    P.dma("sp", KE[g][0][64:128, :], c_E[:, :], T["w"], KE[g][1], KE[g][1])
                    P.dma("sp", kwn[g][0][:], kwT[g], T["kwT"], kwn[g][1], kwn[g][1])
                    for (vv, j) in ((vsl[g], g), (vwn[g], 2 + g)):
                        P.op("pool", lambda E, vv=vv: E.memset(vv[0][:, :, 64:65], 1.0), w=[vv[1]])
                        for k8 in range(8):
                            P.dma("sp", vv[0][:, k8 * 4:(k8 + 1) * 4, 0:64], vtm[k8 * 512:(k8 + 1) * 512, j, :].rearrange("(k p) d -> p k d", p=128), T["vtm"], vv[1], vv[1])
                QP = [[sc.sbt("QP%d_%d" % (i, h), [128, 512], BF16) for h in range(8)] for i in range(2)]
                gt = [sc.sbt("gt%d" % i, [128, 4, 24], F32) for i in range(2)]
                bst = [sc.pst("bst%d" % i, [128, 512], F32) for i in range(2)]
                b_os = [sc.pst("b_os%d" % i, [128, 512], F32) for i in range(2)]
                b_ow = [sc.pst("b_ow%d" % i, [128, 512], F32) for i in range(2)]
                b_xs, b_xst = sc.pst("b_xs", [128, 512], F32)
                b_tp = sc.ps("b_tp", [128, 1024], BF16)
                tp_t = sc.tok("tp", True)
                NCH = 4
                CH = []
                for c in range(NCH):
                    d = {}
                    d["ee"] = [sc.sbt("ee%d_%d" % (c, i), [128, 256], F32) for i in range(4)]
                    d["pb"] = [sc.sbt("pb%d_%d" % (c, i), [128, 256], BF16) for i in range(4)]
                    d["pTc"] = [sc.sbt("pTc%d_%d" % (c, i), [128, 128], BF16) for i in range(2)]
                    d["sm"] = sc.sbt("sm%d" % c, [128, 16], F32)
                    d["P4"] = sc.sbt("P4_%d" % c, [128, 264], F32)
                    d["imp"] = sc.sbt("imp%d" % c, [128, 64], F32)
                    d["scr"] = sc.sbt("scr%d" % c, [128, 64], F32)
                    d["scr2"] = sc.sbt("scr2_%d" % c, [128, 64], F32)
                    d["m8"] = sc.sbt("m8_%d" % c, [128, 16], F32)
                    d["penb"] = sc.sbt("penb%d" % c, [128, 128], BF16)
                    P.op("pool", lambda E, d=d: E.memset(d["penb"][0][:], 0.0), w=[d["penb"][1]])
                    P.op("dve", lambda E, d=d: E.memset(d["P4"][0][:], 0.0), w=[d["P4"][1]])
                    CH.append(d)
                pT = [sc.sbt("pT%d" % i, [128, 512], BF16) for i in range(4)]
                sy, syt = sc.sbt("sy", [128, 8], F32)
                att2 = [sc.sbt("att%d" % i, [128, 4, 512], F32) for i in range(2)]
                attb, attbt = sc.sbt("attb", [128, 4, 512], BF16)
                ast = [sc.sbt("ast%d" % i, [128, 4, 512], BF16) for i in range(2)]
                sidx = [0]
                NSG = ATT_DBG["nqb"] // 4

                def gen_X(sg):
                    ss = slice(sg * 512, (sg + 1) * 512)
                    qp = QP[sg % 2]
                    g_, gt_ = gt[sg % 2]
                    for h in range(8):
                        P.dma("sp", qp[h][0][0:64, :], qT[h, :, ss], T["qT"], qp[h][1], qp[h][1])
                    P.dma("sp", g_[:], gat[ss, :].rearrange("(j p) c -> p j c", p=128), T["gat"], gt_, gt_)
                    yield
                    for g in range(2):
                        act = [gen_Xchain(sg, g, j) for j in range(4)]
                        while act:
                            for gn in list(act):
                                try:
                                    next(gn)
                                    yield
                                except StopIteration:
                                    act.remove(gn)

                def gen_Xchain(sg, g, j):
                    qp = QP[sg % 2]
                    g_, gt_ = gt[sg % 2]
                    att, attt = att2[sg % 2]
                    d = CH[j % NCH]
                    ee, pb, pTc = d["ee"], d["pb"], d["pTc"]
                    sm, smt = d["sm"]
                    P4, P4t = d["P4"]
                    imp, impt = d["imp"]
                    scr, scrt = d["scr"]
                    scr2, scr2t = d["scr2"]
                    m8, m8t = d["m8"]
                    penb, penbt = d["penb"]
                    P4v = P4[:, 0:256].rearrange("p (j f) -> p j f", f=4)
                    P4w = P4[:, 4:260].rearrange("p (j f) -> p j f", f=4)
                    if True:
                        if True:
                            qb = sg * 4 + j
                            js = slice(j * 128, (j + 1) * 128)
                            Nc = min(255, 8 * qb + 7)
                            cb0 = max(0, 8 * qb - 2)
                            cb1 = min(Nc, 8 * qb + 7)
                            ps_s = b_xs[:, 0:256]
                            for hh in range(4):
                                h = g * 4 + hh
                                P.op("pe", lambda E, h=h, g=g, js=js, Nc=Nc: E.matmul(ps_s[:, 0:Nc], lhsT=qp[h][0][0:64, js], rhs=kcm[g][0][:, 0:Nc], start=True, stop=False), r=[qp[h][1], kcm[g][1]], w=[b_xst])
                                P.op("pe", lambda E, cb0=cb0, cb1=cb1, qb=qb: E.matmul(ps_s[:, cb0:cb1], lhsT=ident[:], rhs=band[:, cb0 - (8 * qb - 2):cb1 - (8 * qb - 2)], start=False, stop=True), r=[t_const, cst], w=[b_xst])
                                e_, et_ = ee[hh]
                                P.op("act", lambda E, e_=e_, hh=hh, Nc=Nc: E.activation(out=e_[:, 0:Nc], in_=ps_s[:, 0:Nc], func=AF.Exp, accum_out=sm[:, 8 + hh:9 + hh]), r=[b_xst], w=[et_, smt])
                                yield
                            P.op("dve", lambda E: E.tensor_scalar(out=sm[:, 12:16], in0=sm[:, 8:12], scalar1=1e-20, scalar2=None, op0=ALU.max), r=[smt], w=[smt])
                            P.op("dve", lambda E: E.reciprocal(out=sm[:, 12:16], in_=sm[:, 12:16]), r=[smt], w=[smt])
                            P.op("dve", lambda E: E.tensor_tensor(out=sm[:, 0:4], in0=sm[:, 12:16], in1=g_[:, j, g * 12:(g + 1) * 12].rearrange("p (h c) -> p h c", c=3)[:, :, 0], op=ALU.mult), r=[smt, gt_], w=[smt])
                            for hh in range(4):
                                e_, et_ = ee[hh]
                                if hh == 0:
                                    P.op("dve", lambda E, e_=e_, hh=hh, Nc=Nc: E.tensor_scalar(out=P4[:, 4:4 + Nc], in0=e_[:, 0:Nc], scalar1=sm[:, 12 + hh:13 + hh], scalar2=None, op0=ALU.mult), r=[et_, smt], w=[P4t])
                                else:
                                    P.op("dve", lambda E, e_=e_, hh=hh, Nc=Nc: E.scalar_tensor_tensor(out=P4[:, 4:4 + Nc], in0=e_[:, 0:Nc], scalar=sm[:, 12 + hh:13 + hh], in1=P4[:, 4:4 + Nc], op0=ALU.mult, op1=ALU.add), r=[et_, smt], w=[P4t])
                            yield
                            P.op("dve", lambda E: E.tensor_tensor(out=imp[:], in0=P4v[:, :, 1], in1=P4v[:, :, 2], op=ALU.add), r=[P4t], w=[impt])
                            P.op("dve", lambda E: E.tensor_tensor(out=imp[:], in0=imp[:], in1=P4v[:, :, 3], op=ALU.add), r=[P4t], w=[impt])
                            P.op("dve", lambda E: E.scalar_tensor_tensor(out=imp[:], in0=imp[:], scalar=2.0, in1=P4v[:, :, 0], op0=ALU.mult, op1=ALU.add), r=[P4t], w=[impt])
                            P.op("dve", lambda E: E.tensor_tensor(out=imp[:], in0=imp[:], in1=P4w[:, :, 0], op=ALU.add), r=[P4t], w=[impt])
                            P.op("dve", lambda E, qb=qb: E.tensor_tensor(out=scr[:], in0=imp[:], in1=cA[:, 64 - 2 * qb:128 - 2 * qb], op=ALU.mult), r=[impt, cst], w=[scrt])
                            P.op("dve", lambda E, qb=qb: E.tensor_tensor(out=scr[:], in0=scr[:], in1=cB[:, 64 - 2 * qb:128 - 2 * qb], op=ALU.add), r=[cst], w=[scrt])
                            P.op("dve", lambda E: E.memset(scr[:, 0:1], 1e4), w=[scrt])
                            yield
                            nb = min(64, 2 * qb + 2)
                            if nb > 16:
                                P.op("dve", lambda E: E.max(out=m8[:, 0:8], in_=scr[:]), r=[scrt], w=[m8t])
                                P.op("dve", lambda E: E.match_replace(out=scr2[:], in_to_replace=m8[:, 0:8], in_values=scr[:], imm_value=-3e38), r=[scrt, m8t], w=[scr2t])
                                P.op("dve", lambda E: E.max(out=m8[:, 8:16], in_=scr2[:]), r=[scr2t], w=[m8t])
                                P.op("dve", lambda E, nb=nb: E.tensor_scalar(out=scr2[:, 0:nb], in0=scr[:, 0:nb], scalar1=m8[:, 15:16], scalar2=None, op0=ALU.is_ge), r=[scrt, m8t], w=[scr2t])
                                P.op("dve", lambda E, nb=nb: E.tensor_scalar(out=penb[:, 64:64 + nb], in0=scr2[:, 0:nb], scalar1=-1.0, scalar2=-NEGM, op0=ALU.add, op1=ALU.mult), r=[scr2t], w=[penbt])
                                yield
                            ptr = b_tp[:, 256:384]
                            P.op("pe", lambda E, ptr=ptr: E.transpose(out=ptr, in_=penb[:], identity=ident[:]), r=[penbt, t_const], w=[tp_t])
                            h0 = g * 4
                            P.op("act", lambda E, ptr=ptr, js=js, h0=h0: E.copy(out=qp[h0][0][64:128, js], in_=ptr[64:128, :]), r=[tp_t], w=[qp[h0][1]])
                            for hh in range(1, 4):
                                P.op("pool", lambda E, js=js, h0=h0, hh=hh: E.tensor_copy(out=qp[h0 + hh][0][64:128, js], in_=qp[h0][0][64:128, js]), r=[qp[h0][1]], w=[qp[h0 + hh][1]])
                            yield
                            k2 = 0
                            tpf = b_tp[:].bitcast(F32)
                            for hh in range(4):
                                h = g * 4 + hh
                                e_, et_ = ee[hh]
                                nct = (Nc + 127) // 128
                                for ctile in range(nct):
                                    n = min(128, Nc - ctile * 128)
                                    tpc = tpf[:, (k2 % 2) * 128:(k2 % 2) * 128 + 128]
                                    pc_, pct_ = pTc[k2 % 2]
                                    k2 += 1
                                    P.op("pe", lambda E, tpc=tpc, e_=e_, ctile=ctile, n=n: E.transpose(out=tpc[0:n, :], in_=e_[:, ctile * 128:ctile * 128 + n], identity=identf[:]), r=[et_, t_const], w=[tp_t])
                                    P.op("act", lambda E, pc_=pc_, tpc=tpc, n=n: E.copy(out=pc_[0:n, :], in_=tpc[0:n, :]), r=[tp_t], w=[pct_])
                                    P.op("pe", lambda E, pc_=pc_, n=n, ctile=ctile, nct=nct: E.matmul(b_xs[:, 256:320], lhsT=pc_[0:n, :], rhs=vcm[g][0][0:n, ctile, :], start=(ctile == 0), stop=(ctile == nct - 1), skip_group_check=True), r=[pct_, vcm[g][1]], w=[b_xst])
                                P.op("dve", lambda E, h=h, hh=hh: E.tensor_scalar(out=att[:, j, h * 64:(h + 1) * 64], in0=b_xs[:, 256:320], scalar1=sm[:, hh:hh + 1], scalar2=None, op0=ALU.mult), r=[b_xst, smt], w=[attt])
                                yield

                def gen_Y(sg):
                    ss = slice(sg * 512, (sg + 1) * 512)
                    qp = QP[sg % 2]
                    g_, gt_ = gt[sg % 2]
                    att, attt = att2[sg % 2]
                    pending = []
                    for g in range(2):
                        for hh in range(4):
                            h = g * 4 + hh
                            q_, qt_ = qp[h]
                            os_, ost_ = b_os[hh % 2]
                            ow_, owt_ = b_ow[hh % 2]
                            steps = []
                            for kt in range(0, 4 * sg + 4):
                                steps.append(("s", kt))
                            for kt in range(max(0, 4 * sg - 4), 4 * sg + 4):
                                steps.append(("w", kt))
                            first = {"s": True, "w": True}
                            LA = 1
                            ring = []
                            for i in range(len(steps) + LA):
                                if i == min(3, len(steps) - 1) and pending:
                                    pending.pop(0)()
                                if i < len(steps):
                                    br, kt = steps[i]
                                    r_ = kt - 4 * sg
                                    if br == "s":
                                        jlo, jhi = max(r_, 0), 3
                                    else:
                                        jlo, jhi = max(r_, 0), min(r_ + 4, 3)
                                    c0, c1 = jlo * 128, (jhi + 1) * 128
                                    si = sidx[0] % 4
                                    stp = bst[sidx[0] % 2][0]
                                    stt_ = bst[sidx[0] % 2][1]
                                    sidx[0] += 1
                                    ks_ = slice(kt * 128, (kt + 1) * 128)
                                    ex = []
                                    if r_ >= 0:
                                        ex.append((r_, tric))
                                    if br == "w" and 0 <= r_ + 4 <= 3:
                                        ex.append((r_ + 4, triw))
                                    if br == "s":
                                        P.op("pe", lambda E, stp=stp, ks_=ks_, q_=q_, g=g, c0=c0, c1=c1, ex=ex: E.matmul(stp[:, c0:c1], lhsT=KE[g][0][:, ks_], rhs=q_[:, c0:c1], start=True, stop=(len(ex) == 0)), r=[KE[g][1], qt_], w=[stt_])
                                    else:
                                        P.op("pe", lambda E, stp=stp, ks_=ks_, q_=q_, g=g, c0=c0, c1=c1, ex=ex: E.matmul(stp[:, c0:c1], lhsT=kwn[g][0][:, ks_], rhs=q_[0:64, c0:c1], start=True, stop=(len(ex) == 0)), r=[kwn[g][1], qt_], w=[stt_])
                                    for xi, (jj, tri_) in enumerate(ex):
                                        P.op("pe", lambda E, stp=stp, jj=jj, tri_=tri_, xi=xi, ex=ex: E.matmul(stp[:, jj * 128:(jj + 1) * 128], lhsT=ident[:], rhs=tri_[:], start=False, stop=(xi == len(ex) - 1)), r=[t_const, cst], w=[stt_])
                                    pt_s, pt_st = pT[si]
                                    P.op("act", lambda E, pt_s=pt_s, stp=stp, c0=c0, c1=c1: E.activation(out=pt_s[:, c0:c1], in_=stp[:, c0:c1], func=AF.Exp), r=[stt_], w=[pt_st])
                                    ring.append((br, kt, jlo, jhi, pt_s, pt_st))
                                if i - LA >= 0:
                                    br, kt, jlo, jhi, pt_s, pt_st = ring[i - LA]
                                    ob, obt = (os_, ost_) if br == "s" else (ow_, owt_)
                                    V = vsl[g] if br == "s" else vwn[g]
                                    for jj in range(jlo, jhi + 1):
                                        st_flag = first[br]
                                        first[br] = False
                                        P.op("pe", lambda E, ob=ob, jj=jj, pt_s=pt_s, V=V, kt=kt, st_flag=st_flag: E.matmul(ob[:, jj * 65:jj * 65 + 65], lhsT=pt_s[:, jj * 128:(jj + 1) * 128], rhs=V[0][:, kt, :], start=st_flag, stop=True, skip_group_check=True), r=[pt_st, V[1]], w=[obt])
                                yield
                            def combine(h=h, os_=os_, ost_=ost_, ow_=ow_, owt_=owt_):
                                osv = os_[:, 0:260].rearrange("p (j c) -> p j c", c=65)
                                owv = ow_[:, 0:260].rearrange("p (j c) -> p j c", c=65)
                                P.op("dve", lambda E: E.reciprocal(out=sy[:, 0:4], in_=osv[:, :, 64]), r=[ost_], w=[syt])
                                P.op("dve", lambda E: E.tensor_tensor(out=sy[:, 0:4], in0=sy[:, 0:4], in1=g_[:, :, h * 3 + 1], op=ALU.mult), r=[gt_], w=[syt])
                                P.op("dve", lambda E: E.reciprocal(out=sy[:, 4:8], in_=owv[:, :, 64]), r=[owt_], w=[syt])
                                P.op("dve", lambda E: E.tensor_tensor(out=sy[:, 4:8], in0=sy[:, 4:8], in1=g_[:, :, h * 3 + 2], op=ALU.mult), r=[gt_], w=[syt])
                                cs_ = slice(h * 64, (h + 1) * 64)
                                for jj in range(4):
                                    P.op("dve", lambda E, jj=jj: E.scalar_tensor_tensor(out=att[:, jj, cs_], in0=os_[:, jj * 65:jj * 65 + 64], scalar=sy[:, jj:jj + 1], in1=att[:, jj, cs_], op0=ALU.mult, op1=ALU.add), r=[ost_, syt], w=[attt])
                                    P.op("dve", lambda E, jj=jj: E.scalar_tensor_tensor(out=attb[:, jj, cs_], in0=ow_[:, jj * 65:jj * 65 + 64], scalar=sy[:, 4 + jj:5 + jj], in1=att[:, jj, cs_], op0=ALU.mult, op1=ALU.add), r=[owt_, syt, attt], w=[attbt])
                            pending.append(combine)
                            yield
                    while pending:
                        pending.pop(0)()
                    yield
                    a_, at_ = ast[sg % 2]
                    atr = b_tp[:, 384:896].rearrange("p (a b) -> p a b", b=128)
                    for jj in range(4):
                        for ft in range(4):
                            P.op("pe", lambda E, ft=ft, jj=jj, atr=atr: E.transpose(out=atr[:, ft, :], in_=attb[:, jj, ft * 128:(ft + 1) * 128], identity=ident[:]), r=[attbt, t_const], w=[tp_t])
                        P.op("act", lambda E, a_=a_, atr=atr, jj=jj: E.copy(out=a_[:, :, jj * 128:(jj + 1) * 128], in_=atr), r=[tp_t], w=[at_])
                        yield
                    P.dma("pool", attnT[:, ss].rearrange("(a p) t -> p a t", p=128), a_[:], at_, T["attnT"], at_)
                    yield

                def drain(gn):
                    for _ in gn:
                        pass

                if NSG > 0:
                    drain(gen_X(0))
                for sg in range(NSG):
                    gy = gen_Y(sg) if not ATT_DBG.get("skip_sw") else iter(())
                    gx = gen_X(sg + 1) if sg + 1 < NSG else iter(())
                    ratio = max(1, int(round((32 * sg + 110) / 100.0)))
                    x_done = False
                    y_done = False
                    while not y_done:
                        for _ in range(ratio):
                            try:
                                next(gy)
                            except StopIteration:
                                y_done = True
                                break
                        if not x_done:
                            try:
                                next(gx)
                            except StopIteration:
                                x_done = True
                    if not x_done:
                        drain(gx)

        PH = {"inproj": phase_inproj, "mlp": phase_mlp, "rnn": phase_rnn, "merge": phase_merge, "attn": phase_attn}
        build.phases = PH
        build.ctx = dict(P=P, T=T, nc=nc, xs=xs, x_in=x_in, out_d=out_d)
        plan = build.plan
        plan(PH, build.ctx, locals())
        P.barrier()
        print("ops", P.nops, "waits", P.nwait)
    return nc, dbg


def default_plan(PH, ctx, L):
    T = ctx["T"]
    xs = ctx["xs"]
    cur, curt = ctx["x_in"], T["x_in"]
    for l in range(2):
        PH["inproj"](l, cur, curt)
        PH["attn"](l)
        PH["rnn"](l)
        PH["merge"](l, cur, curt, xs[0], T["xs0"])
        PH["mlp"](l, xs[0], T["xs0"], xs[1], T["xs1"], l == 1)
        cur, curt = xs[1], T["xs1"]


build.plan = default_plan


def host_inputs(inp, b):
    bf = ml_dtypes.bfloat16
    f = np.float32

    def pk(v):
        return np.ascontiguousarray(v.reshape(2, 8, 128).transpose(0, 2, 1)).astype(f)

    def bd(wm):
        o = np.zeros((2, 8, 128, 128), f)
        for c in range(8):
            o[:, c, 0:64, 0:64] = wm[:, 2 * c]
            o[:, c, 64:128, 64:128] = wm[:, 2 * c + 1]
        return o

    i_ = np.arange(128)
    m = {
        "x": np.ascontiguousarray(inp["x"][b]),
        "w_in": inp["w_in"],
        "n1w": pk(inp["norm1_w"]), "n2w": pk(inp["norm2_w"]),
        "fnw": np.ascontiguousarray(np.broadcast_to(inp["final_norm_w"][None, :], (128, D))).astype(f),
        "posk": np.ascontiguousarray(np.repeat(inp["cmp_pos_k"].transpose(0, 2, 1)[..., None], 2, axis=-1)),
        "posv": np.ascontiguousarray(np.repeat(inp["cmp_pos_v"].transpose(0, 2, 1)[..., None], 2, axis=-1)),
        "ckw1": np.ascontiguousarray(inp["cmp_k_w1"].reshape(2, 32, 64, 256).transpose(0, 2, 1, 3)),
        "cvw1": np.ascontiguousarray(inp["cmp_v_w1"].reshape(2, 32, 64, 256).transpose(0, 2, 1, 3)),
        "ckw2": np.ascontiguousarray(inp["cmp_k_w2"].reshape(2, 2, 128, 64).transpose(0, 2, 1, 3)),
        "cvw2": np.ascontiguousarray(inp["cmp_v_w2"].reshape(2, 2, 128, 64).transpose(0, 2, 1, 3)),
        "convw": np.ascontiguousarray(inp["conv_w"].reshape(2, 4, 8, 128).transpose(0, 3, 2, 1)),
        "convb": pk(inp["conv_b"]), "lba": pk(inp["lru_b_a"]), "lbi": pk(inp["lru_b_i"]), "llam": pk(inp["lru_lambda"]),
        "lwa": bd(inp["lru_w_a"]), "lwi": bd(inp["lru_w_i"]),
        "wua": inp["w_up_attn"], "wur": inp["w_up_rnn"], "wo": inp["w_out"], "w1": inp["mlp_w1"], "w2": inp["mlp_w2"],
        "c_ident": np.eye(128, dtype=f).astype(bf),
        "c_tric": np.where(i_[:, None] <= i_[None, :], 0.0, NEGM).astype(bf),
        "c_triw": np.where(i_[:, None] > i_[None, :], 0.0, NEGM).astype(bf),
        "c_E": (np.arange(S)[None, :] // 64 == np.arange(64)[:, None]).astype(f).astype(bf),
        "c_band": np.where((np.arange(9)[None, :] - 2) <= ((i_[:, None] + 1) // 16 - 2), 0.0, NEGM).astype(bf),
    }
    hi = (i_ >= 64).astype(np.int64)[:, None]
    jp = (np.arange(128) - 64)[None, :]
    valid = jp <= hi
    forced = jp > hi - 2
    A = np.where(valid & ~forced, 1.0, 0.0)
    Bm = np.where(valid, np.where(forced, 1e4, 0.0), -1e30)
    m["c_A"] = A.astype(f)
    m["c_B"] = Bm.astype(f)
    return {k: np.ascontiguousarray(v) for k, v in m.items()}


def kernel(**inputs):
    inp = {k: np.asarray(v) for k, v in inputs.items()}
    nc, _ = build(False)
    in_maps = [host_inputs(inp, c % 4) for c in range(8)]
    res = run_bass_kernel_spmd(nc, in_maps, core_ids=list(range(8)))
    return np.stack([np.asarray(res.results[c]["out"]) for c in range(4)], axis=0).astype(np.float32)
```

```python
import numpy as np
import ml_dtypes
from contextlib import ExitStack
import concourse.bass as bass
import concourse.mybir as mybir
from concourse.bass_utils import run_bass_kernel_spmd

F32 = mybir.dt.float32
BF16 = mybir.dt.bfloat16
AF = mybir.ActivationFunctionType
ALU = mybir.AluOpType
AX = mybir.AxisListType

S = 4096
D = 1024
DIN = 5400
NT = S // 128
NEGM = -30000.0
EPS = 1e-6
O_Q, O_KC, O_VC, O_KS, O_VS, O_KW, O_VW, O_GN, O_XR, O_GR, O_GA, O_GB = 0, 512, 640, 768, 896, 1024, 1152, 1280, 1304, 2328, 3352, 4376


ATT_DBG = {"level": 6, "nqb": NT}


class Tok:
    __slots__ = ("w", "r", "sem", "name", "x")

    def __init__(self, name="", x=False):
        self.w = {}
        self.r = {}
        self.sem = None
        self.name = name
        self.x = x


class Prog:
    ENG = ("pe", "act", "dve", "pool", "sp")

    def __init__(self, nc, es, n_dma_sems=80):
        self.nc = nc
        self.eng = {"pe": nc.tensor, "act": nc.scalar, "dve": nc.vector, "pool": nc.gpsimd, "sp": nc.sync}
        self.sems = []
        self.esem = {}
        for e in self.ENG:
            self.esem[e] = len(self.sems)
            self.sems.append(es.enter_context(nc.semaphore("es_" + e)))
        self.dma_ids = []
        for i in range(n_dma_sems):
            self.dma_ids.append(len(self.sems))
            self.sems.append(es.enter_context(nc.semaphore("ds_%d" % i)))
        self.free = list(self.dma_ids)
        self.total = [0] * len(self.sems)
        self.known = {e: [0] * len(self.sems) for e in self.ENG}
        self.nwait = 0
        self.nops = 0

    def _wait(self, eng, deps):
        E = self.eng[eng]
        kn = self.known[eng]
        for s, v in deps.items():
            if s >= 5:
                v = self.total[s]
            if kn[s] < v:
                kn[s] = v
                E.wait_ge(self.sems[s], v)
                self.nwait += 1

    @staticmethod
    def _merge(d, src):
        for s, v in src.items():
            if d.get(s, 0) < v:
                d[s] = v

    def op(self, eng, fn, r=(), w=()):
        deps = {}
        rx = [b for b in r if b.x]
        if rx:
            r = [b for b in r if not b.x]
            w = list(w) + rx
        for b in r:
            self._merge(deps, b.w)
        for b in w:
            self._merge(deps, b.w)
            self._merge(deps, b.r)
        s = self.esem[eng]
        if eng == "pe":
            deps.pop(s, None)
        self._wait(eng, deps)
        self.total[s] += 1
        n = self.total[s]
        fn(self.eng[eng]).then_inc(self.sems[s], 1)
        self.nops += 1
        for b in r:
            b.r[s] = n
        for b in w:
            b.w[s] = n

    def dma(self, eng, out, in_, src, dst, owner):
        deps = {}
        self._merge(deps, src.w)
        self._merge(deps, dst.w)
        self._merge(deps, dst.r)
        self._wait(eng, deps)
        if owner.sem is None:
            owner.sem = self.free.pop()
        s = owner.sem
        self.total[s] += 16
        v = self.total[s]
        self.eng[eng].dma_start(out=out, in_=in_).then_inc(self.sems[s], 16)
        self.nops += 1
        src.r[s] = v
        dst.w[s] = v

    def release(self, toks):
        for t in toks:
            if t.sem is not None:
                self.free.append(t.sem)
                t.sem = None

    def barrier(self):
        for e in self.ENG:
            E = self.eng[e]
            kn = self.known[e]
            for s in range(len(self.sems)):
                if s == self.esem[e]:
                    continue
                v = self.total[s]
                if kn[s] < v:
                    kn[s] = v
                    E.wait_ge(self.sems[s], v)
        arr = {}
        for e in self.ENG:
            s = self.esem[e]
            self.total[s] += 1
            arr[e] = self.total[s]
            if e == "pe":
                self.eng[e].nop().then_inc(self.sems[s], 1) if hasattr(self.eng[e], "nop") else None
            else:
                self.eng[e].nop().then_inc(self.sems[s], 1)
        for e in self.ENG:
            for f in self.ENG:
                if f == e:
                    continue
                s = self.esem[f]
                self.known[e][s] = arr[f]
                self.eng[e].wait_ge(self.sems[s], arr[f])


class Scope:
    def __init__(self, P):
        self.P = P
        self.es = ExitStack()
        self.toks = []

    def __enter__(self):
        self.es.__enter__()
        return self

    def __exit__(self, *a):
        self.P.barrier()
        self.P.release(self.toks)
        return self.es.__exit__(*a)

    uid = [0]

    def sb(self, name, shape, dt):
        Scope.uid[0] += 1
        return self.es.enter_context(self.P.nc.sbuf_tensor("%s_%d" % (name, Scope.uid[0]), list(shape), dt))

    def ps(self, name, shape, dt):
        Scope.uid[0] += 1
        return self.es.enter_context(self.P.nc.psum_tensor("%s_%d" % (name, Scope.uid[0]), list(shape), dt))

    def tok(self, name="", x=False):
        t = Tok(name, x)
        self.toks.append(t)
        return t

    def sbt(self, name, shape, dt):
        return self.sb(name, shape, dt), self.tok(name)

    def pst(self, name, shape, dt):
        return self.ps(name, shape, dt), self.tok(name, True)


def build(debug=False):
    nc = bass.Bass("TRN2", target_bir_lowering=False)
    dbg = {}

    def din(name, shape, dt=F32):
        return nc.dram_tensor(name, list(shape), dt, kind="ExternalInput").ap()

    def dscr(name, shape, dt):
        isd = bool(debug) and (debug is True or name in debug)
        kind = "ExternalOutput" if isd else "Internal"
        t = nc.dram_tensor(name, list(shape), dt, kind=kind).ap()
        if isd:
            dbg[name] = t
        return t

    x_in = din("x", [S, D])
    out_d = nc.dram_tensor("out", [S, D], F32, kind="ExternalOutput").ap()
    w_in = din("w_in", [2, D, DIN])
    n1w = din("n1w", [2, 128, 8])
    n2w = din("n2w", [2, 128, 8])
    fnw = din("fnw", [128, D])
    posk = din("posk", [2, 64, 32, 2])
    posv = din("posv", [2, 64, 32, 2])
    ckw1 = din("ckw1", [2, 64, 32, 256])
    cvw1 = din("cvw1", [2, 64, 32, 256])
    ckw2 = din("ckw2", [2, 128, 2, 64])
    cvw2 = din("cvw2", [2, 128, 2, 64])
    convw = din("convw", [2, 128, 8, 4])
    convb = din("convb", [2, 128, 8])
    lba = din("lba", [2, 128, 8])
    lbi = din("lbi", [2, 128, 8])
    llam = din("llam", [2, 128, 8])
    lwa = din("lwa", [2, 8, 128, 128])
    lwi = din("lwi", [2, 8, 128, 128])
    wua = din("wua", [2, 512, D])
    wur = din("wur", [2, D, D])
    wo = din("wo", [2, D, D])
    w1 = din("w1", [2, D, 4096])
    w2 = din("w2", [2, 4096, D])
    c_ident = din("c_ident", [128, 128], BF16)
    c_tric = din("c_tric", [128, 128], BF16)
    c_triw = din("c_triw", [128, 128], BF16)
    c_E = din("c_E", [64, S], BF16)
    c_band = din("c_band", [128, 9], BF16)
    c_A = din("c_A", [128, 128])
    c_B = din("c_B", [128, 128])

    xs = [dscr("xs0", [S, D], F32), dscr("xs1", [S, D], F32)]
    qT = dscr("qT", [8, 64, S], BF16)
    kcT = dscr("kcT", [2, 64, S], BF16)
    vcT = dscr("vcT", [2, 64, S], BF16)
    ksT = dscr("ksT", [2, 64, S], BF16)
    kwT = dscr("kwT", [2, 64, S], BF16)
    vtm = dscr("vtm", [S, 4, 64], BF16)
    gat = dscr("gat", [S, 24], F32)
    zf = dscr("zf", [4, D, S], F32)
    w1dr = dscr("w1s", [D, 4096], BF16)
    w2dr = dscr("w2s", [4096, D], BF16)
    attnT = dscr("attnT", [512, S], BF16)
    rnnT = dscr("rnnT", [D, S], BF16)

    with ExitStack() as es:
        P = Prog(nc, es)
        T = {n: Tok(n) for n in ["x_in", "out", "w", "xs0", "xs1", "qT", "kcT", "vcT", "ksT", "kwT", "vtm", "gat", "zf", "attnT", "rnnT", "w1s", "w2s"]}
        for t in T.values():
            t.sem = None

        ident = es.enter_context(nc.sbuf_tensor("ident", [128, 128], BF16))
        identf = es.enter_context(nc.sbuf_tensor("identf", [128, 128], F32))
        t_const = Tok("const")
        P.dma("sp", ident[:], c_ident[:, :], T["w"], t_const, t_const)
        P.op("dve", lambda E: E.tensor_copy(out=identf[:], in_=ident[:]), r=[t_const], w=[t_const])

        def convert(sc, dst_ap_fn, src_ap_fn, nrow_tiles, ncols, stg, scale_ap_fn=None, chunk=2048, rows=128, toks=None):
            i = 0
            engs = ("act", "dve", "pool")
            for kt in range(nrow_tiles):
                for c0 in range(0, ncols, chunk):
                    c1 = min(ncols, c0 + chunk)
                    st, stt = stg[i % len(stg)]
                    P.dma("sp", st[0:rows, 0:c1 - c0], src_ap_fn(kt, c0, c1), T["w"], stt, stt)
                    e = engs[i % 3]
                    tk = toks[kt] if toks is not None else None
                    dst = dst_ap_fn(kt, c0, c1)
                    src = st[0:rows, 0:c1 - c0]
                    if scale_ap_fn is None:
                        if e == "act":
                            P.op(e, lambda E, dst=dst, src=src: E.copy(out=dst, in_=src), r=[stt], w=[tk])
                        else:
                            P.op(e, lambda E, dst=dst, src=src: E.tensor_copy(out=dst, in_=src), r=[stt], w=[tk])
                    else:
                        sc_ap, sc_tok = scale_ap_fn(kt)
                        if e == "act":
                            P.op(e, lambda E, dst=dst, src=src, sc_ap=sc_ap: E.activation(out=dst, in_=src, func=AF.Copy, scale=sc_ap), r=[stt, sc_tok], w=[tk])
                        else:
                            P.op(e, lambda E, dst=dst, src=src, sc_ap=sc_ap: E.tensor_scalar(out=dst, in0=src, scalar1=sc_ap, scalar2=None, op0=ALU.mult), r=[stt, sc_tok], w=[tk])
                    i += 1

        def rms_rstd(sc, xt, xtok, junk, junktok, ssq, rstd, sstok):
            P.op("act", lambda E: E.activation(out=junk[:], in_=xt[:], func=AF.Square, accum_out=ssq[:, 0:1]), r=[xtok], w=[junktok, sstok])
            P.op("act", lambda E: E.activation(out=ssq[:, 1:2], in_=ssq[:, 0:1], func=AF.Sqrt, scale=1.0 / D, bias=epsb[:, 0:1]), r=[sstok, t_const], w=[sstok])
            P.op("dve", lambda E: E.reciprocal(out=rstd[:, 0:1], in_=ssq[:, 1:2]), r=[sstok], w=[sstok])

        epsb = es.enter_context(nc.sbuf_tensor("epsb", [128, 4], F32))
        P.op("dve", lambda E: E.memset(epsb[:, 0:1], EPS), w=[t_const])
        P.op("dve", lambda E: E.memset(epsb[:, 1:2], 1.0), w=[t_const])
        P.op("dve", lambda E: E.memset(epsb[:, 2:3], 0.0), w=[t_const])

        def phase_inproj(l, xsrc, xtok):
            with Scope(P) as sc:
                wbf = sc.sb("wbf", [128, 8, DIN], BF16)
                wtok = [sc.tok("wbf%d" % k) for k in range(8)]
                n1 = sc.sb("n1", [128, 8], F32)
                n1t = sc.tok()
                P.dma("sp", n1[:], n1w[l], T["w"], n1t, n1t)
                stg = [sc.sbt("stg%d" % i, [128, 1800], F32) for i in range(3)]
                convert(sc, lambda kt, c0, c1: wbf[:, kt, c0:c1], lambda kt, c0, c1: w_in[l, kt * 128:(kt + 1) * 128, c0:c1], 8, DIN, stg,
                        scale_ap_fn=lambda kt: (n1[:, kt:kt + 1], n1t), chunk=1800, toks=wtok)
                xt = [sc.sbt("xt%d" % i, [128, D], F32) for i in range(2)]
                junk, junkt = sc.sbt("junk", [128, D], BF16)
                ssq = [sc.sbt("ssq%d" % i, [128, 4], F32) for i in range(2)]
                xn = [sc.sbt("xn%d" % i, [128, D], BF16) for i in range(2)]
                xnT = [sc.sbt("xnT%d" % i, [128, 8, 512], BF16) for i in range(2)]
                tp = [sc.pst("tp%d" % i, [128, 8, 128], BF16) for i in range(2)]
                pf = [sc.pst("pf%d" % i, [128, 512], F32) for i in range(4)]
                ptm = [sc.pst("ptm", [128, 512], F32)]
                of32 = [sc.sbt("of32_%d" % i, [128, 512], F32) for i in range(4)]
                obf = [sc.sbt("obf_%d" % i, [128, 512], BF16) for i in range(4)]
                vst = [sc.sbt("vst%d" % i, [128, 256], BF16) for i in range(2)]
                gst = [sc.sbt("gst%d" % i, [128, 24], F32) for i in range(2)]
                cnt = {"pf": 0, "f": 0, "b": 0}
                def gen_prep(sg):
                    xT, xTt = xnT[sg % 2]
                    for j in range(4):
                        tt = sg * 4 + j
                        x_t, x_tt = xt[tt % 2]
                        sq, sqt = ssq[tt % 2]
                        xb, xbt = xn[tt % 2]
                        tpp, tpt = tp[tt % 2]
                        P.dma("sp", x_t[:], xsrc[tt * 128:(tt + 1) * 128, :], xtok, x_tt, x_tt)
                        rms_rstd(sc, x_t, x_tt, junk, junkt, sq, sq[:, 2:3], sqt)
                        yield
                        P.op("dve", lambda E, xb=xb, x_t=x_t, sq=sq: E.tensor_scalar(out=xb[:], in0=x_t[:], scalar1=sq[:, 2:3], scalar2=None, op0=ALU.mult), r=[x_tt, sqt], w=[xbt])
                        for kt in range(8):
                            P.op("pe", lambda E, tpp=tpp, xb=xb, kt=kt: E.transpose(out=tpp[:, kt, :], in_=xb[:, kt * 128:(kt + 1) * 128], identity=ident[:]), r=[xbt, t_const], w=[tpt])
                        P.op("act", lambda E, xT=xT, tpp=tpp, j=j: E.copy(out=xT[:, :, j * 128:(j + 1) * 128], in_=tpp[:]), r=[tpt], w=[xTt])
                        yield
                        pt, ptt = ptm[0]
                        for (c0, n, o0) in ((O_VS, 128, 0), (O_VW, 128, 128), (O_GN, 24, 256)):
                            for kt in range(8):
                                P.op("pe", lambda E, pt=pt, xT=xT, kt=kt, c0=c0, n=n, o0=o0, j=j: E.matmul(pt[:, o0:o0 + n], lhsT=xT[:, kt, j * 128:(j + 1) * 128], rhs=wbf[:, kt, c0:c0 + n], start=(kt == 0), stop=(kt == 7)),
                                     r=[xTt, wtok[kt]], w=[ptt])
                        vs_, vst_ = vst[tt % 2]
                        gs_, gst_ = gst[tt % 2]
                        P.op("dve", lambda E, vs_=vs_, pt=pt: E.tensor_copy(out=vs_[:], in_=pt[:, 0:256]), r=[ptt], w=[vst_])
                        P.op("act", lambda E, gs_=gs_, pt=pt: E.activation(out=gs_[:], in_=pt[:, 256:280], func=AF.Sigmoid), r=[ptt], w=[gst_])
                        P.dma("pool", vtm[tt * 128:(tt + 1) * 128].rearrange("p a d -> p (a d)"), vs_[:], vst_, T["vtm"], vst_)
                        P.dma("pool", gat[tt * 128:(tt + 1) * 128, :], gs_[:], gst_, T["gat"], gst_)
                        yield

                def gen_main(sg):
                    xT, xTt = xnT[sg % 2]
                    tsl = slice(sg * 512, (sg + 1) * 512)
                    jobs = []
                    qTf = qT.rearrange("h d t -> (h d) t")
                    for h2 in range(4):
                        jobs.append((O_Q + h2 * 128, 128, "q", qTf[h2 * 128:(h2 + 1) * 128, tsl], "qT"))
                    jobs.append((O_KC, 128, "c", kcT.rearrange("g d t -> (g d) t")[:, tsl], "kcT"))
                    jobs.append((O_VC, 128, "c", vcT.rearrange("g d t -> (g d) t")[:, tsl], "vcT"))
                    jobs.append((O_KS, 128, "c", ksT.rearrange("g d t -> (g d) t")[:, tsl], "ksT"))
                    jobs.append((O_KW, 128, "c", kwT.rearrange("g d t -> (g d) t")[:, tsl], "kwT"))
                    for ft in range(8):
                        jobs.append((O_XR + ft * 128, 128, "f", zf[0, ft * 128:(ft + 1) * 128, tsl], "zf"))
                    for ft in range(8):
                        jobs.append((O_GR + ft * 128, 128, "gelu", zf[1, ft * 128:(ft + 1) * 128, tsl], "zf"))
                    for ft in range(8):
                        jobs.append((O_GA + ft * 128, 128, "sig", zf[2, ft * 128:(ft + 1) * 128, tsl], "zf"))
                    for ft in range(8):
                        jobs.append((O_GB + ft * 128, 128, "sig", zf[3, ft * 128:(ft + 1) * 128, tsl], "zf"))
                    for (c0, m, kind, dst, dtk) in jobs:
                        pp, ppt = pf[cnt["pf"] % 4]
                        cnt["pf"] += 1
                        for kt in range(8):
                            P.op("pe", lambda E, pp=pp, kt=kt, c0=c0, m=m, xT=xT: E.matmul(pp[0:m, :], lhsT=wbf[:, kt, c0:c0 + m], rhs=xT[:, kt, :], start=(kt == 0), stop=(kt == 7)),
                                 r=[xTt, wtok[kt]], w=[ppt])
                        if kind in ("q", "c"):
                            ob, obt = obf[cnt["b"] % 4]
                            cnt["b"] += 1
                            scl = 0.125 if kind == "q" else 1.0
                            P.op("dve", lambda E, ob=ob, pp=pp, m=m, scl=scl: E.tensor_scalar(out=ob[0:m, :], in0=pp[0:m, :], scalar1=scl, scalar2=None, op0=ALU.mult), r=[ppt], w=[obt])
                            P.dma("pool", dst, ob[0:m, :], obt, T[dtk], obt)
                            yield
                        else:
                            ob, obt = of32[cnt["f"] % 4]
                            cnt["f"] += 1
                            if kind == "f":
                                P.op("dve", lambda E, ob=ob, pp=pp: E.tensor_copy(out=ob[:], in_=pp[:]), r=[ppt], w=[obt])
                            else:
                                fn = AF.Gelu_apprx_tanh if kind == "gelu" else AF.Sigmoid
                                P.op("act", lambda E, ob=ob, pp=pp, fn=fn: E.activation(out=ob[:], in_=pp[:], func=fn), r=[ppt], w=[obt])
                            P.dma("pool", dst, ob[:], obt, T[dtk], obt)
                            yield


                NG = S // 512
                for _ in gen_prep(0):
                    pass
                for sg in range(NG):
                    gm = gen_main(sg)
                    gp = gen_prep(sg + 1) if sg + 1 < NG else iter(())
                    m_done = False
                    p_done = False
                    k = 0
                    while not m_done:
                        try:
                            next(gm)
                        except StopIteration:
                            m_done = True
                        k += 1
                        if not p_done and k % 3 == 0:
                            try:
                                next(gp)
                            except StopIteration:
                                p_done = True
                    if not p_done:
                        for _ in gp:
                            pass

        def phase_mlp(l, xsrc, xtok, xdst, xdtok, final):
            with Scope(P) as sc:
                w1b = sc.sb("w1b", [128, 8, 4096], BF16)
                w1t = [sc.tok() for _ in range(8)]
                w2b = sc.sb("w2b", [128, 32, D], BF16)
                w2t = [sc.tok() for _ in range(32)]
                for kt in range(8):
                    P.dma("sp", w1b[:, kt, :], w1dr[kt * 128:(kt + 1) * 128, :], T["w1s"], w1t[kt], w1t[kt])
                for kt in range(32):
                    P.dma("sp", w2b[:, kt, :], w2dr[kt * 128:(kt + 1) * 128, :], T["w2s"], w2t[kt], w2t[kt])
                fw = None
                if final:
                    fw, fwt = sc.sbt("fw", [128, D], F32)
                    P.dma("sp", fw[:], fnw[:, :], T["w"], fwt, fwt)
                GT = 2
                GW = GT * 128
                xt = [sc.sbt("xt%d" % i, [128, D], F32) for i in range(4)]
                ssq = [sc.sbt("ssq%d" % i, [128, 4], F32) for i in range(4)]
                xn, xnt = sc.sbt("xn", [128, D], BF16)
                xnT = [sc.sbt("xnT%d" % i, [128, 8, GW], BF16) for i in range(2)]
                hT, hTt = sc.sbt("hT", [128, 32, GW], BF16)
                hr, hrt = sc.sbt("hr", [128, 512], F32)
                tp = [sc.pst("tp%d" % i, [128, 8, 128], BF16) for i in range(1)]
                ph = [sc.pst("ph%d" % i, [128, 2, GW], F32) for i in range(3)]
                po = [sc.pst("po%d" % i, [128, 512], F32) for i in range(2)]
                xo = [sc.sbt("xo%d" % i, [128, D], F32) for i in range(2)]
                NGR = NT // GT
                cph = [0]

                def gen_prep(gi):
                    xT, xTt = xnT[gi % 2]
                    for j in range(GT):
                        tt = gi * GT + j
                        x_t, x_tt = xt[tt % 4]
                        sq, sqt = ssq[tt % 4]
                        tpp, tpt = tp[0]
                        P.dma("sp", x_t[:], xsrc[tt * 128:(tt + 1) * 128, :], xtok, x_tt, x_tt)
                        rms_rstd(sc, x_t, x_tt, xn, xnt, sq, sq[:, 2:3], sqt)
                        yield
                        P.op("dve", lambda E, x_t=x_t, sq=sq: E.tensor_scalar(out=xn[:], in0=x_t[:], scalar1=sq[:, 2:3], scalar2=None, op0=ALU.mult), r=[x_tt, sqt], w=[xnt])
                        for kt in range(8):
                            P.op("pe", lambda E, tpp=tpp, kt=kt: E.transpose(out=tpp[:, kt, :], in_=xn[:, kt * 128:(kt + 1) * 128], identity=ident[:]), r=[xnt, t_const], w=[tpt])
                        P.op("act", lambda E, xT=xT, tpp=tpp, j=j: E.copy(out=xT[:, :, j * 128:(j + 1) * 128], in_=tpp[:]), r=[tpt], w=[xTt])
                        yield

                def gen_main(gi):
                    xT, xTt = xnT[gi % 2]
                    for f2 in range(16):
                        pp, ppt = ph[cph[0] % 3]
                        cph[0] += 1
                        for fi in range(2):
                            ft = f2 * 2 + fi
                            for kt in range(8):
                                P.op("pe", lambda E, pp=pp, fi=fi, ft=ft, kt=kt: E.matmul(pp[:, fi, :], lhsT=w1b[:, kt, ft * 128:(ft + 1) * 128], rhs=xT[:, kt, :], start=(kt == 0), stop=(kt == 7)),
                                     r=[xTt, w1t[kt]], w=[ppt])
                        P.op("act", lambda E, pp=pp: E.activation(out=hr[:], in_=pp[:].rearrange("p a b -> p (a b)"), func=AF.Relu), r=[ppt], w=[hrt])
                        e2 = "pool" if f2 % 2 else "dve"
                        P.op(e2, lambda E, f2=f2: E.tensor_tensor(out=hT[:, f2 * 2:(f2 + 1) * 2, :].rearrange("p a b -> p (a b)"), in0=hr[:], in1=hr[:], op=ALU.mult), r=[hrt], w=[hTt])
                        yield
                    for j in range(GT):
                        tt = gi * GT + j
                        x_t, x_tt = xt[tt % 4]
                        xo_, xot_ = xo[tt % 2]
                        for nh in range(2):
                            pq, pqt = po[nh]
                            for kt in range(32):
                                P.op("pe", lambda E, pq=pq, kt=kt, nh=nh, j=j: E.matmul(pq[:], lhsT=hT[:, kt, j * 128:(j + 1) * 128], rhs=w2b[:, kt, nh * 512:(nh + 1) * 512], start=(kt == 0), stop=(kt == 31)),
                                     r=[hTt, w2t[kt]], w=[pqt])
                            P.op("dve", lambda E, xo_=xo_, pq=pq, nh=nh, x_t=x_t: E.tensor_tensor(out=xo_[:, nh * 512:(nh + 1) * 512], in0=pq[:], in1=x_t[:, nh * 512:(nh + 1) * 512], op=ALU.add), r=[pqt, x_tt], w=[xot_])
                            yield
                        if not final:
                            P.dma("pool", xdst[tt * 128:(tt + 1) * 128, :], xo_[:], xot_, xdtok, xot_)
                        else:
                            sq2, sq2t = ssq[tt % 4]
                            rms_rstd(sc, xo_, xot_, xn, xnt, sq2, sq2[:, 3:4], sq2t)
                            P.op("dve", lambda E, xo_=xo_, sq2=sq2: E.scalar_tensor_tensor(out=xo_[:], in0=xo_[:], scalar=sq2[:, 3:4], in1=fw[:], op0=ALU.mult, op1=ALU.mult), r=[sq2t, fwt], w=[xot_])
                            P.dma("pool", out_d[tt * 128:(tt + 1) * 128, :], xo_[:], xot_, T["out"], xot_)
                        yield

                for _ in gen_prep(0):
                    pass
                for gi in range(NGR):
                    gm = gen_main(gi)
                    gp = gen_prep(gi + 1) if gi + 1 < NGR else iter(())
                    m_done = False
                    p_done = False
                    k = 0
                    while not m_done:
                        try:
                            next(gm)
                        except StopIteration:
                            m_done = True
                        k += 1
                        if not p_done and k % 4 == 0:
                            try:
                                next(gp)
                            except StopIteration:
                                p_done = True
                    if not p_done:
                        for _ in gp:
                            pass

        def phase_rnn(l):
            with Scope(P) as sc:
                prm, prmt = sc.sbt("prm", [128, 64], F32)
                P.dma("sp", prm[:, 0:32], convw[l].rearrange("p a b -> p (a b)"), T["w"], prmt, prmt)
                P.dma("sp", prm[:, 32:40], convb[l], T["w"], prmt, prmt)
                P.dma("sp", prm[:, 40:48], lba[l], T["w"], prmt, prmt)
                P.dma("sp", prm[:, 48:56], lbi[l], T["w"], prmt, prmt)
                P.dma("sp", prm[:, 56:64], llam[l], T["w"], prmt, prmt)
                P.op("act", lambda E: E.activation(out=prm[:, 56:64], in_=prm[:, 56:64], func=AF.Exp, scale=-1.0), r=[prmt], w=[prmt])
                P.op("act", lambda E: E.activation(out=prm[:, 56:64], in_=prm[:, 56:64], func=AF.Ln, bias=epsb[:, 1:2]), r=[prmt, t_const], w=[prmt])
                P.op("dve", lambda E: E.tensor_scalar(out=prm[:, 56:64], in0=prm[:, 56:64], scalar1=-8.0, scalar2=None, op0=ALU.mult), r=[prmt], w=[prmt])
                H = S // 2
                wst = [sc.sbt("wst%d" % i, [128, 128], F32) for i in range(2)]
                wab = [sc.sbt("wab%d" % i, [128, 128], BF16) for i in range(2)]
                wib = [sc.sbt("wib%d" % i, [128, 128], BF16) for i in range(2)]
                sets = []
                for i in range(2):
                    d = {}
                    d["xrp"] = sc.sbt("xrp%d" % i, [128, H + 4], F32)
                    d["gg"] = sc.sbt("gg%d" % i, [128, H], F32)
                    d["xc"] = sc.sbt("xc%d" % i, [128, H], F32)
                    d["xcb"] = sc.sbt("xcb%d" % i, [128, H], BF16)
                    d["rr"] = sc.sbt("rr%d" % i, [128, H], F32)
                    d["ig"] = sc.sbt("ig%d" % i, [128, H], F32)
                    d["aa"] = sc.sbt("aa%d" % i, [128, H], F32)
                    d["ro"] = sc.sbt("ro%d" % i, [128, H], BF16)
                    sets.append(d)
                pa = [sc.pst("pa%d" % i, [128, 512], F32) for i in range(6)]
                pkc = [0]
                wts = {}

                def gen_w(ct):
                    wa_, wat_ = wab[ct % 2]
                    wi_, wit_ = wib[ct % 2]
                    ws_, wst_ = wst[0]
                    ws2, wst2 = wst[1]
                    P.dma("sp", ws_[:], lwa[l, ct], T["w"], wst_, wst_)
                    P.op("dve", lambda E, wa_=wa_, ws_=ws_: E.tensor_copy(out=wa_[:], in_=ws_[:]), r=[wst_], w=[wat_])
                    P.dma("sp", ws2[:], lwi[l, ct], T["w"], wst2, wst2)
                    P.op("dve", lambda E, wi_=wi_, ws2=ws2: E.tensor_copy(out=wi_[:], in_=ws2[:]), r=[wst2], w=[wit_])

                def gen_it(it):
                    ct, hf = it // 2, it % 2
                    if hf == 0:
                        gen_w(ct)
                    wa_, wat_ = wab[ct % 2]
                    wi_, wit_ = wib[ct % 2]
                    d = sets[it % 2]
                    dprev = sets[(it + 1) % 2]
                    xrp, xrpt = d["xrp"]
                    gg, ggt = d["gg"]
                    xc, xct = d["xc"]
                    xcb, xcbt = d["xcb"]
                    rr, rrt = d["rr"]
                    ig, igt = d["ig"]
                    aa, aat = d["aa"]
                    ro, rot = d["ro"]
                    t0 = hf * H
                    if hf == 0:
                        P.op("pool", lambda E, xrp=xrp: E.memset(xrp[:, 0:4], 0.0), w=[xrpt])
                        P.dma("sp", xrp[:, 4:H + 4], zf[0, ct * 128:(ct + 1) * 128, 0:H], T["zf"], xrpt, xrpt)
                    else:
                        P.dma("sp", xrp[:, 0:H + 4], zf[0, ct * 128:(ct + 1) * 128, H - 4:S], T["zf"], xrpt, xrpt)
                    P.dma("sp", gg[:], zf[1, ct * 128:(ct + 1) * 128, t0:t0 + H], T["zf"], ggt, ggt)
                    yield
                    P.op("act", lambda E: E.activation(out=xc[:], in_=xrp[:, 4:H + 4], func=AF.Identity, scale=prm[:, ct * 4 + 3:ct * 4 + 4], bias=prm[:, 32 + ct:33 + ct]), r=[xrpt, prmt], w=[xct])
                    yield
                    for i in range(3):
                        P.op("dve", lambda E, i=i: E.scalar_tensor_tensor(out=xc[:], in0=xrp[:, 1 + i:1 + i + H], scalar=prm[:, ct * 4 + i:ct * 4 + i + 1], in1=xc[:], op0=ALU.mult, op1=ALU.add), r=[xrpt, prmt], w=[xct])
                    yield
                    P.op("pool", lambda E: E.tensor_copy(out=xcb[:], in_=xc[:]), r=[xct], w=[xcbt])
                    yield
                    for tg in range(H // 512):
                        sl = slice(tg * 512, (tg + 1) * 512)
                        p1, p1t = pa[pkc[0] % 6]
                        p2, p2t = pa[(pkc[0] + 1) % 6]
                        pkc[0] += 2
                        P.op("pe", lambda E, p1=p1, sl=sl: E.matmul(p1[:], lhsT=wa_[:], rhs=xcb[:, sl], start=True, stop=True), r=[wat_, xcbt], w=[p1t])
                        P.op("pe", lambda E, p2=p2, sl=sl: E.matmul(p2[:], lhsT=wi_[:], rhs=xcb[:, sl], start=True, stop=True), r=[wit_, xcbt], w=[p2t])
                        P.op("act", lambda E, p1=p1, sl=sl: E.activation(out=rr[:, sl], in_=p1[:], func=AF.Sigmoid, bias=prm[:, 40 + ct:41 + ct]), r=[p1t, prmt], w=[rrt])
                        P.op("act", lambda E, p2=p2, sl=sl: E.activation(out=ig[:, sl], in_=p2[:], func=AF.Sigmoid, bias=prm[:, 48 + ct:49 + ct]), r=[p2t, prmt], w=[igt])
                    yield
                    P.op("act", lambda E: E.activation(out=aa[:], in_=rr[:], func=AF.Exp, scale=prm[:, 56 + ct:57 + ct]), r=[rrt, prmt], w=[aat])
                    P.op("pool", lambda E: E.tensor_tensor(out=ig[:], in0=ig[:], in1=xc[:], op=ALU.mult), r=[xct], w=[igt])
                    yield
                    P.op("pool", lambda E: E.tensor_tensor(out=rr[:], in0=aa[:], in1=aa[:], op=ALU.mult), r=[aat], w=[rrt])
                    yield
                    P.op("act", lambda E: E.activation(out=rr[:], in_=rr[:], func=AF.Sqrt, scale=-1.0, bias=epsb[:, 1:2]), r=[rrt, t_const], w=[rrt])
                    yield
                    P.op("dve", lambda E: E.tensor_tensor(out=ig[:], in0=ig[:], in1=rr[:], op=ALU.mult), r=[rrt], w=[igt])
                    if hf == 0:
                        P.op("dve", lambda E: E.tensor_tensor_scan(out=xc[:], data0=aa[:], data1=ig[:], initial=0.0, op0=ALU.mult, op1=ALU.add), r=[aat, igt], w=[xct])
                    else:
                        xcp, xcpt = dprev["xc"]
                        P.op("dve", lambda E: E.tensor_tensor_scan(out=xc[:], data0=aa[:], data1=ig[:], initial=xcp[:, H - 1:H], op0=ALU.mult, op1=ALU.add), r=[aat, igt, xcpt], w=[xct])
                    yield
                    P.op("pool", lambda E: E.tensor_tensor(out=ro[:], in0=xc[:], in1=gg[:], op=ALU.mult), r=[xct, ggt], w=[rot])
                    P.dma("pool", rnnT[ct * 128:(ct + 1) * 128, t0:t0 + H], ro[:], rot, T["rnnT"], rot)
                    yield

                active = []
                nxt = 0
                while nxt < 16 or active:
                    while len(active) < 2 and nxt < 16:
                        active.append(gen_it(nxt))
                        nxt += 1
                    for gnr in list(active):
                        try:
                            next(gnr)
                        except StopIteration:
                            active.remove(gnr)

        def phase_merge(l, xsrc, xtok, xdst, xdtok):
            with Scope(P) as sc:
                wa = sc.sb("wa", [128, 4, D], BF16)
                wat = [sc.tok() for _ in range(4)]
                wr = sc.sb("wr", [128, 8, D], BF16)
                wrt = [sc.tok() for _ in range(8)]
                wob = sc.sb("wob", [128, 8, D], BF16)
                wot = [sc.tok() for _ in range(8)]
                stg = [sc.sbt("stg%d" % i, [128, 1024], F32) for i in range(3)]
                convert(sc, lambda kt, c0, c1: wa[:, kt, c0:c1], lambda kt, c0, c1: wua[l, kt * 128:(kt + 1) * 128, c0:c1], 4, D, stg, chunk=1024, toks=wat)
                convert(sc, lambda kt, c0, c1: wr[:, kt, c0:c1], lambda kt, c0, c1: wur[l, kt * 128:(kt + 1) * 128, c0:c1], 8, D, stg, chunk=1024, toks=wrt)
                convert(sc, lambda kt, c0, c1: wob[:, kt, c0:c1], lambda kt, c0, c1: wo[l, kt * 128:(kt + 1) * 128, c0:c1], 8, D, stg, chunk=1024, toks=wot)
                aT = [sc.sbt("aT%d" % i, [128, 4, 512], BF16) for i in range(2)]
                rT = [sc.sbt("rT%d" % i, [128, 8, 512], BF16) for i in range(2)]
                sa = [sc.sbt("sa%d" % i, [128, 512], F32) for i in range(2)]
                sb_ = [sc.sbt("sb%d" % i, [128, 512], F32) for i in range(2)]
                t1 = [sc.sbt("t1%d" % i, [128, 512], F32) for i in range(2)]
                t2 = [sc.sbt("t2%d" % i, [128, 512], F32) for i in range(2)]
                mT = [sc.sbt("mT%d" % i, [128, 8, 512], BF16) for i in range(2)]
                pA = [sc.pst("pA%d" % i, [128, 512], F32) for i in range(2)]
                pB = [sc.pst("pB%d" % i, [128, 512], F32) for i in range(2)]
                pO = [sc.pst("pO%d" % i, [128, 512], F32) for i in range(2)]
                xt = [sc.sbt("xt%d" % i, [128, D], F32) for i in range(2)]
                xo = [sc.sbt("xo%d" % i, [128, D], F32) for i in range(2)]
                k = 0
                for sg in range(8):
                    tsl = slice(sg * 512, (sg + 1) * 512)
                    a_, at_ = aT[sg % 2]
                    r_, rt_ = rT[sg % 2]
                    m_, mt_ = mT[sg % 2]
                    P.dma("sp", a_[:], attnT[:, tsl].rearrange("(a p) t -> p a t", p=128), T["attnT"], at_, at_)
                    P.dma("sp", r_[:], rnnT[:, tsl].rearrange("(a p) t -> p a t", p=128), T["rnnT"], rt_, rt_)
                    for ft in range(8):
                        fs = slice(ft * 128, (ft + 1) * 128)
                        sa_, sat_ = sa[k % 2]
                        sbb, sbt_ = sb_[k % 2]
                        u1, u1t = t1[k % 2]
                        u2, u2t = t2[k % 2]
                        p_a, pat = pA[k % 2]
                        p_b, pbt = pB[k % 2]
                        k += 1
                        P.dma("sp", sa_[:], zf[2, fs, tsl], T["zf"], sat_, sat_)
                        P.dma("sp", sbb[:], zf[3, fs, tsl], T["zf"], sbt_, sbt_)
                        for kt in range(4):
                            P.op("pe", lambda E, p_a=p_a, kt=kt, fs=fs, a_=a_: E.matmul(p_a[:], lhsT=wa[:, kt, fs], rhs=a_[:, kt, :], start=(kt == 0), stop=(kt == 3)), r=[wat[kt], at_], w=[pat])
                        for kt in range(8):
                            P.op("pe", lambda E, p_b=p_b, kt=kt, fs=fs, r_=r_: E.matmul(p_b[:], lhsT=wr[:, kt, fs], rhs=r_[:, kt, :], start=(kt == 0), stop=(kt == 7)), r=[wrt[kt], rt_], w=[pbt])
                        P.op("dve", lambda E, u1=u1, p_a=p_a, sa_=sa_: E.tensor_tensor(out=u1[:], in0=p_a[:], in1=sa_[:], op=ALU.mult), r=[pat, sat_], w=[u1t])
                        P.op("dve", lambda E, u2=u2, p_b=p_b, sbb=sbb: E.tensor_tensor(out=u2[:], in0=p_b[:], in1=sbb[:], op=ALU.mult), r=[pbt, sbt_], w=[u2t])
                        P.op("pool", lambda E, m_=m_, ft=ft, u1=u1, u2=u2: E.tensor_tensor(out=m_[:, ft, :], in0=u1[:], in1=u2[:], op=ALU.add), r=[u1t, u2t], w=[mt_])
                    for j in range(4):
                        tt = sg * 4 + j
                        x_t, x_tt = xt[tt % 2]
                        xo_, xot_ = xo[tt % 2]
                        P.dma("sp", x_t[:], xsrc[tt * 128:(tt + 1) * 128, :], xtok, x_tt, x_tt)
                        for nh in range(2):
                            pq, pqt = pO[nh]
                            for kt in range(8):
                                P.op("pe", lambda E, pq=pq, kt=kt, nh=nh, m_=m_, j=j: E.matmul(pq[:], lhsT=m_[:, kt, j * 128:(j + 1) * 128], rhs=wob[:, kt, nh * 512:(nh + 1) * 512], start=(kt == 0), stop=(kt == 7)), r=[mt_, wot[kt]], w=[pqt])
                            P.op("dve", lambda E, xo_=xo_, pq=pq, nh=nh, x_t=x_t: E.tensor_tensor(out=xo_[:, nh * 512:(nh + 1) * 512], in0=pq[:], in1=x_t[:, nh * 512:(nh + 1) * 512], op=ALU.add), r=[pqt, x_tt], w=[xot_])
                        P.dma("pool", xdst[tt * 128:(tt + 1) * 128, :], xo_[:], xot_, xdtok, xot_)


        def phase_attn(l):
            with Scope(P) as sc:
                cst = sc.tok("cst")
                tric = sc.sb("tric", [128, 128], BF16)
                triw = sc.sb("triw", [128, 128], BF16)
                Ec = sc.sb("Ec", [64, S], BF16)
                band = sc.sb("band", [128, 9], BF16)
                cA = sc.sb("cA", [128, 128], F32)
                cB = sc.sb("cB", [128, 128], F32)
                for dst, src in ((tric, c_tric), (triw, c_triw), (Ec, c_E), (band, c_band), (cA, c_A), (cB, c_B)):
                    P.dma("sp", dst[:], src[:, :], T["w"], cst, cst)
                kcm = [sc.sbt("kcm%d" % g, [64, 256], BF16) for g in range(2)]
                vcm = [sc.sbt("vcm%d" % g, [128, 2, 64], BF16) for g in range(2)]
                with Scope(P) as s2:
                    stg = [s2.sbt("cstg%d" % i, [64, 2048], F32) for i in range(2)]
                    w2s, w2st = s2.sbt("w2s", [128, 128], F32)
                    pss, psst = s2.sbt("pss", [64, 64], F32)
                    w1b = s2.sb("w1b", [64, 32, 256], BF16)
                    w1bt = s2.tok()
                    w2b, w2bt = s2.sbt("w2b", [128, 2, 64], BF16)
                    posb, posbt = s2.sbt("posb", [64, 32, 2], BF16)
                    kg = [s2.sbt("kg%d" % g, [64, S], BF16) for g in range(2)]
                    hid = [[s2.sbt("hid%d%d" % (g, h), [128, 256], BF16) for h in range(2)] for g in range(2)]
                    bia, biat = s2.sbt("bia", [128, 2], F32)
                    psg = [s2.pst("psg%d" % g, [128, 512], F32) for g in range(2)]
                    psb, psbt = s2.pst("psb", [128, 512], F32)
                    pso, psot = s2.pst("pso", [128, 512], F32)
                    for (w1d, w2d, posd, srcT, srct, is_k) in ((ckw1, ckw2, posk, kcT, "kcT", True), (cvw1, cvw2, posv, vcT, "vcT", False)):
                        convert(s2, lambda kt, c0, c1: w1b[:].rearrange("p a b -> p (a b)")[:, c0:c1], lambda kt, c0, c1: w1d[l].rearrange("p a b -> p (a b)")[:, c0:c1], 1, 32 * 256, stg, chunk=2048, rows=64, toks=[w1bt])
                        P.dma("sp", w2s[:], w2d[l].rearrange("p a b -> p (a b)"), T["w"], w2st, w2st)
                        P.op("dve", lambda E: E.tensor_copy(out=w2b[:].rearrange("p a b -> p (a b)"), in_=w2s[:]), r=[w2st], w=[w2bt])
                        P.dma("sp", pss[:], posd[l].rearrange("p a b -> p (a b)"), T["w"], psst, psst)
                        P.op("dve", lambda E: E.tensor_copy(out=posb[:].rearrange("p a b -> p (a b)"), in_=pss[:]), r=[psst], w=[posbt])
                        for g in range(2):
                            P.dma("sp", kg[g][0][:], srcT[g], T[srct], kg[g][1], kg[g][1])
                        for ht in range(2):
                            hs = slice(ht * 128, (ht + 1) * 128)
                            for p in range(32):
                                for g in range(2):
                                    P.op("pe", lambda E, g=g, p=p, hs=hs: E.matmul(psg[g][0][:, 0:255], lhsT=w1b[:, p, hs], rhs=kg[g][0][:, p:p + 16 * 254 + 1:16], start=(p == 0), stop=(p == 31)),
                                         r=[w1bt, kg[g][1]], w=[psg[g][1]])
                                P.op("pe", lambda E, p=p, hs=hs: E.matmul(psb[:, 0:2], lhsT=w1b[:, p, hs], rhs=posb[:, p, :], start=(p == 0), stop=(p == 31)), r=[w1bt, posbt], w=[psbt])
                            P.op("dve", lambda E: E.tensor_copy(out=bia[:], in_=psb[:, 0:2]), r=[psbt], w=[biat])
                            for g in range(2):
                                P.op("act", lambda E, g=g, ht=ht: E.activation(out=hid[g][ht][0][:, 0:255], in_=psg[g][0][:, 0:255], func=AF.Gelu_apprx_tanh, bias=bia[:, 0:1]), r=[psg[g][1], biat], w=[hid[g][ht][1]])
                        for g in range(2):
                            if is_k:
                                for ht in range(2):
                                    P.op("pe", lambda E, g=g, ht=ht: E.matmul(pso[0:64, 0:255], lhsT=w2b[:, ht, :], rhs=hid[g][ht][0][:, 0:255], start=(ht == 0), stop=(ht == 1)), r=[w2bt, hid[g][ht][1]], w=[psot])
                                P.op("dve", lambda E, g=g: E.tensor_copy(out=kcm[g][0][:, 0:255], in_=pso[0:64, 0:255]), r=[psot], w=[kcm[g][1]])
                            else:
                                for ctile in range(2):
                                    n = 128 if ctile == 0 else 127
                                    for ht in range(2):
                                        P.op("pe", lambda E, g=g, ht=ht, ctile=ctile, n=n: E.matmul(pso[0:n, 256:320], lhsT=hid[g][ht][0][:, ctile * 128:ctile * 128 + n], rhs=w2b[:, ht, :], start=(ht == 0), stop=(ht == 1)), r=[w2bt, hid[g][ht][1]], w=[psot])
                                    P.op("dve", lambda E, g=g, ctile=ctile, n=n: E.tensor_copy(out=vcm[g][0][0:n, ctile, :], in_=pso[0:n, 256:320]), r=[psot], w=[vcm[g][1]])
                KE = [sc.sbt("KE%d" % g, [128, S], BF16) for g in range(2)]
                kwn = [sc.sbt("kwn%d" % g, [64, S], BF16) for g in range(2)]
                vsl = [sc.sbt("vsl%d" % g, [128, 32, 65], BF16) for g in range(2)]
                vwn = [sc.sbt("vwn%d" % g, [128, 32, 65], BF16) for g in range(2)]
                for g in range(2):
                    P.dma("sp", KE[g][0][0:64, :], ksT[g], T["ksT"], KE[g][1], KE[g][1])
                    P.dma("sp", KE[g][0][64:128, :], c_E[:, :], T["w"], KE[g][1], KE[g][1])
                    P.dma("sp", kwn[g][0][:], kwT[g], T["kwT"], kwn[g][1], kwn[g][1])
                    for (vv, j) in ((vsl[g], g), (vwn[g], 2 + g)):
                        P.op("pool", lambda E, vv=vv: E.memset(vv[0][:, :, 64:65], 1.0), w=[vv[1]])
                        for k8 in range(8):
                            P.dma("sp", vv[0][:, k8 * 4:(k8 + 1) * 4, 0:64], vtm[k8 * 512:(k8 + 1) * 512, j, :].rearrange("(k p) d -> p k d", p=128), T["vtm"], vv[1], vv[1])
                QP = [[sc.sbt("QP%d_%d" % (i, h), [128, 512], BF16) for h in range(8)] for i in range(2)]
                gt = [sc.sbt("gt%d" % i, [128, 4, 24], F32) for i in range(2)]
                bst = [sc.pst("bst%d" % i, [128, 512], F32) for i in range(2)]
                b_os = [sc.pst("b_os%d" % i, [128, 512], F32) for i in range(2)]
                b_ow = [sc.pst("b_ow%d" % i, [128, 512], F32) for i in range(2)]
                b_xs, b_xst = sc.pst("b_xs", [128, 512], F32)
                b_tp = sc.ps("b_tp", [128, 1024], BF16)
                tp_t = sc.tok("tp", True)
                NCH = 4
                CH = []
                for c in range(NCH):
                    d = {}
                    d["ee"] = [sc.sbt("ee%d_%d" % (c, i), [128, 256], F32) for i in range(4)]
                    d["pb"] = [sc.sbt("pb%d_%d" % (c, i), [128, 256], BF16) for i in range(4)]
                    d["pTc"] = [sc.sbt("pTc%d_%d" % (c, i), [128, 128], BF16) for i in range(2)]
                    d["sm"] = sc.sbt("sm%d" % c, [128, 16], F32)
                    d["P4"] = sc.sbt("P4_%d" % c, [128, 264], F32)
                    d["imp"] = sc.sbt("imp%d" % c, [128, 64], F32)
                    d["scr"] = sc.sbt("scr%d" % c, [128, 64], F32)
                    d["scr2"] = sc.sbt("scr2_%d" % c, [128, 64], F32)
                    d["m8"] = sc.sbt("m8_%d" % c, [128, 16], F32)
                    d["penb"] = sc.sbt("penb%d" % c, [128, 128], BF16)
                    P.op("pool", lambda E, d=d: E.memset(d["penb"][0][:], 0.0), w=[d["penb"][1]])
                    P.op("dve", lambda E, d=d: E.memset(d["P4"][0][:], 0.0), w=[d["P4"][1]])
                    CH.append(d)
                pT = [sc.sbt("pT%d" % i, [128, 512], BF16) for i in range(4)]
                sy, syt = sc.sbt("sy", [128, 8], F32)
                att2 = [sc.sbt("att%d" % i, [128, 4, 512], F32) for i in range(2)]
                attb, attbt = sc.sbt("attb", [128, 4, 512], BF16)
                ast = [sc.sbt("ast%d" % i, [128, 4, 512], BF16) for i in range(2)]
                sidx = [0]
                NSG = ATT_DBG["nqb"] // 4

                def gen_X(sg):
                    ss = slice(sg * 512, (sg + 1) * 512)
                    qp = QP[sg % 2]
                    g_, gt_ = gt[sg % 2]
                    for h in range(8):
                        P.dma("sp", qp[h][0][0:64, :], qT[h, :, ss], T["qT"], qp[h][1], qp[h][1])
                    P.dma("sp", g_[:], gat[ss, :].rearrange("(j p) c -> p j c", p=128), T["gat"], gt_, gt_)
                    yield
                    for g in range(2):
                        act = [gen_Xchain(sg, g, j) for j in range(4)]
                        while act:
                            for gn in list(act):
                                try:
                                    next(gn)
                                    yield
                                except StopIteration:
                                    act.remove(gn)

                def gen_Xchain(sg, g, j):
                    qp = QP[sg % 2]
                    g_, gt_ = gt[sg % 2]
                    att, attt = att2[sg % 2]
                    d = CH[j % NCH]
                    ee, pb, pTc = d["ee"], d["pb"], d["pTc"]
                    sm, smt = d["sm"]
                    P4, P4t = d["P4"]
                    imp, impt = d["imp"]
                    scr, scrt = d["scr"]
                    scr2, scr2t = d["scr2"]
                    m8, m8t = d["m8"]
                    penb, penbt = d["penb"]
                    P4v = P4[:, 0:256].rearrange("p (j f) -> p j f", f=4)
                    P4w = P4[:, 4:260].rearrange("p (j f) -> p j f", f=4)
                    if True:
                        if True:
                            qb = sg * 4 + j
                            js = slice(j * 128, (j + 1) * 128)
                            Nc = min(255, 8 * qb + 7)
                            cb0 = max(0, 8 * qb - 2)
                            cb1 = min(Nc, 8 * qb + 7)
                            ps_s = b_xs[:, 0:256]
                            for hh in range(4):
                                h = g * 4 + hh
                                P.op("pe", lambda E, h=h, g=g, js=js, Nc=Nc: E.matmul(ps_s[:, 0:Nc], lhsT=qp[h][0][0:64, js], rhs=kcm[g][0][:, 0:Nc], start=True, stop=False), r=[qp[h][1], kcm[g][1]], w=[b_xst])
                                P.op("pe", lambda E, cb0=cb0, cb1=cb1, qb=qb: E.matmul(ps_s[:, cb0:cb1], lhsT=ident[:], rhs=band[:, cb0 - (8 * qb - 2):cb1 - (8 * qb - 2)], start=False, stop=True), r=[t_const, cst], w=[b_xst])
                                e_, et_ = ee[hh]
                                P.op("act", lambda E, e_=e_, hh=hh, Nc=Nc: E.activation(out=e_[:, 0:Nc], in_=ps_s[:, 0:Nc], func=AF.Exp, accum_out=sm[:, 8 + hh:9 + hh]), r=[b_xst], w=[et_, smt])
                                yield
                            P.op("dve", lambda E: E.tensor_scalar(out=sm[:, 12:16], in0=sm[:, 8:12], scalar1=1e-20, scalar2=None, op0=ALU.max), r=[smt], w=[smt])
                            P.op("dve", lambda E: E.reciprocal(out=sm[:, 12:16], in_=sm[:, 12:16]), r=[smt], w=[smt])
                            P.op("dve", lambda E: E.tensor_tensor(out=sm[:, 0:4], in0=sm[:, 12:16], in1=g_[:, j, g * 12:(g + 1) * 12].rearrange("p (h c) -> p h c", c=3)[:, :, 0], op=ALU.mult), r=[smt, gt_], w=[smt])
                            for hh in range(4):
                                e_, et_ = ee[hh]
                                if hh == 0:
                                    P.op("dve", lambda E, e_=e_, hh=hh, Nc=Nc: E.tensor_scalar(out=P4[:, 4:4 + Nc], in0=e_[:, 0:Nc], scalar1=sm[:, 12 + hh:13 + hh], scalar2=None, op0=ALU.mult), r=[et_, smt], w=[P4t])
                                else:
                                    P.op("dve", lambda E, e_=e_, hh=hh, Nc=Nc: E.scalar_tensor_tensor(out=P4[:, 4:4 + Nc], in0=e_[:, 0:Nc], scalar=sm[:, 12 + hh:13 + hh], in1=P4[:, 4:4 + Nc], op0=ALU.mult, op1=ALU.add), r=[et_, smt], w=[P4t])
                            yield
                            P.op("dve", lambda E: E.tensor_tensor(out=imp[:], in0=P4v[:, :, 1], in1=P4v[:, :, 2], op=ALU.add), r=[P4t], w=[impt])
                            P.op("dve", lambda E: E.tensor_tensor(out=imp[:], in0=imp[:], in1=P4v[:, :, 3], op=ALU.add), r=[P4t], w=[impt])
                            P.op("dve", lambda E: E.scalar_tensor_tensor(out=imp[:], in0=imp[:], scalar=2.0, in1=P4v[:, :, 0], op0=ALU.mult, op1=ALU.add), r=[P4t], w=[impt])
                            P.op("dve", lambda E: E.tensor_tensor(out=imp[:], in0=imp[:], in1=P4w[:, :, 0], op=ALU.add), r=[P4t], w=[impt])
                            P.op("dve", lambda E, qb=qb: E.tensor_tensor(out=scr[:], in0=imp[:], in1=cA[:, 64 - 2 * qb:128 - 2 * qb], op=ALU.mult), r=[impt, cst], w=[scrt])
                            P.op("dve", lambda E, qb=qb: E.tensor_tensor(out=scr[:], in0=scr[:], in1=cB[:, 64 - 2 * qb:128 - 2 * qb], op=ALU.add), r=[cst], w=[scrt])
                            P.op("dve", lambda E: E.memset(scr[:, 0:1], 1e4), w=[scrt])
                            yield
                            nb = min(64, 2 * qb + 2)
                            if nb > 16:
                                P.op("dve", lambda E: E.max(out=m8[:, 0:8], in_=scr[:]), r=[scrt], w=[m8t])
                                P.op("dve", lambda E: E.match_replace(out=scr2[:], in_to_replace=m8[:, 0:8], in_values=scr[:], imm_value=-3e38), r=[scrt, m8t], w=[scr2t])
                                P.op("dve", lambda E: E.max(out=m8[:, 8:16], in_=scr2[:]), r=[scr2t], w=[m8t])
                                P.op("dve", lambda E, nb=nb: E.tensor_scalar(out=scr2[:, 0:nb], in0=scr[:, 0:nb], scalar1=m8[:, 15:16], scalar2=None, op0=ALU.is_ge), r=[scrt, m8t], w=[scr2t])
                                P.op("dve", lambda E, nb=nb: E.tensor_scalar(out=penb[:, 64:64 + nb], in0=scr2[:, 0:nb], scalar1=-1.0, scalar2=-NEGM, op0=ALU.add, op1=ALU.mult), r=[scr2t], w=[penbt])
                                yield
                            ptr = b_tp[:, 256:384]
                            P.op("pe", lambda E, ptr=ptr: E.transpose(out=ptr, in_=penb[:], identity=ident[:]), r=[penbt, t_const], w=[tp_t])
                            h0 = g * 4
                            P.op("act", lambda E, ptr=ptr, js=js, h0=h0: E.copy(out=qp[h0][0][64:128, js], in_=ptr[64:128, :]), r=[tp_t], w=[qp[h0][1]])
                            for hh in range(1, 4):
                                P.op("pool", lambda E, js=js, h0=h0, hh=hh: E.tensor_copy(out=qp[h0 + hh][0][64:128, js], in_=qp[h0][0][64:128, js]), r=[qp[h0][1]], w=[qp[h0 + hh][1]])
                            yield
                            k2 = 0
                            tpf = b_tp[:].bitcast(F32)
                            for hh in range(4):
                                h = g * 4 + hh
                                e_, et_ = ee[hh]
                                nct = (Nc + 127) // 128
                                for ctile in range(nct):
                                    n = min(128, Nc - ctile * 128)
                                    tpc = tpf[:, (k2 % 2) * 128:(k2 % 2) * 128 + 128]
                                    pc_, pct_ = pTc[k2 % 2]
                                    k2 += 1
                                    P.op("pe", lambda E, tpc=tpc, e_=e_, ctile=ctile, n=n: E.transpose(out=tpc[0:n, :], in_=e_[:, ctile * 128:ctile * 128 + n], identity=identf[:]), r=[et_, t_const], w=[tp_t])
                                    P.op("act", lambda E, pc_=pc_, tpc=tpc, n=n: E.copy(out=pc_[0:n, :], in_=tpc[0:n, :]), r=[tp_t], w=[pct_])
                                    P.op("pe", lambda E, pc_=pc_, n=n, ctile=ctile, nct=nct: E.matmul(b_xs[:, 256:320], lhsT=pc_[0:n, :], rhs=vcm[g][0][0:n, ctile, :], start=(ctile == 0), stop=(ctile == nct - 1), skip_group_check=True), r=[pct_, vcm[g][1]], w=[b_xst])
                                P.op("dve", lambda E, h=h, hh=hh: E.tensor_scalar(out=att[:, j, h * 64:(h + 1) * 64], in0=b_xs[:, 256:320], scalar1=sm[:, hh:hh + 1], scalar2=None, op0=ALU.mult), r=[b_xst, smt], w=[attt])
                                yield

                def gen_Y(sg):
                    ss = slice(sg * 512, (sg + 1) * 512)
                    qp = QP[sg % 2]
                    g_, gt_ = gt[sg % 2]
                    att, attt = att2[sg % 2]
                    pending = []
                    for g in range(2):
                        for hh in range(4):
                            h = g * 4 + hh
                            q_, qt_ = qp[h]
                            os_, ost_ = b_os[hh % 2]
                            ow_, owt_ = b_ow[hh % 2]
                            steps = []
                            for kt in range(0, 4 * sg + 4):
                                steps.append(("s", kt))
                            for kt in range(max(0, 4 * sg - 4), 4 * sg + 4):
                                steps.append(("w", kt))
                            first = {"s": True, "w": True}
                            LA = 1
                            ring = []
                            for i in range(len(steps) + LA):
                                if i == min(3, len(steps) - 1) and pending:
                                    pending.pop(0)()
                                if i < len(steps):
                                    br, kt = steps[i]
                                    r_ = kt - 4 * sg
                                    if br == "s":
                                        jlo, jhi = max(r_, 0), 3
                                    else:
                                        jlo, jhi = max(r_, 0), min(r_ + 4, 3)
                                    c0, c1 = jlo * 128, (jhi + 1) * 128
                                    si = sidx[0] % 4
                                    stp = bst[sidx[0] % 2][0]
                                    stt_ = bst[sidx[0] % 2][1]
                                    sidx[0] += 1
                                    ks_ = slice(kt * 128, (kt + 1) * 128)
                                    ex = []
                                    if r_ >= 0:
                                        ex.append((r_, tric))
                                    if br == "w" and 0 <= r_ + 4 <= 3:
                                        ex.append((r_ + 4, triw))
                                    if br == "s":
                                        P.op("pe", lambda E, stp=stp, ks_=ks_, q_=q_, g=g, c0=c0, c1=c1, ex=ex: E.matmul(stp[:, c0:c1], lhsT=KE[g][0][:, ks_], rhs=q_[:, c0:c1], start=True, stop=(len(ex) == 0)), r=[KE[g][1], qt_], w=[stt_])
                                    else:
                                        P.op("pe", lambda E, stp=stp, ks_=ks_, q_=q_, g=g, c0=c0, c1=c1, ex=ex: E.matmul(stp[:, c0:c1], lhsT=kwn[g][0][:, ks_], rhs=q_[0:64, c0:c1], start=True, stop=(len(ex) == 0)), r=[kwn[g][1], qt_], w=[stt_])
                                    for xi, (jj, tri_) in enumerate(ex):
                                        P.op("pe", lambda E, stp=stp, jj=jj, tri_=tri_, xi=xi, ex=ex: E.matmul(stp[:, jj * 128:(jj + 1) * 128], lhsT=ident[:], rhs=tri_[:], start=False, stop=(xi == len(ex) - 1)), r=[t_const, cst], w=[stt_])
                                    pt_s, pt_st = pT[si]
                                    P.op("act", lambda E, pt_s=pt_s, stp=stp, c0=c0, c1=c1: E.activation(out=pt_s[:, c0:c1], in_=stp[:, c0:c1], func=AF.Exp), r=[stt_], w=[pt_st])
                                    ring.append((br, kt, jlo, jhi, pt_s, pt_st))
                                if i - LA >= 0:
                                    br, kt, jlo, jhi, pt_s, pt_st = ring[i - LA]
                                    ob, obt = (os_, ost_) if br == "s" else (ow_, owt_)
                                    V = vsl[g] if br == "s" else vwn[g]
                                    for jj in range(jlo, jhi + 1):
                                        st_flag = first[br]
                                        first[br] = False
                                        P.op("pe", lambda E, ob=ob, jj=jj, pt_s=pt_s, V=V, kt=kt, st_flag=st_flag: E.matmul(ob[:, jj * 65:jj * 65 + 65], lhsT=pt_s[:, jj * 128:(jj + 1) * 128], rhs=V[0][:, kt, :], start=st_flag, stop=True, skip_group_check=True), r=[pt_st, V[1]], w=[obt])
                                yield
                            def combine(h=h, os_=os_, ost_=ost_, ow_=ow_, owt_=owt_):
                                osv = os_[:, 0:260].rearrange("p (j c) -> p j c", c=65)
                                owv = ow_[:, 0:260].rearrange("p (j c) -> p j c", c=65)
                                P.op("dve", lambda E: E.reciprocal(out=sy[:, 0:4], in_=osv[:, :, 64]), r=[ost_], w=[syt])
                                P.op("dve", lambda E: E.tensor_tensor(out=sy[:, 0:4], in0=sy[:, 0:4], in1=g_[:, :, h * 3 + 1], op=ALU.mult), r=[gt_], w=[syt])
                                P.op("dve", lambda E: E.reciprocal(out=sy[:, 4:8], in_=owv[:, :, 64]), r=[owt_], w=[syt])
                                P.op("dve", lambda E: E.tensor_tensor(out=sy[:, 4:8], in0=sy[:, 4:8], in1=g_[:, :, h * 3 + 2], op=ALU.mult), r=[gt_], w=[syt])
                                cs_ = slice(h * 64, (h + 1) * 64)
                                for jj in range(4):
                                    P.op("dve", lambda E, jj=jj: E.scalar_tensor_tensor(out=att[:, jj, cs_], in0=os_[:, jj * 65:jj * 65 + 64], scalar=sy[:, jj:jj + 1], in1=att[:, jj, cs_], op0=ALU.mult, op1=ALU.add), r=[ost_, syt], w=[attt])
                                    P.op("dve", lambda E, jj=jj: E.scalar_tensor_tensor(out=attb[:, jj, cs_], in0=ow_[:, jj * 65:jj * 65 + 64], scalar=sy[:, 4 + jj:5 + jj], in1=att[:, jj, cs_], op0=ALU.mult, op1=ALU.add), r=[owt_, syt, attt], w=[attbt])
                            pending.append(combine)
                            yield
                    while pending:
                        pending.pop(0)()
                    yield
                    a_, at_ = ast[sg % 2]
                    atr = b_tp[:, 384:896].rearrange("p (a b) -> p a b", b=128)
                    for jj in range(4):
                        for ft in range(4):
                            P.op("pe", lambda E, ft=ft, jj=jj, atr=atr: E.transpose(out=atr[:, ft, :], in_=attb[:, jj, ft * 128:(ft + 1) * 128], identity=ident[:]), r=[attbt, t_const], w=[tp_t])
                        P.op("act", lambda E, a_=a_, atr=atr, jj=jj: E.copy(out=a_[:, :, jj * 128:(jj + 1) * 128], in_=atr), r=[tp_t], w=[at_])
                        yield
                    P.dma("pool", attnT[:, ss].rearrange("(a p) t -> p a t", p=128), a_[:], at_, T["attnT"], at_)
                    yield

                def drain(gn):
                    for _ in gn:
                        pass

                n2 = sc.sb("n2", [128, 8], F32)
                n2t = sc.tok()
                P.dma("sp", n2[:], n2w[l], T["w"], n2t, n2t)
                wsf = [sc.sbt("wsf%d" % i, [128, 1024], F32) for i in range(3)]
                wsb = [sc.sbt("wsb%d" % i, [128, 1024], BF16) for i in range(3)]

                def gen_wprep():
                    chunks = []
                    for kt in range(8):
                        for c in range(4):
                            chunks.append((w1[l, kt * 128:(kt + 1) * 128, c * 1024:(c + 1) * 1024], w1dr[kt * 128:(kt + 1) * 128, c * 1024:(c + 1) * 1024], "w1s", kt))
                    for kt in range(32):
                        chunks.append((w2[l, kt * 128:(kt + 1) * 128, :], w2dr[kt * 128:(kt + 1) * 128, :], "w2s", None))
                    PRE = 2
                    for k in range(len(chunks) + PRE):
                        if k < len(chunks):
                            src, dst, dn, sk = chunks[k]
                            f_, ft_ = wsf[k % 3]
                            P.dma("pool", f_[:], src, T["w"], ft_, ft_)
                        if k - PRE >= 0:
                            src, dst, dn, sk = chunks[k - PRE]
                            f_, ft_ = wsf[(k - PRE) % 3]
                            b_, bt_ = wsb[(k - PRE) % 3]
                            if sk is not None:
                                P.op("pool", lambda E, b_=b_, f_=f_, sk=sk: E.tensor_scalar(out=b_[:], in0=f_[:], scalar1=n2[:, sk:sk + 1], scalar2=1.0, op0=ALU.mult, op1=ALU.mult), r=[ft_, n2t], w=[bt_])
                            else:
                                P.op("pool", lambda E, b_=b_, f_=f_: E.tensor_copy(out=b_[:], in_=f_[:]), r=[ft_], w=[bt_])
                            P.dma("pool", dst, b_[:], bt_, T[dn], bt_)
                        yield

                gw = gen_wprep()
                gw_done = [False]

                def adv_w():
                    if not gw_done[0]:
                        try:
                            next(gw)
                        except StopIteration:
                            gw_done[0] = True

                if NSG > 0:
                    drain(gen_X(0))
                for sg in range(NSG):
                    gy = gen_Y(sg) if not ATT_DBG.get("skip_sw") else iter(())
                    gx = gen_X(sg + 1) if sg + 1 < NSG else iter(())
                    ratio = max(1, int(round((32 * sg + 110) / 100.0)))
                    x_done = False
                    y_done = False
                    yk = 0
                    while not y_done:
                        for _ in range(ratio):
                            try:
                                next(gy)
                            except StopIteration:
                                y_done = True
                                break
                            yk += 1
                            if yk % 12 == 0:
                                adv_w()
                        if not x_done:
                            try:
                                next(gx)
                            except StopIteration:
                                x_done = True
                    if not x_done:
                        drain(gx)
                drain(gw)

        PH = {"inproj": phase_inproj, "mlp": phase_mlp, "rnn": phase_rnn, "merge": phase_merge, "attn": phase_attn}
        build.phases = PH
        build.ctx = dict(P=P, T=T, nc=nc, xs=xs, x_in=x_in, out_d=out_d)
        plan = build.plan
        plan(PH, build.ctx, locals())
        P.barrier()
        print("ops", P.nops, "waits", P.nwait)
    return nc, dbg


def default_plan(PH, ctx, L):
    T = ctx["T"]
    xs = ctx["xs"]
    cur, curt = ctx["x_in"], T["x_in"]
    for l in range(2):
        PH["inproj"](l, cur, curt)
        PH["attn"](l)
        PH["rnn"](l)
        PH["merge"](l, cur, curt, xs[0], T["xs0"])
        PH["mlp"](l, xs[0], T["xs0"], xs[1], T["xs1"], l == 1)
        cur, curt = xs[1], T["xs1"]


build.plan = default_plan


def host_inputs(inp, b):
    bf = ml_dtypes.bfloat16
    f = np.float32

    def pk(v):
        return np.ascontiguousarray(v.reshape(2, 8, 128).transpose(0, 2, 1)).astype(f)

    def bd(wm):
        o = np.zeros((2, 8, 128, 128), f)
        for c in range(8):
            o[:, c, 0:64, 0:64] = wm[:, 2 * c]
            o[:, c, 64:128, 64:128] = wm[:, 2 * c + 1]
        return o

    i_ = np.arange(128)
    m = {
        "x": np.ascontiguousarray(inp["x"][b]),
        "w_in": inp["w_in"],
        "n1w": pk(inp["norm1_w"]), "n2w": pk(inp["norm2_w"]),
        "fnw": np.ascontiguousarray(np.broadcast_to(inp["final_norm_w"][None, :], (128, D))).astype(f),
        "posk": np.ascontiguousarray(np.repeat(inp["cmp_pos_k"].transpose(0, 2, 1)[..., None], 2, axis=-1)),
        "posv": np.ascontiguousarray(np.repeat(inp["cmp_pos_v"].transpose(0, 2, 1)[..., None], 2, axis=-1)),
        "ckw1": np.ascontiguousarray(inp["cmp_k_w1"].reshape(2, 32, 64, 256).transpose(0, 2, 1, 3)),
        "cvw1": np.ascontiguousarray(inp["cmp_v_w1"].reshape(2, 32, 64, 256).transpose(0, 2, 1, 3)),
        "ckw2": np.ascontiguousarray(inp["cmp_k_w2"].reshape(2, 2, 128, 64).transpose(0, 2, 1, 3)),
        "cvw2": np.ascontiguousarray(inp["cmp_v_w2"].reshape(2, 2, 128, 64).transpose(0, 2, 1, 3)),
        "convw": np.ascontiguousarray(inp["conv_w"].reshape(2, 4, 8, 128).transpose(0, 3, 2, 1)),
        "convb": pk(inp["conv_b"]), "lba": pk(inp["lru_b_a"]), "lbi": pk(inp["lru_b_i"]), "llam": pk(inp["lru_lambda"]),
        "lwa": bd(inp["lru_w_a"]), "lwi": bd(inp["lru_w_i"]),
        "wua": inp["w_up_attn"], "wur": inp["w_up_rnn"], "wo": inp["w_out"], "w1": inp["mlp_w1"], "w2": inp["mlp_w2"],
        "c_ident": np.eye(128, dtype=f).astype(bf),
        "c_tric": np.where(i_[:, None] <= i_[None, :], 0.0, NEGM).astype(bf),
        "c_triw": np.where(i_[:, None] > i_[None, :], 0.0, NEGM).astype(bf),
        "c_E": (np.arange(S)[None, :] // 64 == np.arange(64)[:, None]).astype(f).astype(bf),
        "c_band": np.where((np.arange(9)[None, :] - 2) <= ((i_[:, None] + 1) // 16 - 2), 0.0, NEGM).astype(bf),
    }
    hi = (i_ >= 64).astype(np.int64)[:, None]
    jp = (np.arange(128) - 64)[None, :]
    valid = jp <= hi
    forced = jp > hi - 2
    A = np.where(valid & ~forced, 1.0, 0.0)
    Bm = np.where(valid, np.where(forced, 1e4, 0.0), -1e30)
    m["c_A"] = A.astype(f)
    m["c_B"] = Bm.astype(f)
    return {k: np.ascontiguousarray(v) for k, v in m.items()}


def kernel(**inputs):
    inp = {k: np.asarray(v) for k, v in inputs.items()}
    nc, _ = build(False)
    in_maps = [host_inputs(inp, c % 4) for c in range(8)]
    res = run_bass_kernel_spmd(nc, in_maps, core_ids=list(range(8)))
    return np.stack([np.asarray(res.results[c]["out"]) for c in range(4)], axis=0).astype(np.float32)
```

```python
import numpy as np
import ml_dtypes
from contextlib import ExitStack
import concourse.bass as bass
import concourse.mybir as mybir
from concourse.bass_utils import run_bass_kernel_spmd

F32 = mybir.dt.float32
BF16 = mybir.dt.bfloat16
AF = mybir.ActivationFunctionType
ALU = mybir.AluOpType
AX = mybir.AxisListType

S = 4096
D = 1024
DIN = 5400
NT = S // 128
NEGM = -30000.0
EPS = 1e-6
O_Q, O_KC, O_VC, O_KS, O_VS, O_KW, O_VW, O_GN, O_XR, O_GR, O_GA, O_GB = 0, 512, 640, 768, 896, 1024, 1152, 1280, 1304, 2328, 3352, 4376


ATT_DBG = {"level": 6, "nqb": NT}


class Tok:
    __slots__ = ("w", "r", "sem", "name", "x")

    def __init__(self, name="", x=False):
        self.w = {}
        self.r = {}
        self.sem = None
        self.name = name
        self.x = x


class Prog:
    ENG = ("pe", "act", "dve", "pool", "sp")

    def __init__(self, nc, es, n_dma_sems=80):
        self.nc = nc
        self.eng = {"pe": nc.tensor, "act": nc.scalar, "dve": nc.vector, "pool": nc.gpsimd, "sp": nc.sync}
        self.sems = []
        self.esem = {}
        for e in self.ENG:
            self.esem[e] = len(self.sems)
            self.sems.append(es.enter_context(nc.semaphore("es_" + e)))
        self.dma_ids = []
        for i in range(n_dma_sems):
            self.dma_ids.append(len(self.sems))
            self.sems.append(es.enter_context(nc.semaphore("ds_%d" % i)))
        self.free = list(self.dma_ids)
        self.total = [0] * len(self.sems)
        self.known = {e: [0] * len(self.sems) for e in self.ENG}
        self.nwait = 0
        self.nops = 0

    def _wait(self, eng, deps):
        E = self.eng[eng]
        kn = self.known[eng]
        for s, v in deps.items():
            if s >= 5:
                v = self.total[s]
            if kn[s] < v:
                kn[s] = v
                E.wait_ge(self.sems[s], v)
                self.nwait += 1

    @staticmethod
    def _merge(d, src):
        for s, v in src.items():
            if d.get(s, 0) < v:
                d[s] = v

    def op(self, eng, fn, r=(), w=()):
        deps = {}
        rx = [b for b in r if b.x]
        if rx:
            r = [b for b in r if not b.x]
            w = list(w) + rx
        for b in r:
            self._merge(deps, b.w)
        for b in w:
            self._merge(deps, b.w)
            self._merge(deps, b.r)
        s = self.esem[eng]
        if eng == "pe":
            deps.pop(s, None)
        self._wait(eng, deps)
        self.total[s] += 1
        n = self.total[s]
        fn(self.eng[eng]).then_inc(self.sems[s], 1)
        self.nops += 1
        for b in r:
            b.r[s] = n
        for b in w:
            b.w[s] = n

    def dma(self, eng, out, in_, src, dst, owner):
        deps = {}
        self._merge(deps, src.w)
        self._merge(deps, dst.w)
        self._merge(deps, dst.r)
        self._wait(eng, deps)
        if owner.sem is None:
            owner.sem = self.free.pop()
        s = owner.sem
        self.total[s] += 16
        v = self.total[s]
        self.eng[eng].dma_start(out=out, in_=in_).then_inc(self.sems[s], 16)
        self.nops += 1
        src.r[s] = v
        dst.w[s] = v

    def release(self, toks):
        for t in toks:
            if t.sem is not None:
                self.free.append(t.sem)
                t.sem = None

    def barrier(self):
        for e in self.ENG:
            E = self.eng[e]
            kn = self.known[e]
            for s in range(len(self.sems)):
                if s == self.esem[e]:
                    continue
                v = self.total[s]
                if kn[s] < v:
                    kn[s] = v
                    E.wait_ge(self.sems[s], v)
        arr = {}
        for e in self.ENG:
            s = self.esem[e]
            self.total[s] += 1
            arr[e] = self.total[s]
            if e == "pe":
                self.eng[e].nop().then_inc(self.sems[s], 1) if hasattr(self.eng[e], "nop") else None
            else:
                self.eng[e].nop().then_inc(self.sems[s], 1)
        for e in self.ENG:
            for f in self.ENG:
                if f == e:
                    continue
                s = self.esem[f]
                self.known[e][s] = arr[f]
                self.eng[e].wait_ge(self.sems[s], arr[f])


class Scope:
    def __init__(self, P):
        self.P = P
        self.es = ExitStack()
        self.toks = []

    def __enter__(self):
        self.es.__enter__()
        return self

    def __exit__(self, *a):
        self.P.barrier()
        self.P.release(self.toks)
        return self.es.__exit__(*a)

    uid = [0]

    def sb(self, name, shape, dt):
        Scope.uid[0] += 1
        return self.es.enter_context(self.P.nc.sbuf_tensor("%s_%d" % (name, Scope.uid[0]), list(shape), dt))

    def ps(self, name, shape, dt):
        Scope.uid[0] += 1
        return self.es.enter_context(self.P.nc.psum_tensor("%s_%d" % (name, Scope.uid[0]), list(shape), dt))

    def tok(self, name="", x=False):
        t = Tok(name, x)
        self.toks.append(t)
        return t

    def sbt(self, name, shape, dt):
        return self.sb(name, shape, dt), self.tok(name)

    def pst(self, name, shape, dt):
        return self.ps(name, shape, dt), self.tok(name, True)


def build(debug=False):
    nc = bass.Bass("TRN2", target_bir_lowering=False)
    dbg = {}

    def din(name, shape, dt=F32):
        return nc.dram_tensor(name, list(shape), dt, kind="ExternalInput").ap()

    def dscr(name, shape, dt):
        isd = bool(debug) and (debug is True or name in debug)
        kind = "ExternalOutput" if isd else "Internal"
        t = nc.dram_tensor(name, list(shape), dt, kind=kind).ap()
        if isd:
            dbg[name] = t
        return t

    x_in = din("x", [S, D])
    out_d = nc.dram_tensor("out", [S, D], F32, kind="ExternalOutput").ap()
    w_in = din("w_in", [2, D, DIN])
    n1w = din("n1w", [2, 128, 8])
    n2w = din("n2w", [2, 128, 8])
    fnw = din("fnw", [128, D])
    posk = din("posk", [2, 64, 32, 2])
    posv = din("posv", [2, 64, 32, 2])
    ckw1 = din("ckw1", [2, 64, 32, 256])
    cvw1 = din("cvw1", [2, 64, 32, 256])
    ckw2 = din("ckw2", [2, 128, 2, 64])
    cvw2 = din("cvw2", [2, 128, 2, 64])
    convw = din("convw", [2, 128, 8, 4])
    convb = din("convb", [2, 128, 8])
    lba = din("lba", [2, 128, 8])
    lbi = din("lbi", [2, 128, 8])
    llam = din("llam", [2, 128, 8])
    lwa = din("lwa", [2, 8, 128, 128])
    lwi = din("lwi", [2, 8, 128, 128])
    wua = din("wua", [2, 512, D])
    wur = din("wur", [2, D, D])
    wo = din("wo", [2, D, D])
    w1 = din("w1", [2, D, 4096])
    w2 = din("w2", [2, 4096, D])
    c_ident = din("c_ident", [128, 128], BF16)
    c_tric = din("c_tric", [128, 128], BF16)
    c_triw = din("c_triw", [128, 128], BF16)
    c_E = din("c_E", [64, S], BF16)
    c_band = din("c_band", [128, 9], BF16)
    c_A = din("c_A", [128, 128])
    c_B = din("c_B", [128, 128])

    xs = [dscr("xs0", [S, D], F32), dscr("xs1", [S, D], F32)]
    qT = dscr("qT", [8, 64, S], BF16)
    kcT = dscr("kcT", [2, 64, S], BF16)
    vcT = dscr("vcT", [2, 64, S], BF16)
    ksT = dscr("ksT", [2, 64, S], BF16)
    kwT = dscr("kwT", [2, 64, S], BF16)
    vtm = dscr("vtm", [S, 4, 64], BF16)
    gat = dscr("gat", [S, 24], F32)
    zf = dscr("zf", [4, D, S], F32)
    winS = dscr("winS", [D, DIN], BF16)
    wuaS = dscr("wuaS", [512, D], BF16)
    wurS = dscr("wurS", [D, D], BF16)
    woS = dscr("woS", [D, D], BF16)
    w1dr = dscr("w1s", [D, 4096], BF16)
    w2dr = dscr("w2s", [4096, D], BF16)
    attnT = dscr("attnT", [512, S], BF16)
    rnnT = dscr("rnnT", [D, S], BF16)

    with ExitStack() as es:
        P = Prog(nc, es)
        T = {n: Tok(n) for n in ["x_in", "out", "w", "xs0", "xs1", "qT", "kcT", "vcT", "ksT", "kwT", "vtm", "gat", "zf", "attnT", "rnnT", "w1s", "w2s", "winS", "wmS"]}
        for t in T.values():
            t.sem = None

        ident = es.enter_context(nc.sbuf_tensor("ident", [128, 128], BF16))
        identf = es.enter_context(nc.sbuf_tensor("identf", [128, 128], F32))
        t_const = Tok("const")
        P.dma("sp", ident[:], c_ident[:, :], T["w"], t_const, t_const)
        P.op("dve", lambda E: E.tensor_copy(out=identf[:], in_=ident[:]), r=[t_const], w=[t_const])

        def convert(sc, dst_ap_fn, src_ap_fn, nrow_tiles, ncols, stg, scale_ap_fn=None, chunk=2048, rows=128, toks=None):
            i = 0
            engs = ("act", "dve", "pool")
            for kt in range(nrow_tiles):
                for c0 in range(0, ncols, chunk):
                    c1 = min(ncols, c0 + chunk)
                    st, stt = stg[i % len(stg)]
                    P.dma("sp", st[0:rows, 0:c1 - c0], src_ap_fn(kt, c0, c1), T["w"], stt, stt)
                    e = engs[i % 3]
                    tk = toks[kt] if toks is not None else None
                    dst = dst_ap_fn(kt, c0, c1)
                    src = st[0:rows, 0:c1 - c0]
                    if scale_ap_fn is None:
                        if e == "act":
                            P.op(e, lambda E, dst=dst, src=src: E.copy(out=dst, in_=src), r=[stt], w=[tk])
                        else:
                            P.op(e, lambda E, dst=dst, src=src: E.tensor_copy(out=dst, in_=src), r=[stt], w=[tk])
                    else:
                        sc_ap, sc_tok = scale_ap_fn(kt)
                        if e == "act":
                            P.op(e, lambda E, dst=dst, src=src, sc_ap=sc_ap: E.activation(out=dst, in_=src, func=AF.Copy, scale=sc_ap), r=[stt, sc_tok], w=[tk])
                        else:
                            P.op(e, lambda E, dst=dst, src=src, sc_ap=sc_ap: E.tensor_scalar(out=dst, in0=src, scalar1=sc_ap, scalar2=None, op0=ALU.mult), r=[stt, sc_tok], w=[tk])
                    i += 1

        def rms_rstd(sc, xt, xtok, junk, junktok, ssq, rstd, sstok):
            P.op("act", lambda E: E.activation(out=junk[:], in_=xt[:], func=AF.Square, accum_out=ssq[:, 0:1]), r=[xtok], w=[junktok, sstok])
            P.op("act", lambda E: E.activation(out=ssq[:, 1:2], in_=ssq[:, 0:1], func=AF.Sqrt, scale=1.0 / D, bias=epsb[:, 0:1]), r=[sstok, t_const], w=[sstok])
            P.op("dve", lambda E: E.reciprocal(out=rstd[:, 0:1], in_=ssq[:, 1:2]), r=[sstok], w=[sstok])

        epsb = es.enter_context(nc.sbuf_tensor("epsb", [128, 4], F32))
        P.op("dve", lambda E: E.memset(epsb[:, 0:1], EPS), w=[t_const])
        P.op("dve", lambda E: E.memset(epsb[:, 1:2], 1.0), w=[t_const])
        P.op("dve", lambda E: E.memset(epsb[:, 2:3], 0.0), w=[t_const])

        def phase_inproj(l, xsrc, xtok):
            with Scope(P) as sc:
                wbf = sc.sb("wbf", [128, 8, DIN], BF16)
                wtok = [sc.tok("wbf%d" % k) for k in range(8)]
                n1 = sc.sb("n1", [128, 8], F32)
                n1t = sc.tok()
                P.dma("sp", n1[:], n1w[l], T["w"], n1t, n1t)
                if l == 0:
                    stg = [sc.sbt("stg%d" % i, [128, 1800], F32) for i in range(3)]
                    convert(sc, lambda kt, c0, c1: wbf[:, kt, c0:c1], lambda kt, c0, c1: w_in[l, kt * 128:(kt + 1) * 128, c0:c1], 8, DIN, stg,
                            scale_ap_fn=lambda kt: (n1[:, kt:kt + 1], n1t), chunk=1800, toks=wtok)
                else:
                    for kt in range(8):
                        P.dma("sp", wbf[:, kt, :], winS[kt * 128:(kt + 1) * 128, :], T["winS"], wtok[kt], wtok[kt])
                xt = [sc.sbt("xt%d" % i, [128, D], F32) for i in range(2)]
                junk, junkt = sc.sbt("junk", [128, D], BF16)
                ssq = [sc.sbt("ssq%d" % i, [128, 4], F32) for i in range(2)]
                xn = [sc.sbt("xn%d" % i, [128, D], BF16) for i in range(2)]
                xnT = [sc.sbt("xnT%d" % i, [128, 8, 512], BF16) for i in range(2)]
                tp = [sc.pst("tp%d" % i, [128, 8, 128], BF16) for i in range(2)]
                pf = [sc.pst("pf%d" % i, [128, 512], F32) for i in range(4)]
                ptm = [sc.pst("ptm", [128, 512], F32)]
                of32 = [sc.sbt("of32_%d" % i, [128, 512], F32) for i in range(4)]
                obf = [sc.sbt("obf_%d" % i, [128, 512], BF16) for i in range(4)]
                vst = [sc.sbt("vst%d" % i, [128, 256], BF16) for i in range(2)]
                gst = [sc.sbt("gst%d" % i, [128, 24], F32) for i in range(2)]
                cnt = {"pf": 0, "f": 0, "b": 0}
                def gen_prep(sg):
                    xT, xTt = xnT[sg % 2]
                    for j in range(4):
                        tt = sg * 4 + j
                        x_t, x_tt = xt[tt % 2]
                        sq, sqt = ssq[tt % 2]
                        xb, xbt = xn[tt % 2]
                        tpp, tpt = tp[tt % 2]
                        P.dma("sp", x_t[:], xsrc[tt * 128:(tt + 1) * 128, :], xtok, x_tt, x_tt)
                        rms_rstd(sc, x_t, x_tt, junk, junkt, sq, sq[:, 2:3], sqt)
                        yield
                        P.op("dve", lambda E, xb=xb, x_t=x_t, sq=sq: E.tensor_scalar(out=xb[:], in0=x_t[:], scalar1=sq[:, 2:3], scalar2=None, op0=ALU.mult), r=[x_tt, sqt], w=[xbt])
                        for kt in range(8):
                            P.op("pe", lambda E, tpp=tpp, xb=xb, kt=kt: E.transpose(out=tpp[:, kt, :], in_=xb[:, kt * 128:(kt + 1) * 128], identity=ident[:]), r=[xbt, t_const], w=[tpt])
                        P.op("act", lambda E, xT=xT, tpp=tpp, j=j: E.copy(out=xT[:, :, j * 128:(j + 1) * 128], in_=tpp[:]), r=[tpt], w=[xTt])
                        yield
                        pt, ptt = ptm[0]
                        for (c0, n, o0) in ((O_VS, 128, 0), (O_VW, 128, 128), (O_GN, 24, 256)):
                            for kt in range(8):
                                P.op("pe", lambda E, pt=pt, xT=xT, kt=kt, c0=c0, n=n, o0=o0, j=j: E.matmul(pt[:, o0:o0 + n], lhsT=xT[:, kt, j * 128:(j + 1) * 128], rhs=wbf[:, kt, c0:c0 + n], start=(kt == 0), stop=(kt == 7)),
                                     r=[xTt, wtok[kt]], w=[ptt])
                        vs_, vst_ = vst[tt % 2]
                        gs_, gst_ = gst[tt % 2]
                        P.op("dve", lambda E, vs_=vs_, pt=pt: E.tensor_copy(out=vs_[:], in_=pt[:, 0:256]), r=[ptt], w=[vst_])
                        P.op("act", lambda E, gs_=gs_, pt=pt: E.activation(out=gs_[:], in_=pt[:, 256:280], func=AF.Sigmoid), r=[ptt], w=[gst_])
                        P.dma("pool", vtm[tt * 128:(tt + 1) * 128].rearrange("p a d -> p (a d)"), vs_[:], vst_, T["vtm"], vst_)
                        P.dma("pool", gat[tt * 128:(tt + 1) * 128, :], gs_[:], gst_, T["gat"], gst_)
                        yield

                def gen_main(sg):
                    xT, xTt = xnT[sg % 2]
                    tsl = slice(sg * 512, (sg + 1) * 512)
                    jobs = []
                    qTf = qT.rearrange("h d t -> (h d) t")
                    for h2 in range(4):
                        jobs.append((O_Q + h2 * 128, 128, "q", qTf[h2 * 128:(h2 + 1) * 128, tsl], "qT"))
                    jobs.append((O_KC, 128, "c", kcT.rearrange("g d t -> (g d) t")[:, tsl], "kcT"))
                    jobs.append((O_VC, 128, "c", vcT.rearrange("g d t -> (g d) t")[:, tsl], "vcT"))
                    jobs.append((O_KS, 128, "c", ksT.rearrange("g d t -> (g d) t")[:, tsl], "ksT"))
                    jobs.append((O_KW, 128, "c", kwT.rearrange("g d t -> (g d) t")[:, tsl], "kwT"))
                    for ft in range(8):
                        jobs.append((O_XR + ft * 128, 128, "f", zf[0, ft * 128:(ft + 1) * 128, tsl], "zf"))
                    for ft in range(8):
                        jobs.append((O_GR + ft * 128, 128, "gelu", zf[1, ft * 128:(ft + 1) * 128, tsl], "zf"))
                    for ft in range(8):
                        jobs.append((O_GA + ft * 128, 128, "sig", zf[2, ft * 128:(ft + 1) * 128, tsl], "zf"))
                    for ft in range(8):
                        jobs.append((O_GB + ft * 128, 128, "sig", zf[3, ft * 128:(ft + 1) * 128, tsl], "zf"))
                    for (c0, m, kind, dst, dtk) in jobs:
                        pp, ppt = pf[cnt["pf"] % 4]
                        cnt["pf"] += 1
                        for kt in range(8):
                            P.op("pe", lambda E, pp=pp, kt=kt, c0=c0, m=m, xT=xT: E.matmul(pp[0:m, :], lhsT=wbf[:, kt, c0:c0 + m], rhs=xT[:, kt, :], start=(kt == 0), stop=(kt == 7)),
                                 r=[xTt, wtok[kt]], w=[ppt])
                        if kind in ("q", "c"):
                            ob, obt = obf[cnt["b"] % 4]
                            cnt["b"] += 1
                            scl = 0.125 if kind == "q" else 1.0
                            P.op("dve", lambda E, ob=ob, pp=pp, m=m, scl=scl: E.tensor_scalar(out=ob[0:m, :], in0=pp[0:m, :], scalar1=scl, scalar2=None, op0=ALU.mult), r=[ppt], w=[obt])
                            P.dma("pool", dst, ob[0:m, :], obt, T[dtk], obt)
                            yield
                        else:
                            ob, obt = of32[cnt["f"] % 4]
                            cnt["f"] += 1
                            if kind == "f":
                                P.op("dve", lambda E, ob=ob, pp=pp: E.tensor_copy(out=ob[:], in_=pp[:]), r=[ppt], w=[obt])
                            else:
                                fn = AF.Gelu_apprx_tanh if kind == "gelu" else AF.Sigmoid
                                P.op("act", lambda E, ob=ob, pp=pp, fn=fn: E.activation(out=ob[:], in_=pp[:], func=fn), r=[ppt], w=[obt])
                            P.dma("pool", dst, ob[:], obt, T[dtk], obt)
                            yield


                NG = S // 512
                for _ in gen_prep(0):
                    pass
                for sg in range(NG):
                    gm = gen_main(sg)
                    gp = gen_prep(sg + 1) if sg + 1 < NG else iter(())
                    m_done = False
                    p_done = False
                    k = 0
                    while not m_done:
                        try:
                            next(gm)
                        except StopIteration:
                            m_done = True
                        k += 1
                        if not p_done and k % 3 == 0:
                            try:
                                next(gp)
                            except StopIteration:
                                p_done = True
                    if not p_done:
                        for _ in gp:
                            pass

        def phase_mlp(l, xsrc, xtok, xdst, xdtok, final):
            with Scope(P) as sc:
                w1b = sc.sb("w1b", [128, 8, 4096], BF16)
                w1t = [sc.tok() for _ in range(8)]
                w2b = sc.sb("w2b", [128, 32, D], BF16)
                w2t = [sc.tok() for _ in range(32)]
                for kt in range(8):
                    P.dma("sp", w1b[:, kt, :], w1dr[kt * 128:(kt + 1) * 128, :], T["w1s"], w1t[kt], w1t[kt])
                for kt in range(32):
                    P.dma("sp", w2b[:, kt, :], w2dr[kt * 128:(kt + 1) * 128, :], T["w2s"], w2t[kt], w2t[kt])
                fw = None
                if final:
                    fw, fwt = sc.sbt("fw", [128, D], F32)
                    P.dma("sp", fw[:], fnw[:, :], T["w"], fwt, fwt)
                GT = 2
                GW = GT * 128
                xt = [sc.sbt("xt%d" % i, [128, D], F32) for i in range(4)]
                ssq = [sc.sbt("ssq%d" % i, [128, 4], F32) for i in range(4)]
                xn, xnt = sc.sbt("xn", [128, D], BF16)
                xnT = [sc.sbt("xnT%d" % i, [128, 8, GW], BF16) for i in range(2)]
                hT, hTt = sc.sbt("hT", [128, 32, GW], BF16)
                hr, hrt = sc.sbt("hr", [128, 512], F32)
                tp = [sc.pst("tp%d" % i, [128, 8, 128], BF16) for i in range(1)]
                ph = [sc.pst("ph%d" % i, [128, 2, GW], F32) for i in range(3)]
                po = [sc.pst("po%d" % i, [128, 512], F32) for i in range(2)]
                xo = [sc.sbt("xo%d" % i, [128, D], F32) for i in range(2)]
                NGR = NT // GT
                cph = [0]

                def gen_prep(gi):
                    xT, xTt = xnT[gi % 2]
                    for j in range(GT):
                        tt = gi * GT + j
                        x_t, x_tt = xt[tt % 4]
                        sq, sqt = ssq[tt % 4]
                        tpp, tpt = tp[0]
                        P.dma("sp", x_t[:], xsrc[tt * 128:(tt + 1) * 128, :], xtok, x_tt, x_tt)
                        rms_rstd(sc, x_t, x_tt, xn, xnt, sq, sq[:, 2:3], sqt)
                        yield
                        P.op("dve", lambda E, x_t=x_t, sq=sq: E.tensor_scalar(out=xn[:], in0=x_t[:], scalar1=sq[:, 2:3], scalar2=None, op0=ALU.mult), r=[x_tt, sqt], w=[xnt])
                        for kt in range(8):
                            P.op("pe", lambda E, tpp=tpp, kt=kt: E.transpose(out=tpp[:, kt, :], in_=xn[:, kt * 128:(kt + 1) * 128], identity=ident[:]), r=[xnt, t_const], w=[tpt])
                        P.op("act", lambda E, xT=xT, tpp=tpp, j=j: E.copy(out=xT[:, :, j * 128:(j + 1) * 128], in_=tpp[:]), r=[tpt], w=[xTt])
                        yield

                def gen_main(gi):
                    xT, xTt = xnT[gi % 2]
                    for f2 in range(16):
                        pp, ppt = ph[cph[0] % 3]
                        cph[0] += 1
                        for fi in range(2):
                            ft = f2 * 2 + fi
                            for kt in range(8):
                                P.op("pe", lambda E, pp=pp, fi=fi, ft=ft, kt=kt: E.matmul(pp[:, fi, :], lhsT=w1b[:, kt, ft * 128:(ft + 1) * 128], rhs=xT[:, kt, :], start=(kt == 0), stop=(kt == 7)),
                                     r=[xTt, w1t[kt]], w=[ppt])
                        P.op("act", lambda E, pp=pp: E.activation(out=hr[:], in_=pp[:].rearrange("p a b -> p (a b)"), func=AF.Relu), r=[ppt], w=[hrt])
                        e2 = "pool" if f2 % 2 else "dve"
                        P.op(e2, lambda E, f2=f2: E.tensor_tensor(out=hT[:, f2 * 2:(f2 + 1) * 2, :].rearrange("p a b -> p (a b)"), in0=hr[:], in1=hr[:], op=ALU.mult), r=[hrt], w=[hTt])
                        yield
                    for j in range(GT):
                        tt = gi * GT + j
                        x_t, x_tt = xt[tt % 4]
                        xo_, xot_ = xo[tt % 2]
                        for nh in range(2):
                            pq, pqt = po[nh]
                            for kt in range(32):
                                P.op("pe", lambda E, pq=pq, kt=kt, nh=nh, j=j: E.matmul(pq[:], lhsT=hT[:, kt, j * 128:(j + 1) * 128], rhs=w2b[:, kt, nh * 512:(nh + 1) * 512], start=(kt == 0), stop=(kt == 31)),
                                     r=[hTt, w2t[kt]], w=[pqt])
                            P.op("dve", lambda E, xo_=xo_, pq=pq, nh=nh, x_t=x_t: E.tensor_tensor(out=xo_[:, nh * 512:(nh + 1) * 512], in0=pq[:], in1=x_t[:, nh * 512:(nh + 1) * 512], op=ALU.add), r=[pqt, x_tt], w=[xot_])
                            yield
                        if not final:
                            P.dma("pool", xdst[tt * 128:(tt + 1) * 128, :], xo_[:], xot_, xdtok, xot_)
                        else:
                            sq2, sq2t = ssq[tt % 4]
                            rms_rstd(sc, xo_, xot_, xn, xnt, sq2, sq2[:, 3:4], sq2t)
                            P.op("dve", lambda E, xo_=xo_, sq2=sq2: E.scalar_tensor_tensor(out=xo_[:], in0=xo_[:], scalar=sq2[:, 3:4], in1=fw[:], op0=ALU.mult, op1=ALU.mult), r=[sq2t, fwt], w=[xot_])
                            P.dma("pool", out_d[tt * 128:(tt + 1) * 128, :], xo_[:], xot_, T["out"], xot_)
                        yield

                for _ in gen_prep(0):
                    pass
                for gi in range(NGR):
                    gm = gen_main(gi)
                    gp = gen_prep(gi + 1) if gi + 1 < NGR else iter(())
                    m_done = False
                    p_done = False
                    k = 0
                    while not m_done:
                        try:
                            next(gm)
                        except StopIteration:
                            m_done = True
                        k += 1
                        if not p_done and k % 4 == 0:
                            try:
                                next(gp)
                            except StopIteration:
                                p_done = True
                    if not p_done:
                        for _ in gp:
                            pass

        def phase_rnn(l):
            with Scope(P) as sc:
                prm, prmt = sc.sbt("prm", [128, 64], F32)
                P.dma("sp", prm[:, 0:32], convw[l].rearrange("p a b -> p (a b)"), T["w"], prmt, prmt)
                P.dma("sp", prm[:, 32:40], convb[l], T["w"], prmt, prmt)
                P.dma("sp", prm[:, 40:48], lba[l], T["w"], prmt, prmt)
                P.dma("sp", prm[:, 48:56], lbi[l], T["w"], prmt, prmt)
                P.dma("sp", prm[:, 56:64], llam[l], T["w"], prmt, prmt)
                P.op("act", lambda E: E.activation(out=prm[:, 56:64], in_=prm[:, 56:64], func=AF.Exp, scale=-1.0), r=[prmt], w=[prmt])
                P.op("act", lambda E: E.activation(out=prm[:, 56:64], in_=prm[:, 56:64], func=AF.Ln, bias=epsb[:, 1:2]), r=[prmt, t_const], w=[prmt])
                P.op("dve", lambda E: E.tensor_scalar(out=prm[:, 56:64], in0=prm[:, 56:64], scalar1=-8.0, scalar2=None, op0=ALU.mult), r=[prmt], w=[prmt])
                H = S // 2
                wst = [sc.sbt("wst%d" % i, [128, 128], F32) for i in range(2)]
                wab = [sc.sbt("wab%d" % i, [128, 128], BF16) for i in range(2)]
                wib = [sc.sbt("wib%d" % i, [128, 128], BF16) for i in range(2)]
                sets = []
                for i in range(2):
                    d = {}
                    d["xrp"] = sc.sbt("xrp%d" % i, [128, H + 4], F32)
                    d["gg"] = sc.sbt("gg%d" % i, [128, H], F32)
                    d["xc"] = sc.sbt("xc%d" % i, [128, H], F32)
                    d["xcb"] = sc.sbt("xcb%d" % i, [128, H], BF16)
                    d["rr"] = sc.sbt("rr%d" % i, [128, H], F32)
                    d["ig"] = sc.sbt("ig%d" % i, [128, H], F32)
                    d["aa"] = sc.sbt("aa%d" % i, [128, H], F32)
                    d["ro"] = sc.sbt("ro%d" % i, [128, H], BF16)
                    sets.append(d)
                pa = [sc.pst("pa%d" % i, [128, 512], F32) for i in range(6)]
                pkc = [0]
                wts = {}

                def gen_w(ct):
                    wa_, wat_ = wab[ct % 2]
                    wi_, wit_ = wib[ct % 2]
                    ws_, wst_ = wst[0]
                    ws2, wst2 = wst[1]
                    P.dma("sp", ws_[:], lwa[l, ct], T["w"], wst_, wst_)
                    P.op("dve", lambda E, wa_=wa_, ws_=ws_: E.tensor_copy(out=wa_[:], in_=ws_[:]), r=[wst_], w=[wat_])
                    P.dma("sp", ws2[:], lwi[l, ct], T["w"], wst2, wst2)
                    P.op("dve", lambda E, wi_=wi_, ws2=ws2: E.tensor_copy(out=wi_[:], in_=ws2[:]), r=[wst2], w=[wit_])

                def gen_it(it):
                    ct, hf = it // 2, it % 2
                    if hf == 0:
                        gen_w(ct)
                    wa_, wat_ = wab[ct % 2]
                    wi_, wit_ = wib[ct % 2]
                    d = sets[it % 2]
                    dprev = sets[(it + 1) % 2]
                    xrp, xrpt = d["xrp"]
                    gg, ggt = d["gg"]
                    xc, xct = d["xc"]
                    xcb, xcbt = d["xcb"]
                    rr, rrt = d["rr"]
                    ig, igt = d["ig"]
                    aa, aat = d["aa"]
                    ro, rot = d["ro"]
                    t0 = hf * H
                    if hf == 0:
                        P.op("pool", lambda E, xrp=xrp: E.memset(xrp[:, 0:4], 0.0), w=[xrpt])
                        P.dma("sp", xrp[:, 4:H + 4], zf[0, ct * 128:(ct + 1) * 128, 0:H], T["zf"], xrpt, xrpt)
                    else:
                        P.dma("sp", xrp[:, 0:H + 4], zf[0, ct * 128:(ct + 1) * 128, H - 4:S], T["zf"], xrpt, xrpt)
                    P.dma("sp", gg[:], zf[1, ct * 128:(ct + 1) * 128, t0:t0 + H], T["zf"], ggt, ggt)
                    yield
                    P.op("act", lambda E: E.activation(out=xc[:], in_=xrp[:, 4:H + 4], func=AF.Identity, scale=prm[:, ct * 4 + 3:ct * 4 + 4], bias=prm[:, 32 + ct:33 + ct]), r=[xrpt, prmt], w=[xct])
                    yield
                    for i in range(3):
                        P.op("dve", lambda E, i=i: E.scalar_tensor_tensor(out=xc[:], in0=xrp[:, 1 + i:1 + i + H], scalar=prm[:, ct * 4 + i:ct * 4 + i + 1], in1=xc[:], op0=ALU.mult, op1=ALU.add), r=[xrpt, prmt], w=[xct])
                    yield
                    P.op("pool", lambda E: E.tensor_copy(out=xcb[:], in_=xc[:]), r=[xct], w=[xcbt])
                    yield
                    for tg in range(H // 512):
                        sl = slice(tg * 512, (tg + 1) * 512)
                        p1, p1t = pa[pkc[0] % 6]
                        p2, p2t = pa[(pkc[0] + 1) % 6]
                        pkc[0] += 2
                        P.op("pe", lambda E, p1=p1, sl=sl: E.matmul(p1[:], lhsT=wa_[:], rhs=xcb[:, sl], start=True, stop=True), r=[wat_, xcbt], w=[p1t])
                        P.op("pe", lambda E, p2=p2, sl=sl: E.matmul(p2[:], lhsT=wi_[:], rhs=xcb[:, sl], start=True, stop=True), r=[wit_, xcbt], w=[p2t])
                        P.op("act", lambda E, p1=p1, sl=sl: E.activation(out=rr[:, sl], in_=p1[:], func=AF.Sigmoid, bias=prm[:, 40 + ct:41 + ct]), r=[p1t, prmt], w=[rrt])
                        P.op("act", lambda E, p2=p2, sl=sl: E.activation(out=ig[:, sl], in_=p2[:], func=AF.Sigmoid, bias=prm[:, 48 + ct:49 + ct]), r=[p2t, prmt], w=[igt])
                    yield
                    P.op("act", lambda E: E.activation(out=aa[:], in_=rr[:], func=AF.Exp, scale=prm[:, 56 + ct:57 + ct]), r=[rrt, prmt], w=[aat])
                    P.op("pool", lambda E: E.tensor_tensor(out=ig[:], in0=ig[:], in1=xc[:], op=ALU.mult), r=[xct], w=[igt])
                    yield
                    P.op("pool", lambda E: E.tensor_tensor(out=rr[:], in0=aa[:], in1=aa[:], op=ALU.mult), r=[aat], w=[rrt])
                    yield
                    P.op("act", lambda E: E.activation(out=rr[:], in_=rr[:], func=AF.Sqrt, scale=-1.0, bias=epsb[:, 1:2]), r=[rrt, t_const], w=[rrt])
                    yield
                    P.op("dve", lambda E: E.tensor_tensor(out=ig[:], in0=ig[:], in1=rr[:], op=ALU.mult), r=[rrt], w=[igt])
                    if hf == 0:
                        P.op("dve", lambda E: E.tensor_tensor_scan(out=xc[:], data0=aa[:], data1=ig[:], initial=0.0, op0=ALU.mult, op1=ALU.add), r=[aat, igt], w=[xct])
                    else:
                        xcp, xcpt = dprev["xc"]
                        P.op("dve", lambda E: E.tensor_tensor_scan(out=xc[:], data0=aa[:], data1=ig[:], initial=xcp[:, H - 1:H], op0=ALU.mult, op1=ALU.add), r=[aat, igt, xcpt], w=[xct])
                    yield
                    P.op("pool", lambda E: E.tensor_tensor(out=ro[:], in0=xc[:], in1=gg[:], op=ALU.mult), r=[xct, ggt], w=[rot])
                    P.dma("pool", rnnT[ct * 128:(ct + 1) * 128, t0:t0 + H], ro[:], rot, T["rnnT"], rot)
                    yield

                active = []
                nxt = 0
                while nxt < 16 or active:
                    while len(active) < 2 and nxt < 16:
                        active.append(gen_it(nxt))
                        nxt += 1
                    for gnr in list(active):
                        try:
                            next(gnr)
                        except StopIteration:
                            active.remove(gnr)

        def phase_merge(l, xsrc, xtok, xdst, xdtok):
            with Scope(P) as sc:
                wa = sc.sb("wa", [128, 4, D], BF16)
                wat = [sc.tok() for _ in range(4)]
                wr = sc.sb("wr", [128, 8, D], BF16)
                wrt = [sc.tok() for _ in range(8)]
                wob = sc.sb("wob", [128, 8, D], BF16)
                wot = [sc.tok() for _ in range(8)]
                for kt in range(4):
                    P.dma("sp", wa[:, kt, :], wuaS[kt * 128:(kt + 1) * 128, :], T["wmS"], wat[kt], wat[kt])
                for kt in range(8):
                    P.dma("sp", wr[:, kt, :], wurS[kt * 128:(kt + 1) * 128, :], T["wmS"], wrt[kt], wrt[kt])
                for kt in range(8):
                    P.dma("sp", wob[:, kt, :], woS[kt * 128:(kt + 1) * 128, :], T["wmS"], wot[kt], wot[kt])
                aT = [sc.sbt("aT%d" % i, [128, 4, 512], BF16) for i in range(2)]
                rT = [sc.sbt("rT%d" % i, [128, 8, 512], BF16) for i in range(2)]
                sa = [sc.sbt("sa%d" % i, [128, 512], F32) for i in range(2)]
                sb_ = [sc.sbt("sb%d" % i, [128, 512], F32) for i in range(2)]
                t1 = [sc.sbt("t1%d" % i, [128, 512], F32) for i in range(2)]
                t2 = [sc.sbt("t2%d" % i, [128, 512], F32) for i in range(2)]
                mT = [sc.sbt("mT%d" % i, [128, 8, 512], BF16) for i in range(2)]
                pA = [sc.pst("pA%d" % i, [128, 512], F32) for i in range(2)]
                pB = [sc.pst("pB%d" % i, [128, 512], F32) for i in range(2)]
                pO = [sc.pst("pO%d" % i, [128, 512], F32) for i in range(2)]
                xt = [sc.sbt("xt%d" % i, [128, D], F32) for i in range(2)]
                xo = [sc.sbt("xo%d" % i, [128, D], F32) for i in range(2)]
                k = 0
                for sg in range(8):
                    tsl = slice(sg * 512, (sg + 1) * 512)
                    a_, at_ = aT[sg % 2]
                    r_, rt_ = rT[sg % 2]
                    m_, mt_ = mT[sg % 2]
                    P.dma("sp", a_[:], attnT[:, tsl].rearrange("(a p) t -> p a t", p=128), T["attnT"], at_, at_)
                    P.dma("sp", r_[:], rnnT[:, tsl].rearrange("(a p) t -> p a t", p=128), T["rnnT"], rt_, rt_)
                    for ft in range(8):
                        fs = slice(ft * 128, (ft + 1) * 128)
                        sa_, sat_ = sa[k % 2]
                        sbb, sbt_ = sb_[k % 2]
                        u1, u1t = t1[k % 2]
                        u2, u2t = t2[k % 2]
                        p_a, pat = pA[k % 2]
                        p_b, pbt = pB[k % 2]
                        k += 1
                        P.dma("sp", sa_[:], zf[2, fs, tsl], T["zf"], sat_, sat_)
                        P.dma("sp", sbb[:], zf[3, fs, tsl], T["zf"], sbt_, sbt_)
                        for kt in range(4):
                            P.op("pe", lambda E, p_a=p_a, kt=kt, fs=fs, a_=a_: E.matmul(p_a[:], lhsT=wa[:, kt, fs], rhs=a_[:, kt, :], start=(kt == 0), stop=(kt == 3)), r=[wat[kt], at_], w=[pat])
                        for kt in range(8):
                            P.op("pe", lambda E, p_b=p_b, kt=kt, fs=fs, r_=r_: E.matmul(p_b[:], lhsT=wr[:, kt, fs], rhs=r_[:, kt, :], start=(kt == 0), stop=(kt == 7)), r=[wrt[kt], rt_], w=[pbt])
                        P.op("dve", lambda E, u1=u1, p_a=p_a, sa_=sa_: E.tensor_tensor(out=u1[:], in0=p_a[:], in1=sa_[:], op=ALU.mult), r=[pat, sat_], w=[u1t])
                        P.op("dve", lambda E, u2=u2, p_b=p_b, sbb=sbb: E.tensor_tensor(out=u2[:], in0=p_b[:], in1=sbb[:], op=ALU.mult), r=[pbt, sbt_], w=[u2t])
                        P.op("pool", lambda E, m_=m_, ft=ft, u1=u1, u2=u2: E.tensor_tensor(out=m_[:, ft, :], in0=u1[:], in1=u2[:], op=ALU.add), r=[u1t, u2t], w=[mt_])
                    for j in range(4):
                        tt = sg * 4 + j
                        x_t, x_tt = xt[tt % 2]
                        xo_, xot_ = xo[tt % 2]
                        P.dma("sp", x_t[:], xsrc[tt * 128:(tt + 1) * 128, :], xtok, x_tt, x_tt)
                        for nh in range(2):
                            pq, pqt = pO[nh]
                            for kt in range(8):
                                P.op("pe", lambda E, pq=pq, kt=kt, nh=nh, m_=m_, j=j: E.matmul(pq[:], lhsT=m_[:, kt, j * 128:(j + 1) * 128], rhs=wob[:, kt, nh * 512:(nh + 1) * 512], start=(kt == 0), stop=(kt == 7)), r=[mt_, wot[kt]], w=[pqt])
                            P.op("dve", lambda E, xo_=xo_, pq=pq, nh=nh, x_t=x_t: E.tensor_tensor(out=xo_[:, nh * 512:(nh + 1) * 512], in0=pq[:], in1=x_t[:, nh * 512:(nh + 1) * 512], op=ALU.add), r=[pqt, x_tt], w=[xot_])
                        P.dma("pool", xdst[tt * 128:(tt + 1) * 128, :], xo_[:], xot_, xdtok, xot_)


        def phase_attn(l):
            with Scope(P) as sc:
                cst = sc.tok("cst")
                tric = sc.sb("tric", [128, 128], BF16)
                triw = sc.sb("triw", [128, 128], BF16)
                Ec = sc.sb("Ec", [64, S], BF16)
                band = sc.sb("band", [128, 9], BF16)
                cA = sc.sb("cA", [128, 128], F32)
                cB = sc.sb("cB", [128, 128], F32)
                for dst, src in ((tric, c_tric), (triw, c_triw), (Ec, c_E), (band, c_band), (cA, c_A), (cB, c_B)):
                    P.dma("sp", dst[:], src[:, :], T["w"], cst, cst)
                kcm = [sc.sbt("kcm%d" % g, [64, 256], BF16) for g in range(2)]
                vcm = [sc.sbt("vcm%d" % g, [128, 2, 64], BF16) for g in range(2)]
                with Scope(P) as s2:
                    stg = [s2.sbt("cstg%d" % i, [64, 2048], F32) for i in range(2)]
                    w2s, w2st = s2.sbt("w2s", [128, 128], F32)
                    pss, psst = s2.sbt("pss", [64, 64], F32)
                    w1b = s2.sb("w1b", [64, 32, 256], BF16)
                    w1bt = s2.tok()
                    w2b, w2bt = s2.sbt("w2b", [128, 2, 64], BF16)
                    posb, posbt = s2.sbt("posb", [64, 32, 2], BF16)
                    kg = [s2.sbt("kg%d" % g, [64, S], BF16) for g in range(2)]
                    hid = [[s2.sbt("hid%d%d" % (g, h), [128, 256], BF16) for h in range(2)] for g in range(2)]
                    bia, biat = s2.sbt("bia", [128, 2], F32)
                    psg = [s2.pst("psg%d" % g, [128, 512], F32) for g in range(2)]
                    psb, psbt = s2.pst("psb", [128, 512], F32)
                    pso, psot = s2.pst("pso", [128, 512], F32)
                    for (w1d, w2d, posd, srcT, srct, is_k) in ((ckw1, ckw2, posk, kcT, "kcT", True), (cvw1, cvw2, posv, vcT, "vcT", False)):
                        convert(s2, lambda kt, c0, c1: w1b[:].rearrange("p a b -> p (a b)")[:, c0:c1], lambda kt, c0, c1: w1d[l].rearrange("p a b -> p (a b)")[:, c0:c1], 1, 32 * 256, stg, chunk=2048, rows=64, toks=[w1bt])
                        P.dma("sp", w2s[:], w2d[l].rearrange("p a b -> p (a b)"), T["w"], w2st, w2st)
                        P.op("dve", lambda E: E.tensor_copy(out=w2b[:].rearrange("p a b -> p (a b)"), in_=w2s[:]), r=[w2st], w=[w2bt])
                        P.dma("sp", pss[:], posd[l].rearrange("p a b -> p (a b)"), T["w"], psst, psst)
                        P.op("dve", lambda E: E.tensor_copy(out=posb[:].rearrange("p a b -> p (a b)"), in_=pss[:]), r=[psst], w=[posbt])
                        for g in range(2):
                            P.dma("sp", kg[g][0][:], srcT[g], T[srct], kg[g][1], kg[g][1])
                        for ht in range(2):
                            hs = slice(ht * 128, (ht + 1) * 128)
                            for p in range(32):
                                for g in range(2):
                                    P.op("pe", lambda E, g=g, p=p, hs=hs: E.matmul(psg[g][0][:, 0:255], lhsT=w1b[:, p, hs], rhs=kg[g][0][:, p:p + 16 * 254 + 1:16], start=(p == 0), stop=(p == 31)),
                                         r=[w1bt, kg[g][1]], w=[psg[g][1]])
                                P.op("pe", lambda E, p=p, hs=hs: E.matmul(psb[:, 0:2], lhsT=w1b[:, p, hs], rhs=posb[:, p, :], start=(p == 0), stop=(p == 31)), r=[w1bt, posbt], w=[psbt])
                            P.op("dve", lambda E: E.tensor_copy(out=bia[:], in_=psb[:, 0:2]), r=[psbt], w=[biat])
                            for g in range(2):
                                P.op("act", lambda E, g=g, ht=ht: E.activation(out=hid[g][ht][0][:, 0:255], in_=psg[g][0][:, 0:255], func=AF.Gelu_apprx_tanh, bias=bia[:, 0:1]), r=[psg[g][1], biat], w=[hid[g][ht][1]])
                        for g in range(2):
                            if is_k:
                                for ht in range(2):
                                    P.op("pe", lambda E, g=g, ht=ht: E.matmul(pso[0:64, 0:255], lhsT=w2b[:, ht, :], rhs=hid[g][ht][0][:, 0:255], start=(ht == 0), stop=(ht == 1)), r=[w2bt, hid[g][ht][1]], w=[psot])
                                P.op("dve", lambda E, g=g: E.tensor_copy(out=kcm[g][0][:, 0:255], in_=pso[0:64, 0:255]), r=[psot], w=[kcm[g][1]])
                            else:
                                for ctile in range(2):
                                    n = 128 if ctile == 0 else 127
                                    for ht in range(2):
                                        P.op("pe", lambda E, g=g, ht=ht, ctile=ctile, n=n: E.matmul(pso[0:n, 256:320], lhsT=hid[g][ht][0][:, ctile * 128:ctile * 128 + n], rhs=w2b[:, ht, :], start=(ht == 0), stop=(ht == 1)), r=[w2bt, hid[g][ht][1]], w=[psot])
                                    P.op("dve", lambda E, g=g, ctile=ctile, n=n: E.tensor_copy(out=vcm[g][0][0:n, ctile, :], in_=pso[0:n, 256:320]), r=[psot], w=[vcm[g][1]])
                KE = [sc.sbt("KE%d" % g, [128, S], BF16) for g in range(2)]
                kwn = [sc.sbt("kwn%d" % g, [64, S], BF16) for g in range(2)]
                vsl = [sc.sbt("vsl%d" % g, [128, 32, 65], BF16) for g in range(2)]
                vwn = [sc.sbt("vwn%d" % g, [128, 32, 65], BF16) for g in range(2)]
                for g in range(2):
                    P.dma("sp", KE[g][0][0:64, :], ksT[g], T["ksT"], KE[g][1], KE[g][1])
                    P.dma("sp", KE[g][0][64:128, :], c_E[:, :], T["w"], KE[g][1], KE[g][1])
                    P.dma("sp", kwn[g][0][:], kwT[g], T["kwT"], kwn[g][1], kwn[g][1])
                    for (vv, j) in ((vsl[g], g), (vwn[g], 2 + g)):
                        P.op("pool", lambda E, vv=vv: E.memset(vv[0][:, :, 64:65], 1.0), w=[vv[1]])
                        for k8 in range(8):
                            P.dma("sp", vv[0][:, k8 * 4:(k8 + 1) * 4, 0:64], vtm[k8 * 512:(k8 + 1) * 512, j, :].rearrange("(k p) d -> p k d", p=128), T["vtm"], vv[1], vv[1])
                QP = [[sc.sbt("QP%d_%d" % (i, h), [128, 512], BF16) for h in range(8)] for i in range(2)]
                gt = [sc.sbt("gt%d" % i, [128, 4, 24], F32) for i in range(2)]
                bst = [sc.pst("bst%d" % i, [128, 512], F32) for i in range(2)]
                b_os = [sc.pst("b_os%d" % i, [128, 512], F32) for i in range(2)]
                b_ow = [sc.pst("b_ow%d" % i, [128, 512], F32) for i in range(2)]
                b_xs, b_xst = sc.pst("b_xs", [128, 512], F32)
                b_tp = sc.ps("b_tp", [128, 1024], BF16)
                tp_t = sc.tok("tp", True)
                NCH = 4
                CH = []
                for c in range(NCH):
                    d = {}
                    d["ee"] = [sc.sbt("ee%d_%d" % (c, i), [128, 256], F32) for i in range(4)]
                    d["pb"] = [sc.sbt("pb%d_%d" % (c, i), [128, 256], BF16) for i in range(4)]
                    d["pTc"] = [sc.sbt("pTc%d_%d" % (c, i), [128, 128], BF16) for i in range(2)]
                    d["sm"] = sc.sbt("sm%d" % c, [128, 16], F32)
                    d["P4"] = sc.sbt("P4_%d" % c, [128, 264], F32)
                    d["imp"] = sc.sbt("imp%d" % c, [128, 64], F32)
                    d["scr"] = sc.sbt("scr%d" % c, [128, 64], F32)
                    d["scr2"] = sc.sbt("scr2_%d" % c, [128, 64], F32)
                    d["m8"] = sc.sbt("m8_%d" % c, [128, 16], F32)
                    d["penb"] = sc.sbt("penb%d" % c, [128, 128], BF16)
                    P.op("pool", lambda E, d=d: E.memset(d["penb"][0][:], 0.0), w=[d["penb"][1]])
                    P.op("dve", lambda E, d=d: E.memset(d["P4"][0][:], 0.0), w=[d["P4"][1]])
                    CH.append(d)
                pT = [sc.sbt("pT%d" % i, [128, 512], BF16) for i in range(4)]
                sy, syt = sc.sbt("sy", [128, 8], F32)
                att2 = [sc.sbt("att%d" % i, [128, 4, 512], F32) for i in range(2)]
                attb, attbt = sc.sbt("attb", [128, 4, 512], BF16)
                ast = [sc.sbt("ast%d" % i, [128, 4, 512], BF16) for i in range(2)]
                sidx = [0]
                NSG = ATT_DBG["nqb"] // 4

                def gen_X(sg):
                    ss = slice(sg * 512, (sg + 1) * 512)
                    qp = QP[sg % 2]
                    g_, gt_ = gt[sg % 2]
                    for h in range(8):
                        P.dma("sp", qp[h][0][0:64, :], qT[h, :, ss], T["qT"], qp[h][1], qp[h][1])
                    P.dma("sp", g_[:], gat[ss, :].rearrange("(j p) c -> p j c", p=128), T["gat"], gt_, gt_)
                    yield
                    for g in range(2):
                        act = [gen_Xchain(sg, g, j) for j in range(4)]
                        while act:
                            for gn in list(act):
                                try:
                                    next(gn)
                                    yield
                                except StopIteration:
                                    act.remove(gn)

                def gen_Xchain(sg, g, j):
                    qp = QP[sg % 2]
                    g_, gt_ = gt[sg % 2]
                    att, attt = att2[sg % 2]
                    d = CH[j % NCH]
                    ee, pb, pTc = d["ee"], d["pb"], d["pTc"]
                    sm, smt = d["sm"]
                    P4, P4t = d["P4"]
                    imp, impt = d["imp"]
                    scr, scrt = d["scr"]
                    scr2, scr2t = d["scr2"]
                    m8, m8t = d["m8"]
                    penb, penbt = d["penb"]
                    P4v = P4[:, 0:256].rearrange("p (j f) -> p j f", f=4)
                    P4w = P4[:, 4:260].rearrange("p (j f) -> p j f", f=4)
                    if True:
                        if True:
                            qb = sg * 4 + j
                            js = slice(j * 128, (j + 1) * 128)
                            Nc = min(255, 8 * qb + 7)
                            cb0 = max(0, 8 * qb - 2)
                            cb1 = min(Nc, 8 * qb + 7)
                            ps_s = b_xs[:, 0:256]
                            for hh in range(4):
                                h = g * 4 + hh
                                P.op("pe", lambda E, h=h, g=g, js=js, Nc=Nc: E.matmul(ps_s[:, 0:Nc], lhsT=qp[h][0][0:64, js], rhs=kcm[g][0][:, 0:Nc], start=True, stop=False), r=[qp[h][1], kcm[g][1]], w=[b_xst])
                                P.op("pe", lambda E, cb0=cb0, cb1=cb1, qb=qb: E.matmul(ps_s[:, cb0:cb1], lhsT=ident[:], rhs=band[:, cb0 - (8 * qb - 2):cb1 - (8 * qb - 2)], start=False, stop=True), r=[t_const, cst], w=[b_xst])
                                e_, et_ = ee[hh]
                                P.op("act", lambda E, e_=e_, hh=hh, Nc=Nc: E.activation(out=e_[:, 0:Nc], in_=ps_s[:, 0:Nc], func=AF.Exp, accum_out=sm[:, 8 + hh:9 + hh]), r=[b_xst], w=[et_, smt])
                                yield
                            P.op("dve", lambda E: E.tensor_scalar(out=sm[:, 12:16], in0=sm[:, 8:12], scalar1=1e-20, scalar2=None, op0=ALU.max), r=[smt], w=[smt])
                            P.op("dve", lambda E: E.reciprocal(out=sm[:, 12:16], in_=sm[:, 12:16]), r=[smt], w=[smt])
                            P.op("dve", lambda E: E.tensor_tensor(out=sm[:, 0:4], in0=sm[:, 12:16], in1=g_[:, j, g * 12:(g + 1) * 12].rearrange("p (h c) -> p h c", c=3)[:, :, 0], op=ALU.mult), r=[smt, gt_], w=[smt])
                            for hh in range(4):
                                e_, et_ = ee[hh]
                                if hh == 0:
                                    P.op("dve", lambda E, e_=e_, hh=hh, Nc=Nc: E.tensor_scalar(out=P4[:, 4:4 + Nc], in0=e_[:, 0:Nc], scalar1=sm[:, 12 + hh:13 + hh], scalar2=None, op0=ALU.mult), r=[et_, smt], w=[P4t])
                                else:
                                    P.op("dve", lambda E, e_=e_, hh=hh, Nc=Nc: E.scalar_tensor_tensor(out=P4[:, 4:4 + Nc], in0=e_[:, 0:Nc], scalar=sm[:, 12 + hh:13 + hh], in1=P4[:, 4:4 + Nc], op0=ALU.mult, op1=ALU.add), r=[et_, smt], w=[P4t])
                            yield
                            P.op("dve", lambda E: E.tensor_tensor(out=imp[:], in0=P4v[:, :, 1], in1=P4v[:, :, 2], op=ALU.add), r=[P4t], w=[impt])
                            P.op("dve", lambda E: E.tensor_tensor(out=imp[:], in0=imp[:], in1=P4v[:, :, 3], op=ALU.add), r=[P4t], w=[impt])
                            P.op("dve", lambda E: E.scalar_tensor_tensor(out=imp[:], in0=imp[:], scalar=2.0, in1=P4v[:, :, 0], op0=ALU.mult, op1=ALU.add), r=[P4t], w=[impt])
                            P.op("dve", lambda E: E.tensor_tensor(out=imp[:], in0=imp[:], in1=P4w[:, :, 0], op=ALU.add), r=[P4t], w=[impt])
                            P.op("dve", lambda E, qb=qb: E.tensor_tensor(out=scr[:], in0=imp[:], in1=cA[:, 64 - 2 * qb:128 - 2 * qb], op=ALU.mult), r=[impt, cst], w=[scrt])
                            P.op("dve", lambda E, qb=qb: E.tensor_tensor(out=scr[:], in0=scr[:], in1=cB[:, 64 - 2 * qb:128 - 2 * qb], op=ALU.add), r=[cst], w=[scrt])
                            P.op("dve", lambda E: E.memset(scr[:, 0:1], 1e4), w=[scrt])
                            yield
                            nb = min(64, 2 * qb + 2)
                            if nb > 16:
                                P.op("dve", lambda E: E.max(out=m8[:, 0:8], in_=scr[:]), r=[scrt], w=[m8t])
                                P.op("dve", lambda E: E.match_replace(out=scr2[:], in_to_replace=m8[:, 0:8], in_values=scr[:], imm_value=-3e38), r=[scrt, m8t], w=[scr2t])
                                P.op("dve", lambda E: E.max(out=m8[:, 8:16], in_=scr2[:]), r=[scr2t], w=[m8t])
                                P.op("dve", lambda E, nb=nb: E.tensor_scalar(out=scr2[:, 0:nb], in0=scr[:, 0:nb], scalar1=m8[:, 15:16], scalar2=None, op0=ALU.is_ge), r=[scrt, m8t], w=[scr2t])
                                P.op("dve", lambda E, nb=nb: E.tensor_scalar(out=penb[:, 64:64 + nb], in0=scr2[:, 0:nb], scalar1=-1.0, scalar2=-NEGM, op0=ALU.add, op1=ALU.mult), r=[scr2t], w=[penbt])
                                yield
                            ptr = b_tp[:, 256:384]
                            P.op("pe", lambda E, ptr=ptr: E.transpose(out=ptr, in_=penb[:], identity=ident[:]), r=[penbt, t_const], w=[tp_t])
                            h0 = g * 4
                            P.op("act", lambda E, ptr=ptr, js=js, h0=h0: E.copy(out=qp[h0][0][64:128, js], in_=ptr[64:128, :]), r=[tp_t], w=[qp[h0][1]])
                            for hh in range(1, 4):
                                P.op("pool", lambda E, js=js, h0=h0, hh=hh: E.tensor_copy(out=qp[h0 + hh][0][64:128, js], in_=qp[h0][0][64:128, js]), r=[qp[h0][1]], w=[qp[h0 + hh][1]])
                            yield
                            k2 = 0
                            tpf = b_tp[:].bitcast(F32)
                            for hh in range(4):
                                h = g * 4 + hh
                                e_, et_ = ee[hh]
                                nct = (Nc + 127) // 128
                                for ctile in range(nct):
                                    n = min(128, Nc - ctile * 128)
                                    tpc = tpf[:, (k2 % 2) * 128:(k2 % 2) * 128 + 128]
                                    pc_, pct_ = pTc[k2 % 2]
                                    k2 += 1
                                    P.op("pe", lambda E, tpc=tpc, e_=e_, ctile=ctile, n=n: E.transpose(out=tpc[0:n, :], in_=e_[:, ctile * 128:ctile * 128 + n], identity=identf[:]), r=[et_, t_const], w=[tp_t])
                                    P.op("act", lambda E, pc_=pc_, tpc=tpc, n=n: E.copy(out=pc_[0:n, :], in_=tpc[0:n, :]), r=[tp_t], w=[pct_])
                                    P.op("pe", lambda E, pc_=pc_, n=n, ctile=ctile, nct=nct: E.matmul(b_xs[:, 256:320], lhsT=pc_[0:n, :], rhs=vcm[g][0][0:n, ctile, :], start=(ctile == 0), stop=(ctile == nct - 1), skip_group_check=True), r=[pct_, vcm[g][1]], w=[b_xst])
                                P.op("dve", lambda E, h=h, hh=hh: E.tensor_scalar(out=att[:, j, h * 64:(h + 1) * 64], in0=b_xs[:, 256:320], scalar1=sm[:, hh:hh + 1], scalar2=None, op0=ALU.mult), r=[b_xst, smt], w=[attt])
                                yield

                def gen_Y(sg):
                    ss = slice(sg * 512, (sg + 1) * 512)
                    qp = QP[sg % 2]
                    g_, gt_ = gt[sg % 2]
                    att, attt = att2[sg % 2]
                    pending = []
                    for g in range(2):
                        for hh in range(4):
                            h = g * 4 + hh
                            q_, qt_ = qp[h]
                            os_, ost_ = b_os[hh % 2]
                            ow_, owt_ = b_ow[hh % 2]
                            steps = []
                            for kt in range(0, 4 * sg + 4):
                                steps.append(("s", kt))
                            for kt in range(max(0, 4 * sg - 4), 4 * sg + 4):
                                steps.append(("w", kt))
                            first = {"s": True, "w": True}
                            LA = 1
                            ring = []
                            for i in range(len(steps) + LA):
                                if i == min(3, len(steps) - 1) and pending:
                                    pending.pop(0)()
                                if i < len(steps):
                                    br, kt = steps[i]
                                    r_ = kt - 4 * sg
                                    if br == "s":
                                        jlo, jhi = max(r_, 0), 3
                                    else:
                                        jlo, jhi = max(r_, 0), min(r_ + 4, 3)
                                    c0, c1 = jlo * 128, (jhi + 1) * 128
                                    si = sidx[0] % 4
                                    stp = bst[sidx[0] % 2][0]
                                    stt_ = bst[sidx[0] % 2][1]
                                    sidx[0] += 1
                                    ks_ = slice(kt * 128, (kt + 1) * 128)
                                    ex = []
                                    if r_ >= 0:
                                        ex.append((r_, tric))
                                    if br == "w" and 0 <= r_ + 4 <= 3:
                                        ex.append((r_ + 4, triw))
                                    if br == "s":
                                        P.op("pe", lambda E, stp=stp, ks_=ks_, q_=q_, g=g, c0=c0, c1=c1, ex=ex: E.matmul(stp[:, c0:c1], lhsT=KE[g][0][:, ks_], rhs=q_[:, c0:c1], start=True, stop=(len(ex) == 0)), r=[KE[g][1], qt_], w=[stt_])
                                    else:
                                        P.op("pe", lambda E, stp=stp, ks_=ks_, q_=q_, g=g, c0=c0, c1=c1, ex=ex: E.matmul(stp[:, c0:c1], lhsT=kwn[g][0][:, ks_], rhs=q_[0:64, c0:c1], start=True, stop=(len(ex) == 0)), r=[kwn[g][1], qt_], w=[stt_])
                                    for xi, (jj, tri_) in enumerate(ex):
                                        P.op("pe", lambda E, stp=stp, jj=jj, tri_=tri_, xi=xi, ex=ex: E.matmul(stp[:, jj * 128:(jj + 1) * 128], lhsT=ident[:], rhs=tri_[:], start=False, stop=(xi == len(ex) - 1)), r=[t_const, cst], w=[stt_])
                                    pt_s, pt_st = pT[si]
                                    P.op("act", lambda E, pt_s=pt_s, stp=stp, c0=c0, c1=c1: E.activation(out=pt_s[:, c0:c1], in_=stp[:, c0:c1], func=AF.Exp), r=[stt_], w=[pt_st])
                                    ring.append((br, kt, jlo, jhi, pt_s, pt_st))
                                if i - LA >= 0:
                                    br, kt, jlo, jhi, pt_s, pt_st = ring[i - LA]
                                    ob, obt = (os_, ost_) if br == "s" else (ow_, owt_)
                                    V = vsl[g] if br == "s" else vwn[g]
                                    for jj in range(jlo, jhi + 1):
                                        st_flag = first[br]
                                        first[br] = False
                                        P.op("pe", lambda E, ob=ob, jj=jj, pt_s=pt_s, V=V, kt=kt, st_flag=st_flag: E.matmul(ob[:, jj * 65:jj * 65 + 65], lhsT=pt_s[:, jj * 128:(jj + 1) * 128], rhs=V[0][:, kt, :], start=st_flag, stop=True, skip_group_check=True), r=[pt_st, V[1]], w=[obt])
                                yield
                            def combine(h=h, os_=os_, ost_=ost_, ow_=ow_, owt_=owt_):
                                osv = os_[:, 0:260].rearrange("p (j c) -> p j c", c=65)
                                owv = ow_[:, 0:260].rearrange("p (j c) -> p j c", c=65)
                                P.op("dve", lambda E: E.reciprocal(out=sy[:, 0:4], in_=osv[:, :, 64]), r=[ost_], w=[syt])
                                P.op("dve", lambda E: E.tensor_tensor(out=sy[:, 0:4], in0=sy[:, 0:4], in1=g_[:, :, h * 3 + 1], op=ALU.mult), r=[gt_], w=[syt])
                                P.op("dve", lambda E: E.reciprocal(out=sy[:, 4:8], in_=owv[:, :, 64]), r=[owt_], w=[syt])
                                P.op("dve", lambda E: E.tensor_tensor(out=sy[:, 4:8], in0=sy[:, 4:8], in1=g_[:, :, h * 3 + 2], op=ALU.mult), r=[gt_], w=[syt])
                                cs_ = slice(h * 64, (h + 1) * 64)
                                for jj in range(4):
                                    P.op("dve", lambda E, jj=jj: E.scalar_tensor_tensor(out=att[:, jj, cs_], in0=os_[:, jj * 65:jj * 65 + 64], scalar=sy[:, jj:jj + 1], in1=att[:, jj, cs_], op0=ALU.mult, op1=ALU.add), r=[ost_, syt], w=[attt])
                                    P.op("dve", lambda E, jj=jj: E.scalar_tensor_tensor(out=attb[:, jj, cs_], in0=ow_[:, jj * 65:jj * 65 + 64], scalar=sy[:, 4 + jj:5 + jj], in1=att[:, jj, cs_], op0=ALU.mult, op1=ALU.add), r=[owt_, syt, attt], w=[attbt])
                            pending.append(combine)
                            yield
                    while pending:
                        pending.pop(0)()
                    yield
                    a_, at_ = ast[sg % 2]
                    atr = b_tp[:, 384:896].rearrange("p (a b) -> p a b", b=128)
                    for jj in range(4):
                        for ft in range(4):
                            P.op("pe", lambda E, ft=ft, jj=jj, atr=atr: E.transpose(out=atr[:, ft, :], in_=attb[:, jj, ft * 128:(ft + 1) * 128], identity=ident[:]), r=[attbt, t_const], w=[tp_t])
                        P.op("act", lambda E, a_=a_, atr=atr, jj=jj: E.copy(out=a_[:, :, jj * 128:(jj + 1) * 128], in_=atr), r=[tp_t], w=[at_])
                        yield
                    P.dma("pool", attnT[:, ss].rearrange("(a p) t -> p a t", p=128), a_[:], at_, T["attnT"], at_)
                    yield

                def drain(gn):
                    for _ in gn:
                        pass

                n2 = sc.sb("n2", [128, 8], F32)
                n2t = sc.tok()
                P.dma("sp", n2[:], n2w[l], T["w"], n2t, n2t)
                n1n = sc.sb("n1n", [128, 8], F32)
                if l + 1 < 2:
                    P.dma("sp", n1n[:], n1w[l + 1], T["w"], n2t, n2t)
                wsf = [sc.sbt("wsf%d" % i, [128, 1024], F32) for i in range(3)]
                wsb = [sc.sbt("wsb%d" % i, [128, 1024], BF16) for i in range(3)]

                def gen_wprep():
                    chunks = []
                    for (srcw, dstw, nk) in ((wua, wuaS, 4), (wur, wurS, 8), (wo, woS, 8)):
                        for kt in range(nk):
                            chunks.append((srcw[l, kt * 128:(kt + 1) * 128, :], dstw[kt * 128:(kt + 1) * 128, :], "wmS", None))
                    for kt in range(8):
                        for c in range(4):
                            chunks.append((w1[l, kt * 128:(kt + 1) * 128, c * 1024:(c + 1) * 1024], w1dr[kt * 128:(kt + 1) * 128, c * 1024:(c + 1) * 1024], "w1s", kt))
                    for kt in range(32):
                        chunks.append((w2[l, kt * 128:(kt + 1) * 128, :], w2dr[kt * 128:(kt + 1) * 128, :], "w2s", None))
                    if l + 1 < 2:
                        for kt in range(8):
                            for c in range(6):
                                chunks.append((w_in[l + 1, kt * 128:(kt + 1) * 128, c * 900:(c + 1) * 900], winS[kt * 128:(kt + 1) * 128, c * 900:(c + 1) * 900], "winS", ("n1", kt)))
                    PRE = 2
                    for k in range(len(chunks) + PRE):
                        if k < len(chunks):
                            src, dst, dn, sk = chunks[k]
                            f_, ft_ = wsf[k % 3]
                            P.dma("pool", f_[:, 0:src.shape[-1]], src, T["w"], ft_, ft_)
                        if k - PRE >= 0:
                            src, dst, dn, sk = chunks[k - PRE]
                            f_, ft_ = wsf[(k - PRE) % 3]
                            b_, bt_ = wsb[(k - PRE) % 3]
                            wd = dst.shape[-1]
                            if sk is not None:
                                if isinstance(sk, tuple):
                                    sc_ap = n1n[:, sk[1]:sk[1] + 1]
                                else:
                                    sc_ap = n2[:, sk:sk + 1]
                                P.op("pool", lambda E, b_=b_, f_=f_, sc_ap=sc_ap, wd=wd: E.tensor_scalar(out=b_[:, 0:wd], in0=f_[:, 0:wd], scalar1=sc_ap, scalar2=1.0, op0=ALU.mult, op1=ALU.mult), r=[ft_, n2t], w=[bt_])
                            else:
                                P.op("pool", lambda E, b_=b_, f_=f_, wd=wd: E.tensor_copy(out=b_[:, 0:wd], in_=f_[:, 0:wd]), r=[ft_], w=[bt_])
                            P.dma("pool", dst, b_[:, 0:wd], bt_, T[dn], bt_)
                        yield

                gw = gen_wprep()
                gw_done = [False]

                def adv_w():
                    if not gw_done[0]:
                        try:
                            next(gw)
                        except StopIteration:
                            gw_done[0] = True

                if NSG > 0:
                    drain(gen_X(0))
                for sg in range(NSG):
                    gy = gen_Y(sg) if not ATT_DBG.get("skip_sw") else iter(())
                    gx = gen_X(sg + 1) if sg + 1 < NSG else iter(())
                    ratio = max(1, int(round((32 * sg + 110) / 100.0)))
                    x_done = False
                    y_done = False
                    yk = 0
                    while not y_done:
                        for _ in range(ratio):
                            try:
                                next(gy)
                            except StopIteration:
                                y_done = True
                                break
                            yk += 1
                            if yk % 8 == 0:
                                adv_w()
                        if not x_done:
                            try:
                                next(gx)
                            except StopIteration:
                                x_done = True
                    if not x_done:
                        drain(gx)
                drain(gw)

        PH = {"inproj": phase_inproj, "mlp": phase_mlp, "rnn": phase_rnn, "merge": phase_merge, "attn": phase_attn}
        build.phases = PH
        build.ctx = dict(P=P, T=T, nc=nc, xs=xs, x_in=x_in, out_d=out_d)
        plan = build.plan
        plan(PH, build.ctx, locals())
        P.barrier()
        print("ops", P.nops, "waits", P.nwait)
    return nc, dbg


def default_plan(PH, ctx, L):
    T = ctx["T"]
    xs = ctx["xs"]
    cur, curt = ctx["x_in"], T["x_in"]
    for l in range(2):
        PH["inproj"](l, cur, curt)
        PH["attn"](l)
        PH["rnn"](l)
        PH["merge"](l, cur, curt, xs[0], T["xs0"])
        PH["mlp"](l, xs[0], T["xs0"], xs[1], T["xs1"], l == 1)
        cur, curt = xs[1], T["xs1"]


build.plan = default_plan


def host_inputs(inp, b):
    bf = ml_dtypes.bfloat16
    f = np.float32

    def pk(v):
        return np.ascontiguousarray(v.reshape(2, 8, 128).transpose(0, 2, 1)).astype(f)

    def bd(wm):
        o = np.zeros((2, 8, 128, 128), f)
        for c in range(8):
            o[:, c, 0:64, 0:64] = wm[:, 2 * c]
            o[:, c, 64:128, 64:128] = wm[:, 2 * c + 1]
        return o

    i_ = np.arange(128)
    m = {
        "x": np.ascontiguousarray(inp["x"][b]),
        "w_in": inp["w_in"],
        "n1w": pk(inp["norm1_w"]), "n2w": pk(inp["norm2_w"]),
        "fnw": np.ascontiguousarray(np.broadcast_to(inp["final_norm_w"][None, :], (128, D))).astype(f),
        "posk": np.ascontiguousarray(np.repeat(inp["cmp_pos_k"].transpose(0, 2, 1)[..., None], 2, axis=-1)),
        "posv": np.ascontiguousarray(np.repeat(inp["cmp_pos_v"].transpose(0, 2, 1)[..., None], 2, axis=-1)),
        "ckw1": np.ascontiguousarray(inp["cmp_k_w1"].reshape(2, 32, 64, 256).transpose(0, 2, 1, 3)),
        "cvw1": np.ascontiguousarray(inp["cmp_v_w1"].reshape(2, 32, 64, 256).transpose(0, 2, 1, 3)),
        "ckw2": np.ascontiguousarray(inp["cmp_k_w2"].reshape(2, 2, 128, 64).transpose(0, 2, 1, 3)),
        "cvw2": np.ascontiguousarray(inp["cmp_v_w2"].reshape(2, 2, 128, 64).transpose(0, 2, 1, 3)),
        "convw": np.ascontiguousarray(inp["conv_w"].reshape(2, 4, 8, 128).transpose(0, 3, 2, 1)),
        "convb": pk(inp["conv_b"]), "lba": pk(inp["lru_b_a"]), "lbi": pk(inp["lru_b_i"]), "llam": pk(inp["lru_lambda"]),
        "lwa": bd(inp["lru_w_a"]), "lwi": bd(inp["lru_w_i"]),
        "wua": inp["w_up_attn"], "wur": inp["w_up_rnn"], "wo": inp["w_out"], "w1": inp["mlp_w1"], "w2": inp["mlp_w2"],
        "c_ident": np.eye(128, dtype=f).astype(bf),
        "c_tric": np.where(i_[:, None] <= i_[None, :], 0.0, NEGM).astype(bf),
        "c_triw": np.where(i_[:, None] > i_[None, :], 0.0, NEGM).astype(bf),
        "c_E": (np.arange(S)[None, :] // 64 == np.arange(64)[:, None]).astype(f).astype(bf),
        "c_band": np.where((np.arange(9)[None, :] - 2) <= ((i_[:, None] + 1) // 16 - 2), 0.0, NEGM).astype(bf),
    }
    hi = (i_ >= 64).astype(np.int64)[:, None]
    jp = (np.arange(128) - 64)[None, :]
    valid = jp <= hi
    forced = jp > hi - 2
    A = np.where(valid & ~forced, 1.0, 0.0)
    Bm = np.where(valid, np.where(forced, 1e4, 0.0), -1e30)
    m["c_A"] = A.astype(f)
    m["c_B"] = Bm.astype(f)
    return {k: np.ascontiguousarray(v) for k, v in m.items()}


def kernel(**inputs):
    inp = {k: np.asarray(v) for k, v in inputs.items()}
    nc, _ = build(False)
    in_maps = [host_inputs(inp, c % 4) for c in range(8)]
    res = run_bass_kernel_spmd(nc, in_maps, core_ids=list(range(8)))
    return np.stack([np.asarray(res.results[c]["out"]) for c in range(4)], axis=0).astype(np.float32)
```

```python
import numpy as np
import ml_dtypes
from contextlib import ExitStack
import concourse.bass as bass
import concourse.mybir as mybir
from concourse.bass_utils import run_bass_kernel_spmd

F32 = mybir.dt.float32
BF16 = mybir.dt.bfloat16
AF = mybir.ActivationFunctionType
ALU = mybir.AluOpType
AX = mybir.AxisListType

S = 4096
D = 1024
DIN = 5400
NT = S // 128
NEGM = -30000.0
EPS = 1e-6
O_Q, O_KC, O_VC, O_KS, O_VS, O_KW, O_VW, O_GN, O_XR, O_GR, O_GA, O_GB = 0, 512, 640, 768, 896, 1024, 1152, 1280, 1304, 2328, 3352, 4376


ATT_DBG = {"level": 6, "nqb": NT}


class Tok:
    __slots__ = ("w", "r", "sem", "name", "x")

    def __init__(self, name="", x=False):
        self.w = {}
        self.r = {}
        self.sem = None
        self.name = name
        self.x = x


class Prog:
    ENG = ("pe", "act", "dve", "pool", "sp")

    def __init__(self, nc, es, n_dma_sems=80):
        self.nc = nc
        self.eng = {"pe": nc.tensor, "act": nc.scalar, "dve": nc.vector, "pool": nc.gpsimd, "sp": nc.sync}
        self.sems = []
        self.esem = {}
        for e in self.ENG:
            self.esem[e] = len(self.sems)
            self.sems.append(es.enter_context(nc.semaphore("es_" + e)))
        self.dma_ids = []
        for i in range(n_dma_sems):
            self.dma_ids.append(len(self.sems))
            self.sems.append(es.enter_context(nc.semaphore("ds_%d" % i)))
        self.free = list(self.dma_ids)
        self.total = [0] * len(self.sems)
        self.known = {e: [0] * len(self.sems) for e in self.ENG}
        self.nwait = 0
        self.nops = 0

    def _wait(self, eng, deps):
        E = self.eng[eng]
        kn = self.known[eng]
        for s, v in deps.items():
            if s >= 5:
                v = self.total[s]
            if kn[s] < v:
                kn[s] = v
                E.wait_ge(self.sems[s], v)
                self.nwait += 1

    @staticmethod
    def _merge(d, src):
        for s, v in src.items():
            if d.get(s, 0) < v:
                d[s] = v

    def op(self, eng, fn, r=(), w=()):
        deps = {}
        rx = [b for b in r if b.x]
        if rx:
            r = [b for b in r if not b.x]
            w = list(w) + rx
        for b in r:
            self._merge(deps, b.w)
        for b in w:
            self._merge(deps, b.w)
            self._merge(deps, b.r)
        s = self.esem[eng]
        if eng == "pe":
            deps.pop(s, None)
        self._wait(eng, deps)
        self.total[s] += 1
        n = self.total[s]
        fn(self.eng[eng]).then_inc(self.sems[s], 1)
        self.nops += 1
        for b in r:
            b.r[s] = n
        for b in w:
            b.w[s] = n

    def dma(self, eng, out, in_, src, dst, owner):
        deps = {}
        self._merge(deps, src.w)
        self._merge(deps, dst.w)
        self._merge(deps, dst.r)
        self._wait(eng, deps)
        if owner.sem is None:
            owner.sem = self.free.pop()
        s = owner.sem
        self.total[s] += 16
        v = self.total[s]
        self.eng[eng].dma_start(out=out, in_=in_).then_inc(self.sems[s], 16)
        self.nops += 1
        src.r[s] = v
        dst.w[s] = v

    def release(self, toks):
        for t in toks:
            if t.sem is not None:
                self.free.append(t.sem)
                t.sem = None

    def barrier(self):
        for e in self.ENG:
            E = self.eng[e]
            kn = self.known[e]
            for s in range(len(self.sems)):
                if s == self.esem[e]:
                    continue
                v = self.total[s]
                if kn[s] < v:
                    kn[s] = v
                    E.wait_ge(self.sems[s], v)
        arr = {}
        for e in self.ENG:
            s = self.esem[e]
            self.total[s] += 1
            arr[e] = self.total[s]
            if e == "pe":
                self.eng[e].nop().then_inc(self.sems[s], 1) if hasattr(self.eng[e], "nop") else None
            else:
                self.eng[e].nop().then_inc(self.sems[s], 1)
        for e in self.ENG:
            for f in self.ENG:
                if f == e:
                    continue
                s = self.esem[f]
                self.known[e][s] = arr[f]
                self.eng[e].wait_ge(self.sems[s], arr[f])


class Scope:
    def __init__(self, P):
        self.P = P
        self.es = ExitStack()
        self.toks = []

    def __enter__(self):
        self.es.__enter__()
        return self

    def __exit__(self, *a):
        self.P.barrier()
        self.P.release(self.toks)
        return self.es.__exit__(*a)

    uid = [0]

    def sb(self, name, shape, dt):
        Scope.uid[0] += 1
        return self.es.enter_context(self.P.nc.sbuf_tensor("%s_%d" % (name, Scope.uid[0]), list(shape), dt))

    def ps(self, name, shape, dt):
        Scope.uid[0] += 1
        return self.es.enter_context(self.P.nc.psum_tensor("%s_%d" % (name, Scope.uid[0]), list(shape), dt))

    def tok(self, name="", x=False):
        t = Tok(name, x)
        self.toks.append(t)
        return t

    def sbt(self, name, shape, dt):
        return self.sb(name, shape, dt), self.tok(name)

    def pst(self, name, shape, dt):
        return self.ps(name, shape, dt), self.tok(name, True)


def build(debug=False):
    nc = bass.Bass("TRN2", target_bir_lowering=False)
    dbg = {}

    def din(name, shape, dt=F32):
        return nc.dram_tensor(name, list(shape), dt, kind="ExternalInput").ap()

    def dscr(name, shape, dt):
        isd = bool(debug) and (debug is True or name in debug)
        kind = "ExternalOutput" if isd else "Internal"
        t = nc.dram_tensor(name, list(shape), dt, kind=kind).ap()
        if isd:
            dbg[name] = t
        return t

    x_in = din("x", [S, D])
    out_d = nc.dram_tensor("out", [S, D], F32, kind="ExternalOutput").ap()
    w_in = din("w_in", [2, D, DIN])
    n1w = din("n1w", [2, 128, 8])
    n2w = din("n2w", [2, 128, 8])
    fnw = din("fnw", [128, D])
    posk = din("posk", [2, 64, 32, 2])
    posv = din("posv", [2, 64, 32, 2])
    ckw1 = din("ckw1", [2, 64, 32, 256])
    cvw1 = din("cvw1", [2, 64, 32, 256])
    ckw2 = din("ckw2", [2, 128, 2, 64])
    cvw2 = din("cvw2", [2, 128, 2, 64])
    convw = din("convw", [2, 128, 8, 4])
    convb = din("convb", [2, 128, 8])
    lba = din("lba", [2, 128, 8])
    lbi = din("lbi", [2, 128, 8])
    llam = din("llam", [2, 128, 8])
    lwa = din("lwa", [2, 8, 128, 128])
    lwi = din("lwi", [2, 8, 128, 128])
    wua = din("wua", [2, 512, D])
    wur = din("wur", [2, D, D])
    wo = din("wo", [2, D, D])
    w1 = din("w1", [2, D, 4096])
    w2 = din("w2", [2, 4096, D])
    c_ident = din("c_ident", [128, 128], BF16)
    c_tric = din("c_tric", [128, 128], BF16)
    c_triw = din("c_triw", [128, 128], BF16)
    c_E = din("c_E", [64, S], BF16)
    c_band = din("c_band", [128, 9], BF16)
    c_A = din("c_A", [128, 128])
    c_B = din("c_B", [128, 128])

    xs = [dscr("xs0", [S, D], F32), dscr("xs1", [S, D], F32)]
    qT = dscr("qT", [8, 64, S], BF16)
    kcT = dscr("kcT", [2, 64, S], BF16)
    vcT = dscr("vcT", [2, 64, S], BF16)
    ksT = dscr("ksT", [2, 64, S], BF16)
    kwT = dscr("kwT", [2, 64, S], BF16)
    vtm = dscr("vtm", [S, 4, 64], BF16)
    gat = dscr("gat", [S, 24], F32)
    zf = dscr("zf", [4, D, S], F32)
    winS = dscr("winS", [D, DIN], BF16)
    wuaS = dscr("wuaS", [512, D], BF16)
    wurS = dscr("wurS", [D, D], BF16)
    woS = dscr("woS", [D, D], BF16)
    w1dr = dscr("w1s", [D, 4096], BF16)
    w2dr = dscr("w2s", [4096, D], BF16)
    attnT = dscr("attnT", [512, S], BF16)
    rnnT = dscr("rnnT", [D, S], BF16)

    with ExitStack() as es:
        P = Prog(nc, es)
        T = {n: Tok(n) for n in ["x_in", "out", "w", "xs0", "xs1", "qT", "kcT", "vcT", "ksT", "kwT", "vtm", "gat", "zf", "attnT", "rnnT", "w1s", "w2s", "winS", "wmS"]}
        for t in T.values():
            t.sem = None

        ident = es.enter_context(nc.sbuf_tensor("ident", [128, 128], BF16))
        identf = es.enter_context(nc.sbuf_tensor("identf", [128, 128], F32))
        t_const = Tok("const")
        P.dma("sp", ident[:], c_ident[:, :], T["w"], t_const, t_const)
        P.op("dve", lambda E: E.tensor_copy(out=identf[:], in_=ident[:]), r=[t_const], w=[t_const])

        def convert(sc, dst_ap_fn, src_ap_fn, nrow_tiles, ncols, stg, scale_ap_fn=None, chunk=2048, rows=128, toks=None):
            i = 0
            engs = ("act", "dve", "pool")
            for kt in range(nrow_tiles):
                for c0 in range(0, ncols, chunk):
                    c1 = min(ncols, c0 + chunk)
                    st, stt = stg[i % len(stg)]
                    P.dma("sp", st[0:rows, 0:c1 - c0], src_ap_fn(kt, c0, c1), T["w"], stt, stt)
                    e = engs[i % 3]
                    tk = toks[kt] if toks is not None else None
                    dst = dst_ap_fn(kt, c0, c1)
                    src = st[0:rows, 0:c1 - c0]
                    if scale_ap_fn is None:
                        if e == "act":
                            P.op(e, lambda E, dst=dst, src=src: E.copy(out=dst, in_=src), r=[stt], w=[tk])
                        else:
                            P.op(e, lambda E, dst=dst, src=src: E.tensor_copy(out=dst, in_=src), r=[stt], w=[tk])
                    else:
                        sc_ap, sc_tok = scale_ap_fn(kt)
                        if e == "act":
                            P.op(e, lambda E, dst=dst, src=src, sc_ap=sc_ap: E.activation(out=dst, in_=src, func=AF.Copy, scale=sc_ap), r=[stt, sc_tok], w=[tk])
                        else:
                            P.op(e, lambda E, dst=dst, src=src, sc_ap=sc_ap: E.tensor_scalar(out=dst, in0=src, scalar1=sc_ap, scalar2=None, op0=ALU.mult), r=[stt, sc_tok], w=[tk])
                    i += 1

        def rms_rstd(sc, xt, xtok, junk, junktok, ssq, rstd, sstok):
            P.op("act", lambda E: E.activation(out=junk[:], in_=xt[:], func=AF.Square, accum_out=ssq[:, 0:1]), r=[xtok], w=[junktok, sstok])
            P.op("act", lambda E: E.activation(out=ssq[:, 1:2], in_=ssq[:, 0:1], func=AF.Sqrt, scale=1.0 / D, bias=epsb[:, 0:1]), r=[sstok, t_const], w=[sstok])
            P.op("dve", lambda E: E.reciprocal(out=rstd[:, 0:1], in_=ssq[:, 1:2]), r=[sstok], w=[sstok])

        epsb = es.enter_context(nc.sbuf_tensor("epsb", [128, 4], F32))
        P.op("dve", lambda E: E.memset(epsb[:, 0:1], EPS), w=[t_const])
        P.op("dve", lambda E: E.memset(epsb[:, 1:2], 1.0), w=[t_const])
        P.op("dve", lambda E: E.memset(epsb[:, 2:3], 0.0), w=[t_const])

        def phase_inproj(l, xsrc, xtok):
            with Scope(P) as sc:
                wbf = sc.sb("wbf", [128, 8, DIN], BF16)
                wtok = [sc.tok("wbf%d" % k) for k in range(8)]
                n1 = sc.sb("n1", [128, 8], F32)
                n1t = sc.tok()
                P.dma("sp", n1[:], n1w[l], T["w"], n1t, n1t)
                if l == 0:
                    stg = [sc.sbt("stg%d" % i, [128, 1800], F32) for i in range(3)]
                    convert(sc, lambda kt, c0, c1: wbf[:, kt, c0:c1], lambda kt, c0, c1: w_in[l, kt * 128:(kt + 1) * 128, c0:c1], 8, DIN, stg,
                            scale_ap_fn=lambda kt: (n1[:, kt:kt + 1], n1t), chunk=1800, toks=wtok)
                else:
                    for kt in range(8):
                        P.dma("sp", wbf[:, kt, :], winS[kt * 128:(kt + 1) * 128, :], T["winS"], wtok[kt], wtok[kt])
                xt = [sc.sbt("xt%d" % i, [128, D], F32) for i in range(2)]
                junk, junkt = sc.sbt("junk", [128, D], BF16)
                ssq = [sc.sbt("ssq%d" % i, [128, 4], F32) for i in range(2)]
                xn = [sc.sbt("xn%d" % i, [128, D], BF16) for i in range(2)]
                xnT = [sc.sbt("xnT%d" % i, [128, 8, 512], BF16) for i in range(2)]
                tp = [sc.pst("tp%d" % i, [128, 8, 128], BF16) for i in range(2)]
                pf = [sc.pst("pf%d" % i, [128, 512], F32) for i in range(4)]
                ptm = [sc.pst("ptm", [128, 512], F32)]
                of32 = [sc.sbt("of32_%d" % i, [128, 512], F32) for i in range(4)]
                obf = [sc.sbt("obf_%d" % i, [128, 512], BF16) for i in range(4)]
                vst = [sc.sbt("vst%d" % i, [128, 256], BF16) for i in range(2)]
                gst = [sc.sbt("gst%d" % i, [128, 24], F32) for i in range(2)]
                cnt = {"pf": 0, "f": 0, "b": 0}
                def gen_prep(sg):
                    xT, xTt = xnT[sg % 2]
                    for j in range(4):
                        tt = sg * 4 + j
                        x_t, x_tt = xt[tt % 2]
                        sq, sqt = ssq[tt % 2]
                        xb, xbt = xn[tt % 2]
                        tpp, tpt = tp[tt % 2]
                        P.dma("sp", x_t[:], xsrc[tt * 128:(tt + 1) * 128, :], xtok, x_tt, x_tt)
                        rms_rstd(sc, x_t, x_tt, junk, junkt, sq, sq[:, 2:3], sqt)
                        yield
                        P.op("dve", lambda E, xb=xb, x_t=x_t, sq=sq: E.tensor_scalar(out=xb[:], in0=x_t[:], scalar1=sq[:, 2:3], scalar2=None, op0=ALU.mult), r=[x_tt, sqt], w=[xbt])
                        for kt in range(8):
                            P.op("pe", lambda E, tpp=tpp, xb=xb, kt=kt: E.transpose(out=tpp[:, kt, :], in_=xb[:, kt * 128:(kt + 1) * 128], identity=ident[:]), r=[xbt, t_const], w=[tpt])
                        P.op("act", lambda E, xT=xT, tpp=tpp, j=j: E.copy(out=xT[:, :, j * 128:(j + 1) * 128], in_=tpp[:]), r=[tpt], w=[xTt])
                        yield
                        pt, ptt = ptm[0]
                        for (c0, n, o0) in ((O_VS, 128, 0), (O_VW, 128, 128), (O_GN, 24, 256)):
                            for kt in range(8):
                                P.op("pe", lambda E, pt=pt, xT=xT, kt=kt, c0=c0, n=n, o0=o0, j=j: E.matmul(pt[:, o0:o0 + n], lhsT=xT[:, kt, j * 128:(j + 1) * 128], rhs=wbf[:, kt, c0:c0 + n], start=(kt == 0), stop=(kt == 7)),
                                     r=[xTt, wtok[kt]], w=[ptt])
                        vs_, vst_ = vst[tt % 2]
                        gs_, gst_ = gst[tt % 2]
                        P.op("dve", lambda E, vs_=vs_, pt=pt: E.tensor_copy(out=vs_[:], in_=pt[:, 0:256]), r=[ptt], w=[vst_])
                        P.op("act", lambda E, gs_=gs_, pt=pt: E.activation(out=gs_[:], in_=pt[:, 256:280], func=AF.Sigmoid), r=[ptt], w=[gst_])
                        P.dma("pool", vtm[tt * 128:(tt + 1) * 128].rearrange("p a d -> p (a d)"), vs_[:], vst_, T["vtm"], vst_)
                        P.dma("pool", gat[tt * 128:(tt + 1) * 128, :], gs_[:], gst_, T["gat"], gst_)
                        yield

                def gen_main(sg):
                    xT, xTt = xnT[sg % 2]
                    tsl = slice(sg * 512, (sg + 1) * 512)
                    jobs = []
                    qTf = qT.rearrange("h d t -> (h d) t")
                    for h2 in range(4):
                        jobs.append((O_Q + h2 * 128, 128, "q", qTf[h2 * 128:(h2 + 1) * 128, tsl], "qT"))
                    jobs.append((O_KC, 128, "c", kcT.rearrange("g d t -> (g d) t")[:, tsl], "kcT"))
                    jobs.append((O_VC, 128, "c", vcT.rearrange("g d t -> (g d) t")[:, tsl], "vcT"))
                    jobs.append((O_KS, 128, "c", ksT.rearrange("g d t -> (g d) t")[:, tsl], "ksT"))
                    jobs.append((O_KW, 128, "c", kwT.rearrange("g d t -> (g d) t")[:, tsl], "kwT"))
                    for ft in range(8):
                        jobs.append((O_XR + ft * 128, 128, "f", zf[0, ft * 128:(ft + 1) * 128, tsl], "zf"))
                    for ft in range(8):
                        jobs.append((O_GR + ft * 128, 128, "gelu", zf[1, ft * 128:(ft + 1) * 128, tsl], "zf"))
                    for ft in range(8):
                        jobs.append((O_GA + ft * 128, 128, "sig", zf[2, ft * 128:(ft + 1) * 128, tsl], "zf"))
                    for ft in range(8):
                        jobs.append((O_GB + ft * 128, 128, "sig", zf[3, ft * 128:(ft + 1) * 128, tsl], "zf"))
                    for (c0, m, kind, dst, dtk) in jobs:
                        pp, ppt = pf[cnt["pf"] % 4]
                        cnt["pf"] += 1
                        for kt in range(8):
                            P.op("pe", lambda E, pp=pp, kt=kt, c0=c0, m=m, xT=xT: E.matmul(pp[0:m, :], lhsT=wbf[:, kt, c0:c0 + m], rhs=xT[:, kt, :], start=(kt == 0), stop=(kt == 7)),
                                 r=[xTt, wtok[kt]], w=[ppt])
                        if kind in ("q", "c"):
                            ob, obt = obf[cnt["b"] % 4]
                            cnt["b"] += 1
                            scl = 0.125 if kind == "q" else 1.0
                            P.op("dve", lambda E, ob=ob, pp=pp, m=m, scl=scl: E.tensor_scalar(out=ob[0:m, :], in0=pp[0:m, :], scalar1=scl, scalar2=None, op0=ALU.mult), r=[ppt], w=[obt])
                            P.dma("pool", dst, ob[0:m, :], obt, T[dtk], obt)
                            yield
                        else:
                            ob, obt = of32[cnt["f"] % 4]
                            cnt["f"] += 1
                            if kind == "f":
                                P.op("dve", lambda E, ob=ob, pp=pp: E.tensor_copy(out=ob[:], in_=pp[:]), r=[ppt], w=[obt])
                            else:
                                fn = AF.Gelu_apprx_tanh if kind == "gelu" else AF.Sigmoid
                                P.op("act", lambda E, ob=ob, pp=pp, fn=fn: E.activation(out=ob[:], in_=pp[:], func=fn), r=[ppt], w=[obt])
                            P.dma("pool", dst, ob[:], obt, T[dtk], obt)
                            yield


                NG = S // 512
                for _ in gen_prep(0):
                    pass
                for sg in range(NG):
                    gm = gen_main(sg)
                    gp = gen_prep(sg + 1) if sg + 1 < NG else iter(())
                    m_done = False
                    p_done = False
                    k = 0
                    while not m_done:
                        try:
                            next(gm)
                        except StopIteration:
                            m_done = True
                        k += 1
                        if not p_done and k % 3 == 0:
                            try:
                                next(gp)
                            except StopIteration:
                                p_done = True
                    if not p_done:
                        for _ in gp:
                            pass

        def phase_mlp(l, xsrc, xtok, xdst, xdtok, final):
            with Scope(P) as sc:
                w1b = sc.sb("w1b", [128, 8, 4096], BF16)
                w1t = [sc.tok() for _ in range(8)]
                w2b = sc.sb("w2b", [128, 32, D], BF16)
                w2t = [sc.tok() for _ in range(32)]
                for kt in range(8):
                    P.dma("sp", w1b[:, kt, :], w1dr[kt * 128:(kt + 1) * 128, :], T["w1s"], w1t[kt], w1t[kt])
                for kt in range(32):
                    P.dma("sp", w2b[:, kt, :], w2dr[kt * 128:(kt + 1) * 128, :], T["w2s"], w2t[kt], w2t[kt])
                fw = None
                if final:
                    fw, fwt = sc.sbt("fw", [128, D], F32)
                    P.dma("sp", fw[:], fnw[:, :], T["w"], fwt, fwt)
                GT = 2
                GW = GT * 128
                xt = [sc.sbt("xt%d" % i, [128, D], F32) for i in range(4)]
                ssq = [sc.sbt("ssq%d" % i, [128, 4], F32) for i in range(4)]
                xn, xnt = sc.sbt("xn", [128, D], BF16)
                xnT = [sc.sbt("xnT%d" % i, [128, 8, GW], BF16) for i in range(2)]
                hT, hTt = sc.sbt("hT", [128, 32, GW], BF16)
                hr, hrt = sc.sbt("hr", [128, 512], F32)
                tp = [sc.pst("tp%d" % i, [128, 8, 128], BF16) for i in range(1)]
                ph = [sc.pst("ph%d" % i, [128, 2, GW], F32) for i in range(3)]
                po = [sc.pst("po%d" % i, [128, 512], F32) for i in range(2)]
                xo = [sc.sbt("xo%d" % i, [128, D], F32) for i in range(2)]
                NGR = NT // GT
                cph = [0]

                def gen_prep(gi):
                    xT, xTt = xnT[gi % 2]
                    for j in range(GT):
                        tt = gi * GT + j
                        x_t, x_tt = xt[tt % 4]
                        sq, sqt = ssq[tt % 4]
                        tpp, tpt = tp[0]
                        P.dma("sp", x_t[:], xsrc[tt * 128:(tt + 1) * 128, :], xtok, x_tt, x_tt)
                        rms_rstd(sc, x_t, x_tt, xn, xnt, sq, sq[:, 2:3], sqt)
                        yield
                        P.op("dve", lambda E, x_t=x_t, sq=sq: E.tensor_scalar(out=xn[:], in0=x_t[:], scalar1=sq[:, 2:3], scalar2=None, op0=ALU.mult), r=[x_tt, sqt], w=[xnt])
                        for kt in range(8):
                            P.op("pe", lambda E, tpp=tpp, kt=kt: E.transpose(out=tpp[:, kt, :], in_=xn[:, kt * 128:(kt + 1) * 128], identity=ident[:]), r=[xnt, t_const], w=[tpt])
                        P.op("act", lambda E, xT=xT, tpp=tpp, j=j: E.copy(out=xT[:, :, j * 128:(j + 1) * 128], in_=tpp[:]), r=[tpt], w=[xTt])
                        yield

                def gen_main(gi):
                    xT, xTt = xnT[gi % 2]
                    for f2 in range(16):
                        pp, ppt = ph[cph[0] % 3]
                        cph[0] += 1
                        for fi in range(2):
                            ft = f2 * 2 + fi
                            for kt in range(8):
                                P.op("pe", lambda E, pp=pp, fi=fi, ft=ft, kt=kt: E.matmul(pp[:, fi, :], lhsT=w1b[:, kt, ft * 128:(ft + 1) * 128], rhs=xT[:, kt, :], start=(kt == 0), stop=(kt == 7)),
                                     r=[xTt, w1t[kt]], w=[ppt])
                        P.op("act", lambda E, pp=pp: E.activation(out=hr[:], in_=pp[:].rearrange("p a b -> p (a b)"), func=AF.Relu), r=[ppt], w=[hrt])
                        e2 = "pool" if f2 % 2 else "dve"
                        P.op(e2, lambda E, f2=f2: E.tensor_tensor(out=hT[:, f2 * 2:(f2 + 1) * 2, :].rearrange("p a b -> p (a b)"), in0=hr[:], in1=hr[:], op=ALU.mult), r=[hrt], w=[hTt])
                        yield
                    for j in range(GT):
                        tt = gi * GT + j
                        x_t, x_tt = xt[tt % 4]
                        xo_, xot_ = xo[tt % 2]
                        for nh in range(2):
                            pq, pqt = po[nh]
                            for kt in range(32):
                                P.op("pe", lambda E, pq=pq, kt=kt, nh=nh, j=j: E.matmul(pq[:], lhsT=hT[:, kt, j * 128:(j + 1) * 128], rhs=w2b[:, kt, nh * 512:(nh + 1) * 512], start=(kt == 0), stop=(kt == 31)),
                                     r=[hTt, w2t[kt]], w=[pqt])
                            P.op("dve", lambda E, xo_=xo_, pq=pq, nh=nh, x_t=x_t: E.tensor_tensor(out=xo_[:, nh * 512:(nh + 1) * 512], in0=pq[:], in1=x_t[:, nh * 512:(nh + 1) * 512], op=ALU.add), r=[pqt, x_tt], w=[xot_])
                            yield
                        if not final:
                            P.dma("pool", xdst[tt * 128:(tt + 1) * 128, :], xo_[:], xot_, xdtok, xot_)
                        else:
                            sq2, sq2t = ssq[tt % 4]
                            rms_rstd(sc, xo_, xot_, xn, xnt, sq2, sq2[:, 3:4], sq2t)
                            P.op("dve", lambda E, xo_=xo_, sq2=sq2: E.scalar_tensor_tensor(out=xo_[:], in0=xo_[:], scalar=sq2[:, 3:4], in1=fw[:], op0=ALU.mult, op1=ALU.mult), r=[sq2t, fwt], w=[xot_])
                            P.dma("pool", out_d[tt * 128:(tt + 1) * 128, :], xo_[:], xot_, T["out"], xot_)
                        yield

                for _ in gen_prep(0):
                    pass
                for gi in range(NGR):
                    gm = gen_main(gi)
                    gp = gen_prep(gi + 1) if gi + 1 < NGR else iter(())
                    m_done = False
                    p_done = False
                    k = 0
                    while not m_done:
                        try:
                            next(gm)
                        except StopIteration:
                            m_done = True
                        k += 1
                        if not p_done and k % 4 == 0:
                            try:
                                next(gp)
                            except StopIteration:
                                p_done = True
                    if not p_done:
                        for _ in gp:
                            pass

        def phase_rnn(l):
            with Scope(P) as sc:
                prm, prmt = sc.sbt("prm", [128, 64], F32)
                P.dma("sp", prm[:, 0:32], convw[l].rearrange("p a b -> p (a b)"), T["w"], prmt, prmt)
                P.dma("sp", prm[:, 32:40], convb[l], T["w"], prmt, prmt)
                P.dma("sp", prm[:, 40:48], lba[l], T["w"], prmt, prmt)
                P.dma("sp", prm[:, 48:56], lbi[l], T["w"], prmt, prmt)
                P.dma("sp", prm[:, 56:64], llam[l], T["w"], prmt, prmt)
                P.op("act", lambda E: E.activation(out=prm[:, 56:64], in_=prm[:, 56:64], func=AF.Exp, scale=-1.0), r=[prmt], w=[prmt])
                P.op("act", lambda E: E.activation(out=prm[:, 56:64], in_=prm[:, 56:64], func=AF.Ln, bias=epsb[:, 1:2]), r=[prmt, t_const], w=[prmt])
                P.op("dve", lambda E: E.tensor_scalar(out=prm[:, 56:64], in0=prm[:, 56:64], scalar1=-8.0, scalar2=None, op0=ALU.mult), r=[prmt], w=[prmt])
                H = S // 2
                wst = [sc.sbt("wst%d" % i, [128, 128], F32) for i in range(2)]
                wab = [sc.sbt("wab%d" % i, [128, 128], BF16) for i in range(2)]
                wib = [sc.sbt("wib%d" % i, [128, 128], BF16) for i in range(2)]
                sets = []
                for i in range(2):
                    d = {}
                    d["xrp"] = sc.sbt("xrp%d" % i, [128, H + 4], F32)
                    d["gg"] = sc.sbt("gg%d" % i, [128, H], F32)
                    d["xc"] = sc.sbt("xc%d" % i, [128, H], F32)
                    d["xcb"] = sc.sbt("xcb%d" % i, [128, H], BF16)
                    d["rr"] = sc.sbt("rr%d" % i, [128, H], F32)
                    d["ig"] = sc.sbt("ig%d" % i, [128, H], F32)
                    d["aa"] = sc.sbt("aa%d" % i, [128, H], F32)
                    d["ro"] = sc.sbt("ro%d" % i, [128, H], BF16)
                    sets.append(d)
                pa = [sc.pst("pa%d" % i, [128, 512], F32) for i in range(6)]
                pkc = [0]
                wts = {}

                def gen_w(ct):
                    wa_, wat_ = wab[ct % 2]
                    wi_, wit_ = wib[ct % 2]
                    ws_, wst_ = wst[0]
                    ws2, wst2 = wst[1]
                    P.dma("sp", ws_[:], lwa[l, ct], T["w"], wst_, wst_)
                    P.op("dve", lambda E, wa_=wa_, ws_=ws_: E.tensor_copy(out=wa_[:], in_=ws_[:]), r=[wst_], w=[wat_])
                    P.dma("sp", ws2[:], lwi[l, ct], T["w"], wst2, wst2)
                    P.op("dve", lambda E, wi_=wi_, ws2=ws2: E.tensor_copy(out=wi_[:], in_=ws2[:]), r=[wst2], w=[wit_])

                def gen_it(it):
                    ct, hf = it // 2, it % 2
                    if hf == 0:
                        gen_w(ct)
                    wa_, wat_ = wab[ct % 2]
                    wi_, wit_ = wib[ct % 2]
                    d = sets[it % 2]
                    dprev = sets[(it + 1) % 2]
                    xrp, xrpt = d["xrp"]
                    gg, ggt = d["gg"]
                    xc, xct = d["xc"]
                    xcb, xcbt = d["xcb"]
                    rr, rrt = d["rr"]
                    ig, igt = d["ig"]
                    aa, aat = d["aa"]
                    ro, rot = d["ro"]
                    t0 = hf * H
                    if hf == 0:
                        P.op("pool", lambda E, xrp=xrp: E.memset(xrp[:, 0:4], 0.0), w=[xrpt])
                        P.dma("sp", xrp[:, 4:H + 4], zf[0, ct * 128:(ct + 1) * 128, 0:H], T["zf"], xrpt, xrpt)
                    else:
                        P.dma("sp", xrp[:, 0:H + 4], zf[0, ct * 128:(ct + 1) * 128, H - 4:S], T["zf"], xrpt, xrpt)
                    P.dma("sp", gg[:], zf[1, ct * 128:(ct + 1) * 128, t0:t0 + H], T["zf"], ggt, ggt)
                    yield
                    P.op("act", lambda E: E.activation(out=xc[:], in_=xrp[:, 4:H + 4], func=AF.Identity, scale=prm[:, ct * 4 + 3:ct * 4 + 4], bias=prm[:, 32 + ct:33 + ct]), r=[xrpt, prmt], w=[xct])
                    yield
                    for i in range(3):
                        P.op("dve", lambda E, i=i: E.scalar_tensor_tensor(out=xc[:], in0=xrp[:, 1 + i:1 + i + H], scalar=prm[:, ct * 4 + i:ct * 4 + i + 1], in1=xc[:], op0=ALU.mult, op1=ALU.add), r=[xrpt, prmt], w=[xct])
                    yield
                    P.op("pool", lambda E: E.tensor_copy(out=xcb[:], in_=xc[:]), r=[xct], w=[xcbt])
                    yield
                    for tg in range(H // 512):
                        sl = slice(tg * 512, (tg + 1) * 512)
                        p1, p1t = pa[pkc[0] % 6]
                        p2, p2t = pa[(pkc[0] + 1) % 6]
                        pkc[0] += 2
                        P.op("pe", lambda E, p1=p1, sl=sl: E.matmul(p1[:], lhsT=wa_[:], rhs=xcb[:, sl], start=True, stop=True), r=[wat_, xcbt], w=[p1t])
                        P.op("pe", lambda E, p2=p2, sl=sl: E.matmul(p2[:], lhsT=wi_[:], rhs=xcb[:, sl], start=True, stop=True), r=[wit_, xcbt], w=[p2t])
                        P.op("act", lambda E, p1=p1, sl=sl: E.activation(out=rr[:, sl], in_=p1[:], func=AF.Sigmoid, bias=prm[:, 40 + ct:41 + ct]), r=[p1t, prmt], w=[rrt])
                        P.op("act", lambda E, p2=p2, sl=sl: E.activation(out=ig[:, sl], in_=p2[:], func=AF.Sigmoid, bias=prm[:, 48 + ct:49 + ct]), r=[p2t, prmt], w=[igt])
                    yield
                    P.op("act", lambda E: E.activation(out=aa[:], in_=rr[:], func=AF.Exp, scale=prm[:, 56 + ct:57 + ct]), r=[rrt, prmt], w=[aat])
                    P.op("pool", lambda E: E.tensor_tensor(out=ig[:], in0=ig[:], in1=xc[:], op=ALU.mult), r=[xct], w=[igt])
                    yield
                    P.op("pool", lambda E: E.tensor_tensor(out=rr[:], in0=aa[:], in1=aa[:], op=ALU.mult), r=[aat], w=[rrt])
                    yield
                    P.op("act", lambda E: E.activation(out=rr[:], in_=rr[:], func=AF.Sqrt, scale=-1.0, bias=epsb[:, 1:2]), r=[rrt, t_const], w=[rrt])
                    yield
                    P.op("dve", lambda E: E.tensor_tensor(out=ig[:], in0=ig[:], in1=rr[:], op=ALU.mult), r=[rrt], w=[igt])
                    if hf == 0:
                        P.op("dve", lambda E: E.tensor_tensor_scan(out=xc[:], data0=aa[:], data1=ig[:], initial=0.0, op0=ALU.mult, op1=ALU.add), r=[aat, igt], w=[xct])
                    else:
                        xcp, xcpt = dprev["xc"]
                        P.op("dve", lambda E: E.tensor_tensor_scan(out=xc[:], data0=aa[:], data1=ig[:], initial=xcp[:, H - 1:H], op0=ALU.mult, op1=ALU.add), r=[aat, igt, xcpt], w=[xct])
                    yield
                    P.op("pool", lambda E: E.tensor_tensor(out=ro[:], in0=xc[:], in1=gg[:], op=ALU.mult), r=[xct, ggt], w=[rot])
                    P.dma("pool", rnnT[ct * 128:(ct + 1) * 128, t0:t0 + H], ro[:], rot, T["rnnT"], rot)
                    yield

                active = []
                nxt = 0
                while nxt < 16 or active:
                    while len(active) < 2 and nxt < 16:
                        active.append(gen_it(nxt))
                        nxt += 1
                    for gnr in list(active):
                        try:
                            next(gnr)
                        except StopIteration:
                            active.remove(gnr)

        def phase_merge(l, xsrc, xtok, xdst, xdtok):
            with Scope(P) as sc:
                wa = sc.sb("wa", [128, 4, D], BF16)
                wat = [sc.tok() for _ in range(4)]
                wr = sc.sb("wr", [128, 8, D], BF16)
                wrt = [sc.tok() for _ in range(8)]
                wob = sc.sb("wob", [128, 8, D], BF16)
                wot = [sc.tok() for _ in range(8)]
                for kt in range(4):
                    P.dma("sp", wa[:, kt, :], wuaS[kt * 128:(kt + 1) * 128, :], T["wmS"], wat[kt], wat[kt])
                for kt in range(8):
                    P.dma("sp", wr[:, kt, :], wurS[kt * 128:(kt + 1) * 128, :], T["wmS"], wrt[kt], wrt[kt])
                for kt in range(8):
                    P.dma("sp", wob[:, kt, :], woS[kt * 128:(kt + 1) * 128, :], T["wmS"], wot[kt], wot[kt])
                aT = [sc.sbt("aT%d" % i, [128, 4, 512], BF16) for i in range(2)]
                rT = [sc.sbt("rT%d" % i, [128, 8, 512], BF16) for i in range(2)]
                sa = [sc.sbt("sa%d" % i, [128, 512], F32) for i in range(2)]
                sb_ = [sc.sbt("sb%d" % i, [128, 512], F32) for i in range(2)]
                t1 = [sc.sbt("t1%d" % i, [128, 512], F32) for i in range(2)]
                t2 = [sc.sbt("t2%d" % i, [128, 512], F32) for i in range(2)]
                mT = [sc.sbt("mT%d" % i, [128, 8, 512], BF16) for i in range(2)]
                pA = [sc.pst("pA%d" % i, [128, 512], F32) for i in range(2)]
                pB = [sc.pst("pB%d" % i, [128, 512], F32) for i in range(2)]
                pO = [sc.pst("pO%d" % i, [128, 512], F32) for i in range(2)]
                xt = [sc.sbt("xt%d" % i, [128, D], F32) for i in range(2)]
                xo = [sc.sbt("xo%d" % i, [128, D], F32) for i in range(2)]
                k = 0
                for sg in range(8):
                    tsl = slice(sg * 512, (sg + 1) * 512)
                    a_, at_ = aT[sg % 2]
                    r_, rt_ = rT[sg % 2]
                    m_, mt_ = mT[sg % 2]
                    P.dma("sp", a_[:], attnT[:, tsl].rearrange("(a p) t -> p a t", p=128), T["attnT"], at_, at_)
                    P.dma("sp", r_[:], rnnT[:, tsl].rearrange("(a p) t -> p a t", p=128), T["rnnT"], rt_, rt_)
                    for ft in range(8):
                        fs = slice(ft * 128, (ft + 1) * 128)
                        sa_, sat_ = sa[k % 2]
                        sbb, sbt_ = sb_[k % 2]
                        u1, u1t = t1[k % 2]
                        u2, u2t = t2[k % 2]
                        p_a, pat = pA[k % 2]
                        p_b, pbt = pB[k % 2]
                        k += 1
                        P.dma("sp", sa_[:], zf[2, fs, tsl], T["zf"], sat_, sat_)
                        P.dma("sp", sbb[:], zf[3, fs, tsl], T["zf"], sbt_, sbt_)
                        for kt in range(4):
                            P.op("pe", lambda E, p_a=p_a, kt=kt, fs=fs, a_=a_: E.matmul(p_a[:], lhsT=wa[:, kt, fs], rhs=a_[:, kt, :], start=(kt == 0), stop=(kt == 3)), r=[wat[kt], at_], w=[pat])
                        for kt in range(8):
                            P.op("pe", lambda E, p_b=p_b, kt=kt, fs=fs, r_=r_: E.matmul(p_b[:], lhsT=wr[:, kt, fs], rhs=r_[:, kt, :], start=(kt == 0), stop=(kt == 7)), r=[wrt[kt], rt_], w=[pbt])
                        P.op("dve", lambda E, u1=u1, p_a=p_a, sa_=sa_: E.tensor_tensor(out=u1[:], in0=p_a[:], in1=sa_[:], op=ALU.mult), r=[pat, sat_], w=[u1t])
                        P.op("dve", lambda E, u2=u2, p_b=p_b, sbb=sbb: E.tensor_tensor(out=u2[:], in0=p_b[:], in1=sbb[:], op=ALU.mult), r=[pbt, sbt_], w=[u2t])
                        P.op("pool", lambda E, m_=m_, ft=ft, u1=u1, u2=u2: E.tensor_tensor(out=m_[:, ft, :], in0=u1[:], in1=u2[:], op=ALU.add), r=[u1t, u2t], w=[mt_])
                    for j in range(4):
                        tt = sg * 4 + j
                        x_t, x_tt = xt[tt % 2]
                        xo_, xot_ = xo[tt % 2]
                        P.dma("sp", x_t[:], xsrc[tt * 128:(tt + 1) * 128, :], xtok, x_tt, x_tt)
                        for nh in range(2):
                            pq, pqt = pO[nh]
                            for kt in range(8):
                                P.op("pe", lambda E, pq=pq, kt=kt, nh=nh, m_=m_, j=j: E.matmul(pq[:], lhsT=m_[:, kt, j * 128:(j + 1) * 128], rhs=wob[:, kt, nh * 512:(nh + 1) * 512], start=(kt == 0), stop=(kt == 7)), r=[mt_, wot[kt]], w=[pqt])
                            P.op("dve", lambda E, xo_=xo_, pq=pq, nh=nh, x_t=x_t: E.tensor_tensor(out=xo_[:, nh * 512:(nh + 1) * 512], in0=pq[:], in1=x_t[:, nh * 512:(nh + 1) * 512], op=ALU.add), r=[pqt, x_tt], w=[xot_])
                        P.dma("pool", xdst[tt * 128:(tt + 1) * 128, :], xo_[:], xot_, xdtok, xot_)


        def phase_attn(l):
            with Scope(P) as sc:
                cst = sc.tok("cst")
                tric = sc.sb("tric", [128, 128], BF16)
                triw = sc.sb("triw", [128, 128], BF16)
                Ec = sc.sb("Ec", [64, S], BF16)
                band = sc.sb("band", [128, 9], BF16)
                cA = sc.sb("cA", [128, 128], F32)
                cB = sc.sb("cB", [128, 128], F32)
                for dst, src in ((tric, c_tric), (triw, c_triw), (Ec, c_E), (band, c_band), (cA, c_A), (cB, c_B)):
                    P.dma("sp", dst[:], src[:, :], T["w"], cst, cst)
                kcm = [sc.sbt("kcm%d" % g, [64, 256], BF16) for g in range(2)]
                vcm = [sc.sbt("vcm%d" % g, [128, 2, 64], BF16) for g in range(2)]
                with Scope(P) as s2:
                    stg = [s2.sbt("cstg%d" % i, [64, 2048], F32) for i in range(2)]
                    w2s, w2st = s2.sbt("w2s", [128, 128], F32)
                    pss, psst = s2.sbt("pss", [64, 64], F32)
                    w1b = s2.sb("w1b", [64, 32, 256], BF16)
                    w1bt = s2.tok()
                    w2b, w2bt = s2.sbt("w2b", [128, 2, 64], BF16)
                    posb, posbt = s2.sbt("posb", [64, 32, 2], BF16)
                    kg = [s2.sbt("kg%d" % g, [64, S], BF16) for g in range(2)]
                    hid = [[s2.sbt("hid%d%d" % (g, h), [128, 256], BF16) for h in range(2)] for g in range(2)]
                    bia, biat = s2.sbt("bia", [128, 2], F32)
                    psg = [s2.pst("psg%d" % g, [128, 512], F32) for g in range(2)]
                    psb, psbt = s2.pst("psb", [128, 512], F32)
                    pso, psot = s2.pst("pso", [128, 512], F32)
                    for (w1d, w2d, posd, srcT, srct, is_k) in ((ckw1, ckw2, posk, kcT, "kcT", True), (cvw1, cvw2, posv, vcT, "vcT", False)):
                        convert(s2, lambda kt, c0, c1: w1b[:].rearrange("p a b -> p (a b)")[:, c0:c1], lambda kt, c0, c1: w1d[l].rearrange("p a b -> p (a b)")[:, c0:c1], 1, 32 * 256, stg, chunk=2048, rows=64, toks=[w1bt])
                        P.dma("sp", w2s[:], w2d[l].rearrange("p a b -> p (a b)"), T["w"], w2st, w2st)
                        P.op("dve", lambda E: E.tensor_copy(out=w2b[:].rearrange("p a b -> p (a b)"), in_=w2s[:]), r=[w2st], w=[w2bt])
                        P.dma("sp", pss[:], posd[l].rearrange("p a b -> p (a b)"), T["w"], psst, psst)
                        P.op("dve", lambda E: E.tensor_copy(out=posb[:].rearrange("p a b -> p (a b)"), in_=pss[:]), r=[psst], w=[posbt])
                        for g in range(2):
                            P.dma("sp", kg[g][0][:], srcT[g], T[srct], kg[g][1], kg[g][1])
                        for ht in range(2):
                            hs = slice(ht * 128, (ht + 1) * 128)
                            for p in range(32):
                                for g in range(2):
                                    P.op("pe", lambda E, g=g, p=p, hs=hs: E.matmul(psg[g][0][:, 0:255], lhsT=w1b[:, p, hs], rhs=kg[g][0][:, p:p + 16 * 254 + 1:16], start=(p == 0), stop=(p == 31)),
                                         r=[w1bt, kg[g][1]], w=[psg[g][1]])
                                P.op("pe", lambda E, p=p, hs=hs: E.matmul(psb[:, 0:2], lhsT=w1b[:, p, hs], rhs=posb[:, p, :], start=(p == 0), stop=(p == 31)), r=[w1bt, posbt], w=[psbt])
                            P.op("dve", lambda E: E.tensor_copy(out=bia[:], in_=psb[:, 0:2]), r=[psbt], w=[biat])
                            for g in range(2):
                                P.op("act", lambda E, g=g, ht=ht: E.activation(out=hid[g][ht][0][:, 0:255], in_=psg[g][0][:, 0:255], func=AF.Gelu_apprx_tanh, bias=bia[:, 0:1]), r=[psg[g][1], biat], w=[hid[g][ht][1]])
                        for g in range(2):
                            if is_k:
                                for ht in range(2):
                                    P.op("pe", lambda E, g=g, ht=ht: E.matmul(pso[0:64, 0:255], lhsT=w2b[:, ht, :], rhs=hid[g][ht][0][:, 0:255], start=(ht == 0), stop=(ht == 1)), r=[w2bt, hid[g][ht][1]], w=[psot])
                                P.op("dve", lambda E, g=g: E.tensor_copy(out=kcm[g][0][:, 0:255], in_=pso[0:64, 0:255]), r=[psot], w=[kcm[g][1]])
                            else:
                                for ctile in range(2):
                                    n = 128 if ctile == 0 else 127
                                    for ht in range(2):
                                        P.op("pe", lambda E, g=g, ht=ht, ctile=ctile, n=n: E.matmul(pso[0:n, 256:320], lhsT=hid[g][ht][0][:, ctile * 128:ctile * 128 + n], rhs=w2b[:, ht, :], start=(ht == 0), stop=(ht == 1)), r=[w2bt, hid[g][ht][1]], w=[psot])
                                    P.op("dve", lambda E, g=g, ctile=ctile, n=n: E.tensor_copy(out=vcm[g][0][0:n, ctile, :], in_=pso[0:n, 256:320]), r=[psot], w=[vcm[g][1]])
                KE = [sc.sbt("KE%d" % g, [128, S], BF16) for g in range(2)]
                kwn = [sc.sbt("kwn%d" % g, [64, S], BF16) for g in range(2)]
                vsl = [sc.sbt("vsl%d" % g, [128, 32, 65], BF16) for g in range(2)]
                vwn = [sc.sbt("vwn%d" % g, [128, 32, 65], BF16) for g in range(2)]
                for g in range(2):
                    P.dma("sp", KE[g][0][0:64, :], ksT[g], T["ksT"], KE[g][1], KE[g][1])
                    P.dma("sp", KE[g][0][64:128, :], c_E[:, :], T["w"], KE[g][1], KE[g][1])
                    P.dma("sp", kwn[g][0][:], kwT[g], T["kwT"], kwn[g][1], kwn[g][1])
                    for (vv, j) in ((vsl[g], g), (vwn[g], 2 + g)):
                        P.op("pool", lambda E, vv=vv: E.memset(vv[0][:, :, 64:65], 1.0), w=[vv[1]])
                        for k8 in range(8):
                            P.dma("sp", vv[0][:, k8 * 4:(k8 + 1) * 4, 0:64], vtm[k8 * 512:(k8 + 1) * 512, j, :].rearrange("(k p) d -> p k d", p=128), T["vtm"], vv[1], vv[1])
                QP = [[sc.sbt("QP%d_%d" % (i, h), [128, 512], BF16) for h in range(8)] for i in range(2)]
                gt = [sc.sbt("gt%d" % i, [128, 4, 24], F32) for i in range(2)]
                bst = [sc.pst("bst%d" % i, [128, 512], F32) for i in range(3)]
                b_os = [sc.pst("b_os%d" % i, [128, 512], F32) for i in range(2)]
                b_ow = [sc.pst("b_ow%d" % i, [128, 512], F32) for i in range(1)]
                b_xs, b_xst = sc.pst("b_xs", [128, 512], F32)
                b_tp = sc.ps("b_tp", [128, 1024], BF16)
                tp_t = sc.tok("tp", True)
                NCH = 4
                CH = []
                for c in range(NCH):
                    d = {}
                    d["ee"] = [sc.sbt("ee%d_%d" % (c, i), [128, 256], F32) for i in range(4)]
                    d["pb"] = [sc.sbt("pb%d_%d" % (c, i), [128, 256], BF16) for i in range(4)]
                    d["pTc"] = [sc.sbt("pTc%d_%d" % (c, i), [128, 128], BF16) for i in range(2)]
                    d["sm"] = sc.sbt("sm%d" % c, [128, 16], F32)
                    d["P4"] = sc.sbt("P4_%d" % c, [128, 264], F32)
                    d["imp"] = sc.sbt("imp%d" % c, [128, 64], F32)
                    d["scr"] = sc.sbt("scr%d" % c, [128, 64], F32)
                    d["scr2"] = sc.sbt("scr2_%d" % c, [128, 64], F32)
                    d["m8"] = sc.sbt("m8_%d" % c, [128, 16], F32)
                    d["penb"] = sc.sbt("penb%d" % c, [128, 128], BF16)
                    P.op("pool", lambda E, d=d: E.memset(d["penb"][0][:], 0.0), w=[d["penb"][1]])
                    P.op("dve", lambda E, d=d: E.memset(d["P4"][0][:], 0.0), w=[d["P4"][1]])
                    CH.append(d)
                pT = [sc.sbt("pT%d" % i, [128, 512], BF16) for i in range(4)]
                sy, syt = sc.sbt("sy", [128, 8], F32)
                att2 = [sc.sbt("att%d" % i, [128, 4, 512], F32) for i in range(2)]
                attb, attbt = sc.sbt("attb", [128, 4, 512], BF16)
                ast = [sc.sbt("ast%d" % i, [128, 4, 512], BF16) for i in range(2)]
                sidx = [0]
                NSG = ATT_DBG["nqb"] // 4

                def gen_X(sg):
                    ss = slice(sg * 512, (sg + 1) * 512)
                    qp = QP[sg % 2]
                    g_, gt_ = gt[sg % 2]
                    for h in range(8):
                        P.dma("sp", qp[h][0][0:64, :], qT[h, :, ss], T["qT"], qp[h][1], qp[h][1])
                    P.dma("sp", g_[:], gat[ss, :].rearrange("(j p) c -> p j c", p=128), T["gat"], gt_, gt_)
                    yield
                    for g in range(2):
                        act = [gen_Xchain(sg, g, j) for j in range(4)]
                        while act:
                            for gn in list(act):
                                try:
                                    next(gn)
                                    yield
                                except StopIteration:
                                    act.remove(gn)

                def gen_Xchain(sg, g, j):
                    qp = QP[sg % 2]
                    g_, gt_ = gt[sg % 2]
                    att, attt = att2[sg % 2]
                    d = CH[j % NCH]
                    ee, pb, pTc = d["ee"], d["pb"], d["pTc"]
                    sm, smt = d["sm"]
                    P4, P4t = d["P4"]
                    imp, impt = d["imp"]
                    scr, scrt = d["scr"]
                    scr2, scr2t = d["scr2"]
                    m8, m8t = d["m8"]
                    penb, penbt = d["penb"]
                    P4v = P4[:, 0:256].rearrange("p (j f) -> p j f", f=4)
                    P4w = P4[:, 4:260].rearrange("p (j f) -> p j f", f=4)
                    if True:
                        if True:
                            qb = sg * 4 + j
                            js = slice(j * 128, (j + 1) * 128)
                            Nc = min(255, 8 * qb + 7)
                            cb0 = max(0, 8 * qb - 2)
                            cb1 = min(Nc, 8 * qb + 7)
                            ps_s = b_xs[:, 0:256]
                            for hh in range(4):
                                h = g * 4 + hh
                                P.op("pe", lambda E, h=h, g=g, js=js, Nc=Nc: E.matmul(ps_s[:, 0:Nc], lhsT=qp[h][0][0:64, js], rhs=kcm[g][0][:, 0:Nc], start=True, stop=False), r=[qp[h][1], kcm[g][1]], w=[b_xst])
                                P.op("pe", lambda E, cb0=cb0, cb1=cb1, qb=qb: E.matmul(ps_s[:, cb0:cb1], lhsT=ident[:], rhs=band[:, cb0 - (8 * qb - 2):cb1 - (8 * qb - 2)], start=False, stop=True), r=[t_const, cst], w=[b_xst])
                                e_, et_ = ee[hh]
                                P.op("act", lambda E, e_=e_, hh=hh, Nc=Nc: E.activation(out=e_[:, 0:Nc], in_=ps_s[:, 0:Nc], func=AF.Exp, accum_out=sm[:, 8 + hh:9 + hh]), r=[b_xst], w=[et_, smt])
                                yield
                            P.op("dve", lambda E: E.tensor_scalar(out=sm[:, 12:16], in0=sm[:, 8:12], scalar1=1e-20, scalar2=None, op0=ALU.max), r=[smt], w=[smt])
                            P.op("dve", lambda E: E.reciprocal(out=sm[:, 12:16], in_=sm[:, 12:16]), r=[smt], w=[smt])
                            P.op("dve", lambda E: E.tensor_tensor(out=sm[:, 0:4], in0=sm[:, 12:16], in1=g_[:, j, g * 12:(g + 1) * 12].rearrange("p (h c) -> p h c", c=3)[:, :, 0], op=ALU.mult), r=[smt, gt_], w=[smt])
                            for hh in range(4):
                                e_, et_ = ee[hh]
                                if hh == 0:
                                    P.op("dve", lambda E, e_=e_, hh=hh, Nc=Nc: E.tensor_scalar(out=P4[:, 4:4 + Nc], in0=e_[:, 0:Nc], scalar1=sm[:, 12 + hh:13 + hh], scalar2=None, op0=ALU.mult), r=[et_, smt], w=[P4t])
                                else:
                                    P.op("dve", lambda E, e_=e_, hh=hh, Nc=Nc: E.scalar_tensor_tensor(out=P4[:, 4:4 + Nc], in0=e_[:, 0:Nc], scalar=sm[:, 12 + hh:13 + hh], in1=P4[:, 4:4 + Nc], op0=ALU.mult, op1=ALU.add), r=[et_, smt], w=[P4t])
                            yield
                            P.op("dve", lambda E: E.tensor_tensor(out=imp[:], in0=P4v[:, :, 1], in1=P4v[:, :, 2], op=ALU.add), r=[P4t], w=[impt])
                            P.op("dve", lambda E: E.tensor_tensor(out=imp[:], in0=imp[:], in1=P4v[:, :, 3], op=ALU.add), r=[P4t], w=[impt])
                            P.op("dve", lambda E: E.scalar_tensor_tensor(out=imp[:], in0=imp[:], scalar=2.0, in1=P4v[:, :, 0], op0=ALU.mult, op1=ALU.add), r=[P4t], w=[impt])
                            P.op("dve", lambda E: E.tensor_tensor(out=imp[:], in0=imp[:], in1=P4w[:, :, 0], op=ALU.add), r=[P4t], w=[impt])
                            P.op("dve", lambda E, qb=qb: E.tensor_tensor(out=scr[:], in0=imp[:], in1=cA[:, 64 - 2 * qb:128 - 2 * qb], op=ALU.mult), r=[impt, cst], w=[scrt])
                            P.op("dve", lambda E, qb=qb: E.tensor_tensor(out=scr[:], in0=scr[:], in1=cB[:, 64 - 2 * qb:128 - 2 * qb], op=ALU.add), r=[cst], w=[scrt])
                            P.op("dve", lambda E: E.memset(scr[:, 0:1], 1e4), w=[scrt])
                            yield
                            nb = min(64, 2 * qb + 2)
                            if nb > 16:
                                P.op("dve", lambda E: E.max(out=m8[:, 0:8], in_=scr[:]), r=[scrt], w=[m8t])
                                P.op("dve", lambda E: E.match_replace(out=scr2[:], in_to_replace=m8[:, 0:8], in_values=scr[:], imm_value=-3e38), r=[scrt, m8t], w=[scr2t])
                                P.op("dve", lambda E: E.max(out=m8[:, 8:16], in_=scr2[:]), r=[scr2t], w=[m8t])
                                P.op("dve", lambda E, nb=nb: E.tensor_scalar(out=scr2[:, 0:nb], in0=scr[:, 0:nb], scalar1=m8[:, 15:16], scalar2=None, op0=ALU.is_ge), r=[scrt, m8t], w=[scr2t])
                                P.op("dve", lambda E, nb=nb: E.tensor_scalar(out=penb[:, 64:64 + nb], in0=scr2[:, 0:nb], scalar1=-1.0, scalar2=-NEGM, op0=ALU.add, op1=ALU.mult), r=[scr2t], w=[penbt])
                                yield
                            ptr = b_tp[:, 256:384]
                            P.op("pe", lambda E, ptr=ptr: E.transpose(out=ptr, in_=penb[:], identity=ident[:]), r=[penbt, t_const], w=[tp_t])
                            h0 = g * 4
                            P.op("act", lambda E, ptr=ptr, js=js, h0=h0: E.copy(out=qp[h0][0][64:128, js], in_=ptr[64:128, :]), r=[tp_t], w=[qp[h0][1]])
                            for hh in range(1, 4):
                                P.op("pool", lambda E, js=js, h0=h0, hh=hh: E.tensor_copy(out=qp[h0 + hh][0][64:128, js], in_=qp[h0][0][64:128, js]), r=[qp[h0][1]], w=[qp[h0 + hh][1]])
                            yield
                            k2 = 0
                            tpf = b_tp[:].bitcast(F32)
                            for hh in range(4):
                                h = g * 4 + hh
                                e_, et_ = ee[hh]
                                nct = (Nc + 127) // 128
                                for ctile in range(nct):
                                    n = min(128, Nc - ctile * 128)
                                    tpc = tpf[:, (k2 % 2) * 128:(k2 % 2) * 128 + 128]
                                    pc_, pct_ = pTc[k2 % 2]
                                    k2 += 1
                                    P.op("pe", lambda E, tpc=tpc, e_=e_, ctile=ctile, n=n: E.transpose(out=tpc[0:n, :], in_=e_[:, ctile * 128:ctile * 128 + n], identity=identf[:]), r=[et_, t_const], w=[tp_t])
                                    P.op("act", lambda E, pc_=pc_, tpc=tpc, n=n: E.copy(out=pc_[0:n, :], in_=tpc[0:n, :]), r=[tp_t], w=[pct_])
                                    P.op("pe", lambda E, pc_=pc_, n=n, ctile=ctile, nct=nct: E.matmul(b_xs[:, 256:320], lhsT=pc_[0:n, :], rhs=vcm[g][0][0:n, ctile, :], start=(ctile == 0), stop=(ctile == nct - 1), skip_group_check=True), r=[pct_, vcm[g][1]], w=[b_xst])
                                P.op("dve", lambda E, h=h, hh=hh: E.tensor_scalar(out=att[:, j, h * 64:(h + 1) * 64], in0=b_xs[:, 256:320], scalar1=sm[:, hh:hh + 1], scalar2=None, op0=ALU.mult), r=[b_xst, smt], w=[attt])
                                yield

                def gen_Y(sg):
                    ss = slice(sg * 512, (sg + 1) * 512)
                    qp = QP[sg % 2]
                    g_, gt_ = gt[sg % 2]
                    att, attt = att2[sg % 2]
                    pending = []
                    for g in range(2):
                        for hh in range(4):
                            h = g * 4 + hh
                            q_, qt_ = qp[h]
                            os_, ost_ = b_os[hh % 2]
                            ow_, owt_ = b_ow[0]
                            steps = []
                            for kt in range(0, 4 * sg + 4):
                                steps.append(("s", kt))
                            for kt in range(max(0, 4 * sg - 4), 4 * sg + 4):
                                steps.append(("w", kt))
                            first = {"s": True, "w": True}
                            LA = int(ATT_DBG.get("LA", 2))
                            ring = []
                            for i in range(len(steps) + LA):
                                if i == min(3, len(steps) - 1) and pending:
                                    pending.pop(0)()
                                if i < len(steps):
                                    br, kt = steps[i]
                                    r_ = kt - 4 * sg
                                    if br == "s":
                                        jlo, jhi = max(r_, 0), 3
                                    else:
                                        jlo, jhi = max(r_, 0), min(r_ + 4, 3)
                                    c0, c1 = jlo * 128, (jhi + 1) * 128
                                    si = sidx[0] % 4
                                    stp = bst[sidx[0] % 3][0]
                                    stt_ = bst[sidx[0] % 3][1]
                                    sidx[0] += 1
                                    ks_ = slice(kt * 128, (kt + 1) * 128)
                                    ex = []
                                    if r_ >= 0:
                                        ex.append((r_, tric))
                                    if br == "w" and 0 <= r_ + 4 <= 3:
                                        ex.append((r_ + 4, triw))
                                    if br == "s":
                                        P.op("pe", lambda E, stp=stp, ks_=ks_, q_=q_, g=g, c0=c0, c1=c1, ex=ex: E.matmul(stp[:, c0:c1], lhsT=KE[g][0][:, ks_], rhs=q_[:, c0:c1], start=True, stop=(len(ex) == 0)), r=[KE[g][1], qt_], w=[stt_])
                                    else:
                                        P.op("pe", lambda E, stp=stp, ks_=ks_, q_=q_, g=g, c0=c0, c1=c1, ex=ex: E.matmul(stp[:, c0:c1], lhsT=kwn[g][0][:, ks_], rhs=q_[0:64, c0:c1], start=True, stop=(len(ex) == 0)), r=[kwn[g][1], qt_], w=[stt_])
                                    for xi, (jj, tri_) in enumerate(ex):
                                        P.op("pe", lambda E, stp=stp, jj=jj, tri_=tri_, xi=xi, ex=ex: E.matmul(stp[:, jj * 128:(jj + 1) * 128], lhsT=ident[:], rhs=tri_[:], start=False, stop=(xi == len(ex) - 1)), r=[t_const, cst], w=[stt_])
                                    pt_s, pt_st = pT[si]
                                    P.op("act", lambda E, pt_s=pt_s, stp=stp, c0=c0, c1=c1: E.activation(out=pt_s[:, c0:c1], in_=stp[:, c0:c1], func=AF.Exp), r=[stt_], w=[pt_st])
                                    ring.append((br, kt, jlo, jhi, pt_s, pt_st))
                                if i - LA >= 0:
                                    br, kt, jlo, jhi, pt_s, pt_st = ring[i - LA]
                                    ob, obt = (os_, ost_) if br == "s" else (ow_, owt_)
                                    V = vsl[g] if br == "s" else vwn[g]
                                    for jj in range(jlo, jhi + 1):
                                        st_flag = first[br]
                                        first[br] = False
                                        P.op("pe", lambda E, ob=ob, jj=jj, pt_s=pt_s, V=V, kt=kt, st_flag=st_flag: E.matmul(ob[:, jj * 65:jj * 65 + 65], lhsT=pt_s[:, jj * 128:(jj + 1) * 128], rhs=V[0][:, kt, :], start=st_flag, stop=True, skip_group_check=True), r=[pt_st, V[1]], w=[obt])
                                yield
                            def combine(h=h, os_=os_, ost_=ost_, ow_=ow_, owt_=owt_):
                                osv = os_[:, 0:260].rearrange("p (j c) -> p j c", c=65)
                                owv = ow_[:, 0:260].rearrange("p (j c) -> p j c", c=65)
                                P.op("dve", lambda E: E.reciprocal(out=sy[:, 0:4], in_=osv[:, :, 64]), r=[ost_], w=[syt])
                                P.op("dve", lambda E: E.tensor_tensor(out=sy[:, 0:4], in0=sy[:, 0:4], in1=g_[:, :, h * 3 + 1], op=ALU.mult), r=[gt_], w=[syt])
                                P.op("dve", lambda E: E.reciprocal(out=sy[:, 4:8], in_=owv[:, :, 64]), r=[owt_], w=[syt])
                                P.op("dve", lambda E: E.tensor_tensor(out=sy[:, 4:8], in0=sy[:, 4:8], in1=g_[:, :, h * 3 + 2], op=ALU.mult), r=[gt_], w=[syt])
                                cs_ = slice(h * 64, (h + 1) * 64)
                                for jj in range(4):
                                    P.op("dve", lambda E, jj=jj: E.scalar_tensor_tensor(out=att[:, jj, cs_], in0=os_[:, jj * 65:jj * 65 + 64], scalar=sy[:, jj:jj + 1], in1=att[:, jj, cs_], op0=ALU.mult, op1=ALU.add), r=[ost_, syt], w=[attt])
                                    P.op("dve", lambda E, jj=jj: E.scalar_tensor_tensor(out=attb[:, jj, cs_], in0=ow_[:, jj * 65:jj * 65 + 64], scalar=sy[:, 4 + jj:5 + jj], in1=att[:, jj, cs_], op0=ALU.mult, op1=ALU.add), r=[owt_, syt, attt], w=[attbt])
                            pending.append(combine)
                            yield
                    while pending:
                        pending.pop(0)()
                    yield
                    a_, at_ = ast[sg % 2]
                    atr = b_tp[:, 384:896].rearrange("p (a b) -> p a b", b=128)
                    for jj in range(4):
                        for ft in range(4):
                            P.op("pe", lambda E, ft=ft, jj=jj, atr=atr: E.transpose(out=atr[:, ft, :], in_=attb[:, jj, ft * 128:(ft + 1) * 128], identity=ident[:]), r=[attbt, t_const], w=[tp_t])
                        P.op("act", lambda E, a_=a_, atr=atr, jj=jj: E.copy(out=a_[:, :, jj * 128:(jj + 1) * 128], in_=atr), r=[tp_t], w=[at_])
                        yield
                    P.dma("pool", attnT[:, ss].rearrange("(a p) t -> p a t", p=128), a_[:], at_, T["attnT"], at_)
                    yield

                def drain(gn):
                    for _ in gn:
                        pass

                n2 = sc.sb("n2", [128, 8], F32)
                n2t = sc.tok()
                P.dma("sp", n2[:], n2w[l], T["w"], n2t, n2t)
                n1n = sc.sb("n1n", [128, 8], F32)
                if l + 1 < 2:
                    P.dma("sp", n1n[:], n1w[l + 1], T["w"], n2t, n2t)
                wsf = [sc.sbt("wsf%d" % i, [128, 1024], F32) for i in range(3)]
                wsb = [sc.sbt("wsb%d" % i, [128, 1024], BF16) for i in range(3)]

                def gen_wprep():
                    chunks = []
                    for (srcw, dstw, nk) in ((wua, wuaS, 4), (wur, wurS, 8), (wo, woS, 8)):
                        for kt in range(nk):
                            chunks.append((srcw[l, kt * 128:(kt + 1) * 128, :], dstw[kt * 128:(kt + 1) * 128, :], "wmS", None))
                    for kt in range(8):
                        for c in range(4):
                            chunks.append((w1[l, kt * 128:(kt + 1) * 128, c * 1024:(c + 1) * 1024], w1dr[kt * 128:(kt + 1) * 128, c * 1024:(c + 1) * 1024], "w1s", kt))
                    for kt in range(32):
                        chunks.append((w2[l, kt * 128:(kt + 1) * 128, :], w2dr[kt * 128:(kt + 1) * 128, :], "w2s", None))
                    if l + 1 < 2:
                        for kt in range(8):
                            for c in range(6):
                                chunks.append((w_in[l + 1, kt * 128:(kt + 1) * 128, c * 900:(c + 1) * 900], winS[kt * 128:(kt + 1) * 128, c * 900:(c + 1) * 900], "winS", ("n1", kt)))
                    PRE = 2
                    for k in range(len(chunks) + PRE):
                        if k < len(chunks):
                            src, dst, dn, sk = chunks[k]
                            f_, ft_ = wsf[k % 3]
                            P.dma("pool", f_[:, 0:src.shape[-1]], src, T["w"], ft_, ft_)
                        if k - PRE >= 0:
                            src, dst, dn, sk = chunks[k - PRE]
                            f_, ft_ = wsf[(k - PRE) % 3]
                            b_, bt_ = wsb[(k - PRE) % 3]
                            wd = dst.shape[-1]
                            if sk is not None:
                                if isinstance(sk, tuple):
                                    sc_ap = n1n[:, sk[1]:sk[1] + 1]
                                else:
                                    sc_ap = n2[:, sk:sk + 1]
                                P.op("pool", lambda E, b_=b_, f_=f_, sc_ap=sc_ap, wd=wd: E.tensor_scalar(out=b_[:, 0:wd], in0=f_[:, 0:wd], scalar1=sc_ap, scalar2=1.0, op0=ALU.mult, op1=ALU.mult), r=[ft_, n2t], w=[bt_])
                            else:
                                P.op("pool", lambda E, b_=b_, f_=f_, wd=wd: E.tensor_copy(out=b_[:, 0:wd], in_=f_[:, 0:wd]), r=[ft_], w=[bt_])
                            P.dma("pool", dst, b_[:, 0:wd], bt_, T[dn], bt_)
                        yield

                gw = gen_wprep()
                gw_done = [False]

                def adv_w():
                    if not gw_done[0]:
                        try:
                            next(gw)
                        except StopIteration:
                            gw_done[0] = True

                if NSG > 0:
                    drain(gen_X(0))
                for sg in range(NSG):
                    gy = gen_Y(sg) if not ATT_DBG.get("skip_sw") else iter(())
                    gx = gen_X(sg + 1) if sg + 1 < NSG else iter(())
                    ratio = max(1, int(round((32 * sg + 110) / 100.0 * ATT_DBG.get("rs", 1.0))))
                    x_done = False
                    y_done = False
                    yk = 0
                    while not y_done:
                        for _ in range(ratio):
                            try:
                                next(gy)
                            except StopIteration:
                                y_done = True
                                break
                            yk += 1
                            if yk % 8 == 0:
                                adv_w()
                        if not x_done:
                            try:
                                next(gx)
                            except StopIteration:
                                x_done = True
                    if not x_done:
                        drain(gx)
                drain(gw)

        PH = {"inproj": phase_inproj, "mlp": phase_mlp, "rnn": phase_rnn, "merge": phase_merge, "attn": phase_attn}
        build.phases = PH
        build.ctx = dict(P=P, T=T, nc=nc, xs=xs, x_in=x_in, out_d=out_d)
        plan = build.plan
        plan(PH, build.ctx, locals())
        P.barrier()
        print("ops", P.nops, "waits", P.nwait)
    return nc, dbg


def default_plan(PH, ctx, L):
    T = ctx["T"]
    xs = ctx["xs"]
    cur, curt = ctx["x_in"], T["x_in"]
    for l in range(2):
        PH["inproj"](l, cur, curt)
        PH["attn"](l)
        PH["rnn"](l)
        PH["merge"](l, cur, curt, xs[0], T["xs0"])
        PH["mlp"](l, xs[0], T["xs0"], xs[1], T["xs1"], l == 1)
        cur, curt = xs[1], T["xs1"]


build.plan = default_plan


def host_inputs(inp, b):
    bf = ml_dtypes.bfloat16
    f = np.float32

    def pk(v):
        return np.ascontiguousarray(v.reshape(2, 8, 128).transpose(0, 2, 1)).astype(f)

    def bd(wm):
        o = np.zeros((2, 8, 128, 128), f)
        for c in range(8):
            o[:, c, 0:64, 0:64] = wm[:, 2 * c]
            o[:, c, 64:128, 64:128] = wm[:, 2 * c + 1]
        return o

    i_ = np.arange(128)
    m = {
        "x": np.ascontiguousarray(inp["x"][b]),
        "w_in": inp["w_in"],
        "n1w": pk(inp["norm1_w"]), "n2w": pk(inp["norm2_w"]),
        "fnw": np.ascontiguousarray(np.broadcast_to(inp["final_norm_w"][None, :], (128, D))).astype(f),
        "posk": np.ascontiguousarray(np.repeat(inp["cmp_pos_k"].transpose(0, 2, 1)[..., None], 2, axis=-1)),
        "posv": np.ascontiguousarray(np.repeat(inp["cmp_pos_v"].transpose(0, 2, 1)[..., None], 2, axis=-1)),
        "ckw1": np.ascontiguousarray(inp["cmp_k_w1"].reshape(2, 32, 64, 256).transpose(0, 2, 1, 3)),
        "cvw1": np.ascontiguousarray(inp["cmp_v_w1"].reshape(2, 32, 64, 256).transpose(0, 2, 1, 3)),
        "ckw2": np.ascontiguousarray(inp["cmp_k_w2"].reshape(2, 2, 128, 64).transpose(0, 2, 1, 3)),
        "cvw2": np.ascontiguousarray(inp["cmp_v_w2"].reshape(2, 2, 128, 64).transpose(0, 2, 1, 3)),
        "convw": np.ascontiguousarray(inp["conv_w"].reshape(2, 4, 8, 128).transpose(0, 3, 2, 1)),
        "convb": pk(inp["conv_b"]), "lba": pk(inp["lru_b_a"]), "lbi": pk(inp["lru_b_i"]), "llam": pk(inp["lru_lambda"]),
        "lwa": bd(inp["lru_w_a"]), "lwi": bd(inp["lru_w_i"]),
        "wua": inp["w_up_attn"], "wur": inp["w_up_rnn"], "wo": inp["w_out"], "w1": inp["mlp_w1"], "w2": inp["mlp_w2"],
        "c_ident": np.eye(128, dtype=f).astype(bf),
        "c_tric": np.where(i_[:, None] <= i_[None, :], 0.0, NEGM).astype(bf),
        "c_triw": np.where(i_[:, None] > i_[None, :], 0.0, NEGM).astype(bf),
        "c_E": (np.arange(S)[None, :] // 64 == np.arange(64)[:, None]).astype(f).astype(bf),
        "c_band": np.where((np.arange(9)[None, :] - 2) <= ((i_[:, None] + 1) // 16 - 2), 0.0, NEGM).astype(bf),
    }
    hi = (i_ >= 64).astype(np.int64)[:, None]
    jp = (np.arange(128) - 64)[None, :]
    valid = jp <= hi
    forced = jp > hi - 2
    A = np.where(valid & ~forced, 1.0, 0.0)
    Bm = np.where(valid, np.where(forced, 1e4, 0.0), -1e30)
    m["c_A"] = A.astype(f)
    m["c_B"] = Bm.astype(f)
    return {k: np.ascontiguousarray(v) for k, v in m.items()}


def kernel(**inputs):
    inp = {k: np.asarray(v) for k, v in inputs.items()}
    nc, _ = build(False)
    in_maps = [host_inputs(inp, c % 4) for c in range(8)]
    res = run_bass_kernel_spmd(nc, in_maps, core_ids=list(range(8)))
    return np.stack([np.asarray(res.results[c]["out"]) for c in range(4)], axis=0).astype(np.float32)
```

```python
import numpy as np
import ml_dtypes
from contextlib import ExitStack
import concourse.bass as bass
import concourse.mybir as mybir
from concourse.bass_utils import run_bass_kernel_spmd

F32 = mybir.dt.float32
BF16 = mybir.dt.bfloat16
AF = mybir.ActivationFunctionType
ALU = mybir.AluOpType
AX = mybir.AxisListType

S = 4096
D = 1024
DIN = 5400
NT = S // 128
NEGM = -30000.0
EPS = 1e-6
O_Q, O_KC, O_VC, O_KS, O_VS, O_KW, O_VW, O_GN, O_XR, O_GR, O_GA, O_GB = 0, 512, 640, 768, 896, 1024, 1152, 1280, 1304, 2328, 3352, 4376


ATT_DBG = {"level": 6, "nqb": NT}


class Tok:
    __slots__ = ("w", "r", "sem", "name", "x")

    def __init__(self, name="", x=False):
        self.w = {}
        self.r = {}
        self.sem = None
        self.name = name
        self.x = x


class Prog:
    ENG = ("pe", "act", "dve", "pool", "sp")

    def __init__(self, nc, es, n_dma_sems=80):
        self.nc = nc
        self.eng = {"pe": nc.tensor, "act": nc.scalar, "dve": nc.vector, "pool": nc.gpsimd, "sp": nc.sync}
        self.sems = []
        self.esem = {}
        for e in self.ENG:
            self.esem[e] = len(self.sems)
            self.sems.append(es.enter_context(nc.semaphore("es_" + e)))
        self.dma_ids = []
        for i in range(n_dma_sems):
            self.dma_ids.append(len(self.sems))
            self.sems.append(es.enter_context(nc.semaphore("ds_%d" % i)))
        self.free = list(self.dma_ids)
        self.total = [0] * len(self.sems)
        self.known = {e: [0] * len(self.sems) for e in self.ENG}
        self.nwait = 0
        self.nops = 0

    def _wait(self, eng, deps):
        E = self.eng[eng]
        kn = self.known[eng]
        for s, v in deps.items():
            if s >= 5:
                v = self.total[s]
            if kn[s] < v:
                kn[s] = v
                E.wait_ge(self.sems[s], v)
                self.nwait += 1

    @staticmethod
    def _merge(d, src):
        for s, v in src.items():
            if d.get(s, 0) < v:
                d[s] = v

    def op(self, eng, fn, r=(), w=()):
        deps = {}
        rx = [b for b in r if b.x]
        if rx:
            r = [b for b in r if not b.x]
            w = list(w) + rx
        for b in r:
            self._merge(deps, b.w)
        for b in w:
            self._merge(deps, b.w)
            self._merge(deps, b.r)
        s = self.esem[eng]
        if eng == "pe":
            deps.pop(s, None)
        self._wait(eng, deps)
        self.total[s] += 1
        n = self.total[s]
        fn(self.eng[eng]).then_inc(self.sems[s], 1)
        self.nops += 1
        for b in r:
            b.r[s] = n
        for b in w:
            b.w[s] = n

    def dma(self, eng, out, in_, src, dst, owner):
        deps = {}
        self._merge(deps, src.w)
        self._merge(deps, dst.w)
        self._merge(deps, dst.r)
        self._wait(eng, deps)
        if owner.sem is None:
            owner.sem = self.free.pop()
        s = owner.sem
        self.total[s] += 16
        v = self.total[s]
        self.eng[eng].dma_start(out=out, in_=in_).then_inc(self.sems[s], 16)
        self.nops += 1
        src.r[s] = v
        dst.w[s] = v

    def release(self, toks):
        for t in toks:
            if t.sem is not None:
                self.free.append(t.sem)
                t.sem = None

    def barrier(self):
        for e in self.ENG:
            E = self.eng[e]
            kn = self.known[e]
            for s in range(len(self.sems)):
                if s == self.esem[e]:
                    continue
                v = self.total[s]
                if kn[s] < v:
                    kn[s] = v
                    E.wait_ge(self.sems[s], v)
        arr = {}
        for e in self.ENG:
            s = self.esem[e]
            self.total[s] += 1
            arr[e] = self.total[s]
            if e == "pe":
                self.eng[e].nop().then_inc(self.sems[s], 1) if hasattr(self.eng[e], "nop") else None
            else:
                self.eng[e].nop().then_inc(self.sems[s], 1)
        for e in self.ENG:
            for f in self.ENG:
                if f == e:
                    continue
                s = self.esem[f]
                self.known[e][s] = arr[f]
                self.eng[e].wait_ge(self.sems[s], arr[f])


class Scope:
    def __init__(self, P):
        self.P = P
        self.es = ExitStack()
        self.toks = []

    def __enter__(self):
        self.es.__enter__()
        return self

    def __exit__(self, *a):
        self.P.barrier()
        self.P.release(self.toks)
        return self.es.__exit__(*a)

    uid = [0]

    def sb(self, name, shape, dt):
        Scope.uid[0] += 1
        return self.es.enter_context(self.P.nc.sbuf_tensor("%s_%d" % (name, Scope.uid[0]), list(shape), dt))

    def ps(self, name, shape, dt):
        Scope.uid[0] += 1
        return self.es.enter_context(self.P.nc.psum_tensor("%s_%d" % (name, Scope.uid[0]), list(shape), dt))

    def tok(self, name="", x=False):
        t = Tok(name, x)
        self.toks.append(t)
        return t

    def sbt(self, name, shape, dt):
        return self.sb(name, shape, dt), self.tok(name)

    def pst(self, name, shape, dt):
        return self.ps(name, shape, dt), self.tok(name, True)


def build(debug=False):
    nc = bass.Bass("TRN2", target_bir_lowering=False)
    dbg = {}

    def din(name, shape, dt=F32):
        return nc.dram_tensor(name, list(shape), dt, kind="ExternalInput").ap()

    def dscr(name, shape, dt):
        isd = bool(debug) and (debug is True or name in debug)
        kind = "ExternalOutput" if isd else "Internal"
        t = nc.dram_tensor(name, list(shape), dt, kind=kind).ap()
        if isd:
            dbg[name] = t
        return t

    x_in = din("x", [S, D])
    out_d = nc.dram_tensor("out", [S, D], F32, kind="ExternalOutput").ap()
    w_in = din("w_in", [2, D, DIN])
    n1w = din("n1w", [2, 128, 8])
    n2w = din("n2w", [2, 128, 8])
    fnw = din("fnw", [128, D])
    posk = din("posk", [2, 64, 32, 2])
    posv = din("posv", [2, 64, 32, 2])
    ckw1 = din("ckw1", [2, 64, 32, 256])
    cvw1 = din("cvw1", [2, 64, 32, 256])
    ckw2 = din("ckw2", [2, 128, 2, 64])
    cvw2 = din("cvw2", [2, 128, 2, 64])
    convw = din("convw", [2, 128, 8, 4])
    convb = din("convb", [2, 128, 8])
    lba = din("lba", [2, 128, 8])
    lbi = din("lbi", [2, 128, 8])
    llam = din("llam", [2, 128, 8])
    lwa = din("lwa", [2, 8, 128, 128])
    lwi = din("lwi", [2, 8, 128, 128])
    wua = din("wua", [2, 512, D])
    wur = din("wur", [2, D, D])
    wo = din("wo", [2, D, D])
    w1 = din("w1", [2, D, 4096])
    w2 = din("w2", [2, 4096, D])
    c_ident = din("c_ident", [128, 128], BF16)
    c_tric = din("c_tric", [128, 128], BF16)
    c_triw = din("c_triw", [128, 128], BF16)
    c_E = din("c_E", [64, S], BF16)
    c_band = din("c_band", [128, 9], BF16)
    c_A = din("c_A", [128, 128])
    c_B = din("c_B", [128, 128])

    xs = [dscr("xs0", [S, D], F32), dscr("xs1", [S, D], F32)]
    qT = dscr("qT", [8, 64, S], BF16)
    kcT = dscr("kcT", [2, 64, S], BF16)
    vcT = dscr("vcT", [2, 64, S], BF16)
    ksT = dscr("ksT", [2, 64, S], BF16)
    kwT = dscr("kwT", [2, 64, S], BF16)
    vtm = dscr("vtm", [S, 4, 64], BF16)
    gat = dscr("gat", [S, 24], F32)
    zf = dscr("zf", [4, D, S], F32)
    winS = dscr("winS", [D, DIN], BF16)
    wuaS = dscr("wuaS", [512, D], BF16)
    wurS = dscr("wurS", [D, D], BF16)
    woS = dscr("woS", [D, D], BF16)
    w1dr = dscr("w1s", [D, 4096], BF16)
    w2dr = dscr("w2s", [4096, D], BF16)
    attnT = dscr("attnT", [512, S], BF16)
    rnnT = dscr("rnnT", [D, S], BF16)

    with ExitStack() as es:
        P = Prog(nc, es)
        T = {n: Tok(n) for n in ["x_in", "out", "w", "xs0", "xs1", "qT", "kcT", "vcT", "ksT", "kwT", "vtm", "gat", "zf", "attnT", "rnnT", "w1s", "w2s", "winS", "wmS"]}
        for t in T.values():
            t.sem = None

        ident = es.enter_context(nc.sbuf_tensor("ident", [128, 128], BF16))
        identf = es.enter_context(nc.sbuf_tensor("identf", [128, 128], F32))
        t_const = Tok("const")
        P.dma("sp", ident[:], c_ident[:, :], T["w"], t_const, t_const)
        P.op("dve", lambda E: E.tensor_copy(out=identf[:], in_=ident[:]), r=[t_const], w=[t_const])

        def convert(sc, dst_ap_fn, src_ap_fn, nrow_tiles, ncols, stg, scale_ap_fn=None, chunk=2048, rows=128, toks=None):
            i = 0
            engs = ("act", "dve", "pool")
            for kt in range(nrow_tiles):
                for c0 in range(0, ncols, chunk):
                    c1 = min(ncols, c0 + chunk)
                    st, stt = stg[i % len(stg)]
                    P.dma("sp", st[0:rows, 0:c1 - c0], src_ap_fn(kt, c0, c1), T["w"], stt, stt)
                    e = engs[i % 3]
                    tk = toks[kt] if toks is not None else None
                    dst = dst_ap_fn(kt, c0, c1)
                    src = st[0:rows, 0:c1 - c0]
                    if scale_ap_fn is None:
                        if e == "act":
                            P.op(e, lambda E, dst=dst, src=src: E.copy(out=dst, in_=src), r=[stt], w=[tk])
                        else:
                            P.op(e, lambda E, dst=dst, src=src: E.tensor_copy(out=dst, in_=src), r=[stt], w=[tk])
                    else:
                        sc_ap, sc_tok = scale_ap_fn(kt)
                        if e == "act":
                            P.op(e, lambda E, dst=dst, src=src, sc_ap=sc_ap: E.activation(out=dst, in_=src, func=AF.Copy, scale=sc_ap), r=[stt, sc_tok], w=[tk])
                        else:
                            P.op(e, lambda E, dst=dst, src=src, sc_ap=sc_ap: E.tensor_scalar(out=dst, in0=src, scalar1=sc_ap, scalar2=None, op0=ALU.mult), r=[stt, sc_tok], w=[tk])
                    i += 1

        def rms_rstd(sc, xt, xtok, junk, junktok, ssq, rstd, sstok):
            P.op("act", lambda E: E.activation(out=junk[:], in_=xt[:], func=AF.Square, accum_out=ssq[:, 0:1]), r=[xtok], w=[junktok, sstok])
            P.op("act", lambda E: E.activation(out=ssq[:, 1:2], in_=ssq[:, 0:1], func=AF.Sqrt, scale=1.0 / D, bias=epsb[:, 0:1]), r=[sstok, t_const], w=[sstok])
            P.op("dve", lambda E: E.reciprocal(out=rstd[:, 0:1], in_=ssq[:, 1:2]), r=[sstok], w=[sstok])

        epsb = es.enter_context(nc.sbuf_tensor("epsb", [128, 4], F32))
        P.op("dve", lambda E: E.memset(epsb[:, 0:1], EPS), w=[t_const])
        P.op("dve", lambda E: E.memset(epsb[:, 1:2], 1.0), w=[t_const])
        P.op("dve", lambda E: E.memset(epsb[:, 2:3], 0.0), w=[t_const])

        def phase_inproj(l, xsrc, xtok):
            with Scope(P) as sc:
                wbf = sc.sb("wbf", [128, 8, DIN], BF16)
                wtok = [sc.tok("wbf%d" % k) for k in range(8)]
                n1 = sc.sb("n1", [128, 8], F32)
                n1t = sc.tok()
                P.dma("sp", n1[:], n1w[l], T["w"], n1t, n1t)
                if l == 0:
                    stg = [sc.sbt("stg%d" % i, [128, 1800], F32) for i in range(3)]
                    convert(sc, lambda kt, c0, c1: wbf[:, kt, c0:c1], lambda kt, c0, c1: w_in[l, kt * 128:(kt + 1) * 128, c0:c1], 8, DIN, stg,
                            scale_ap_fn=lambda kt: (n1[:, kt:kt + 1], n1t), chunk=1800, toks=wtok)
                else:
                    for kt in range(8):
                        P.dma("sp", wbf[:, kt, :], winS[kt * 128:(kt + 1) * 128, :], T["winS"], wtok[kt], wtok[kt])
                xt = [sc.sbt("xt%d" % i, [128, D], F32) for i in range(2)]
                junk, junkt = sc.sbt("junk", [128, D], BF16)
                ssq = [sc.sbt("ssq%d" % i, [128, 4], F32) for i in range(2)]
                xn = [sc.sbt("xn%d" % i, [128, D], BF16) for i in range(2)]
                xnT = [sc.sbt("xnT%d" % i, [128, 8, 512], BF16) for i in range(2)]
                tp = [sc.pst("tp%d" % i, [128, 8, 128], BF16) for i in range(2)]
                pf = [sc.pst("pf%d" % i, [128, 512], F32) for i in range(4)]
                ptm = [sc.pst("ptm%d" % i, [128, 512], F32) for i in range(2)]
                of32 = [sc.sbt("of32_%d" % i, [128, 512], F32) for i in range(4)]
                obf = [sc.sbt("obf_%d" % i, [128, 512], BF16) for i in range(4)]
                vst = [sc.sbt("vst%d" % i, [128, 256], BF16) for i in range(2)]
                gst = [sc.sbt("gst%d" % i, [128, 24], F32) for i in range(2)]
                cnt = {"pf": 0, "f": 0, "b": 0}
                def gen_prep(sg):
                    xT, xTt = xnT[sg % 2]
                    for j in range(4):
                        tt = sg * 4 + j
                        x_t, x_tt = xt[tt % 2]
                        sq, sqt = ssq[tt % 2]
                        xb, xbt = xn[tt % 2]
                        tpp, tpt = tp[tt % 2]
                        P.dma("sp", x_t[:], xsrc[tt * 128:(tt + 1) * 128, :], xtok, x_tt, x_tt)
                        rms_rstd(sc, x_t, x_tt, junk, junkt, sq, sq[:, 2:3], sqt)
                        yield
                        P.op("dve", lambda E, xb=xb, x_t=x_t, sq=sq: E.tensor_scalar(out=xb[:], in0=x_t[:], scalar1=sq[:, 2:3], scalar2=None, op0=ALU.mult), r=[x_tt, sqt], w=[xbt])
                        yield
                        yield
                        for kt in range(8):
                            P.op("pe", lambda E, tpp=tpp, xb=xb, kt=kt: E.transpose(out=tpp[:, kt, :], in_=xb[:, kt * 128:(kt + 1) * 128], identity=ident[:]), r=[xbt, t_const], w=[tpt])
                        P.op("act", lambda E, xT=xT, tpp=tpp, j=j: E.copy(out=xT[:, :, j * 128:(j + 1) * 128], in_=tpp[:]), r=[tpt], w=[xTt])
                        yield
                        pt, ptt = ptm[tt % 2]
                        for (c0, n, o0) in ((O_VS, 128, 0), (O_VW, 128, 128), (O_GN, 24, 256)):
                            for kt in range(8):
                                P.op("pe", lambda E, pt=pt, xT=xT, kt=kt, c0=c0, n=n, o0=o0, j=j: E.matmul(pt[:, o0:o0 + n], lhsT=xT[:, kt, j * 128:(j + 1) * 128], rhs=wbf[:, kt, c0:c0 + n], start=(kt == 0), stop=(kt == 7)),
                                     r=[xTt, wtok[kt]], w=[ptt])
                        vs_, vst_ = vst[tt % 2]
                        gs_, gst_ = gst[tt % 2]
                        P.op("dve", lambda E, vs_=vs_, pt=pt: E.tensor_copy(out=vs_[:], in_=pt[:, 0:256]), r=[ptt], w=[vst_])
                        P.op("act", lambda E, gs_=gs_, pt=pt: E.activation(out=gs_[:], in_=pt[:, 256:280], func=AF.Sigmoid), r=[ptt], w=[gst_])
                        P.dma("pool", vtm[tt * 128:(tt + 1) * 128].rearrange("p a d -> p (a d)"), vs_[:], vst_, T["vtm"], vst_)
                        P.dma("pool", gat[tt * 128:(tt + 1) * 128, :], gs_[:], gst_, T["gat"], gst_)
                        yield

                def gen_main(sg):
                    xT, xTt = xnT[sg % 2]
                    tsl = slice(sg * 512, (sg + 1) * 512)
                    jobs = []
                    qTf = qT.rearrange("h d t -> (h d) t")
                    for h2 in range(4):
                        jobs.append((O_Q + h2 * 128, 128, "q", qTf[h2 * 128:(h2 + 1) * 128, tsl], "qT"))
                    jobs.append((O_KC, 128, "c", kcT.rearrange("g d t -> (g d) t")[:, tsl], "kcT"))
                    jobs.append((O_VC, 128, "c", vcT.rearrange("g d t -> (g d) t")[:, tsl], "vcT"))
                    jobs.append((O_KS, 128, "c", ksT.rearrange("g d t -> (g d) t")[:, tsl], "ksT"))
                    jobs.append((O_KW, 128, "c", kwT.rearrange("g d t -> (g d) t")[:, tsl], "kwT"))
                    for ft in range(8):
                        jobs.append((O_XR + ft * 128, 128, "f", zf[0, ft * 128:(ft + 1) * 128, tsl], "zf"))
                    for ft in range(8):
                        jobs.append((O_GR + ft * 128, 128, "gelu", zf[1, ft * 128:(ft + 1) * 128, tsl], "zf"))
                    for ft in range(8):
                        jobs.append((O_GA + ft * 128, 128, "sig", zf[2, ft * 128:(ft + 1) * 128, tsl], "zf"))
                    for ft in range(8):
                        jobs.append((O_GB + ft * 128, 128, "sig", zf[3, ft * 128:(ft + 1) * 128, tsl], "zf"))
                    for (c0, m, kind, dst, dtk) in jobs:
                        pp, ppt = pf[cnt["pf"] % 4]
                        cnt["pf"] += 1
                        for kt in range(8):
                            P.op("pe", lambda E, pp=pp, kt=kt, c0=c0, m=m, xT=xT: E.matmul(pp[0:m, :], lhsT=wbf[:, kt, c0:c0 + m], rhs=xT[:, kt, :], start=(kt == 0), stop=(kt == 7)),
                                 r=[xTt, wtok[kt]], w=[ppt])
                        if kind in ("q", "c"):
                            ob, obt = obf[cnt["b"] % 4]
                            cnt["b"] += 1
                            scl = 0.125 if kind == "q" else 1.0
                            P.op("dve", lambda E, ob=ob, pp=pp, m=m, scl=scl: E.tensor_scalar(out=ob[0:m, :], in0=pp[0:m, :], scalar1=scl, scalar2=None, op0=ALU.mult), r=[ppt], w=[obt])
                            P.dma("pool", dst, ob[0:m, :], obt, T[dtk], obt)
                            yield
                        else:
                            ob, obt = of32[cnt["f"] % 4]
                            cnt["f"] += 1
                            if kind == "f":
                                P.op("dve", lambda E, ob=ob, pp=pp: E.tensor_copy(out=ob[:], in_=pp[:]), r=[ppt], w=[obt])
                            else:
                                fn = AF.Gelu_apprx_tanh if kind == "gelu" else AF.Sigmoid
                                P.op("act", lambda E, ob=ob, pp=pp, fn=fn: E.activation(out=ob[:], in_=pp[:], func=fn), r=[ppt], w=[obt])
                            P.dma("pool", dst, ob[:], obt, T[dtk], obt)
                            yield


                NG = S // 512
                for _ in gen_prep(0):
                    pass
                for sg in range(NG):
                    gm = gen_main(sg)
                    gp = gen_prep(sg + 1) if sg + 1 < NG else iter(())
                    m_done = False
                    p_done = False
                    k = 0
                    while not m_done:
                        try:
                            next(gm)
                        except StopIteration:
                            m_done = True
                        k += 1
                        if not p_done and k % 2 == 0:
                            try:
                                next(gp)
                            except StopIteration:
                                p_done = True
                    if not p_done:
                        for _ in gp:
                            pass

        def phase_mlp(l, xsrc, xtok, xdst, xdtok, final):
            with Scope(P) as sc:
                w1b = sc.sb("w1b", [128, 8, 4096], BF16)
                w1t = [sc.tok() for _ in range(8)]
                w2b = sc.sb("w2b", [128, 32, D], BF16)
                w2t = [sc.tok() for _ in range(32)]
                for kt in range(8):
                    P.dma("sp", w1b[:, kt, :], w1dr[kt * 128:(kt + 1) * 128, :], T["w1s"], w1t[kt], w1t[kt])
                for kt in range(32):
                    P.dma("sp", w2b[:, kt, :], w2dr[kt * 128:(kt + 1) * 128, :], T["w2s"], w2t[kt], w2t[kt])
                fw = None
                if final:
                    fw, fwt = sc.sbt("fw", [128, D], F32)
                    P.dma("sp", fw[:], fnw[:, :], T["w"], fwt, fwt)
                GT = 2
                GW = GT * 128
                xt = [sc.sbt("xt%d" % i, [128, D], F32) for i in range(4)]
                ssq = [sc.sbt("ssq%d" % i, [128, 4], F32) for i in range(4)]
                xn, xnt = sc.sbt("xn", [128, D], BF16)
                xnT = [sc.sbt("xnT%d" % i, [128, 8, GW], BF16) for i in range(2)]
                hT, hTt = sc.sbt("hT", [128, 32, GW], BF16)
                hr, hrt = sc.sbt("hr", [128, 512], F32)
                tp = [sc.pst("tp%d" % i, [128, 8, 128], BF16) for i in range(1)]
                ph = [sc.pst("ph%d" % i, [128, 2, GW], F32) for i in range(3)]
                po = [sc.pst("po%d" % i, [128, 512], F32) for i in range(2)]
                xo = [sc.sbt("xo%d" % i, [128, D], F32) for i in range(2)]
                NGR = NT // GT
                cph = [0]

                def gen_prep(gi):
                    xT, xTt = xnT[gi % 2]
                    for j in range(GT):
                        tt = gi * GT + j
                        x_t, x_tt = xt[tt % 4]
                        sq, sqt = ssq[tt % 4]
                        tpp, tpt = tp[0]
                        P.dma("sp", x_t[:], xsrc[tt * 128:(tt + 1) * 128, :], xtok, x_tt, x_tt)
                        rms_rstd(sc, x_t, x_tt, xn, xnt, sq, sq[:, 2:3], sqt)
                        yield
                        P.op("dve", lambda E, x_t=x_t, sq=sq: E.tensor_scalar(out=xn[:], in0=x_t[:], scalar1=sq[:, 2:3], scalar2=None, op0=ALU.mult), r=[x_tt, sqt], w=[xnt])
                        for kt in range(8):
                            P.op("pe", lambda E, tpp=tpp, kt=kt: E.transpose(out=tpp[:, kt, :], in_=xn[:, kt * 128:(kt + 1) * 128], identity=ident[:]), r=[xnt, t_const], w=[tpt])
                        P.op("act", lambda E, xT=xT, tpp=tpp, j=j: E.copy(out=xT[:, :, j * 128:(j + 1) * 128], in_=tpp[:]), r=[tpt], w=[xTt])
                        yield

                def gen_main(gi):
                    xT, xTt = xnT[gi % 2]
                    for f2 in range(16):
                        pp, ppt = ph[cph[0] % 3]
                        cph[0] += 1
                        for fi in range(2):
                            ft = f2 * 2 + fi
                            for kt in range(8):
                                P.op("pe", lambda E, pp=pp, fi=fi, ft=ft, kt=kt: E.matmul(pp[:, fi, :], lhsT=w1b[:, kt, ft * 128:(ft + 1) * 128], rhs=xT[:, kt, :], start=(kt == 0), stop=(kt == 7)),
                                     r=[xTt, w1t[kt]], w=[ppt])
                        P.op("act", lambda E, pp=pp: E.activation(out=hr[:], in_=pp[:].rearrange("p a b -> p (a b)"), func=AF.Relu), r=[ppt], w=[hrt])
                        e2 = "pool" if f2 % 2 else "dve"
                        P.op(e2, lambda E, f2=f2: E.tensor_tensor(out=hT[:, f2 * 2:(f2 + 1) * 2, :].rearrange("p a b -> p (a b)"), in0=hr[:], in1=hr[:], op=ALU.mult), r=[hrt], w=[hTt])
                        yield
                    for j in range(GT):
                        tt = gi * GT + j
                        x_t, x_tt = xt[tt % 4]
                        xo_, xot_ = xo[tt % 2]
                        for nh in range(2):
                            pq, pqt = po[nh]
                            for kt in range(32):
                                P.op("pe", lambda E, pq=pq, kt=kt, nh=nh, j=j: E.matmul(pq[:], lhsT=hT[:, kt, j * 128:(j + 1) * 128], rhs=w2b[:, kt, nh * 512:(nh + 1) * 512], start=(kt == 0), stop=(kt == 31)),
                                     r=[hTt, w2t[kt]], w=[pqt])
                            P.op("dve", lambda E, xo_=xo_, pq=pq, nh=nh, x_t=x_t: E.tensor_tensor(out=xo_[:, nh * 512:(nh + 1) * 512], in0=pq[:], in1=x_t[:, nh * 512:(nh + 1) * 512], op=ALU.add), r=[pqt, x_tt], w=[xot_])
                            yield
                        if not final:
                            P.dma("pool", xdst[tt * 128:(tt + 1) * 128, :], xo_[:], xot_, xdtok, xot_)
                        else:
                            sq2, sq2t = ssq[tt % 4]
                            rms_rstd(sc, xo_, xot_, xn, xnt, sq2, sq2[:, 3:4], sq2t)
                            P.op("dve", lambda E, xo_=xo_, sq2=sq2: E.scalar_tensor_tensor(out=xo_[:], in0=xo_[:], scalar=sq2[:, 3:4], in1=fw[:], op0=ALU.mult, op1=ALU.mult), r=[sq2t, fwt], w=[xot_])
                            P.dma("pool", out_d[tt * 128:(tt + 1) * 128, :], xo_[:], xot_, T["out"], xot_)
                        yield

                for _ in gen_prep(0):
                    pass
                for gi in range(NGR):
                    gm = gen_main(gi)
                    gp = gen_prep(gi + 1) if gi + 1 < NGR else iter(())
                    m_done = False
                    p_done = False
                    k = 0
                    while not m_done:
                        try:
                            next(gm)
                        except StopIteration:
                            m_done = True
                        k += 1
                        if not p_done and k % 4 == 0:
                            try:
                                next(gp)
                            except StopIteration:
                                p_done = True
                    if not p_done:
                        for _ in gp:
                            pass

        def phase_rnn(l):
            with Scope(P) as sc:
                prm, prmt = sc.sbt("prm", [128, 64], F32)
                P.dma("sp", prm[:, 0:32], convw[l].rearrange("p a b -> p (a b)"), T["w"], prmt, prmt)
                P.dma("sp", prm[:, 32:40], convb[l], T["w"], prmt, prmt)
                P.dma("sp", prm[:, 40:48], lba[l], T["w"], prmt, prmt)
                P.dma("sp", prm[:, 48:56], lbi[l], T["w"], prmt, prmt)
                P.dma("sp", prm[:, 56:64], llam[l], T["w"], prmt, prmt)
                P.op("act", lambda E: E.activation(out=prm[:, 56:64], in_=prm[:, 56:64], func=AF.Exp, scale=-1.0), r=[prmt], w=[prmt])
                P.op("act", lambda E: E.activation(out=prm[:, 56:64], in_=prm[:, 56:64], func=AF.Ln, bias=epsb[:, 1:2]), r=[prmt, t_const], w=[prmt])
                P.op("dve", lambda E: E.tensor_scalar(out=prm[:, 56:64], in0=prm[:, 56:64], scalar1=-8.0, scalar2=None, op0=ALU.mult), r=[prmt], w=[prmt])
                H = S // 2
                wst = [sc.sbt("wst%d" % i, [128, 128], F32) for i in range(2)]
                wab = [sc.sbt("wab%d" % i, [128, 128], BF16) for i in range(2)]
                wib = [sc.sbt("wib%d" % i, [128, 128], BF16) for i in range(2)]
                sets = []
                for i in range(2):
                    d = {}
                    d["xrp"] = sc.sbt("xrp%d" % i, [128, H + 4], F32)
                    d["gg"] = sc.sbt("gg%d" % i, [128, H], F32)
                    d["xc"] = sc.sbt("xc%d" % i, [128, H], F32)
                    d["xcb"] = sc.sbt("xcb%d" % i, [128, H], BF16)
                    d["rr"] = sc.sbt("rr%d" % i, [128, H], F32)
                    d["ig"] = sc.sbt("ig%d" % i, [128, H], F32)
                    d["aa"] = sc.sbt("aa%d" % i, [128, H], F32)
                    d["ro"] = sc.sbt("ro%d" % i, [128, H], BF16)
                    sets.append(d)
                pa = [sc.pst("pa%d" % i, [128, 512], F32) for i in range(6)]
                pkc = [0]
                wts = {}

                def gen_w(ct):
                    wa_, wat_ = wab[ct % 2]
                    wi_, wit_ = wib[ct % 2]
                    ws_, wst_ = wst[0]
                    ws2, wst2 = wst[1]
                    P.dma("sp", ws_[:], lwa[l, ct], T["w"], wst_, wst_)
                    P.op("dve", lambda E, wa_=wa_, ws_=ws_: E.tensor_copy(out=wa_[:], in_=ws_[:]), r=[wst_], w=[wat_])
                    P.dma("sp", ws2[:], lwi[l, ct], T["w"], wst2, wst2)
                    P.op("dve", lambda E, wi_=wi_, ws2=ws2: E.tensor_copy(out=wi_[:], in_=ws2[:]), r=[wst2], w=[wit_])

                def gen_it(it):
                    ct, hf = it // 2, it % 2
                    if hf == 0:
                        gen_w(ct)
                    wa_, wat_ = wab[ct % 2]
                    wi_, wit_ = wib[ct % 2]
                    d = sets[it % 2]
                    dprev = sets[(it + 1) % 2]
                    xrp, xrpt = d["xrp"]
                    gg, ggt = d["gg"]
                    xc, xct = d["xc"]
                    xcb, xcbt = d["xcb"]
                    rr, rrt = d["rr"]
                    ig, igt = d["ig"]
                    aa, aat = d["aa"]
                    ro, rot = d["ro"]
                    t0 = hf * H
                    if hf == 0:
                        P.op("pool", lambda E, xrp=xrp: E.memset(xrp[:, 0:4], 0.0), w=[xrpt])
                        P.dma("sp", xrp[:, 4:H + 4], zf[0, ct * 128:(ct + 1) * 128, 0:H], T["zf"], xrpt, xrpt)
                    else:
                        P.dma("sp", xrp[:, 0:H + 4], zf[0, ct * 128:(ct + 1) * 128, H - 4:S], T["zf"], xrpt, xrpt)
                    P.dma("sp", gg[:], zf[1, ct * 128:(ct + 1) * 128, t0:t0 + H], T["zf"], ggt, ggt)
                    yield
                    P.op("act", lambda E: E.activation(out=xc[:], in_=xrp[:, 4:H + 4], func=AF.Identity, scale=prm[:, ct * 4 + 3:ct * 4 + 4], bias=prm[:, 32 + ct:33 + ct]), r=[xrpt, prmt], w=[xct])
                    yield
                    for i in range(3):
                        P.op("dve", lambda E, i=i: E.scalar_tensor_tensor(out=xc[:], in0=xrp[:, 1 + i:1 + i + H], scalar=prm[:, ct * 4 + i:ct * 4 + i + 1], in1=xc[:], op0=ALU.mult, op1=ALU.add), r=[xrpt, prmt], w=[xct])
                    yield
                    P.op("pool", lambda E: E.tensor_copy(out=xcb[:], in_=xc[:]), r=[xct], w=[xcbt])
                    yield
                    for tg in range(H // 512):
                        sl = slice(tg * 512, (tg + 1) * 512)
                        p1, p1t = pa[pkc[0] % 6]
                        p2, p2t = pa[(pkc[0] + 1) % 6]
                        pkc[0] += 2
                        P.op("pe", lambda E, p1=p1, sl=sl: E.matmul(p1[:], lhsT=wa_[:], rhs=xcb[:, sl], start=True, stop=True), r=[wat_, xcbt], w=[p1t])
                        P.op("pe", lambda E, p2=p2, sl=sl: E.matmul(p2[:], lhsT=wi_[:], rhs=xcb[:, sl], start=True, stop=True), r=[wit_, xcbt], w=[p2t])
                        P.op("act", lambda E, p1=p1, sl=sl: E.activation(out=rr[:, sl], in_=p1[:], func=AF.Sigmoid, bias=prm[:, 40 + ct:41 + ct]), r=[p1t, prmt], w=[rrt])
                        P.op("act", lambda E, p2=p2, sl=sl: E.activation(out=ig[:, sl], in_=p2[:], func=AF.Sigmoid, bias=prm[:, 48 + ct:49 + ct]), r=[p2t, prmt], w=[igt])
                    yield
                    P.op("act", lambda E: E.activation(out=aa[:], in_=rr[:], func=AF.Exp, scale=prm[:, 56 + ct:57 + ct]), r=[rrt, prmt], w=[aat])
                    P.op("pool", lambda E: E.tensor_tensor(out=ig[:], in0=ig[:], in1=xc[:], op=ALU.mult), r=[xct], w=[igt])
                    yield
                    P.op("pool", lambda E: E.tensor_tensor(out=rr[:], in0=aa[:], in1=aa[:], op=ALU.mult), r=[aat], w=[rrt])
                    yield
                    P.op("act", lambda E: E.activation(out=rr[:], in_=rr[:], func=AF.Sqrt, scale=-1.0, bias=epsb[:, 1:2]), r=[rrt, t_const], w=[rrt])
                    yield
                    P.op("dve", lambda E: E.tensor_tensor(out=ig[:], in0=ig[:], in1=rr[:], op=ALU.mult), r=[rrt], w=[igt])
                    if hf == 0:
                        P.op("dve", lambda E: E.tensor_tensor_scan(out=xc[:], data0=aa[:], data1=ig[:], initial=0.0, op0=ALU.mult, op1=ALU.add), r=[aat, igt], w=[xct])
                    else:
                        xcp, xcpt = dprev["xc"]
                        P.op("dve", lambda E: E.tensor_tensor_scan(out=xc[:], data0=aa[:], data1=ig[:], initial=xcp[:, H - 1:H], op0=ALU.mult, op1=ALU.add), r=[aat, igt, xcpt], w=[xct])
                    yield
                    P.op("pool", lambda E: E.tensor_tensor(out=ro[:], in0=xc[:], in1=gg[:], op=ALU.mult), r=[xct, ggt], w=[rot])
                    P.dma("pool", rnnT[ct * 128:(ct + 1) * 128, t0:t0 + H], ro[:], rot, T["rnnT"], rot)
                    yield

                active = []
                nxt = 0
                while nxt < 16 or active:
                    while len(active) < 2 and nxt < 16:
                        active.append(gen_it(nxt))
                        nxt += 1
                    for gnr in list(active):
                        try:
                            next(gnr)
                        except StopIteration:
                            active.remove(gnr)

        def phase_merge(l, xsrc, xtok, xdst, xdtok):
            with Scope(P) as sc:
                wa = sc.sb("wa", [128, 4, D], BF16)
                wat = [sc.tok() for _ in range(4)]
                wr = sc.sb("wr", [128, 8, D], BF16)
                wrt = [sc.tok() for _ in range(8)]
                wob = sc.sb("wob", [128, 8, D], BF16)
                wot = [sc.tok() for _ in range(8)]
                for kt in range(4):
                    P.dma("sp", wa[:, kt, :], wuaS[kt * 128:(kt + 1) * 128, :], T["wmS"], wat[kt], wat[kt])
                for kt in range(8):
                    P.dma("sp", wr[:, kt, :], wurS[kt * 128:(kt + 1) * 128, :], T["wmS"], wrt[kt], wrt[kt])
                for kt in range(8):
                    P.dma("sp", wob[:, kt, :], woS[kt * 128:(kt + 1) * 128, :], T["wmS"], wot[kt], wot[kt])
                aT = [sc.sbt("aT%d" % i, [128, 4, 512], BF16) for i in range(2)]
                rT = [sc.sbt("rT%d" % i, [128, 8, 512], BF16) for i in range(2)]
                sa = [sc.sbt("sa%d" % i, [128, 512], F32) for i in range(2)]
                sb_ = [sc.sbt("sb%d" % i, [128, 512], F32) for i in range(2)]
                t1 = [sc.sbt("t1%d" % i, [128, 512], F32) for i in range(2)]
                t2 = [sc.sbt("t2%d" % i, [128, 512], F32) for i in range(2)]
                mT = [sc.sbt("mT%d" % i, [128, 8, 512], BF16) for i in range(2)]
                pA = [sc.pst("pA%d" % i, [128, 512], F32) for i in range(2)]
                pB = [sc.pst("pB%d" % i, [128, 512], F32) for i in range(2)]
                pO = [sc.pst("pO%d" % i, [128, 512], F32) for i in range(2)]
                xt = [sc.sbt("xt%d" % i, [128, D], F32) for i in range(2)]
                xo = [sc.sbt("xo%d" % i, [128, D], F32) for i in range(2)]
                k = 0
                for sg in range(8):
                    tsl = slice(sg * 512, (sg + 1) * 512)
                    a_, at_ = aT[sg % 2]
                    r_, rt_ = rT[sg % 2]
                    m_, mt_ = mT[sg % 2]
                    P.dma("sp", a_[:], attnT[:, tsl].rearrange("(a p) t -> p a t", p=128), T["attnT"], at_, at_)
                    P.dma("sp", r_[:], rnnT[:, tsl].rearrange("(a p) t -> p a t", p=128), T["rnnT"], rt_, rt_)
                    for ft in range(8):
                        fs = slice(ft * 128, (ft + 1) * 128)
                        sa_, sat_ = sa[k % 2]
                        sbb, sbt_ = sb_[k % 2]
                        u1, u1t = t1[k % 2]
                        u2, u2t = t2[k % 2]
                        p_a, pat = pA[k % 2]
                        p_b, pbt = pB[k % 2]
                        k += 1
                        P.dma("sp", sa_[:], zf[2, fs, tsl], T["zf"], sat_, sat_)
                        P.dma("sp", sbb[:], zf[3, fs, tsl], T["zf"], sbt_, sbt_)
                        for kt in range(4):
                            P.op("pe", lambda E, p_a=p_a, kt=kt, fs=fs, a_=a_: E.matmul(p_a[:], lhsT=wa[:, kt, fs], rhs=a_[:, kt, :], start=(kt == 0), stop=(kt == 3)), r=[wat[kt], at_], w=[pat])
                        for kt in range(8):
                            P.op("pe", lambda E, p_b=p_b, kt=kt, fs=fs, r_=r_: E.matmul(p_b[:], lhsT=wr[:, kt, fs], rhs=r_[:, kt, :], start=(kt == 0), stop=(kt == 7)), r=[wrt[kt], rt_], w=[pbt])
                        P.op("dve", lambda E, u1=u1, p_a=p_a, sa_=sa_: E.tensor_tensor(out=u1[:], in0=p_a[:], in1=sa_[:], op=ALU.mult), r=[pat, sat_], w=[u1t])
                        P.op("dve", lambda E, u2=u2, p_b=p_b, sbb=sbb: E.tensor_tensor(out=u2[:], in0=p_b[:], in1=sbb[:], op=ALU.mult), r=[pbt, sbt_], w=[u2t])
                        P.op("pool", lambda E, m_=m_, ft=ft, u1=u1, u2=u2: E.tensor_tensor(out=m_[:, ft, :], in0=u1[:], in1=u2[:], op=ALU.add), r=[u1t, u2t], w=[mt_])
                    for j in range(4):
                        tt = sg * 4 + j
                        x_t, x_tt = xt[tt % 2]
                        xo_, xot_ = xo[tt % 2]
                        P.dma("sp", x_t[:], xsrc[tt * 128:(tt + 1) * 128, :], xtok, x_tt, x_tt)
                        for nh in range(2):
                            pq, pqt = pO[nh]
                            for kt in range(8):
                                P.op("pe", lambda E, pq=pq, kt=kt, nh=nh, m_=m_, j=j: E.matmul(pq[:], lhsT=m_[:, kt, j * 128:(j + 1) * 128], rhs=wob[:, kt, nh * 512:(nh + 1) * 512], start=(kt == 0), stop=(kt == 7)), r=[mt_, wot[kt]], w=[pqt])
                            P.op("dve", lambda E, xo_=xo_, pq=pq, nh=nh, x_t=x_t: E.tensor_tensor(out=xo_[:, nh * 512:(nh + 1) * 512], in0=pq[:], in1=x_t[:, nh * 512:(nh + 1) * 512], op=ALU.add), r=[pqt, x_tt], w=[xot_])
                        P.dma("pool", xdst[tt * 128:(tt + 1) * 128, :], xo_[:], xot_, xdtok, xot_)


        def phase_attn(l):
            with Scope(P) as sc:
                cst = sc.tok("cst")
                tric = sc.sb("tric", [128, 128], BF16)
                triw = sc.sb("triw", [128, 128], BF16)
                band = sc.sb("band", [128, 9], BF16)
                cA = sc.sb("cA", [128, 128], F32)
                cB = sc.sb("cB", [128, 128], F32)
                for dst, src in ((tric, c_tric), (triw, c_triw), (band, c_band), (cA, c_A), (cB, c_B)):
                    P.dma("sp", dst[:], src[:, :], T["w"], cst, cst)
                kcm = [sc.sbt("kcm%d" % g, [64, 256], BF16) for g in range(2)]
                vcm = [sc.sbt("vcm%d" % g, [128, 2, 64], BF16) for g in range(2)]
                with Scope(P) as s2:
                    stg = [s2.sbt("cstg%d" % i, [64, 2048], F32) for i in range(2)]
                    w2s, w2st = s2.sbt("w2s", [128, 128], F32)
                    pss, psst = s2.sbt("pss", [64, 64], F32)
                    w1b = s2.sb("w1b", [64, 32, 256], BF16)
                    w1bt = s2.tok()
                    w2b, w2bt = s2.sbt("w2b", [128, 2, 64], BF16)
                    posb, posbt = s2.sbt("posb", [64, 32, 2], BF16)
                    kg = [s2.sbt("kg%d" % g, [64, S], BF16) for g in range(2)]
                    hid = [[s2.sbt("hid%d%d" % (g, h), [128, 256], BF16) for h in range(2)] for g in range(2)]
                    bia, biat = s2.sbt("bia", [128, 2], F32)
                    psg = [s2.pst("psg%d" % g, [128, 512], F32) for g in range(2)]
                    psb, psbt = s2.pst("psb", [128, 512], F32)
                    pso, psot = s2.pst("pso", [128, 512], F32)
                    for (w1d, w2d, posd, srcT, srct, is_k) in ((ckw1, ckw2, posk, kcT, "kcT", True), (cvw1, cvw2, posv, vcT, "vcT", False)):
                        convert(s2, lambda kt, c0, c1: w1b[:].rearrange("p a b -> p (a b)")[:, c0:c1], lambda kt, c0, c1: w1d[l].rearrange("p a b -> p (a b)")[:, c0:c1], 1, 32 * 256, stg, chunk=2048, rows=64, toks=[w1bt])
                        P.dma("sp", w2s[:], w2d[l].rearrange("p a b -> p (a b)"), T["w"], w2st, w2st)
                        P.op("dve", lambda E: E.tensor_copy(out=w2b[:].rearrange("p a b -> p (a b)"), in_=w2s[:]), r=[w2st], w=[w2bt])
                        P.dma("sp", pss[:], posd[l].rearrange("p a b -> p (a b)"), T["w"], psst, psst)
                        P.op("dve", lambda E: E.tensor_copy(out=posb[:].rearrange("p a b -> p (a b)"), in_=pss[:]), r=[psst], w=[posbt])
                        for g in range(2):
                            P.dma("sp", kg[g][0][:], srcT[g], T[srct], kg[g][1], kg[g][1])
                        for ht in range(2):
                            hs = slice(ht * 128, (ht + 1) * 128)
                            for p in range(32):
                                for g in range(2):
                                    P.op("pe", lambda E, g=g, p=p, hs=hs: E.matmul(psg[g][0][:, 0:255], lhsT=w1b[:, p, hs], rhs=kg[g][0][:, p:p + 16 * 254 + 1:16], start=(p == 0), stop=(p == 31)),
                                         r=[w1bt, kg[g][1]], w=[psg[g][1]])
                                P.op("pe", lambda E, p=p, hs=hs: E.matmul(psb[:, 0:2], lhsT=w1b[:, p, hs], rhs=posb[:, p, :], start=(p == 0), stop=(p == 31)), r=[w1bt, posbt], w=[psbt])
                            P.op("dve", lambda E: E.tensor_copy(out=bia[:], in_=psb[:, 0:2]), r=[psbt], w=[biat])
                            for g in range(2):
                                P.op("act", lambda E, g=g, ht=ht: E.activation(out=hid[g][ht][0][:, 0:255], in_=psg[g][0][:, 0:255], func=AF.Gelu_apprx_tanh, bias=bia[:, 0:1]), r=[psg[g][1], biat], w=[hid[g][ht][1]])
                        for g in range(2):
                            if is_k:
                                for ht in range(2):
                                    P.op("pe", lambda E, g=g, ht=ht: E.matmul(pso[0:64, 0:255], lhsT=w2b[:, ht, :], rhs=hid[g][ht][0][:, 0:255], start=(ht == 0), stop=(ht == 1)), r=[w2bt, hid[g][ht][1]], w=[psot])
                                P.op("dve", lambda E, g=g: E.tensor_copy(out=kcm[g][0][:, 0:255], in_=pso[0:64, 0:255]), r=[psot], w=[kcm[g][1]])
                            else:
                                for ctile in range(2):
                                    n = 128 if ctile == 0 else 127
                                    for ht in range(2):
                                        P.op("pe", lambda E, g=g, ht=ht, ctile=ctile, n=n: E.matmul(pso[0:n, 256:320], lhsT=hid[g][ht][0][:, ctile * 128:ctile * 128 + n], rhs=w2b[:, ht, :], start=(ht == 0), stop=(ht == 1)), r=[w2bt, hid[g][ht][1]], w=[psot])
                                    P.op("dve", lambda E, g=g, ctile=ctile, n=n: E.tensor_copy(out=vcm[g][0][0:n, ctile, :], in_=pso[0:n, 256:320]), r=[psot], w=[vcm[g][1]])
                KE = [sc.sbt("KE%d" % g, [128, S], BF16) for g in range(2)]
                kwn = [sc.sbt("kwn%d" % g, [128, S], BF16) for g in range(2)]
                vsl = [sc.sbt("vsl%d" % g, [128, 32, 65], BF16) for g in range(2)]
                vwn = [sc.sbt("vwn%d" % g, [128, 32, 65], BF16) for g in range(2)]
                for g in range(2):
                    P.dma("sp", KE[g][0][0:64, :], ksT[g], T["ksT"], KE[g][1], KE[g][1])
                    P.dma("sp", KE[g][0][64:128, :], c_E[:, :], T["w"], KE[g][1], KE[g][1])
                    P.op("pool", lambda E, g=g: E.memset(kwn[g][0][64:128, :], 0.0), w=[kwn[g][1]])
                    P.dma("sp", kwn[g][0][0:64, :], kwT[g], T["kwT"], kwn[g][1], kwn[g][1])
                    for (vv, j) in ((vsl[g], g), (vwn[g], 2 + g)):
                        P.op("pool", lambda E, vv=vv: E.memset(vv[0][:, :, 64:65], 1.0), w=[vv[1]])
                        for k8 in range(8):
                            P.dma("sp", vv[0][:, k8 * 4:(k8 + 1) * 4, 0:64], vtm[k8 * 512:(k8 + 1) * 512, j, :].rearrange("(k p) d -> p k d", p=128), T["vtm"], vv[1], vv[1])
                QP = [[sc.sbt("QP%d_%d" % (i, h), [128, 512], BF16) for h in range(8)] for i in range(2)]
                gt = [sc.sbt("gt%d" % i, [128, 4, 24], F32) for i in range(2)]
                bst = [sc.pst("bst%d" % i, [128, 512], F32) for i in range(3)]
                b_os = [sc.pst("b_os%d" % i, [128, 512], F32) for i in range(2)]
                b_ow = [sc.pst("b_ow%d" % i, [128, 512], F32) for i in range(1)]
                b_xs, b_xst = sc.pst("b_xs", [128, 512], F32)
                b_tp = sc.ps("b_tp", [128, 1024], BF16)
                tp_t = sc.tok("tp", True)
                NCH = 4
                CH = []
                for c in range(NCH):
                    d = {}
                    d["ee"] = [sc.sbt("ee%d_%d" % (c, i), [128, 256], F32) for i in range(4)]
                    d["pb"] = [sc.sbt("pb%d_%d" % (c, i), [128, 256], BF16) for i in range(4)]
                    d["pTc"] = [sc.sbt("pTc%d_%d" % (c, i), [128, 128], BF16) for i in range(2)]
                    d["sm"] = sc.sbt("sm%d" % c, [128, 16], F32)
                    d["P4"] = sc.sbt("P4_%d" % c, [128, 264], F32)
                    d["imp"] = sc.sbt("imp%d" % c, [128, 64], F32)
                    d["scr"] = sc.sbt("scr%d" % c, [128, 64], F32)
                    d["scr2"] = sc.sbt("scr2_%d" % c, [128, 64], F32)
                    d["m8"] = sc.sbt("m8_%d" % c, [128, 16], F32)
                    d["penb"] = sc.sbt("penb%d" % c, [128, 128], BF16)
                    P.op("pool", lambda E, d=d: E.memset(d["penb"][0][:], 0.0), w=[d["penb"][1]])
                    P.op("dve", lambda E, d=d: E.memset(d["P4"][0][:], 0.0), w=[d["P4"][1]])
                    CH.append(d)
                pT = [sc.sbt("pT%d" % i, [128, 512], BF16) for i in range(4)]
                sy, syt = sc.sbt("sy", [128, 8], F32)
                att2 = [sc.sbt("att%d" % i, [128, 4, 512], F32) for i in range(2)]
                attb, attbt = sc.sbt("attb", [128, 4, 512], BF16)
                ast = [sc.sbt("ast%d" % i, [128, 4, 512], BF16) for i in range(2)]
                sidx = [0]
                NSG = ATT_DBG["nqb"] // 4

                def gen_X(sg):
                    ss = slice(sg * 512, (sg + 1) * 512)
                    qp = QP[sg % 2]
                    g_, gt_ = gt[sg % 2]
                    for h in range(8):
                        P.dma("sp", qp[h][0][0:64, :], qT[h, :, ss], T["qT"], qp[h][1], qp[h][1])
                    P.dma("sp", g_[:], gat[ss, :].rearrange("(j p) c -> p j c", p=128), T["gat"], gt_, gt_)
                    yield
                    for g in range(2):
                        act = [gen_Xchain(sg, g, j) for j in range(4)]
                        while act:
                            for gn in list(act):
                                try:
                                    next(gn)
                                    yield
                                except StopIteration:
                                    act.remove(gn)

                def gen_Xchain(sg, g, j):
                    qp = QP[sg % 2]
                    g_, gt_ = gt[sg % 2]
                    att, attt = att2[sg % 2]
                    d = CH[j % NCH]
                    ee, pb, pTc = d["ee"], d["pb"], d["pTc"]
                    sm, smt = d["sm"]
                    P4, P4t = d["P4"]
                    imp, impt = d["imp"]
                    scr, scrt = d["scr"]
                    scr2, scr2t = d["scr2"]
                    m8, m8t = d["m8"]
                    penb, penbt = d["penb"]
                    P4v = P4[:, 0:256].rearrange("p (j f) -> p j f", f=4)
                    P4w = P4[:, 4:260].rearrange("p (j f) -> p j f", f=4)
                    if True:
                        if True:
                            qb = sg * 4 + j
                            js = slice(j * 128, (j + 1) * 128)
                            Nc = min(255, 8 * qb + 7)
                            cb0 = max(0, 8 * qb - 2)
                            cb1 = min(Nc, 8 * qb + 7)
                            ps_s = b_xs[:, 0:256]
                            for hh in range(4):
                                h = g * 4 + hh
                                P.op("pe", lambda E, h=h, g=g, js=js, Nc=Nc: E.matmul(ps_s[:, 0:Nc], lhsT=qp[h][0][0:64, js], rhs=kcm[g][0][:, 0:Nc], start=True, stop=False), r=[qp[h][1], kcm[g][1]], w=[b_xst])
                                P.op("pe", lambda E, cb0=cb0, cb1=cb1, qb=qb: E.matmul(ps_s[:, cb0:cb1], lhsT=ident[:], rhs=band[:, cb0 - (8 * qb - 2):cb1 - (8 * qb - 2)], start=False, stop=True), r=[t_const, cst], w=[b_xst])
                                e_, et_ = ee[hh]
                                P.op("act", lambda E, e_=e_, hh=hh, Nc=Nc: E.activation(out=e_[:, 0:Nc], in_=ps_s[:, 0:Nc], func=AF.Exp, accum_out=sm[:, 8 + hh:9 + hh]), r=[b_xst], w=[et_, smt])
                                yield
                            P.op("dve", lambda E: E.tensor_scalar(out=sm[:, 12:16], in0=sm[:, 8:12], scalar1=1e-20, scalar2=None, op0=ALU.max), r=[smt], w=[smt])
                            P.op("dve", lambda E: E.reciprocal(out=sm[:, 12:16], in_=sm[:, 12:16]), r=[smt], w=[smt])
                            P.op("dve", lambda E: E.tensor_tensor(out=sm[:, 0:4], in0=sm[:, 12:16], in1=g_[:, j, g * 12:(g + 1) * 12].rearrange("p (h c) -> p h c", c=3)[:, :, 0], op=ALU.mult), r=[smt, gt_], w=[smt])
                            for hh in range(4):
                                e_, et_ = ee[hh]
                                if hh == 0:
                                    P.op("dve", lambda E, e_=e_, hh=hh, Nc=Nc: E.tensor_scalar(out=P4[:, 4:4 + Nc], in0=e_[:, 0:Nc], scalar1=sm[:, 12 + hh:13 + hh], scalar2=None, op0=ALU.mult), r=[et_, smt], w=[P4t])
                                else:
                                    P.op("dve", lambda E, e_=e_, hh=hh, Nc=Nc: E.scalar_tensor_tensor(out=P4[:, 4:4 + Nc], in0=e_[:, 0:Nc], scalar=sm[:, 12 + hh:13 + hh], in1=P4[:, 4:4 + Nc], op0=ALU.mult, op1=ALU.add), r=[et_, smt], w=[P4t])
                            yield
                            P.op("dve", lambda E: E.tensor_tensor(out=imp[:], in0=P4v[:, :, 1], in1=P4v[:, :, 2], op=ALU.add), r=[P4t], w=[impt])
                            P.op("dve", lambda E: E.tensor_tensor(out=imp[:], in0=imp[:], in1=P4v[:, :, 3], op=ALU.add), r=[P4t], w=[impt])
                            P.op("dve", lambda E: E.scalar_tensor_tensor(out=imp[:], in0=imp[:], scalar=2.0, in1=P4v[:, :, 0], op0=ALU.mult, op1=ALU.add), r=[P4t], w=[impt])
                            P.op("dve", lambda E: E.tensor_tensor(out=imp[:], in0=imp[:], in1=P4w[:, :, 0], op=ALU.add), r=[P4t], w=[impt])
                            P.op("dve", lambda E, qb=qb: E.tensor_tensor(out=scr[:], in0=imp[:], in1=cA[:, 64 - 2 * qb:128 - 2 * qb], op=ALU.mult), r=[impt, cst], w=[scrt])
                            P.op("dve", lambda E, qb=qb: E.tensor_tensor(out=scr[:], in0=scr[:], in1=cB[:, 64 - 2 * qb:128 - 2 * qb], op=ALU.add), r=[cst], w=[scrt])
                            P.op("dve", lambda E: E.memset(scr[:, 0:1], 1e4), w=[scrt])
                            yield
                            nb = min(64, 2 * qb + 2)
                            if nb > 16:
                                P.op("dve", lambda E: E.max(out=m8[:, 0:8], in_=scr[:]), r=[scrt], w=[m8t])
                                P.op("dve", lambda E: E.match_replace(out=scr2[:], in_to_replace=m8[:, 0:8], in_values=scr[:], imm_value=-3e38), r=[scrt, m8t], w=[scr2t])
                                P.op("dve", lambda E: E.max(out=m8[:, 8:16], in_=scr2[:]), r=[scr2t], w=[m8t])
                                P.op("dve", lambda E, nb=nb: E.tensor_scalar(out=scr2[:, 0:nb], in0=scr[:, 0:nb], scalar1=m8[:, 15:16], scalar2=None, op0=ALU.is_ge), r=[scrt, m8t], w=[scr2t])
                                P.op("dve", lambda E, nb=nb: E.tensor_scalar(out=penb[:, 64:64 + nb], in0=scr2[:, 0:nb], scalar1=-1.0, scalar2=-NEGM, op0=ALU.add, op1=ALU.mult), r=[scr2t], w=[penbt])
                                yield
                            ptr = b_tp[:, 256:384]
                            P.op("pe", lambda E, ptr=ptr: E.transpose(out=ptr, in_=penb[:], identity=ident[:]), r=[penbt, t_const], w=[tp_t])
                            h0 = g * 4
                            P.op("act", lambda E, ptr=ptr, js=js, h0=h0: E.copy(out=qp[h0][0][64:128, js], in_=ptr[64:128, :]), r=[tp_t], w=[qp[h0][1]])
                            for hh in range(1, 4):
                                P.op("pool", lambda E, js=js, h0=h0, hh=hh: E.tensor_copy(out=qp[h0 + hh][0][64:128, js], in_=qp[h0][0][64:128, js]), r=[qp[h0][1]], w=[qp[h0 + hh][1]])
                            yield
                            k2 = 0
                            tpf = b_tp[:].bitcast(F32)
                            for hh in range(4):
                                h = g * 4 + hh
                                e_, et_ = ee[hh]
                                nct = (Nc + 127) // 128
                                for ctile in range(nct):
                                    n = min(128, Nc - ctile * 128)
                                    tpc = tpf[:, (k2 % 2) * 128:(k2 % 2) * 128 + 128]
                                    pc_, pct_ = pTc[k2 % 2]
                                    k2 += 1
                                    P.op("pe", lambda E, tpc=tpc, e_=e_, ctile=ctile, n=n: E.transpose(out=tpc[0:n, :], in_=e_[:, ctile * 128:ctile * 128 + n], identity=identf[:]), r=[et_, t_const], w=[tp_t])
                                    P.op("act", lambda E, pc_=pc_, tpc=tpc, n=n: E.copy(out=pc_[0:n, :], in_=tpc[0:n, :]), r=[tp_t], w=[pct_])
                                    P.op("pe", lambda E, pc_=pc_, n=n, ctile=ctile, nct=nct: E.matmul(b_xs[:, 256:320], lhsT=pc_[0:n, :], rhs=vcm[g][0][0:n, ctile, :], start=(ctile == 0), stop=(ctile == nct - 1), skip_group_check=True), r=[pct_, vcm[g][1]], w=[b_xst])
                                P.op("dve", lambda E, h=h, hh=hh: E.tensor_scalar(out=att[:, j, h * 64:(h + 1) * 64], in0=b_xs[:, 256:320], scalar1=sm[:, hh:hh + 1], scalar2=None, op0=ALU.mult), r=[b_xst, smt], w=[attt])
                                yield

                def gen_Y(sg):
                    ss = slice(sg * 512, (sg + 1) * 512)
                    qp = QP[sg % 2]
                    g_, gt_ = gt[sg % 2]
                    att, attt = att2[sg % 2]
                    pending = []
                    for g in range(2):
                        for hh in range(4):
                            h = g * 4 + hh
                            q_, qt_ = qp[h]
                            os_, ost_ = b_os[hh % 2]
                            ow_, owt_ = b_ow[0]
                            steps = []
                            for kt in range(0, 4 * sg + 4):
                                steps.append(("s", kt))
                            for kt in range(max(0, 4 * sg - 4), 4 * sg + 4):
                                steps.append(("w", kt))
                            first = {"s": True, "w": True}
                            LA = int(ATT_DBG.get("LA", 2))
                            ring = []
                            for i in range(len(steps) + LA):
                                if i == min(3, len(steps) - 1) and pending:
                                    pending.pop(0)()
                                if i < len(steps):
                                    br, kt = steps[i]
                                    r_ = kt - 4 * sg
                                    if br == "s":
                                        jlo, jhi = max(r_, 0), 3
                                    else:
                                        jlo, jhi = max(r_, 0), min(r_ + 4, 3)
                                    c0, c1 = jlo * 128, (jhi + 1) * 128
                                    si = sidx[0] % 4
                                    stp = bst[sidx[0] % 3][0]
                                    stt_ = bst[sidx[0] % 3][1]
                                    sidx[0] += 1
                                    ks_ = slice(kt * 128, (kt + 1) * 128)
                                    ex = []
                                    if r_ >= 0:
                                        ex.append((r_, tric))
                                    if br == "w" and 0 <= r_ + 4 <= 3:
                                        ex.append((r_ + 4, triw))
                                    if br == "s":
                                        P.op("pe", lambda E, stp=stp, ks_=ks_, q_=q_, g=g, c0=c0, c1=c1, ex=ex: E.matmul(stp[:, c0:c1], lhsT=KE[g][0][:, ks_], rhs=q_[:, c0:c1], start=True, stop=(len(ex) == 0)), r=[KE[g][1], qt_], w=[stt_])
                                    else:
                                        P.op("pe", lambda E, stp=stp, ks_=ks_, q_=q_, g=g, c0=c0, c1=c1, ex=ex: E.matmul(stp[:, c0:c1], lhsT=kwn[g][0][:, ks_], rhs=q_[:, c0:c1], start=True, stop=(len(ex) == 0)), r=[kwn[g][1], qt_], w=[stt_])
                                    for xi, (jj, tri_) in enumerate(ex):
                                        P.op("pe", lambda E, stp=stp, jj=jj, tri_=tri_, xi=xi, ex=ex: E.matmul(stp[:, jj * 128:(jj + 1) * 128], lhsT=ident[:], rhs=tri_[:], start=False, stop=(xi == len(ex) - 1)), r=[t_const, cst], w=[stt_])
                                    pt_s, pt_st = pT[si]
                                    P.op("act", lambda E, pt_s=pt_s, stp=stp, c0=c0, c1=c1: E.activation(out=pt_s[:, c0:c1], in_=stp[:, c0:c1], func=AF.Exp), r=[stt_], w=[pt_st])
                                    ring.append((br, kt, jlo, jhi, pt_s, pt_st))
                                if i - LA >= 0:
                                    br, kt, jlo, jhi, pt_s, pt_st = ring[i - LA]
                                    ob, obt = (os_, ost_) if br == "s" else (ow_, owt_)
                                    V = vsl[g] if br == "s" else vwn[g]
                                    for jj in range(jlo, jhi + 1):
                                        st_flag = first[br]
                                        first[br] = False
                                        P.op("pe", lambda E, ob=ob, jj=jj, pt_s=pt_s, V=V, kt=kt, st_flag=st_flag: E.matmul(ob[:, jj * 65:jj * 65 + 65], lhsT=pt_s[:, jj * 128:(jj + 1) * 128], rhs=V[0][:, kt, :], start=st_flag, stop=True, skip_group_check=True), r=[pt_st, V[1]], w=[obt])
                                yield
                            def combine(h=h, os_=os_, ost_=ost_, ow_=ow_, owt_=owt_):
                                osv = os_[:, 0:260].rearrange("p (j c) -> p j c", c=65)
                                owv = ow_[:, 0:260].rearrange("p (j c) -> p j c", c=65)
                                P.op("dve", lambda E: E.reciprocal(out=sy[:, 0:4], in_=osv[:, :, 64]), r=[ost_], w=[syt])
                                P.op("dve", lambda E: E.tensor_tensor(out=sy[:, 0:4], in0=sy[:, 0:4], in1=g_[:, :, h * 3 + 1], op=ALU.mult), r=[gt_], w=[syt])
                                P.op("dve", lambda E: E.reciprocal(out=sy[:, 4:8], in_=owv[:, :, 64]), r=[owt_], w=[syt])
                                P.op("dve", lambda E: E.tensor_tensor(out=sy[:, 4:8], in0=sy[:, 4:8], in1=g_[:, :, h * 3 + 2], op=ALU.mult), r=[gt_], w=[syt])
                                cs_ = slice(h * 64, (h + 1) * 64)
                                for jj in range(4):
                                    P.op("dve", lambda E, jj=jj: E.scalar_tensor_tensor(out=att[:, jj, cs_], in0=os_[:, jj * 65:jj * 65 + 64], scalar=sy[:, jj:jj + 1], in1=att[:, jj, cs_], op0=ALU.mult, op1=ALU.add), r=[ost_, syt], w=[attt])
                                    P.op("dve", lambda E, jj=jj: E.scalar_tensor_tensor(out=attb[:, jj, cs_], in0=ow_[:, jj * 65:jj * 65 + 64], scalar=sy[:, 4 + jj:5 + jj], in1=att[:, jj, cs_], op0=ALU.mult, op1=ALU.add), r=[owt_, syt, attt], w=[attbt])
                            pending.append(combine)
                            yield
                    while pending:
                        pending.pop(0)()
                    yield
                    a_, at_ = ast[sg % 2]
                    atr = b_tp[:, 384:896].rearrange("p (a b) -> p a b", b=128)
                    for jj in range(4):
                        for ft in range(4):
                            P.op("pe", lambda E, ft=ft, jj=jj, atr=atr: E.transpose(out=atr[:, ft, :], in_=attb[:, jj, ft * 128:(ft + 1) * 128], identity=ident[:]), r=[attbt, t_const], w=[tp_t])
                        P.op("act", lambda E, a_=a_, atr=atr, jj=jj: E.copy(out=a_[:, :, jj * 128:(jj + 1) * 128], in_=atr), r=[tp_t], w=[at_])
                        yield
                    P.dma("pool", attnT[:, ss].rearrange("(a p) t -> p a t", p=128), a_[:], at_, T["attnT"], at_)
                    yield

                def drain(gn):
                    for _ in gn:
                        pass

                n2 = sc.sb("n2", [128, 8], F32)
                n2t = sc.tok()
                P.dma("sp", n2[:], n2w[l], T["w"], n2t, n2t)
                n1n = sc.sb("n1n", [128, 8], F32)
                if l + 1 < 2:
                    P.dma("sp", n1n[:], n1w[l + 1], T["w"], n2t, n2t)
                wsf = [sc.sbt("wsf%d" % i, [128, 1024], F32) for i in range(3)]
                wsb = [sc.sbt("wsb%d" % i, [128, 1024], BF16) for i in range(3)]

                def gen_wprep():
                    chunks = []
                    for (srcw, dstw, nk) in ((wua, wuaS, 4), (wur, wurS, 8), (wo, woS, 8)):
                        for kt in range(nk):
                            chunks.append((srcw[l, kt * 128:(kt + 1) * 128, :], dstw[kt * 128:(kt + 1) * 128, :], "wmS", None))
                    for kt in range(8):
                        for c in range(4):
                            chunks.append((w1[l, kt * 128:(kt + 1) * 128, c * 1024:(c + 1) * 1024], w1dr[kt * 128:(kt + 1) * 128, c * 1024:(c + 1) * 1024], "w1s", kt))
                    for kt in range(32):
                        chunks.append((w2[l, kt * 128:(kt + 1) * 128, :], w2dr[kt * 128:(kt + 1) * 128, :], "w2s", None))
                    if l + 1 < 2:
                        for kt in range(8):
                            for c in range(6):
                                chunks.append((w_in[l + 1, kt * 128:(kt + 1) * 128, c * 900:(c + 1) * 900], winS[kt * 128:(kt + 1) * 128, c * 900:(c + 1) * 900], "winS", ("n1", kt)))
                    PRE = 2
                    for k in range(len(chunks) + PRE):
                        if k < len(chunks):
                            src, dst, dn, sk = chunks[k]
                            f_, ft_ = wsf[k % 3]
                            P.dma("pool", f_[:, 0:src.shape[-1]], src, T["w"], ft_, ft_)
                        if k - PRE >= 0:
                            src, dst, dn, sk = chunks[k - PRE]
                            f_, ft_ = wsf[(k - PRE) % 3]
                            b_, bt_ = wsb[(k - PRE) % 3]
                            wd = dst.shape[-1]
                            if sk is not None:
                                if isinstance(sk, tuple):
                                    sc_ap = n1n[:, sk[1]:sk[1] + 1]
                                else:
                                    sc_ap = n2[:, sk:sk + 1]
                                P.op("pool", lambda E, b_=b_, f_=f_, sc_ap=sc_ap, wd=wd: E.tensor_scalar(out=b_[:, 0:wd], in0=f_[:, 0:wd], scalar1=sc_ap, scalar2=1.0, op0=ALU.mult, op1=ALU.mult), r=[ft_, n2t], w=[bt_])
                            else:
                                P.op("pool", lambda E, b_=b_, f_=f_, wd=wd: E.tensor_copy(out=b_[:, 0:wd], in_=f_[:, 0:wd]), r=[ft_], w=[bt_])
                            P.dma("pool", dst, b_[:, 0:wd], bt_, T[dn], bt_)
                        yield

                gw = gen_wprep()
                gw_done = [False]

                def adv_w():
                    if not gw_done[0]:
                        try:
                            next(gw)
                        except StopIteration:
                            gw_done[0] = True

                if NSG > 0:
                    drain(gen_X(0))
                for sg in range(NSG):
                    gy = gen_Y(sg) if not ATT_DBG.get("skip_sw") else iter(())
                    gx = gen_X(sg + 1) if sg + 1 < NSG else iter(())
                    ratio = max(1, int(round((32 * sg + 110) / 100.0 * ATT_DBG.get("rs", 1.0))))
                    x_done = False
                    y_done = False
                    yk = 0
                    while not y_done:
                        for _ in range(ratio):
                            try:
                                next(gy)
                            except StopIteration:
                                y_done = True
                                break
                            yk += 1
                            if yk % 8 == 0:
                                adv_w()
                        if not x_done:
                            try:
                                next(gx)
                            except StopIteration:
                                x_done = True
                    if not x_done:
                        drain(gx)
                drain(gw)

        PH = {"inproj": phase_inproj, "mlp": phase_mlp, "rnn": phase_rnn, "merge": phase_merge, "attn": phase_attn}
        build.phases = PH
        build.ctx = dict(P=P, T=T, nc=nc, xs=xs, x_in=x_in, out_d=out_d)
        plan = build.plan
        plan(PH, build.ctx, locals())
        P.barrier()
        print("ops", P.nops, "waits", P.nwait)
    return nc, dbg


def default_plan(PH, ctx, L):
    T = ctx["T"]
    xs = ctx["xs"]
    cur, curt = ctx["x_in"], T["x_in"]
    for l in range(2):
        PH["inproj"](l, cur, curt)
        PH["attn"](l)
        PH["rnn"](l)
        PH["merge"](l, cur, curt, xs[0], T["xs0"])
        PH["mlp"](l, xs[0], T["xs0"], xs[1], T["xs1"], l == 1)
        cur, curt = xs[1], T["xs1"]


build.plan = default_plan


def host_inputs(inp, b):
    bf = ml_dtypes.bfloat16
    f = np.float32

    def pk(v):
        return np.ascontiguousarray(v.reshape(2, 8, 128).transpose(0, 2, 1)).astype(f)

    def bd(wm):
        o = np.zeros((2, 8, 128, 128), f)
        for c in range(8):
            o[:, c, 0:64, 0:64] = wm[:, 2 * c]
            o[:, c, 64:128, 64:128] = wm[:, 2 * c + 1]
        return o

    i_ = np.arange(128)
    m = {
        "x": np.ascontiguousarray(inp["x"][b]),
        "w_in": inp["w_in"],
        "n1w": pk(inp["norm1_w"]), "n2w": pk(inp["norm2_w"]),
        "fnw": np.ascontiguousarray(np.broadcast_to(inp["final_norm_w"][None, :], (128, D))).astype(f),
        "posk": np.ascontiguousarray(np.repeat(inp["cmp_pos_k"].transpose(0, 2, 1)[..., None], 2, axis=-1)),
        "posv": np.ascontiguousarray(np.repeat(inp["cmp_pos_v"].transpose(0, 2, 1)[..., None], 2, axis=-1)),
        "ckw1": np.ascontiguousarray(inp["cmp_k_w1"].reshape(2, 32, 64, 256).transpose(0, 2, 1, 3)),
        "cvw1": np.ascontiguousarray(inp["cmp_v_w1"].reshape(2, 32, 64, 256).transpose(0, 2, 1, 3)),
        "ckw2": np.ascontiguousarray(inp["cmp_k_w2"].reshape(2, 2, 128, 64).transpose(0, 2, 1, 3)),
        "cvw2": np.ascontiguousarray(inp["cmp_v_w2"].reshape(2, 2, 128, 64).transpose(0, 2, 1, 3)),
        "convw": np.ascontiguousarray(inp["conv_w"].reshape(2, 4, 8, 128).transpose(0, 3, 2, 1)),
        "convb": pk(inp["conv_b"]), "lba": pk(inp["lru_b_a"]), "lbi": pk(inp["lru_b_i"]), "llam": pk(inp["lru_lambda"]),
        "lwa": bd(inp["lru_w_a"]), "lwi": bd(inp["lru_w_i"]),
        "wua": inp["w_up_attn"], "wur": inp["w_up_rnn"], "wo": inp["w_out"], "w1": inp["mlp_w1"], "w2": inp["mlp_w2"],
        "c_ident": np.eye(128, dtype=f).astype(bf),
        "c_tric": np.where(i_[:, None] <= i_[None, :], 0.0, NEGM).astype(bf),
        "c_triw": np.where(i_[:, None] > i_[None, :], 0.0, NEGM).astype(bf),
        "c_E": (np.arange(S)[None, :] // 64 == np.arange(64)[:, None]).astype(f).astype(bf),
        "c_band": np.where((np.arange(9)[None, :] - 2) <= ((i_[:, None] + 1) // 16 - 2), 0.0, NEGM).astype(bf),
    }
    hi = (i_ >= 64).astype(np.int64)[:, None]
    jp = (np.arange(128) - 64)[None, :]
    valid = jp <= hi
    forced = jp > hi - 2
    A = np.where(valid & ~forced, 1.0, 0.0)
    Bm = np.where(valid, np.where(forced, 1e4, 0.0), -1e30)
    m["c_A"] = A.astype(f)
    m["c_B"] = Bm.astype(f)
    return {k: np.ascontiguousarray(v) for k, v in m.items()}


def kernel(**inputs):
    inp = {k: np.asarray(v) for k, v in inputs.items()}
    nc, _ = build(False)
    in_maps = [host_inputs(inp, c % 4) for c in range(8)]
    res = run_bass_kernel_spmd(nc, in_maps, core_ids=list(range(8)))
    return np.stack([np.asarray(res.results[c]["out"]) for c in range(4)], axis=0).astype(np.float32)
```
